# Optimizing a Trainium2 kernel written in Bass

```python
import jax
import jax.numpy as jnp
from jax import lax
import numpy as np

D_MODEL = 4096
BATCH = 32
SEQ = 256
DEPTH = 2
DEC_BATCH = 8
DEC_SEQ = 4096
PAST_LEN = 256

GRID_W = 64
HEAD_DIM = 128
BRANCH_W = D_MODEL // 2
N_HEADS = BRANCH_W // HEAD_DIM
MIX_W = 2 * BRANCH_W
N_AB = (DEPTH + 1) // 2
N_CD = DEPTH // 2
WIN_R = 8
WIN_C = 16
SG_CHUNK = 128
HGRN_DK = 128
HGRN_DV = HEAD_DIM
C_K = N_HEADS * HGRN_DK
HGRN_CHUNK = 32
Q_LORA = D_MODEL // 4
KV_LORA = 512
NOPE_D = 128
ROPE_D = 64
V_D = HEAD_DIM
QK_D = NOPE_D + ROPE_D
ROPE_BASE = 10000.0
Q_BLOCK = 128
EPS = 1e-6
NEG_INF = -1e30
AB_SIZES = (BRANCH_W, BRANCH_W, BRANCH_W, BRANCH_W, BRANCH_W, BRANCH_W, BRANCH_W)
CD_SIZES = (C_K, C_K, C_K, N_HEADS * HGRN_DV, N_HEADS * HGRN_DV, Q_LORA, KV_LORA, ROPE_D, N_HEADS * V_D)
AB_IN = sum(AB_SIZES)
CD_IN = sum(CD_SIZES)

kernel_name = 'hybrid_diffusion_na_sgmlp_hgrn2_mla_step'


def split_cols(p, sizes):
    return jnp.split(p, np.cumsum(sizes)[:-1].tolist(), axis=-1)


def rms_norm(x, w):
    xf = x.astype(jnp.float32)
    y = xf * lax.rsqrt(jnp.mean(xf * xf, axis=-1, keepdims=True) + EPS)
    return (y * w.astype(jnp.float32)).astype(x.dtype)


def modulate(x, norm_w, cond, w_ada, b_ada):
    mod = jax.nn.silu(cond) @ w_ada + b_ada
    shift, scale, gate = jnp.split(mod, 3, axis=-1)
    return rms_norm(x, norm_w) * (1 + scale) + shift, gate


def rope_1d(x, pos):
    half = x.shape[-1] // 2
    inv = ROPE_BASE ** (-jnp.arange(half, dtype=jnp.float32) / half)
    ang = pos.astype(jnp.float32)[:, None] * inv[None, :]
    cos = jnp.cos(ang)[None, :, None, :].astype(x.dtype)
    sin = jnp.sin(ang)[None, :, None, :].astype(x.dtype)
    x1, x2 = x[..., :half], x[..., half:]
    return jnp.concatenate([x1 * cos - x2 * sin, x1 * sin + x2 * cos], axis=-1)


def axial_rope(x):
    t = jnp.arange(x.shape[1])
    h = x.shape[-1] // 2
    return jnp.concatenate([rope_1d(x[..., :h], t // GRID_W), rope_1d(x[..., h:], t % GRID_W)], axis=-1)


def block_attend(q, k, v):
    b, lq, h, dq = q.shape
    nb = lq // Q_BLOCK
    scale = dq ** -0.5
    qb = q.reshape(b, nb, Q_BLOCK, h, dq).swapaxes(0, 1)

    def one_block(qi):
        s = jnp.einsum('bqhd,bkhd->bhqk', qi, k, preferred_element_type=jnp.float32) * scale
        p = jax.nn.softmax(s, axis=-1).astype(v.dtype)
        return jnp.einsum('bhqk,bkhd->bqhd', p, v)

    o = lax.map(one_block, qb)
    return o.swapaxes(0, 1).reshape(b, lq, h, v.shape[-1])


def na_latent(q, k, v, k_ctx, v_ctx, rpb):
    b, l, h, d = q.shape
    rows = l // GRID_W
    kr = min(WIN_R, rows)
    qg = q.reshape(b, rows, GRID_W, h, d)
    kg = k.reshape(b, rows, GRID_W, h, d)
    vg = v.reshape(b, rows, GRID_W, h, d)
    cols = jnp.arange(GRID_W)
    cstart = jnp.clip(cols - WIN_C // 2, 0, GRID_W - WIN_C)
    col_mask = (cols[None, :] >= cstart[:, None]) & (cols[None, :] < cstart[:, None] + WIN_C)
    dc_idx = jnp.clip(cols[None, :] - cols[:, None] + WIN_C - 1, 0, 2 * WIN_C - 2)
    rpb32 = rpb.astype(jnp.float32)
    scale = d ** -0.5

    def one_row(r):
        rs = jnp.clip(r - kr // 2, 0, rows - kr)
        kb = lax.dynamic_slice_in_dim(kg, rs, kr, axis=1)
        vb = lax.dynamic_slice_in_dim(vg, rs, kr, axis=1)
        qr = lax.dynamic_index_in_dim(qg, r, axis=1, keepdims=False)
        s_lat = jnp.einsum('bqhd,bikhd->bhqik', qr, kb, preferred_element_type=jnp.float32) * scale
        dr_idx = rs + jnp.arange(kr) - r + WIN_R - 1
        bias = rpb32[:, dr_idx[:, None, None], dc_idx[None, :, :]].transpose(0, 2, 1, 3)
        s_lat = jnp.where(col_mask[:, None, :], s_lat + bias, NEG_INF)
        s_ctx = jnp.einsum('bqhd,bchd->bhqc', qr, k_ctx, preferred_element_type=jnp.float32) * scale
        s = jnp.concatenate([s_lat.reshape(b, h, GRID_W, kr * GRID_W), s_ctx], axis=-1)
        p = jax.nn.softmax(s, axis=-1).astype(v.dtype)
        p_lat = p[..., :kr * GRID_W].reshape(b, h, GRID_W, kr, GRID_W)
        p_ctx = p[..., kr * GRID_W:]
        return jnp.einsum('bhqik,bikhd->bqhd', p_lat, vb) + jnp.einsum('bhqc,bchd->bqhd', p_ctx, v_ctx)

    o = lax.map(one_row, jnp.arange(rows))
    return o.transpose(1, 0, 2, 3, 4).reshape(b, l, h, d)


def ab_heads(h, w_in, q_norm, k_norm):
    b, l, _ = h.shape
    qa, ka, va, ga, ub, vb, gb = split_cols(h @ w_in, AB_SIZES)
    heads = lambda t: t.reshape(b, l, N_HEADS, HEAD_DIM)
    return rms_norm(heads(qa), q_norm), rms_norm(heads(ka), k_norm), heads(va), ga, ub, vb, gb


def spatial_gating(u, v, sg_norm, sg_w, sg_b):
    b, l, _ = u.shape
    n = l // SG_CHUNK
    u = jax.nn.gelu(u)
    v = rms_norm(jax.nn.gelu(v), sg_norm).reshape(b, n, SG_CHUNK, N_HEADS, HEAD_DIM)
    mixed = jnp.einsum('gts,bnsgc->bntgc', sg_w, v) + sg_b.T[None, None, :, :, None]
    return u * mixed.reshape(b, l, BRANCH_W)


def merge(o1, g1, o2, g2):
    b, l = g1.shape[0], g1.shape[1]
    return jnp.concatenate([o1.reshape(b, l, -1) * jax.nn.silu(g1), o2.reshape(b, l, -1) * jax.nn.silu(g2)], axis=-1)


def hgrn_gates(z, lb):
    lbf = lb.astype(jnp.float32)
    f = lbf + (1 - lbf) * jax.nn.sigmoid(z.astype(jnp.float32))
    return jnp.log(f), 1 - f


def hgrn_chunk_scan(q, k, v, logf, s0):
    b, l, h, _ = q.shape
    nc = l // HGRN_CHUNK
    to_chunks = lambda t: t.reshape(b, nc, HGRN_CHUNK, h, t.shape[-1]).transpose(1, 0, 3, 2, 4).astype(jnp.float32)
    lower = jnp.tril(jnp.ones((HGRN_CHUNK, HGRN_CHUNK), dtype=bool))

    def step(s, inp):
        qi, ki, vi, gi = inp
        a = jnp.cumsum(gi, axis=2)
        diff = jnp.where(lower[None, None, :, :, None], a[:, :, :, None, :] - a[:, :, None, :, :], -jnp.inf)
        p = jnp.einsum('bhtk,bhtsk,bhsk->bhts', qi, jnp.exp(diff), ki)
        o = jnp.einsum('bhts,bhsv->bhtv', p, vi) + jnp.einsum('bhtk,bhkv->bhtv', qi * jnp.exp(a), s)
        a_last = a[:, :, -1:, :]
        s = jnp.exp(a_last[:, :, 0, :])[..., None] * s + jnp.einsum('bhsk,bhsv->bhkv', ki * jnp.exp(a_last - a), vi)
        return s, o

    s_fin, o = lax.scan(step, s0.astype(jnp.float32), (to_chunks(q), to_chunks(k), to_chunks(v), to_chunks(logf)))
    return o.transpose(1, 0, 3, 2, 4).reshape(b, l, h, v.shape[-1]), s_fin


def hgrn_mixer(qc, f_fwd, f_bwd, ic, lb_f, lb_b, s0_f, s0_b, out_norm):
    b, l, _ = qc.shape
    heads = lambda t: t.reshape(b, l, N_HEADS, -1)
    flip = lambda t: jnp.flip(t, axis=1)
    lf_f, k_f = hgrn_gates(f_fwd, lb_f)
    lf_b, k_b = hgrn_gates(f_bwd, lb_b)
    q, i = heads(qc), heads(ic)
    o_f, s_f = hgrn_chunk_scan(q, heads(k_f), i, heads(lf_f), s0_f)
    o_b, s_b = hgrn_chunk_scan(flip(q), flip(heads(k_b)), flip(i), flip(heads(lf_b)), s0_b)
    o = rms_norm(o_f + flip(o_b), out_norm).astype(qc.dtype)
    return o.reshape(b, l, N_HEADS * HGRN_DV), jnp.stack([s_f, s_b], axis=1)


def mla_queries(cq, q_a_norm, w_q_up, q_gain, rotate):
    b, l, _ = cq.shape
    q = (rms_norm(cq, q_a_norm) @ w_q_up).reshape(b, l, N_HEADS, QK_D)
    q_nope = rms_norm(q[..., :NOPE_D], q_gain[:NOPE_D])
    q_rope = rms_norm(q[..., NOPE_D:], q_gain[NOPE_D:])
    if rotate:
        q_rope = axial_rope(q_rope)
    return jnp.concatenate([q_nope, q_rope], axis=-1)


def mla_keys(ckv_n, kr_n, w_kv_up, k_gain, rotate):
    b, l, _ = ckv_n.shape
    kv = (ckv_n @ w_kv_up).reshape(b, l, N_HEADS, NOPE_D + V_D)
    k_nope = rms_norm(kv[..., :NOPE_D], k_gain[:NOPE_D])
    k_rope = kr_n[:, :, None, :]
    if rotate:
        k_rope = axial_rope(k_rope)
    k = jnp.concatenate([k_nope, jnp.broadcast_to(k_rope, (b, l, N_HEADS, ROPE_D))], axis=-1)
    return k, kv[..., NOPE_D:]


def setup_inputs(seed: int = 0) -> dict:
    key = jax.random.key(seed)
    keys = iter(jax.random.split(key, 29))

    def nrm(shape, scale=1.0):
        return scale * jax.random.normal(next(keys), shape, jnp.float32)

    def gain(shape):
        return 1.0 + 0.1 * jax.random.normal(next(keys), shape, jnp.float32)

    return {
        'x_prompt': nrm((BATCH, SEQ, D_MODEL)),
        'x_sample': nrm((DEC_BATCH, DEC_SEQ, D_MODEL)),
        'cache_na_k': nrm((DEC_BATCH, N_AB, PAST_LEN, N_HEADS, HEAD_DIM)),
        'cache_na_v': nrm((DEC_BATCH, N_AB, PAST_LEN, N_HEADS, HEAD_DIM)),
        'state_hgrn': nrm((DEC_BATCH, N_CD, 2, N_HEADS, HGRN_DK, HGRN_DV), 0.5),
        'cache_mla_ckv': nrm((DEC_BATCH, N_CD, PAST_LEN, KV_LORA)),
        'cache_mla_krope': nrm((DEC_BATCH, N_CD, PAST_LEN, ROPE_D)),
        'c': nrm((DEC_BATCH, D_MODEL)),
        'c_ctx': nrm((D_MODEL,)),
        'norm_w': gain((DEPTH, D_MODEL)),
        'w_ada': nrm((DEPTH, D_MODEL, 3 * D_MODEL), 0.5 * D_MODEL ** -0.5),
        'b_ada': nrm((DEPTH, 3 * D_MODEL), 0.02),
        'w_out': nrm((DEPTH, MIX_W, D_MODEL), MIX_W ** -0.5),
        'w_in_ab': nrm((N_AB, D_MODEL, AB_IN), D_MODEL ** -0.5),
        'na_q_norm': gain((N_AB, HEAD_DIM)),
        'na_k_norm': gain((N_AB, HEAD_DIM)),
        'na_rpb': nrm((N_AB, N_HEADS, 2 * WIN_R - 1, 2 * WIN_C - 1), 0.1),
        'sg_norm': gain((N_AB, BRANCH_W)),
        'sg_w': nrm((N_AB, N_HEADS, SG_CHUNK, SG_CHUNK), SG_CHUNK ** -0.5),
        'sg_b': nrm((N_AB, N_HEADS, SG_CHUNK), 0.02),
        'w_in_cd': nrm((N_CD, D_MODEL, CD_IN), D_MODEL ** -0.5),
        'hgrn_lb': nrm((DEPTH, 2, C_K)),
        'hgrn_out_norm': gain((N_CD, HGRN_DV)),
        'mla_q_a_norm': gain((N_CD, Q_LORA)),
        'mla_w_q_up': nrm((N_CD, Q_LORA, N_HEADS * QK_D), Q_LORA ** -0.5),
        'mla_kv_a_norm': gain((N_CD, KV_LORA)),
        'mla_w_kv_up': nrm((N_CD, KV_LORA, N_HEADS * (NOPE_D + V_D)), KV_LORA ** -0.5),
        'mla_q_norm': gain((N_CD, QK_D)),
        'mla_k_norm': gain((N_CD, QK_D)),
    }


def reference(x_prompt, x_sample, cache_na_k, cache_na_v, state_hgrn, cache_mla_ckv, cache_mla_krope, c, c_ctx,
              norm_w, w_ada, b_ada, w_out,
              w_in_ab, na_q_norm, na_k_norm, na_rpb, sg_norm, sg_w, sg_b,
              w_in_cd, hgrn_lb, hgrn_out_norm, mla_q_a_norm, mla_w_q_up, mla_kv_a_norm, mla_w_kv_up,
              mla_q_norm, mla_k_norm):
    lb_cum = jnp.cumsum(jax.nn.softmax(hgrn_lb.astype(jnp.float32), axis=0), axis=0)
    lower_bounds = lb_cum - lb_cum[:1]
    cond_ctx = c_ctx[None, None, :]
    cond_lat = c[:, None, :]
    bp = x_prompt.shape[0]
    xp, xs = x_prompt, x_sample
    na_k_new, na_v_new, hgrn_new, ckv_new, kr_new = [], [], [], [], []
    for layer in range(DEPTH):
        j = layer // 2
        hp, gate_p = modulate(xp, norm_w[layer], cond_ctx, w_ada[layer], b_ada[layer])
        hs, gate_s = modulate(xs, norm_w[layer], cond_lat, w_ada[layer], b_ada[layer])
        if layer % 2 == 0:
            qa, ka, va, ga, ub, vb, gb = ab_heads(hp, w_in_ab[j], na_q_norm[j], na_k_norm[j])
            oa = block_attend(qa, ka, va)
            ob = spatial_gating(ub, vb, sg_norm[j], sg_w[j], sg_b[j])
            mix_p = merge(oa, ga, ob, gb)
            na_k_new.append(ka)
            na_v_new.append(va)
            qa, ka, va, ga, ub, vb, gb = ab_heads(hs, w_in_ab[j], na_q_norm[j], na_k_norm[j])
            oa = na_latent(qa, ka, va, cache_na_k[:, j], cache_na_v[:, j], na_rpb[j])
            ob = spatial_gating(ub, vb, sg_norm[j], sg_w[j], sg_b[j])
            mix_s = merge(oa, ga, ob, gb)
        else:
            lb_f, lb_b = lower_bounds[layer, 0], lower_bounds[layer, 1]
            qc, ffw, fbw, ic, gc, cq, ckv, kr, gd = split_cols(hp @ w_in_cd[j], CD_SIZES)
            zeros = jnp.zeros((bp, N_HEADS, HGRN_DK, HGRN_DV), jnp.float32)
            oc, st = hgrn_mixer(qc, ffw, fbw, ic, lb_f, lb_b, zeros, zeros, hgrn_out_norm[j])
            q = mla_queries(cq, mla_q_a_norm[j], mla_w_q_up[j], mla_q_norm[j], False)
            ckv_n = rms_norm(ckv, mla_kv_a_norm[j])
            kr_n = rms_norm(kr, mla_k_norm[j][NOPE_D:])
            k, v = mla_keys(ckv_n, kr_n, mla_w_kv_up[j], mla_k_norm[j], False)
            od = block_attend(q, k, v)
            mix_p = merge(oc, gc, od, gd)
            hgrn_new.append(st)
            ckv_new.append(ckv_n)
            kr_new.append(kr_n)
            qc, ffw, fbw, ic, gc, cq, ckv, kr, gd = split_cols(hs @ w_in_cd[j], CD_SIZES)
            oc, _ = hgrn_mixer(qc, ffw, fbw, ic, lb_f, lb_b, state_hgrn[:, j, 0], state_hgrn[:, j, 1], hgrn_out_norm[j])
            q = mla_queries(cq, mla_q_a_norm[j], mla_w_q_up[j], mla_q_norm[j], True)
            k_lat, v_lat = mla_keys(rms_norm(ckv, mla_kv_a_norm[j]), rms_norm(kr, mla_k_norm[j][NOPE_D:]),
                                    mla_w_kv_up[j], mla_k_norm[j], True)
            k_ctx, v_ctx = mla_keys(cache_mla_ckv[:, j], cache_mla_krope[:, j], mla_w_kv_up[j], mla_k_norm[j], False)
            od = block_attend(q, jnp.concatenate([k_ctx, k_lat], axis=1), jnp.concatenate([v_ctx, v_lat], axis=1))
            mix_s = merge(oc, gc, od, gd)
        xp = xp + gate_p * (mix_p @ w_out[layer])
        xs = xs + gate_s * (mix_s @ w_out[layer])
    new_na_k = jnp.stack(na_k_new, axis=1)
    new_na_v = jnp.stack(na_v_new, axis=1)
    new_hgrn_state = jnp.stack(hgrn_new, axis=1)
    new_mla_ckv = jnp.stack(ckv_new, axis=1)
    new_mla_krope = jnp.stack(kr_new, axis=1)
    return (xp, xs, new_na_k, new_na_v, new_hgrn_state, new_mla_ckv, new_mla_krope)
```

```python
import os
import numpy as np
import ml_dtypes
from contextlib import ExitStack
import concourse.bass as bass
import concourse.mybir as mybir
from concourse.bass_utils import run_bass_kernel_spmd

F32 = mybir.dt.float32
BF16 = mybir.dt.bfloat16
AF = mybir.ActivationFunctionType
ALU = mybir.AluOpType
AX = mybir.AxisListType

ENGS = ('pe', 'act', 'dve', 'pool', 'sp')
SEM_LIMIT = 8000
EPS = 1e-6
GK = 1.5957691216057308


class Rec:
    def __init__(self):
        self.call = None

    def __getattr__(self, name):
        def f(*a, **k):
            self.call = (name, a, k)
            return self
        return f


class Event:
    __slots__ = ('sv', 'eng')

    def __init__(self, eng=None):
        self.sv = None
        self.eng = eng


class Chan:
    def __init__(self, P):
        self.P = P
        self.sem = None
        self.count = 0
        self.last = None

    def bump(self, n, ev):
        if self.sem is None or self.count + n > SEM_LIMIT:
            self.sem = self.P.new_sem()
            self.count = 0
        self.count += n
        ev.sv = (self.sem, self.count)
        self.last = ev
        return ev


class Res:
    def __init__(self, P, name):
        self.P = P
        self.name = name
        self.last_write = None
        self.readers = {}
        self.chan = {}

    def dchan(self, eng):
        if eng not in self.chan:
            self.chan[eng] = self.P.get_dma_chan(eng)
        return self.chan[eng]


class Prog:
    def __init__(self, nc, stack):
        self.nc = nc
        self.stack = stack
        self.ops = {e: [] for e in ENGS}
        self.echan = {e: Chan(self) for e in ENGS}
        self.cur = {e: Event(e) for e in ENGS}
        self.dma_chans = []
        self.free_chans = {e: [] for e in ENGS}
        self.resources = []
        self.nsem = 0
        self.bar_seen = {}
        self.pending = {e: False for e in ENGS}

    def new_sem(self):
        self.nsem += 1
        return self.stack.enter_context(self.nc.semaphore(f"s{self.nsem}"))

    def get_dma_chan(self, eng):
        if self.free_chans[eng]:
            return self.free_chans[eng].pop()
        ch = Chan(self)
        self.dma_chans.append(ch)
        return ch

    def res(self, name):
        r = Res(self, name)
        self.resources.append(r)
        return r

    def _deps(self, eng, reads, writes):
        waits = []

        def add(ev):
            if ev is None:
                return
            if ev.sv is None:
                assert ev.eng == eng, (ev.eng, eng)
                return
            if ev.eng == eng and eng == 'pe':
                return
            waits.append(ev)

        for r in reads:
            add(r.last_write)
        for w in writes:
            add(w.last_write)
            for ev in w.readers.values():
                add(ev)
        return waits

    def _commit(self, ev, key, reads, writes):
        for w in writes:
            w.last_write = ev
            w.readers = {}
        for r in reads:
            r.readers[key] = ev

    def op(self, eng, fn, reads=(), writes=(), sig=True):
        waits = self._deps(eng, reads, writes)
        ev = self.cur[eng]
        inc = None
        if sig:
            self.echan[eng].bump(1, ev)
            inc = (ev.sv[0], 1)
            self.cur[eng] = Event(eng)
        self.pending[eng] = not sig
        rec = Rec()
        fn(rec)
        assert rec.call is not None
        self.ops[eng].append((rec.call, waits, inc))
        self._commit(ev, eng, reads, writes)
        return ev

    def dma(self, eng, out, in_, reads=(), writes=(), cres=None, **kw):
        waits = self._deps(eng, reads, writes)
        if cres is None:
            cres = writes[0] if writes else reads[0]
        ch = cres.dchan(eng)
        ev = Event('dma')
        ch.bump(16, ev)
        self.ops[eng].append((('dma_start', (), dict(out=out, in_=in_, **kw)), waits, (ev.sv[0], 16)))
        self._commit(ev, ch, reads, writes)
        return ev

    def barrier(self, release=True):
        evs = []
        for e in ENGS:
            if self.pending[e]:
                self.op(e, lambda eng: eng.nop())
            ch = self.echan[e]
            if ch.last is not None:
                evs.append(ch.last)
        for ch in self.dma_chans:
            if ch.last is not None:
                evs.append(ch.last)
        evs = [ev for ev in evs if self.bar_seen.get(id(ev.sv[0]), 0) < ev.sv[1]]
        for ev in evs:
            self.bar_seen[id(ev.sv[0])] = ev.sv[1]
        for e in ENGS:
            self.ops[e].append((None, list(evs), None))
        for r in self.resources:
            r.last_write = None
            r.readers = {}
            if release:
                for e_, ch_ in r.chan.items():
                    self.free_chans[e_].append(ch_)
                r.chan = {}
        if release:
            self.resources = []

    def emit(self):
        nc = self.nc
        self.barrier()
        P = self

        def run(eng_name, eng):
            known = {}
            for fn, waits, inc in P.ops[eng_name]:
                for ev in waits:
                    sem, val = ev.sv
                    if known.get(id(sem), 0) < val:
                        eng.wait_ge(sem, val)
                        known[id(sem)] = val
                if fn is not None:
                    inst = getattr(eng, fn[0])(*fn[1], **fn[2])
                    if inc is not None:
                        inst.then_inc(inc[0], inc[1])

        with nc.Block() as block:
            @block.tensor
            def _(e):
                run('pe', e)

            @block.scalar
            def _(e):
                run('act', e)

            @block.vector
            def _(e):
                run('dve', e)

            @block.gpsimd
            def _(e):
                run('pool', e)

            @block.sync
            def _(e):
                run('sp', e)


class Cfg:
    def __init__(self, D=4096, LS=4096, NPS=4):
        self.D = D
        self.LS = LS
        self.NPS = NPS
        self.SEQ = 256
        self.PAST = 256
        self.KD = D // 128
        self.BW = D // 2
        self.NH = self.BW // 128
        self.QL = D // 4
        self.KQ = self.QL // 128
        self.KVL = 512
        self.ROPE = 64
        self.T = LS + NPS * 256
        self.ROWS = LS // 64
        self.NQB = LS // 512
        self.AB = [('qa', 'fm', self.BW), ('ka', 'fm', self.BW), ('va', 'tm', self.BW), ('ga', 'fm', self.BW),
                   ('ub', 'fm', self.BW), ('vb', 'tm', self.BW), ('gb', 'fm', self.BW)]
        self.CD = [('qc', 'fm', self.BW), ('ffw', 'tm32', self.BW), ('fbw', 'tm32', self.BW), ('ic', 'tm', self.BW),
                   ('gc', 'fm', self.BW), ('cq', 'fm', self.QL), ('ckv', 'fm', 512), ('kr', 'fm', 64),
                   ('gd', 'fm', self.BW)]
        self.AB_IN = sum(s[2] for s in self.AB)
        self.CD_IN = sum(s[2] for s in self.CD)
        self.build_patterns()

    def build_patterns(self):
        ROWS = self.ROWS
        cols = np.arange(64)
        cstart = np.clip(cols - 8, 0, 48)
        colmask = (cols[None, :] >= cstart[:, None]) & (cols[None, :] < cstart[:, None] + 16)
        pats = {}
        self.pat_list = []
        self.qb_keys = []
        for qb in range(self.NQB):
            r = 8 * qb + np.arange(8)
            rs = np.clip(r - 4, 0, ROWS - 8)
            kt0 = rs.min() // 2
            kt1 = (rs.max() + 7) // 2
            lst = []
            for kt in range(kt0, kt1 + 1):
                m = np.zeros((2, 64, 8, 64), np.float32)
                for kr in range(2):
                    rp = 2 * kt + kr
                    for qr in range(8):
                        if rs[qr] <= rp < rs[qr] + 8:
                            m[kr, :, qr, :] = colmask.T
                m = m.reshape(128, 512)
                delta = 2 * kt - 8 * qb
                key = (m.tobytes(), delta)
                if key not in pats:
                    pats[key] = len(self.pat_list)
                    self.pat_list.append((m, delta))
                lst.append((kt, pats[key]))
            self.qb_keys.append(lst)
        self.NPAT = len(self.pat_list)


def host_consts(cfg):
    c = {}
    c['identb'] = np.eye(128, dtype=np.float32).astype(ml_dtypes.bfloat16)
    c['identf'] = np.eye(128, dtype=np.float32)
    c['mask01'] = np.stack([m for m, _ in cfg.pat_list], axis=1).astype(ml_dtypes.bfloat16)
    idx = np.arange(128)
    same = (idx[:, None] // 32) == (idx[None, :] // 32)
    hm = np.zeros((2, 128, 512), np.float32)
    for d in range(2):
        if d == 0:
            Mi = same & (idx[:, None] <= idx[None, :])
            Mr = same & (idx[:, None] > idx[None, :])
        else:
            Mi = same & (idx[:, None] >= idx[None, :])
            Mr = same & (idx[:, None] < idx[None, :])
        Mi = Mi.astype(np.float32)
        mid = (idx // 32) * 32 + 15
        Mc = Mi - Mi[:, mid]
        hm[d, :, 0:128] = Mi
        hm[d, :, 128:256] = Mc
        hm[d, :, 256:384] = Mr.astype(np.float32)
    c['hmat'] = np.ascontiguousarray(hm.transpose(1, 0, 2))
    t = np.arange(cfg.LS)
    inv = (10000.0 ** (-np.arange(16, dtype=np.float32) / 16)).astype(np.float32)
    cos = np.zeros((64, cfg.LS), np.float32)
    sin = np.zeros((64, cfg.LS), np.float32)
    prot = np.zeros((64, 64), np.float32)
    for j in range(64):
        b, jj = j // 32, j % 32
        i = jj % 16
        pos = (t // 64 if b == 0 else t % 64).astype(np.float32)
        ang = pos * inv[i]
        cos[j] = np.cos(ang)
        sin[j] = np.sin(ang)
        if jj < 16:
            prot[j + 16, j] = -1.0
        else:
            prot[j - 16, j] = 1.0
    c['ropecos'] = cos
    c['ropesin'] = sin
    c['prot'] = prot
    return c


def build_program(cfg, debug=()):
    D, KD, BW, NH, T, LS, NPS, QL, KQ = cfg.D, cfg.KD, cfg.BW, cfg.NH, cfg.T, cfg.LS, cfg.NPS, cfg.QL, cfg.KQ
    NPT = NPS * 256
    nc = bass.Bass("TRN2", target_bir_lowering=False)
    dr = {}

    def din(name, shape, dt=F32):
        dr[name] = nc.dram_tensor(name, list(shape), dt, kind="ExternalInput").ap()

    def dout(name, shape, dt=F32):
        dr[name] = nc.dram_tensor(name, list(shape), dt, kind="ExternalOutput").ap()

    def dscr(name, shape, dt):
        kind = "ExternalOutput" if name in debug else "Internal"
        dr[name] = nc.dram_tensor(name, list(shape), dt, kind=kind).ap()

    din('x_all', [T, D])
    din('cond2', [2, D])
    din('c_na_k', [256, BW])
    din('c_na_v', [256, BW])
    din('st_hgrn', [2, NH, 128, 128])
    din('c_ckv', [256, 512])
    din('c_kr', [256, 64])
    din('norm_w', [2, D])
    din('w_ada', [2, D, 3 * D])
    din('b_ada', [2, 3 * D])
    din('w_out', [2, D, D])
    din('w_in_ab', [D, cfg.AB_IN])
    din('na_q_norm', [128])
    din('na_k_norm', [128])
    din('rp', [NH, 23, 127])
    din('sg_norm', [BW])
    din('sg_w', [NH, 128, 128])
    din('sg_b', [NH * 128])
    din('w_in_cd', [D, cfg.CD_IN])
    din('hgrn_lb', [2, 2, BW])
    din('hgrn_out_norm', [128])
    din('mla_q_a_norm', [QL])
    din('mla_w_q_up', [QL, NH * 192])
    din('mla_kv_a_norm', [512])
    din('mla_w_kv_up', [512, NH * 256])
    din('mla_q_norm', [192])
    din('mla_k_norm', [192])
    din('identb', [128, 128], BF16)
    din('identf', [128, 128])
    din('mask01', [128, cfg.NPAT, 512], BF16)
    din('hmat', [128, 2, 512])
    din('ropecos', [64, LS])
    din('ropesin', [64, LS])
    din('prot', [64, 64])
    dout('y_all', [T, D])
    dout('o_na_k', [NPT, BW])
    dout('o_na_v', [NPT, BW])
    dout('o_hgrn', [NPS, 2, NH, 128, 128])
    dout('o_ckv', [NPT, 512])
    dout('o_kr', [NPT, 64])
    for (nm, kind, w) in cfg.AB + cfg.CD:
        if kind == 'fm':
            dscr('s_' + nm, [w, T], BF16)
        elif kind == 'tm':
            dscr('s_' + nm, [T, w], BF16)
        else:
            dscr('s_' + nm, [T, w], F32)
    dscr('s_mix', [D, T], BF16)
    dscr('s_x1', [T, D], F32)
    dscr('s_gate', [2, 2, D], F32)
    dscr('s_tf', [NH, 23, 64, 64], F32)
    dscr('s_of', [BW, T], F32)
    dscr('s_cqn', [QL, T], BF16)

    st = ExitStack()
    with st:
        P = Prog(nc, st)
        ARENA = 46 * 1024
        arena = st.enter_context(nc.sbuf_tensor("arena", [128, ARENA], F32))
        psum = [st.enter_context(nc.psum_tensor(f"ps{i}", [128, 512], F32)) for i in range(8)]
        PERS = 1024
        pos = [0]

        def alloc(n, dt=F32, np_=128):
            words = (n + 1) // 2 if dt == BF16 else n
            words = (words + 7) // 8 * 8
            a = arena[:, pos[0]:pos[0] + words]
            pos[0] += words
            assert pos[0] <= ARENA, pos[0]
            if dt == BF16:
                a = a.bitcast(BF16)[:, :n]
            else:
                a = a[:, :n]
            return a

        identb = alloc(128, BF16)
        identf = alloc(128)
        onesb = alloc(128, BF16)
        AT = [alloc(KD * 2) for _ in range(2)]
        SH = [alloc(KD * 2) for _ in range(2)]
        epsc = alloc(1)
        assert pos[0] <= PERS
        rP = P.res('pers')
        P.dma('sp', identb, dr['identb'], writes=[rP])
        P.dma('sp', identf, dr['identf'], writes=[rP])
        P.op('dve', lambda e: e.memset(onesb, 1.0), writes=[rP])
        P.op('dve', lambda e: e.memset(epsc, EPS), writes=[rP])
        P.barrier(release=False)
        P.resources = []

        bank_rr = [0]

        def new_phase():
            P.barrier()
            pos[0] = PERS
            rps = [P.res(f'ps{i}') for i in range(8)]
            return rps

        def V3(ap, a, b):
            return ap.rearrange("p (a b) -> p a b", a=a, b=b)

        cnt = [0]

        def evac(out, in_, reads, writes, eng=None):
            cnt[0] += 1
            if eng is None:
                eng = 'act' if cnt[0] % 2 else 'dve'
            if eng == 'act':
                P.op('act', lambda e: e.activation(out=out, in_=in_, func=AF.Copy), reads=reads, writes=writes)
            else:
                P.op('dve', lambda e: e.tensor_copy(out=out, in_=in_), reads=reads, writes=writes)

        def rstd_chain(dst, src, inv_n, reads, writes):
            P.op('dve', lambda e: e.tensor_scalar(out=dst, in0=src, scalar1=inv_n, scalar2=EPS, op0=ALU.mult, op1=ALU.add),
                 reads=reads, writes=writes)
            P.op('act', lambda e: e.activation(out=dst, in_=dst, func=AF.Ln), reads=writes, writes=writes)
            P.op('act', lambda e: e.activation(out=dst, in_=dst, func=AF.Exp, scale=-0.5), reads=writes, writes=writes)

        def sigmoid_ops(dst, src, scale, R):
            P.op('act', lambda e: e.activation(out=dst, in_=src, func=AF.Exp, scale=-scale), reads=R, writes=R)
            P.op('dve', lambda e: e.tensor_scalar_add(out=dst, in0=dst, scalar1=1.0), reads=R, writes=R)
            P.op('dve', lambda e: e.reciprocal(out=dst, in_=dst), reads=R, writes=R)

        def silu_ops(dst, src, tmp, R):
            sigmoid_ops(tmp, src, 1.0, R)
            P.op('dve', lambda e: e.tensor_tensor(out=dst, in0=src, in1=tmp, op=ALU.mult), reads=R, writes=R)

        def gelu_ops(dst, src, tmp, tmp2, R):
            P.op('act', lambda e: e.activation(out=tmp, in_=src, func=AF.Square), reads=R, writes=R)
            P.op('dve', lambda e: e.tensor_scalar(out=tmp, in0=tmp, scalar1=0.044715, scalar2=1.0, op0=ALU.mult, op1=ALU.add),
                 reads=R, writes=R)
            P.op('dve', lambda e: e.tensor_tensor(out=tmp, in0=tmp, in1=src, op=ALU.mult), reads=R, writes=R)
            sigmoid_ops(tmp2, tmp, GK, R)
            P.op('dve', lambda e: e.tensor_tensor(out=dst, in0=src, in1=tmp2, op=ALU.mult), reads=R, writes=R)

        stages = os.environ.get("MK_STAGES", "all").split(',')
        ON = lambda nm: stages == ['all'] or nm in stages
        for l in (range(2) if ON('mod') else []):
            rps = new_phase()
            R = P.res('m')
            cT = alloc(KD * 2)
            tmpc = alloc(KD * 2)
            sT = alloc(KD * 2, BF16)
            bT = alloc(3 * KD)
            nwT = alloc(KD)
            modT = alloc(3 * KD * 2)
            wts = [alloc(KD * 512, BF16) for _ in range(2)]
            rW = [P.res('w0'), P.res('w1')]
            for c in range(2):
                P.dma('sp', V3(cT, 2, KD)[:, c, :], dr['cond2'][c].rearrange("(k p) -> p k", p=128), writes=[R],
                      allow_slow_non_contiguous=True)
            P.dma('sp', bT, dr['b_ada'][l].rearrange("(j p) -> p j", p=128), writes=[R], allow_slow_non_contiguous=True)
            P.dma('sp', nwT, dr['norm_w'][l].rearrange("(j p) -> p j", p=128), writes=[R], allow_slow_non_contiguous=True)
            silu_ops(cT, cT, tmpc, [R])
            P.op('dve', lambda e: e.tensor_copy(out=V3(sT, KD, 2), in_=V3(cT, 2, KD).rearrange("p c k -> p k c")), reads=[R],
                 writes=[R])
            ncb = 3 * D // 512
            for cb in range(ncb):
                s = cb % 2
                wt = V3(wts[s], KD, 512)
                P.dma('pool', wt, dr['w_ada'][l][:, cb * 512:(cb + 1) * 512].rearrange("(k p) c -> p k c", p=128),
                      writes=[rW[s]])
                b = cb % 8
                for j in range(4):
                    for k in range(KD):
                        P.op('pe', lambda e, b=b, j=j, k=k, wt=wt: e.matmul(
                            psum[b][:, 2 * j:2 * j + 2], lhsT=wt[:, k, j * 128:(j + 1) * 128], rhs=V3(sT, KD, 2)[:, k, :],
                            start=(k == 0), stop=(k == KD - 1)), reads=[rW[s], R], writes=[rps[b]],
                            sig=(j == 3 and k == KD - 1))
                P.op('dve', lambda e, b=b, cb=cb: e.tensor_tensor(
                    out=V3(modT, 3 * KD, 2)[:, cb * 4:(cb + 1) * 4, :], in0=V3(psum[b][:, 0:8], 4, 2),
                    in1=bT[:, cb * 4:(cb + 1) * 4].unsqueeze(2).broadcast_to([128, 4, 2]), op=ALU.add),
                    reads=[rps[b], R], writes=[R])
            m3 = V3(modT, 3 * KD, 2)
            P.op('dve', lambda e: e.tensor_copy(out=V3(SH[l], KD, 2), in_=m3[:, 0:KD, :]), reads=[R], writes=[rP])
            P.op('dve', lambda e: e.tensor_scalar_add(out=V3(AT[l], KD, 2), in0=m3[:, KD:2 * KD, :], scalar1=1.0),
                 reads=[R], writes=[rP])
            P.op('dve', lambda e: e.tensor_tensor(out=V3(AT[l], KD, 2), in0=V3(AT[l], KD, 2),
                                                  in1=nwT.unsqueeze(2).broadcast_to([128, KD, 2]), op=ALU.mult),
                 reads=[R, rP], writes=[rP])
            gtmp = alloc(2 * KD)
            P.op('dve', lambda e: e.tensor_copy(out=V3(gtmp, 2, KD), in_=m3[:, 2 * KD:3 * KD, :].rearrange("p k c -> p c k")),
                 reads=[R], writes=[R])
            for c in range(2):
                P.dma('sp', dr['s_gate'][l][c].rearrange("(k p) -> p k", p=128), V3(gtmp, 2, KD)[:, c, :], reads=[R],
                      allow_slow_non_contiguous=True)

        def phase_inproj(l, W, segs, xsrc):
            rps = new_phase()
            TB = min(1024, LS)
            hT = alloc(KD * TB, BF16)
            hT3 = V3(hT, KD, TB)
            wts = [V3(alloc(KD * 512, BF16), KD, 512) for _ in range(2)]
            rW = [P.res('w0'), P.res('w1')]
            xts = [alloc(D) for _ in range(2)]
            rX = [P.res('x0'), P.res('x1')]
            xn = alloc(D, BF16)
            rXn = P.res('xn')
            ss = alloc(1)
            rstd = alloc(1)
            rS = P.res('ss')
            stg = [alloc(512) for _ in range(4)]
            rStg = [P.res(f'stg{i}') for i in range(4)]
            rH = P.res('hT')
            sc = [0]
            wc = [0]
            bc = [0]
            SK = os.environ.get("MK_SKIP", "")
            for tb0 in range(0, T, TB):
                for ti in range(TB // 128 if 'L' not in SK else 0):
                    tok0 = tb0 + ti * 128
                    cond = 0 if tok0 < LS else 1
                    s = ti % 2
                    P.dma('sp', xts[s], xsrc[tok0:tok0 + 128, :], writes=[rX[s]])
                    if 'A' in SK:
                        continue
                    P.op('act', lambda e, s=s: e.activation(out=xn, in_=xts[s], func=AF.Square, accum_out=ss),
                         reads=[rX[s]], writes=[rXn, rS])
                    rstd_chain(rstd, ss, 1.0 / D, [rS], [rS])
                    P.op('act', lambda e, s=s: e.activation(out=xn, in_=xts[s], func=AF.Copy, scale=rstd),
                         reads=[rX[s], rS], writes=[rXn])
                    for kg in range(0, KD if 'T' not in SK else 0, 8):
                        nk = min(8, KD - kg)
                        b = bc[0] % 8
                        bc[0] += 1
                        pb = psum[b][:].bitcast(BF16)
                        for j in range(nk):
                            P.op('pe', lambda e, pb=pb, j=j, kg=kg: e.transpose(
                                pb[:, j * 128:(j + 1) * 128], xn[:, (kg + j) * 128:(kg + j + 1) * 128], identb),
                                reads=[rXn, rP], writes=[rps[b]], sig=(j == nk - 1))
                        for j in range(nk):
                            k = kg + j
                            P.op('dve', lambda e, pb=pb, j=j, k=k, ti=ti, cond=cond: e.tensor_scalar(
                                out=hT3[:, k, ti * 128:(ti + 1) * 128], in0=pb[:, j * 128:(j + 1) * 128],
                                scalar1=AT[l][:, 2 * k + cond:2 * k + cond + 1],
                                scalar2=SH[l][:, 2 * k + cond:2 * k + cond + 1], op0=ALU.mult, op1=ALU.add),
                                reads=[rps[b], rP], writes=[rH])
                off = 0
                for (nm, kind, w) in (segs if 'W' not in SK else []):
                    if (os.environ.get("MK_SEG") and kind != os.environ.get("MK_SEG")) or (os.environ.get("MK_ONLY") and nm != os.environ.get("MK_ONLY")):
                        off += w
                        continue
                    for c0 in range(0, w, 512):
                        cw = min(512, w - c0)
                        s = wc[0] % 2
                        wc[0] += 1
                        wt = wts[s]
                        P.dma('pool', wt[:, :, :cw],
                              W[:, off + c0:off + c0 + cw].rearrange("(k p) c -> p k c", p=128), writes=[rW[s]])
                        if kind == 'fm':
                            for j0 in range(0, cw, 128):
                                m = min(128, cw - j0)
                                for t0 in range(0, TB, 512):
                                    b = bc[0] % 8
                                    bc[0] += 1
                                    for k in range(KD):
                                        P.op('pe', lambda e, b=b, m=m, k=k, j0=j0, t0=t0, wt=wt: e.matmul(
                                            psum[b][:m, :], lhsT=wt[:, k, j0:j0 + m], rhs=hT3[:, k, t0:t0 + 512],
                                            start=(k == 0), stop=(k == KD - 1)), reads=[rW[s], rH], writes=[rps[b]],
                                            sig=(k == KD - 1))
                                    q = sc[0] % 4
                                    sc[0] += 1
                                    so = stg[q].bitcast(BF16)[:m, 0:512]
                                    evac(so, psum[b][:m, :], [rps[b]], [rStg[q]])
                                    P.dma('sp', dr['s_' + nm][c0 + j0:c0 + j0 + m, tb0 + t0:tb0 + t0 + 512], so,
                                          reads=[rStg[q]])
                        else:
                            for ti in range(TB // 128):
                                tok0 = tb0 + ti * 128
                                b = bc[0] % 8
                                bc[0] += 1
                                for k in range(KD):
                                    P.op('pe', lambda e, b=b, k=k, ti=ti, wt=wt, cw=cw: e.matmul(
                                        psum[b][:, :cw], lhsT=hT3[:, k, ti * 128:(ti + 1) * 128], rhs=wt[:, k, :cw],
                                        start=(k == 0), stop=(k == KD - 1)), reads=[rW[s], rH], writes=[rps[b]],
                                        sig=(k == KD - 1))
                                q = sc[0] % 4
                                sc[0] += 1
                                if nm == 'va' and tok0 >= LS:
                                    s32 = stg[q][:, 0:cw]
                                    evac(s32, psum[b][:, :cw], [rps[b]], [rStg[q]])
                                    P.dma('sp', dr['o_na_v'][tok0 - LS:tok0 - LS + 128, c0:c0 + cw], s32, reads=[rStg[q]])
                                    q2 = sc[0] % 4
                                    sc[0] += 1
                                    so = stg[q2].bitcast(BF16)[:, 0:cw]
                                    P.op('dve', lambda e: e.tensor_copy(out=so, in_=s32), reads=[rStg[q]], writes=[rStg[q2]])
                                    P.dma('sp', dr['s_' + nm][tok0:tok0 + 128, c0:c0 + cw], so, reads=[rStg[q2]])
                                    continue
                                if kind == 'tm':
                                    so = stg[q].bitcast(BF16)[:, 0:cw]
                                else:
                                    so = stg[q][:, 0:cw]
                                evac(so, psum[b][:, :cw], [rps[b]], [rStg[q]])
                                P.dma('sp', dr['s_' + nm][tok0:tok0 + 128, c0:c0 + cw], so, reads=[rStg[q]])
                    off += w

        def phase_outproj(l, xsrc, ydst):
            rps = new_phase()
            TB = min(1024, LS)
            mixT = V3(alloc(KD * TB, BF16), KD, TB)
            rM = P.res('mix')
            wts = [V3(alloc(KD * 512, BF16), KD, 512) for _ in range(2)]
            rW = [P.res('w0'), P.res('w1')]
            gbc = [alloc(D) for _ in range(2)]
            rG = P.res('g')
            xb = [alloc(512) for _ in range(4)]
            rXb = [P.res(f'xb{i}') for i in range(4)]
            tmps = [alloc(512) for _ in range(2)]
            rTmp = [P.res('t0'), P.res('t1')]
            for c in range(2):
                P.dma('sp', gbc[c], dr['s_gate'][l][c:c + 1, :].partition_broadcast(128), writes=[rG])
            wc = 0
            bc = 0
            xc = 0
            for tb0 in range(0, T, TB):
                P.dma('sp', mixT, dr['s_mix'][:, tb0:tb0 + TB].rearrange("(k p) t -> p k t", p=128), writes=[rM])
                for c0 in range(0, D, 512):
                    s = wc % 2
                    wc += 1
                    wt = wts[s]
                    P.dma('pool', wt, dr['w_out'][l][:, c0:c0 + 512].rearrange("(k p) c -> p k c", p=128), writes=[rW[s]])
                    for ti in range(TB // 128):
                        tok0 = tb0 + ti * 128
                        cond = 0 if tok0 < LS else 1
                        b = bc % 8
                        bc += 1
                        q = xc % 4
                        xc += 1
                        P.dma('sp', xb[q], xsrc[tok0:tok0 + 128, c0:c0 + 512], writes=[rXb[q]])
                        for k in range(KD):
                            P.op('pe', lambda e, b=b, k=k, ti=ti, wt=wt: e.matmul(
                                psum[b][:], lhsT=mixT[:, k, ti * 128:(ti + 1) * 128], rhs=wt[:, k, :],
                                start=(k == 0), stop=(k == KD - 1)), reads=[rW[s], rM], writes=[rps[b]], sig=(k == KD - 1))
                        P.op('dve', lambda e: e.tensor_tensor(out=tmps[q % 2], in0=psum[b][:], in1=gbc[cond][:, c0:c0 + 512],
                                                              op=ALU.mult), reads=[rps[b], rG], writes=[rTmp[q % 2]])
                        P.op('dve', lambda e: e.tensor_tensor(out=xb[q], in0=tmps[q % 2], in1=xb[q], op=ALU.add),
                             reads=[rTmp[q % 2]], writes=[rXb[q]])
                        P.dma('sp', ydst[tok0:tok0 + 128, c0:c0 + 512], xb[q], reads=[rXb[q]])

        def fm_norm(dst, src, gain, np_, n, R, rps, b, sq, rstd, extra_dst=None):
            P.op('act', lambda e: e.activation(out=sq[:np_, :n], in_=src, func=AF.Square), reads=R, writes=R)
            P.op('pe', lambda e: e.matmul(psum[b][:np_, :n], lhsT=onesb[:np_, :np_], rhs=sq[:np_, :n], start=True, stop=True),
                 reads=R + [rP], writes=[rps[b]])
            rstd_chain(rstd[:np_, :n], psum[b][:np_, :n], 1.0 / np_, [rps[b]] + R, R)
            P.op('dve', lambda e: e.scalar_tensor_tensor(out=dst, in0=src, scalar=gain, in1=rstd[:np_, :n],
                                                         op0=ALU.mult, op1=ALU.mult), reads=R + [rP], writes=R)
            if extra_dst is not None:
                P.op('dve', lambda e: e.scalar_tensor_tensor(out=extra_dst, in0=src, scalar=gain, in1=rstd[:np_, :n],
                                                             op0=ALU.mult, op1=ALU.mult), reads=R + [rP], writes=R)

        def attn_block(qT, q2T, keys, nq, o32, R, rps, pts, rPt, rec):
            n = len(keys)
            for i, (kT, k2T, v, mask) in enumerate(keys):
                sb = i % 2
                P.op('pe', lambda e, kT=kT, sb=sb: e.matmul(psum[sb][:, :nq], lhsT=kT, rhs=qT, start=True,
                                                            stop=(k2T is None)), reads=R, writes=[rps[sb]],
                     sig=(k2T is None))
                if k2T is not None:
                    P.op('pe', lambda e, k2T=k2T, sb=sb: e.matmul(psum[sb][:, :nq], lhsT=k2T, rhs=q2T, start=False,
                                                                  stop=True), reads=R, writes=[rps[sb]])
                pt = pts[sb]
                P.op('act', lambda e, pt=pt, sb=sb: e.activation(out=pt[:, :nq], in_=psum[sb][:, :nq], func=AF.Exp),
                     reads=[rps[sb]], writes=[rPt[sb]])
                if mask is not None:
                    P.op('dve', lambda e, pt=pt, mask=mask: e.tensor_tensor(out=pt[:, :nq], in0=pt[:, :nq], in1=mask,
                                                                            op=ALU.mult), reads=R + [rPt[sb]],
                         writes=[rPt[sb]])
                P.op('pe', lambda e, v=v, pt=pt, i=i: e.matmul(psum[2][:, :nq], lhsT=v, rhs=pt[:, :nq], start=(i == 0),
                                                               stop=(i == n - 1)), reads=R + [rPt[sb]], writes=[rps[2]],
                     sig=False)
                P.op('pe', lambda e, pt=pt, i=i: e.matmul(psum[3][:, :nq], lhsT=onesb, rhs=pt[:, :nq], start=(i == 0),
                                                          stop=(i == n - 1)), reads=[rPt[sb], rP], writes=[rps[3]])
            P.op('dve', lambda e: e.reciprocal(out=rec[:, :nq], in_=psum[3][:, :nq]), reads=[rps[3]], writes=R)
            P.op('dve', lambda e: e.tensor_tensor(out=o32, in0=psum[2][:, :nq], in1=rec[:, :nq], op=ALU.mult),
                 reads=[rps[2], rps[3]] + R, writes=R)

        def phase_mixA():
            rps = new_phase()
            R = P.res('a')
            NP = cfg.NPAT
            mask01 = V3(alloc(NP * 512, BF16), NP, 512)
            P.dma('sp', mask01, dr['mask01'], writes=[R])
            qg = alloc(1)
            kg = alloc(1)
            P.dma('sp', qg, dr['na_q_norm'].rearrange("(p o) -> p o", o=1), writes=[R])
            P.dma('sp', kg, dr['na_k_norm'].rearrange("(p o) -> p o", o=1), writes=[R])
            P.op('dve', lambda e: e.tensor_scalar_mul(out=qg, in0=qg, scalar1=128.0 ** -0.5), reads=[R], writes=[R])
            rTF = P.res('tf')
            for ck in range(64):
                P.dma('sp', dr['s_tf'][:, :, ck, :], dr['rp'][:, :, 63 - ck:63 - ck + 64], writes=[rTF])
            qraw = alloc(T, BF16)
            kraw = alloc(T, BF16)
            qn = alloc(T, BF16)
            kn = alloc(T, BF16)
            sg = alloc(T, BF16)
            vt = V3(alloc(T, BF16), T // 128, 128)
            kc32 = V3(alloc(256, BF16), 2, 128)
            kcT = alloc(256, BF16)
            vc = V3(alloc(256, BF16), 2, 128)
            stage32 = V3(alloc(NP * 512), NP, 512)
            EB = V3(alloc(NP * 512, BF16), NP, 512)
            sq = alloc(512, BF16)
            rstd = alloc(512)
            knf = alloc(512)
            tmp = alloc(512)
            pts = [alloc(512, BF16) for _ in range(2)]
            rPt = [P.res('pt0'), P.res('pt1')]
            rec = alloc(512)
            o32 = alloc(512)
            mst = alloc(512, BF16)
            rMst = P.res('mst')
            tst = alloc(128)
            rTst = P.res('tst')
            rL = P.res('loads')
            rE = P.res('eb')
            for h in range(NH):
                hs = slice(h * 128, (h + 1) * 128)
                P.dma('sp', qraw, dr['s_qa'][hs, :], writes=[rL])
                P.dma('sp', kraw, dr['s_ka'][hs, :], writes=[rL])
                P.dma('sp', sg, dr['s_ga'][hs, :], writes=[rL])
                P.dma('sp', vt, dr['s_va'][:, hs].rearrange("(n p) c -> p n c", p=128), writes=[rL])
                P.dma('pool', kc32, dr['c_na_k'][:, hs].rearrange("(n p) c -> p n c", p=128), writes=[rL])
                P.dma('pool', vc, dr['c_na_v'][:, hs].rearrange("(n p) c -> p n c", p=128), writes=[rL])
                for n_ in range(2):
                    pb = psum[4][:].bitcast(BF16)
                    P.op('pe', lambda e, n_=n_, pb=pb: e.transpose(pb[:, n_ * 128:(n_ + 1) * 128], kc32[:, n_, :], identb),
                         reads=[rL, rP], writes=[rps[4]])
                P.op('dve', lambda e: e.tensor_copy(out=kcT, in_=psum[4][:].bitcast(BF16)[:, 0:256]), reads=[rps[4]],
                     writes=[rL])
                for t0 in range(0, T, 512):
                    silu_ops(sg[:, t0:t0 + 512], sg[:, t0:t0 + 512], tmp, [rL, R])
                for t0 in range(0, T, 512):
                    fm_norm(qn[:, t0:t0 + 512], qraw[:, t0:t0 + 512], qg, 128, 512, [rL, R], rps, 5, sq, rstd)
                    fm_norm(kn[:, t0:t0 + 512], kraw[:, t0:t0 + 512], kg, 128, 512, [rL, R], rps, 6, sq, rstd,
                            extra_dst=(knf if t0 >= LS else None))
                    if t0 >= LS:
                        for j in range(4):
                            P.op('pe', lambda e, j=j: e.transpose(psum[7][:, j * 128:(j + 1) * 128],
                                                                  knf[:, j * 128:(j + 1) * 128], identf),
                                 reads=[rL, R, rP], writes=[rps[7]])
                            P.op('act', lambda e, j=j: e.activation(out=tst, in_=psum[7][:, j * 128:(j + 1) * 128],
                                                                    func=AF.Copy), reads=[rps[7]], writes=[rTst])
                            P.dma('sp', dr['o_na_k'][t0 - LS + j * 128:t0 - LS + (j + 1) * 128, hs], tst, reads=[rTst])
                for pi, (_, delta) in enumerate(cfg.pat_list):
                    for kr in range(2):
                        e0 = 11 - delta - kr
                        P.dma('sp', stage32[kr * 64:(kr + 1) * 64, pi, :].rearrange("p (a b) -> p a b", a=8, b=64),
                              dr['s_tf'][h, e0:e0 + 8, :, :].rearrange("e ck cq -> ck e cq"), reads=[rTF], writes=[rE])
                for pi in range(NP):
                    P.op('act', lambda e, pi=pi: e.activation(out=stage32[:, pi, :], in_=stage32[:, pi, :], func=AF.Exp),
                         reads=[rE], writes=[rE])
                    P.op('dve', lambda e, pi=pi: e.tensor_tensor(out=EB[:, pi, :], in0=stage32[:, pi, :],
                                                                 in1=mask01[:, pi, :], op=ALU.mult), reads=[rE, R],
                         writes=[rE])
                RR = [rL, R, rE]
                for qb in range(cfg.NQB):
                    keys = []
                    for (kt, pi) in cfg.qb_keys[qb]:
                        keys.append((kn[:, kt * 128:(kt + 1) * 128], None, vt[:, kt, :], EB[:, pi, :]))
                    for n_ in range(2):
                        keys.append((kcT[:, n_ * 128:(n_ + 1) * 128], None, vc[:, n_, :], None))
                    attn_block(qn[:, qb * 512:(qb + 1) * 512], None, keys, 512, o32, RR, rps, pts, rPt, rec)
                    P.op('dve', lambda e, qb=qb: e.tensor_tensor(out=mst, in0=o32, in1=sg[:, qb * 512:(qb + 1) * 512],
                                                                 op=ALU.mult), reads=RR, writes=[rMst])
                    P.dma('sp', dr['s_mix'][hs, qb * 512:(qb + 1) * 512], mst, reads=[rMst])
                for s_ in range(NPS):
                    t0 = LS + s_ * 256
                    keys = [(kn[:, t0 + n_ * 128:t0 + (n_ + 1) * 128], None, vt[:, t0 // 128 + n_, :], None)
                            for n_ in range(2)]
                    attn_block(qn[:, t0:t0 + 256], None, keys, 256, o32[:, :256], RR, rps, pts, rPt, rec)
                    P.op('dve', lambda e, t0=t0: e.tensor_tensor(out=mst[:, :256], in0=o32[:, :256], in1=sg[:, t0:t0 + 256],
                                                                 op=ALU.mult), reads=RR, writes=[rMst])
                    P.dma('sp', dr['s_mix'][hs, t0:t0 + 256], mst[:, :256], reads=[rMst])

        def phase_mixB():
            rps = new_phase()
            R = P.res('b')
            NB = NH * 128
            sgw32 = V3(alloc(NB), NH, 128)
            sgwb = V3(alloc(NB, BF16), NH, 128)
            sgwT = V3(alloc(NB, BF16), NH, 128)
            sgb = alloc(NB)
            sgn = alloc(BW)
            P.dma('sp', sgw32, dr['sg_w'].rearrange("g t s -> t g s"), writes=[R])
            P.dma('sp', sgb, dr['sg_b'].rearrange("(o n) -> o n", o=1).partition_broadcast(128), writes=[R])
            P.dma('sp', sgn, dr['sg_norm'].rearrange("(o n) -> o n", o=1).partition_broadcast(128), writes=[R])
            P.op('dve', lambda e: e.tensor_copy(out=sgwb, in_=sgw32), reads=[R], writes=[R])
            for g in range(NH):
                pb = psum[0][:].bitcast(BF16)
                P.op('pe', lambda e, g=g, pb=pb: e.transpose(pb[:, 0:128], sgwb[:, g, :], identb), reads=[R, rP],
                     writes=[rps[0]])
                P.op('dve', lambda e, g=g, pb=pb: e.tensor_copy(out=sgwT[:, g, :], in_=pb[:, 0:128]), reads=[rps[0]],
                     writes=[R])
            vb = alloc(BW, BF16)
            gv = alloc(BW)
            t1 = alloc(BW)
            t2 = alloc(BW)
            vn = alloc(BW, BF16)
            uT = alloc(NB, BF16)
            gT = alloc(NB, BF16)
            gu = alloc(NB)
            ss = alloc(1)
            rstd = alloc(1)
            mo = alloc(NB, BF16)
            rMo = P.res('mo')
            rL = P.res('ld')
            RR = [R, rL]
            for ti in range(T // 128):
                tk = slice(ti * 128, (ti + 1) * 128)
                P.dma('sp', vb, dr['s_vb'][tk, :], writes=[rL])
                P.dma('sp', V3(uT, NH, 128), dr['s_ub'][:, tk].rearrange("(g c) t -> c g t", c=128), writes=[rL])
                P.dma('sp', V3(gT, NH, 128), dr['s_gb'][:, tk].rearrange("(g c) t -> c g t", c=128), writes=[rL])
                gelu_ops(gv, vb, t1, t2, RR)
                P.op('act', lambda e: e.activation(out=t1, in_=gv, func=AF.Square, accum_out=ss), reads=RR, writes=RR)
                rstd_chain(rstd, ss, 1.0 / BW, RR, RR)
                P.op('dve', lambda e: e.scalar_tensor_tensor(out=vn, in0=gv, scalar=rstd, in1=sgn, op0=ALU.mult,
                                                             op1=ALU.mult), reads=RR, writes=RR)
                gelu_ops(gu, uT, t1[:, :NB], t2[:, :NB], RR)
                for g in range(NH):
                    b = (g * 128) // 512
                    P.op('pe', lambda e, g=g, b=b: e.matmul(psum[b][:, (g * 128) % 512:(g * 128) % 512 + 128],
                                                            lhsT=vn[:, g * 128:(g + 1) * 128], rhs=sgwT[:, g, :],
                                                            start=True, stop=True), reads=RR, writes=[rps[b]])
                for b in range((NB + 511) // 512):
                    w = min(512, NB - b * 512)
                    cs = slice(b * 512, b * 512 + w)
                    P.op('dve', lambda e, b=b, w=w, cs=cs: e.tensor_tensor(out=t1[:, cs], in0=psum[b][:, :w], in1=sgb[:, cs],
                                                                           op=ALU.add), reads=[rps[b]] + RR, writes=RR)
                P.op('dve', lambda e: e.tensor_tensor(out=gu, in0=gu, in1=t1[:, :NB], op=ALU.mult), reads=RR, writes=RR)
                silu_ops(t1[:, :NB], gT, t2[:, :NB], RR)
                P.op('dve', lambda e: e.tensor_tensor(out=mo, in0=gu, in1=t1[:, :NB], op=ALU.mult), reads=RR, writes=[rMo])
                P.dma('sp', dr['s_mix'][BW:2 * BW, tk].rearrange("(g c) t -> c g t", c=128), V3(mo, NH, 128), reads=[rMo])

        def phase_hgrn():
            rps = new_phase()
            R = P.res('c')
            hmat = V3(alloc(1024), 2, 512)
            P.dma('sp', hmat, dr['hmat'], writes=[R])
            maskb = V3(alloc(256, BF16), 2, 128)
            P.op('dve', lambda e: e.tensor_copy(out=maskb, in_=hmat[:, :, 0:128]), reads=[R], writes=[R])
            lbb = [alloc(BW) for _ in range(2)]
            oml = [alloc(BW) for _ in range(2)]
            tl = alloc(BW)
            og = alloc(1)
            P.dma('sp', og, dr['hgrn_out_norm'].rearrange("(p o) -> p o", o=1), writes=[R])
            for d in range(2):
                P.dma('sp', lbb[d], dr['hgrn_lb'][1, d:d + 1, :].partition_broadcast(128), writes=[R])
                P.dma('sp', tl, dr['hgrn_lb'][0, d:d + 1, :].partition_broadcast(128), writes=[R])
                P.op('dve', lambda e, d=d: e.tensor_tensor(out=lbb[d], in0=lbb[d], in1=tl, op=ALU.subtract), reads=[R],
                     writes=[R])
                sigmoid_ops(lbb[d], lbb[d], 1.0, [R])
                P.op('dve', lambda e, d=d: e.tensor_scalar(out=oml[d], in0=lbb[d], scalar1=-1.0, scalar2=1.0, op0=ALU.mult,
                                                           op1=ALU.add), reads=[R], writes=[R])
            z = alloc(BW)
            f = alloc(BW)
            gl = alloc(BW)
            kk = alloc(BW)
            khat = alloc(BW, BF16)
            ktil = alloc(BW, BF16)
            vt = alloc(BW, BF16)
            qraw = V3(alloc(BW, BF16), NH, 128)
            gcT = V3(alloc(BW, BF16), NH, 128)
            S = V3(alloc(NH * 128), NH, 128)
            EE = alloc(256)
            qinc = alloc(128)
            qtil = alloc(128, BF16)
            ktT = alloc(128, BF16)
            pTm = alloc(128, BF16)
            oall = V3(alloc(BW), NH, 128)
            ofl = V3(alloc(BW), NH, 128)
            sqb = alloc(BW, BF16)
            mo = alloc(BW, BF16)
            rMo = P.res('mo')
            rL = P.res('ld')
            rS = P.res('S')
            rT = P.res('tm')
            rF = P.res('fmh')
            rO = P.res('oall')
            nbk = (BW + 511) // 512
            seqs = [(0, LS // 128, None)] + [((LS + s_ * 256) // 128, 2, s_) for s_ in range(NPS)]
            for d in range(2):
                zname = 's_ffw' if d == 0 else 's_fbw'
                Mi = hmat[:, d, 0:128]
                MiMc = hmat[:, d, 0:256]
                Mc = hmat[:, d, 128:256]
                Mr = hmat[:, d, 256:384]
                for (tile0, ntl, ps_idx) in seqs:
                    if ps_idx is None:
                        P.dma('sp', S, dr['st_hgrn'][d].rearrange("h k v -> k h v"), writes=[rS])
                    else:
                        P.op('dve', lambda e: e.memset(S, 0.0), writes=[rS])
                    order = range(ntl) if d == 0 else range(ntl - 1, -1, -1)
                    for tl_ in order:
                        ti = tile0 + tl_
                        tk = slice(ti * 128, (ti + 1) * 128)
                        P.dma('sp', z, dr[zname][tk, :], writes=[rL])
                        P.dma('sp', vt, dr['s_ic'][tk, :], writes=[rL])
                        P.dma('sp', qraw, dr['s_qc'][:, tk].rearrange("(h k) t -> k h t", k=128), writes=[rL])
                        if d == 1:
                            P.dma('sp', gcT, dr['s_gc'][:, tk].rearrange("(h k) t -> k h t", k=128), writes=[rL])
                            P.dma('sp', ofl, dr['s_of'][:, tk].rearrange("(h k) t -> k h t", k=128), writes=[rL])
                        RT = [rL, R, rT]
                        sigmoid_ops(f, z, 1.0, RT)
                        P.op('dve', lambda e, d=d: e.tensor_tensor(out=f, in0=f, in1=oml[d], op=ALU.mult), reads=RT, writes=RT)
                        P.op('dve', lambda e, d=d: e.tensor_tensor(out=f, in0=f, in1=lbb[d], op=ALU.add), reads=RT, writes=RT)
                        P.op('act', lambda e: e.activation(out=gl, in_=f, func=AF.Ln), reads=RT, writes=RT)
                        P.op('dve', lambda e: e.tensor_scalar(out=kk, in0=f, scalar1=-1.0, scalar2=1.0, op0=ALU.mult,
                                                              op1=ALU.add), reads=RT, writes=RT)
                        for (Mx, dst, sc_) in ((Mr, khat, 1.0), (Mc, ktil, -1.0)):
                            for b in range(nbk):
                                w = min(512, BW - b * 512)
                                P.op('pe', lambda e, b=b, w=w, Mx=Mx: e.matmul(psum[b][:, :w], lhsT=Mx,
                                                                              rhs=gl[:, b * 512:b * 512 + w], start=True,
                                                                              stop=True), reads=RT, writes=[rps[b]])
                                P.op('act', lambda e, b=b, w=w, sc_=sc_: e.activation(out=z[:, b * 512:b * 512 + w],
                                                                                     in_=psum[b][:, :w], func=AF.Exp,
                                                                                     scale=sc_), reads=[rps[b]] + RT,
                                     writes=RT)
                            P.op('dve', lambda e, dst=dst: e.tensor_tensor(out=dst, in0=kk, in1=z, op=ALU.mult), reads=RT,
                                 writes=RT)
                        RF = [rL, R, rT, rF]
                        for h in range(NH):
                            hs = slice(h * 128, (h + 1) * 128)
                            P.op('pe', lambda e, hs=hs: e.matmul(psum[4][:, 0:256], lhsT=gl[:, hs], rhs=MiMc, start=True,
                                                                 stop=True), reads=RT, writes=[rps[4]])
                            P.op('act', lambda e: e.activation(out=EE, in_=psum[4][:, 0:256], func=AF.Exp), reads=[rps[4]] + RF,
                                 writes=RF)
                            P.op('dve', lambda e, h=h: e.tensor_tensor(out=qinc, in0=qraw[:, h, :], in1=EE[:, 0:128],
                                                                       op=ALU.mult), reads=RF, writes=RF)
                            P.op('dve', lambda e, h=h: e.tensor_tensor(out=qtil, in0=qraw[:, h, :], in1=EE[:, 128:256],
                                                                       op=ALU.mult), reads=RF, writes=RF)
                            pb = psum[5][:].bitcast(BF16)
                            P.op('pe', lambda e, hs=hs, pb=pb: e.transpose(pb[:, 0:128], ktil[:, hs], identb), reads=RT + [rP],
                                 writes=[rps[5]])
                            P.op('act', lambda e, pb=pb: e.activation(out=ktT, in_=pb[:, 0:128], func=AF.Copy),
                                 reads=[rps[5]] + RF, writes=RF)
                            P.op('pe', lambda e: e.matmul(psum[5][:, 128:256], lhsT=ktT, rhs=qtil, start=True, stop=True),
                                 reads=RF, writes=[rps[5]])
                            P.op('dve', lambda e, d=d: e.tensor_tensor(out=pTm, in0=psum[5][:, 128:256], in1=maskb[:, d, :],
                                                                       op=ALU.mult), reads=[rps[5]] + RF, writes=RF)
                            P.op('pe', lambda e, hs=hs: e.matmul(psum[6][:, 0:128], lhsT=vt[:, hs], rhs=pTm, start=True,
                                                                 stop=False), reads=RF, writes=[rps[6]], sig=False)
                            corder = range(4) if d == 0 else range(3, -1, -1)
                            for ci, c in enumerate(corder):
                                cs = slice(c * 32, (c + 1) * 32)
                                P.op('pe', lambda e, h=h, cs=cs, ci=ci: e.matmul(psum[6][:, cs], lhsT=S[:, h, :],
                                                                                 rhs=qinc[:, cs], start=False,
                                                                                 stop=(ci == 3)), reads=RF + [rS],
                                     writes=[rps[6]], sig=True)
                                P.op('pe', lambda e, hs=hs, cs=cs, c=c: e.matmul(psum[7][:, 0:128], lhsT=khat[cs, hs],
                                                                                 rhs=vt[cs, hs], start=True, stop=True,
                                                                                 tile_position=(c * 32, 0)), reads=RT,
                                     writes=[rps[7]])
                                col = c * 32 + 31 if d == 0 else c * 32
                                P.op('dve', lambda e, h=h, col=col: e.scalar_tensor_tensor(
                                    out=S[:, h, :], in0=S[:, h, :], scalar=EE[:, col:col + 1], in1=psum[7][:, 0:128],
                                    op0=ALU.mult, op1=ALU.add), reads=[rps[7]] + RF, writes=[rS])
                            P.op('act', lambda e, h=h: e.activation(out=oall[:, h, :], in_=psum[6][:, 0:128], func=AF.Copy),
                                 reads=[rps[6]], writes=[rO])
                        if d == 0:
                            P.dma('sp', dr['s_of'][:, tk].rearrange("(h k) t -> k h t", k=128), oall, reads=[rO])
                        else:
                            RO = [rO, rL, R]
                            oa2 = oall.rearrange("p a b -> p (a b)")
                            P.op('dve', lambda e: e.tensor_tensor(out=oall, in0=oall, in1=ofl, op=ALU.add), reads=RO, writes=RO)
                            P.op('act', lambda e: e.activation(out=sqb, in_=oa2, func=AF.Square), reads=RO, writes=RO)
                            for b in range(nbk):
                                w = min(512, BW - b * 512)
                                P.op('pe', lambda e, b=b, w=w: e.matmul(psum[b][:, :w], lhsT=onesb, rhs=sqb[:, b * 512:b * 512 + w],
                                                                        start=True, stop=True), reads=RO + [rP], writes=[rps[b]])
                                rstd_chain(f[:, b * 512:b * 512 + w], psum[b][:, :w], 1.0 / 128, [rps[b]] + RO + [rT], RO + [rT])
                            P.op('dve', lambda e: e.scalar_tensor_tensor(out=oa2, in0=oa2, scalar=og, in1=f, op0=ALU.mult,
                                                                         op1=ALU.mult), reads=RO + [rT], writes=RO)
                            g2 = gcT.rearrange("p a b -> p (a b)")
                            silu_ops(gl, g2, kk, RO + [rT])
                            P.op('dve', lambda e: e.tensor_tensor(out=mo, in0=oa2, in1=gl, op=ALU.mult), reads=RO + [rT],
                                 writes=[rMo])
                            P.dma('sp', dr['s_mix'][0:BW, tk].rearrange("(h k) t -> k h t", k=128), V3(mo, NH, 128),
                                  reads=[rMo])
                    if ps_idx is not None:
                        P.dma('sp', dr['o_hgrn'][ps_idx, d].rearrange("h k v -> k h v"), S, reads=[rS])

        def phase_mla():
            rps = new_phase()
            R = P.res('d')
            NK = 256 + T
            ckvT = V3(alloc(4 * NK, BF16), 4, NK)
            k2T = alloc(NK, BF16)
            qag = alloc(KQ)
            kvag = alloc(4)
            qng = alloc(1)
            qrg = alloc(1)
            kng = alloc(1)
            krg = alloc(1)
            prot = alloc(64)
            with_nc = dict(allow_slow_non_contiguous=True)
            P.dma('sp', qag, dr['mla_q_a_norm'].rearrange("(k p) -> p k", p=128), writes=[R], **with_nc)
            P.dma('sp', kvag, dr['mla_kv_a_norm'].rearrange("(k p) -> p k", p=128), writes=[R], **with_nc)
            P.dma('sp', qng, dr['mla_q_norm'][0:128].rearrange("(p o) -> p o", o=1), writes=[R])
            P.dma('sp', qrg[:64], dr['mla_q_norm'][128:192].rearrange("(p o) -> p o", o=1), writes=[R])
            P.dma('sp', kng, dr['mla_k_norm'][0:128].rearrange("(p o) -> p o", o=1), writes=[R])
            P.dma('sp', krg[:64], dr['mla_k_norm'][128:192].rearrange("(p o) -> p o", o=1), writes=[R])
            P.dma('sp', prot[:64], dr['prot'], writes=[R])
            P.op('dve', lambda e: e.tensor_scalar_mul(out=qng, in0=qng, scalar1=192.0 ** -0.5), reads=[R], writes=[R])
            P.op('dve', lambda e: e.tensor_scalar_mul(out=qrg[:64], in0=qrg[:64], scalar1=192.0 ** -0.5), reads=[R], writes=[R])
            raw = alloc(max(KQ, 4) * 512, BF16)
            sq = alloc(max(KQ, 4) * 512, BF16)
            rstd = alloc(512)
            cqn = V3(alloc(KQ * 512, BF16), KQ, 512)
            c32 = alloc(512)
            x32 = alloc(512)
            cosb = alloc(512)
            sinb = alloc(512)
            tst = alloc(512)
            rTst = P.res('tst')
            rL = P.res('ld')
            rCq = P.res('cqn')
            RR = [R, rL]
            cc = V3(alloc(1024, BF16), 2, 512)
            ck = V3(alloc(128, BF16), 2, 64)
            P.dma('pool', cc, dr['c_ckv'].rearrange("(n p) c -> p n c", p=128), writes=[rL])
            P.dma('pool', ck, dr['c_kr'].rearrange("(n p) c -> p n c", p=128), writes=[rL])
            for n_ in range(2):
                pb = psum[0][:].bitcast(BF16)
                for k in range(4):
                    P.op('pe', lambda e, n_=n_, k=k, pb=pb: e.transpose(pb[:, k * 128:(k + 1) * 128],
                                                                        cc[:, n_, k * 128:(k + 1) * 128], identb),
                         reads=[rL, rP], writes=[rps[0]])
                P.op('dve', lambda e, n_=n_, pb=pb: e.tensor_copy(out=ckvT[:, :, n_ * 128:(n_ + 1) * 128],
                                                                  in_=V3(pb[:, 0:512], 4, 128)), reads=[rps[0]], writes=[R])
                P.op('pe', lambda e, n_=n_, pb=pb: e.transpose(pb[:64, 512:640], ck[:, n_, :], identb), reads=[rL, rP],
                     writes=[rps[0]])
                P.op('dve', lambda e, n_=n_, pb=pb: e.tensor_copy(out=k2T[:64, n_ * 128:(n_ + 1) * 128], in_=pb[:64, 512:640]),
                     reads=[rps[0]], writes=[R])

            def rope(dst, src, t0, n):
                P.dma('sp', cosb[:64, :n], dr['ropecos'][:, t0:t0 + n], writes=[rL])
                P.dma('sp', sinb[:64, :n], dr['ropesin'][:, t0:t0 + n], writes=[rL])
                P.op('pe', lambda e: e.matmul(psum[7][:64, :n], lhsT=prot[:64, :64], rhs=src, start=True, stop=True),
                     reads=RR, writes=[rps[7]])
                P.op('dve', lambda e: e.tensor_tensor(out=sinb[:64, :n], in0=psum[7][:64, :n], in1=sinb[:64, :n], op=ALU.mult),
                     reads=[rps[7], rL], writes=[rL])
                P.op('dve', lambda e: e.tensor_tensor(out=cosb[:64, :n], in0=src, in1=cosb[:64, :n], op=ALU.mult),
                     reads=RR, writes=[rL])
                P.op('dve', lambda e: e.tensor_tensor(out=dst, in0=cosb[:64, :n], in1=sinb[:64, :n], op=ALU.add),
                     reads=[rL], writes=RR)

            for t0 in range(0, T, 512):
                is_p = t0 >= LS
                r3 = V3(raw[:, :KQ * 512], KQ, 512)
                P.dma('sp', r3, dr['s_cq'][:, t0:t0 + 512].rearrange("(k p) t -> p k t", p=128), writes=[rL])
                P.op('act', lambda e: e.activation(out=sq[:, :KQ * 512], in_=raw[:, :KQ * 512], func=AF.Square), reads=RR, writes=RR)
                for k in range(KQ):
                    P.op('pe', lambda e, k=k: e.matmul(psum[1][:], lhsT=onesb, rhs=sq[:, k * 512:(k + 1) * 512], start=(k == 0),
                                                       stop=(k == KQ - 1)), reads=RR + [rP], writes=[rps[1]], sig=(k == KQ - 1))
                rstd_chain(rstd, psum[1][:], 1.0 / QL, [rps[1]] + RR, RR)
                for k in range(KQ):
                    P.op('dve', lambda e, k=k: e.scalar_tensor_tensor(out=cqn[:, k, :], in0=r3[:, k, :], scalar=qag[:, k:k + 1],
                                                                      in1=rstd, op0=ALU.mult, op1=ALU.mult), reads=RR,
                         writes=[rCq])
                P.dma('sp', dr['s_cqn'][:, t0:t0 + 512].rearrange("(k p) t -> p k t", p=128), cqn, reads=[rCq])
                r3 = V3(raw[:, :4 * 512], 4, 512)
                P.dma('sp', r3, dr['s_ckv'][:, t0:t0 + 512].rearrange("(k p) t -> p k t", p=128), writes=[rL])
                P.op('act', lambda e: e.activation(out=sq[:, :4 * 512], in_=raw[:, :4 * 512], func=AF.Square), reads=RR, writes=RR)
                for k in range(4):
                    P.op('pe', lambda e, k=k: e.matmul(psum[2][:], lhsT=onesb, rhs=sq[:, k * 512:(k + 1) * 512], start=(k == 0),
                                                       stop=(k == 3)), reads=RR + [rP], writes=[rps[2]], sig=(k == 3))
                rstd_chain(rstd, psum[2][:], 1.0 / 512, [rps[2]] + RR, RR)
                for k in range(4):
                    P.op('dve', lambda e, k=k, r3=r3: e.scalar_tensor_tensor(
                        out=ckvT[:, k, 256 + t0:256 + t0 + 512], in0=r3[:, k, :], scalar=kvag[:, k:k + 1], in1=rstd,
                        op0=ALU.mult, op1=ALU.mult), reads=RR, writes=RR)
                    if is_p:
                        P.op('dve', lambda e, k=k, r3=r3: e.scalar_tensor_tensor(
                            out=c32, in0=r3[:, k, :], scalar=kvag[:, k:k + 1], in1=rstd, op0=ALU.mult, op1=ALU.mult),
                            reads=RR, writes=RR)
                        for j in range(4):
                            P.op('pe', lambda e, j=j: e.transpose(psum[3][:, j * 128:(j + 1) * 128],
                                                                  c32[:, j * 128:(j + 1) * 128], identf), reads=RR + [rP],
                                 writes=[rps[3]])
                        P.op('act', lambda e: e.activation(out=tst, in_=psum[3][:], func=AF.Copy), reads=[rps[3]], writes=[rTst])
                        P.dma('sp', dr['o_ckv'][t0 - LS:t0 - LS + 512, k * 128:(k + 1) * 128].rearrange("(j p) c -> p j c", p=128),
                              V3(tst, 4, 128), reads=[rTst])
                P.dma('sp', raw[:64, :512], dr['s_kr'][:, t0:t0 + 512], writes=[rL])
                fm_norm(x32[:64, :], raw[:64, :512], krg[:64], 64, 512, RR, rps, 4, sq, rstd)
                if is_p:
                    P.op('dve', lambda e, t0=t0: e.tensor_copy(out=k2T[:64, 256 + t0:256 + t0 + 512], in_=x32[:64, :]), reads=RR,
                         writes=RR)
                    for j in range(4):
                        P.op('pe', lambda e, j=j: e.transpose(psum[3][:, j * 64:(j + 1) * 64], x32[:64, j * 128:(j + 1) * 128],
                                                              identf[:64, :64]), reads=RR + [rP], writes=[rps[3]])
                    P.op('act', lambda e: e.activation(out=tst[:, :256], in_=psum[3][:, :256], func=AF.Copy), reads=[rps[3]],
                         writes=[rTst])
                    P.dma('sp', dr['o_kr'][t0 - LS:t0 - LS + 512, :].rearrange("(j p) c -> p j c", p=128), V3(tst[:, :256], 4, 64),
                          reads=[rTst])
                else:
                    rope(k2T[:64, 256 + t0:256 + t0 + 512], x32[:64, :], t0, 512)
            wq = V3(alloc(KQ * 192, BF16), KQ, 192)
            wkv = V3(alloc(4 * 256, BF16), 4, 256)
            kT = alloc(NK, BF16)
            vt = V3(alloc(NK, BF16), NK // 128, 128)
            gd = alloc(T, BF16)
            qn = alloc(512, BF16)
            qr = alloc(512, BF16)
            pts = [alloc(512, BF16) for _ in range(2)]
            rPt = [P.res('pt0'), P.res('pt1')]
            rec = alloc(512)
            o32 = alloc(512)
            mst = alloc(512, BF16)
            rMst = P.res('mst')
            rK = P.res('kv')
            rQ = P.res('q')
            for h in range(NH):
                hs = slice(h * 128, (h + 1) * 128)
                P.dma('pool', wq, dr['mla_w_q_up'][:, h * 192:(h + 1) * 192].rearrange("(k p) c -> p k c", p=128), writes=[rK])
                P.dma('pool', wkv, dr['mla_w_kv_up'][:, h * 256:(h + 1) * 256].rearrange("(k p) c -> p k c", p=128), writes=[rK])
                P.dma('sp', gd, dr['s_gd'][hs, :], writes=[rK])
                RK = [R, rK]
                for t0 in range(0, T, 512):
                    silu_ops(gd[:, t0:t0 + 512], gd[:, t0:t0 + 512], c32, RK + [rL])
                for c0 in range(0, NK, 512):
                    n = min(512, NK - c0)
                    for k in range(4):
                        P.op('pe', lambda e, k=k, c0=c0, n=n: e.matmul(psum[4][:, :n], lhsT=wkv[:, k, 0:128],
                                                                      rhs=ckvT[:, k, c0:c0 + n], start=(k == 0), stop=(k == 3)),
                             reads=RK, writes=[rps[4]], sig=(k == 3))
                    P.op('act', lambda e, n=n: e.activation(out=x32[:, :n], in_=psum[4][:, :n], func=AF.Copy), reads=[rps[4]],
                         writes=[rL])
                    fm_norm(kT[:, c0:c0 + n], x32[:, :n], kng, 128, n, [rL, R, rK], rps, 5, sq, rstd)
                for n_ in range(NK // 128):
                    b = 6 + (n_ % 2)
                    for k in range(4):
                        P.op('pe', lambda e, k=k, n_=n_, b=b: e.matmul(psum[b][:, 0:128], lhsT=ckvT[:, k, n_ * 128:(n_ + 1) * 128],
                                                                      rhs=wkv[:, k, 128:256], start=(k == 0), stop=(k == 3)),
                             reads=RK, writes=[rps[b]], sig=(k == 3))
                    evac(vt[:, n_, :], psum[b][:, 0:128], [rps[b]], [rK])
                RA = [R, rK, rQ]
                blocks = [(qb * 512, 512, 0, (256 + LS) // 128, True) for qb in range(LS // 512)]
                blocks += [(LS + s_ * 256, 256, (256 + LS + s_ * 256) // 128, 2, False) for s_ in range(NPS)]
                for (t0, nq, kt0, nkt, rot) in blocks:
                    P.dma('sp', cqn[:, :, :nq], dr['s_cqn'][:, t0:t0 + nq].rearrange("(k p) t -> p k t", p=128), writes=[rCq])
                    for k in range(KQ):
                        P.op('pe', lambda e, k=k, nq=nq: e.matmul(psum[4][:, :nq], lhsT=wq[:, k, 0:128], rhs=cqn[:, k, :nq],
                                                                  start=(k == 0), stop=(k == KQ - 1)), reads=[rCq, rK],
                             writes=[rps[4]], sig=(k == KQ - 1))
                    P.op('act', lambda e, nq=nq: e.activation(out=x32[:, :nq], in_=psum[4][:, :nq], func=AF.Copy), reads=[rps[4]],
                         writes=[rL])
                    fm_norm(qn[:, :nq], x32[:, :nq], qng, 128, nq, [rL, R, rQ], rps, 5, sq, rstd)
                    for k in range(KQ):
                        P.op('pe', lambda e, k=k, nq=nq: e.matmul(psum[6][:64, :nq], lhsT=wq[:, k, 128:192], rhs=cqn[:, k, :nq],
                                                                  start=(k == 0), stop=(k == KQ - 1)), reads=[rCq, rK],
                             writes=[rps[6]], sig=(k == KQ - 1))
                    P.op('act', lambda e, nq=nq: e.activation(out=x32[:64, :nq], in_=psum[6][:64, :nq], func=AF.Copy),
                         reads=[rps[6]], writes=[rL])
                    if rot:
                        fm_norm(c32[:64, :nq], x32[:64, :nq], qrg[:64], 64, nq, [rL, R, rQ], rps, 5, sq, rstd)
                        rope(qr[:64, :nq], c32[:64, :nq], t0, nq)
                        P.op('dve', lambda e: e.tensor_copy(out=qr[:64, 0:1], in_=qr[:64, 0:1]), reads=[rL, R], writes=[rQ])
                    else:
                        fm_norm(qr[:64, :nq], x32[:64, :nq], qrg[:64], 64, nq, [rL, R, rQ], rps, 5, sq, rstd)
                    keys = [(kT[:, (kt0 + i) * 128:(kt0 + i + 1) * 128], k2T[:64, (kt0 + i) * 128:(kt0 + i + 1) * 128],
                             vt[:, kt0 + i, :], None) for i in range(nkt)]
                    attn_block(qn[:, :nq], qr[:64, :nq], keys, nq, o32[:, :nq], RA + [rL], rps, pts, rPt, rec)
                    P.op('dve', lambda e, t0=t0, nq=nq: e.tensor_tensor(out=mst[:, :nq], in0=o32[:, :nq], in1=gd[:, t0:t0 + nq],
                                                                        op=ALU.mult), reads=RA + [rL], writes=[rMst])
                    P.dma('sp', dr['s_mix'][BW + h * 128:BW + (h + 1) * 128, t0:t0 + nq], mst[:, :nq], reads=[rMst])

        if ON('in0'):
            phase_inproj(0, dr['w_in_ab'], cfg.AB, dr['x_all'])
        if ON('mixA'):
            phase_mixA()
        if ON('mixB'):
            phase_mixB()
        if ON('out0'):
            phase_outproj(0, dr['x_all'], dr['s_x1'])
        if ON('in1'):
            phase_inproj(1, dr['w_in_cd'], cfg.CD, dr['s_x1'])
        if ON('hgrn'):
            phase_hgrn()
        if ON('mla'):
            phase_mla()
        if ON('out1'):
            phase_outproj(1, dr['s_x1'], dr['y_all'])
        P.emit()
    return nc


def prep_rp(na_rpb):
    NH = na_rpb.shape[0]
    rp = np.zeros((NH, 23, 127), np.float32)
    rp[:, 4:19, 48:79] = na_rpb[:, ::-1, ::-1]
    return rp


_CACHE = {}


def make_in_maps(cfg, inputs, ncores):
    consts = host_consts(cfg)
    NPS, LS, D = cfg.NPS, cfg.LS, cfg.D
    f = lambda a: np.ascontiguousarray(np.asarray(a))
    shared = {
        'norm_w': f(inputs['norm_w']), 'w_ada': f(inputs['w_ada']), 'b_ada': f(inputs['b_ada']),
        'w_out': f(inputs['w_out']), 'w_in_ab': f(inputs['w_in_ab'][0]), 'na_q_norm': f(inputs['na_q_norm'][0]),
        'na_k_norm': f(inputs['na_k_norm'][0]), 'rp': prep_rp(np.asarray(inputs['na_rpb'][0])),
        'sg_norm': f(inputs['sg_norm'][0]), 'sg_w': f(inputs['sg_w'][0]), 'sg_b': f(inputs['sg_b'][0]).reshape(-1),
        'w_in_cd': f(inputs['w_in_cd'][0]), 'hgrn_lb': f(inputs['hgrn_lb']),
        'hgrn_out_norm': f(inputs['hgrn_out_norm'][0]), 'mla_q_a_norm': f(inputs['mla_q_a_norm'][0]),
        'mla_w_q_up': f(inputs['mla_w_q_up'][0]), 'mla_kv_a_norm': f(inputs['mla_kv_a_norm'][0]),
        'mla_w_kv_up': f(inputs['mla_w_kv_up'][0]), 'mla_q_norm': f(inputs['mla_q_norm'][0]),
        'mla_k_norm': f(inputs['mla_k_norm'][0]),
    }
    shared.update(consts)
    maps = []
    for c in range(ncores):
        m = dict(shared)
        xs = np.asarray(inputs['x_sample'][c])
        xp = np.asarray(inputs['x_prompt'][c * NPS:(c + 1) * NPS]).reshape(NPS * 256, D)
        m['x_all'] = np.ascontiguousarray(np.concatenate([xs, xp], axis=0))
        m['cond2'] = np.ascontiguousarray(np.stack([np.asarray(inputs['c'][c]), np.asarray(inputs['c_ctx'])], axis=0))
        m['c_na_k'] = f(inputs['cache_na_k'][c, 0]).reshape(256, -1)
        m['c_na_v'] = f(inputs['cache_na_v'][c, 0]).reshape(256, -1)
        m['st_hgrn'] = f(inputs['state_hgrn'][c, 0])
        m['c_ckv'] = f(inputs['cache_mla_ckv'][c, 0])
        m['c_kr'] = f(inputs['cache_mla_krope'][c, 0])
        maps.append(m)
    return maps


def assemble(cfg, results, ncores):
    NPS, LS, D, NH, BW = cfg.NPS, cfg.LS, cfg.D, cfg.NH, cfg.BW
    yp, ys, nk, nv, hg, ckv, kr = [], [], [], [], [], [], []
    for c in range(ncores):
        r = results[c]
        ya = np.asarray(r['y_all'])
        ys.append(ya[:LS][None])
        yp.append(ya[LS:].reshape(NPS, 256, D))
        nk.append(np.asarray(r['o_na_k']).reshape(NPS, 1, 256, NH, 128))
        nv.append(np.asarray(r['o_na_v']).reshape(NPS, 1, 256, NH, 128))
        hg.append(np.asarray(r['o_hgrn']).reshape(NPS, 1, 2, NH, 128, 128))
        ckv.append(np.asarray(r['o_ckv']).reshape(NPS, 1, 256, 512))
        kr.append(np.asarray(r['o_kr']).reshape(NPS, 1, 256, 64))
    cat = lambda l: np.ascontiguousarray(np.concatenate(l, axis=0).astype(np.float32))
    return (cat(yp), cat(ys), cat(nk), cat(nv), cat(hg), cat(ckv), cat(kr))


def kernel(**inputs):
    cfg = Cfg(D=4096, LS=4096, NPS=4)
    ncores = 8
    if 'nc' not in _CACHE:
        _CACHE['nc'] = build_program(cfg)
    nc = _CACHE['nc']
    maps = make_in_maps(cfg, inputs, ncores)
    res = run_bass_kernel_spmd(nc, maps, core_ids=list(range(ncores)))
    return assemble(cfg, res.results, ncores)
```

```python
import os
import numpy as np
import ml_dtypes
from contextlib import ExitStack
import concourse.bass as bass
import concourse.mybir as mybir
from concourse.bass_utils import run_bass_kernel_spmd

F32 = mybir.dt.float32
BF16 = mybir.dt.bfloat16
AF = mybir.ActivationFunctionType
ALU = mybir.AluOpType
AX = mybir.AxisListType

ENGS = ('pe', 'act', 'dve', 'pool', 'sp')
SEM_LIMIT = 8000
EPS = 1e-6
GK = 1.5957691216057308


class Rec:
    def __init__(self):
        self.call = None

    def __getattr__(self, name):
        def f(*a, **k):
            self.call = (name, a, k)
            return self
        return f


class Event:
    __slots__ = ('sv', 'eng')

    def __init__(self, eng=None):
        self.sv = None
        self.eng = eng


class Chan:
    def __init__(self, P):
        self.P = P
        self.sem = None
        self.count = 0
        self.last = None

    def bump(self, n, ev):
        if self.sem is None or self.count + n > SEM_LIMIT:
            self.sem = self.P.new_sem()
            self.count = 0
        self.count += n
        ev.sv = (self.sem, self.count)
        self.last = ev
        return ev


class Res:
    def __init__(self, P, name):
        self.P = P
        self.name = name
        self.last_write = None
        self.readers = {}
        self.chan = {}

    def dchan(self, eng):
        if eng not in self.chan:
            self.chan[eng] = self.P.get_dma_chan(eng)
        return self.chan[eng]


class Prog:
    def __init__(self, nc, stack):
        self.nc = nc
        self.stack = stack
        self.ops = {e: [] for e in ENGS}
        self.echan = {e: Chan(self) for e in ENGS}
        self.cur = {e: Event(e) for e in ENGS}
        self.dma_chans = []
        self.free_chans = {e: [] for e in ENGS}
        self.resources = []
        self.nsem = 0
        self.bar_seen = {}
        self.pending = {e: False for e in ENGS}

    def new_sem(self):
        self.nsem += 1
        return self.stack.enter_context(self.nc.semaphore(f"s{self.nsem}"))

    def get_dma_chan(self, eng):
        if self.free_chans[eng]:
            return self.free_chans[eng].pop()
        ch = Chan(self)
        self.dma_chans.append(ch)
        return ch

    def res(self, name):
        r = Res(self, name)
        self.resources.append(r)
        return r

    def _deps(self, eng, reads, writes):
        waits = []

        def add(ev):
            if ev is None:
                return
            if ev.sv is None:
                assert ev.eng == eng, (ev.eng, eng)
                return
            if ev.eng == eng and eng == 'pe':
                return
            waits.append(ev)

        for r in reads:
            add(r.last_write)
        for w in writes:
            add(w.last_write)
            for ev in w.readers.values():
                add(ev)
        return waits

    def _commit(self, ev, key, reads, writes):
        for w in writes:
            w.last_write = ev
            w.readers = {}
        for r in reads:
            r.readers[key] = ev

    def op(self, eng, fn, reads=(), writes=(), sig=True):
        waits = self._deps(eng, reads, writes)
        ev = self.cur[eng]
        inc = None
        if sig:
            self.echan[eng].bump(1, ev)
            inc = (ev.sv[0], 1)
            self.cur[eng] = Event(eng)
        self.pending[eng] = not sig
        rec = Rec()
        fn(rec)
        assert rec.call is not None
        self.ops[eng].append((rec.call, waits, inc))
        self._commit(ev, eng, reads, writes)
        return ev

    def dma(self, eng, out, in_, reads=(), writes=(), cres=None, **kw):
        waits = self._deps(eng, reads, writes)
        if cres is None:
            cres = writes[0] if writes else reads[0]
        ch = cres.dchan(eng)
        ev = Event('dma')
        ch.bump(16, ev)
        self.ops[eng].append((('dma_start', (), dict(out=out, in_=in_, **kw)), waits, (ev.sv[0], 16)))
        self._commit(ev, ch, reads, writes)
        return ev

    def barrier(self, release=True):
        evs = []
        for e in ENGS:
            if self.pending[e]:
                self.op(e, lambda eng: eng.nop())
            ch = self.echan[e]
            if ch.last is not None:
                evs.append(ch.last)
        for ch in self.dma_chans:
            if ch.last is not None:
                evs.append(ch.last)
        evs = [ev for ev in evs if self.bar_seen.get(id(ev.sv[0]), 0) < ev.sv[1]]
        for ev in evs:
            self.bar_seen[id(ev.sv[0])] = ev.sv[1]
        for e in ENGS:
            self.ops[e].append((None, list(evs), None))
        for r in self.resources:
            r.last_write = None
            r.readers = {}
            if release:
                for e_, ch_ in r.chan.items():
                    self.free_chans[e_].append(ch_)
                r.chan = {}
        if release:
            self.resources = []

    def emit(self):
        nc = self.nc
        self.barrier()
        P = self

        def run(eng_name, eng):
            known = {}
            for fn, waits, inc in P.ops[eng_name]:
                for ev in waits:
                    sem, val = ev.sv
                    if known.get(id(sem), 0) < val:
                        eng.wait_ge(sem, val)
                        known[id(sem)] = val
                if fn is not None:
                    inst = getattr(eng, fn[0])(*fn[1], **fn[2])
                    if inc is not None:
                        inst.then_inc(inc[0], inc[1])

        with nc.Block() as block:
            @block.tensor
            def _(e):
                run('pe', e)

            @block.scalar
            def _(e):
                run('act', e)

            @block.vector
            def _(e):
                run('dve', e)

            @block.gpsimd
            def _(e):
                run('pool', e)

            @block.sync
            def _(e):
                run('sp', e)


class Cfg:
    def __init__(self, D=4096, LS=4096, NPS=4):
        self.D = D
        self.LS = LS
        self.NPS = NPS
        self.SEQ = 256
        self.PAST = 256
        self.KD = D // 128
        self.BW = D // 2
        self.NH = self.BW // 128
        self.QL = D // 4
        self.KQ = self.QL // 128
        self.KVL = 512
        self.ROPE = 64
        self.T = LS + NPS * 256
        self.ROWS = LS // 64
        self.NQB = LS // 512
        self.AB = [('qa', 'fm', self.BW), ('ka', 'fm', self.BW), ('va', 'tm', self.BW), ('ga', 'fm', self.BW),
                   ('ub', 'fm', self.BW), ('vb', 'tm', self.BW), ('gb', 'fm', self.BW)]
        self.CD = [('qc', 'fm', self.BW), ('ffw', 'tm32', self.BW), ('fbw', 'tm32', self.BW), ('ic', 'tm', self.BW),
                   ('gc', 'fm', self.BW), ('cq', 'fm', self.QL), ('ckv', 'fm', 512), ('kr', 'fm', 64),
                   ('gd', 'fm', self.BW)]
        self.AB_IN = sum(s[2] for s in self.AB)
        self.CD_IN = sum(s[2] for s in self.CD)
        self.build_patterns()

    def build_patterns(self):
        ROWS = self.ROWS
        cols = np.arange(64)
        cstart = np.clip(cols - 8, 0, 48)
        colmask = (cols[None, :] >= cstart[:, None]) & (cols[None, :] < cstart[:, None] + 16)
        pats = {}
        self.pat_list = []
        self.qb_keys = []
        for qb in range(self.NQB):
            r = 8 * qb + np.arange(8)
            rs = np.clip(r - 4, 0, ROWS - 8)
            kt0 = rs.min() // 2
            kt1 = (rs.max() + 7) // 2
            lst = []
            for kt in range(kt0, kt1 + 1):
                m = np.zeros((2, 64, 8, 64), np.float32)
                for kr in range(2):
                    rp = 2 * kt + kr
                    for qr in range(8):
                        if rs[qr] <= rp < rs[qr] + 8:
                            m[kr, :, qr, :] = colmask.T
                m = m.reshape(128, 512)
                delta = 2 * kt - 8 * qb
                key = (m.tobytes(), delta)
                if key not in pats:
                    pats[key] = len(self.pat_list)
                    self.pat_list.append((m, delta))
                lst.append((kt, pats[key]))
            self.qb_keys.append(lst)
        self.NPAT = len(self.pat_list)


def host_consts(cfg):
    c = {}
    c['identb'] = np.eye(128, dtype=np.float32).astype(ml_dtypes.bfloat16)
    c['identf'] = np.eye(128, dtype=np.float32)
    c['mask01'] = np.stack([m for m, _ in cfg.pat_list], axis=1).astype(ml_dtypes.bfloat16)
    idx = np.arange(128)
    same = (idx[:, None] // 32) == (idx[None, :] // 32)
    hm = np.zeros((2, 128, 512), np.float32)
    for d in range(2):
        if d == 0:
            Mi = same & (idx[:, None] <= idx[None, :])
            Mr = same & (idx[:, None] > idx[None, :])
        else:
            Mi = same & (idx[:, None] >= idx[None, :])
            Mr = same & (idx[:, None] < idx[None, :])
        Mi = Mi.astype(np.float32)
        mid = (idx // 32) * 32 + 15
        Mc = Mi - Mi[:, mid]
        hm[d, :, 0:128] = Mi
        hm[d, :, 128:256] = Mc
        hm[d, :, 256:384] = Mr.astype(np.float32)
    c['hmat'] = np.ascontiguousarray(hm.transpose(1, 0, 2))
    t = np.arange(cfg.LS)
    inv = (10000.0 ** (-np.arange(16, dtype=np.float32) / 16)).astype(np.float32)
    cos = np.zeros((64, cfg.LS), np.float32)
    sin = np.zeros((64, cfg.LS), np.float32)
    prot = np.zeros((64, 64), np.float32)
    for j in range(64):
        b, jj = j // 32, j % 32
        i = jj % 16
        pos = (t // 64 if b == 0 else t % 64).astype(np.float32)
        ang = pos * inv[i]
        cos[j] = np.cos(ang)
        sin[j] = np.sin(ang)
        if jj < 16:
            prot[j + 16, j] = -1.0
        else:
            prot[j - 16, j] = 1.0
    c['ropecos'] = cos
    c['ropesin'] = sin
    c['prot'] = prot
    return c


def build_program(cfg, debug=()):
    D, KD, BW, NH, T, LS, NPS, QL, KQ = cfg.D, cfg.KD, cfg.BW, cfg.NH, cfg.T, cfg.LS, cfg.NPS, cfg.QL, cfg.KQ
    NPT = NPS * 256
    nc = bass.Bass("TRN2", target_bir_lowering=False)
    dr = {}

    def din(name, shape, dt=F32):
        dr[name] = nc.dram_tensor(name, list(shape), dt, kind="ExternalInput").ap()

    def dout(name, shape, dt=F32):
        dr[name] = nc.dram_tensor(name, list(shape), dt, kind="ExternalOutput").ap()

    def dscr(name, shape, dt):
        kind = "ExternalOutput" if name in debug else "Internal"
        dr[name] = nc.dram_tensor(name, list(shape), dt, kind=kind).ap()

    din('x_all', [T, D])
    din('cond2', [2, D])
    din('c_na_k', [256, BW])
    din('c_na_v', [256, BW])
    din('st_hgrn', [2, NH, 128, 128])
    din('c_ckv', [256, 512])
    din('c_kr', [256, 64])
    din('norm_w', [2, D])
    din('w_ada', [2, D, 3 * D])
    din('b_ada', [2, 3 * D])
    din('w_out', [2, D, D])
    din('w_in_ab', [D, cfg.AB_IN])
    din('na_q_norm', [128])
    din('na_k_norm', [128])
    din('rp', [NH, 23, 127])
    din('sg_norm', [BW])
    din('sg_w', [NH, 128, 128])
    din('sg_b', [NH * 128])
    din('w_in_cd', [D, cfg.CD_IN])
    din('hgrn_lb', [2, 2, BW])
    din('hgrn_out_norm', [128])
    din('mla_q_a_norm', [QL])
    din('mla_w_q_up', [QL, NH * 192])
    din('mla_kv_a_norm', [512])
    din('mla_w_kv_up', [512, NH * 256])
    din('mla_q_norm', [192])
    din('mla_k_norm', [192])
    din('identb', [128, 128], BF16)
    din('identf', [128, 128])
    din('mask01', [128, cfg.NPAT, 512], BF16)
    din('hmat', [128, 2, 512])
    din('ropecos', [64, LS])
    din('ropesin', [64, LS])
    din('prot', [64, 64])
    dout('y_all', [T, D])
    dout('o_na_k', [NPT, BW])
    dout('o_na_v', [NPT, BW])
    dout('o_hgrn', [NPS, 2, NH, 128, 128])
    dout('o_ckv', [NPT, 512])
    dout('o_kr', [NPT, 64])
    for (nm, kind, w) in cfg.AB + cfg.CD:
        if kind == 'fm':
            dscr('s_' + nm, [w, T], BF16)
        elif kind == 'tm':
            dscr('s_' + nm, [T, w], BF16)
        else:
            dscr('s_' + nm, [T, w], F32)
    dscr('s_mix', [D, T], BF16)
    dscr('s_x1', [T, D], F32)
    dscr('s_gate', [2, 2, D], F32)
    dscr('s_tf', [NH, 23, 64, 64], F32)
    dscr('s_of', [BW, T], F32)
    dscr('s_cqn', [QL, T], BF16)

    st = ExitStack()
    with st:
        P = Prog(nc, st)
        ARENA = 46 * 1024
        arena = st.enter_context(nc.sbuf_tensor("arena", [128, ARENA], F32))
        psum = [st.enter_context(nc.psum_tensor(f"ps{i}", [128, 512], F32)) for i in range(8)]
        PERS = 1024
        pos = [0]

        def alloc(n, dt=F32, np_=128):
            words = (n + 1) // 2 if dt == BF16 else n
            words = (words + 7) // 8 * 8
            a = arena[:, pos[0]:pos[0] + words]
            pos[0] += words
            assert pos[0] <= ARENA, pos[0]
            if dt == BF16:
                a = a.bitcast(BF16)[:, :n]
            else:
                a = a[:, :n]
            return a

        identb = alloc(128, BF16)
        identf = alloc(128)
        onesb = alloc(128, BF16)
        AT = [alloc(KD * 2) for _ in range(2)]
        SH = [alloc(KD * 2) for _ in range(2)]
        epsc = alloc(1)
        assert pos[0] <= PERS
        rP = P.res('pers')
        P.dma('sp', identb, dr['identb'], writes=[rP])
        P.dma('sp', identf, dr['identf'], writes=[rP])
        P.op('dve', lambda e: e.memset(onesb, 1.0), writes=[rP])
        P.op('dve', lambda e: e.memset(epsc, EPS), writes=[rP])
        P.barrier(release=False)
        P.resources = []

        bank_rr = [0]

        def new_phase():
            P.barrier()
            pos[0] = PERS
            rps = [P.res(f'ps{i}') for i in range(8)]
            return rps

        def V3(ap, a, b):
            return ap.rearrange("p (a b) -> p a b", a=a, b=b)

        cnt = [0]

        def evac(out, in_, reads, writes, eng=None):
            cnt[0] += 1
            if eng is None:
                eng = 'act' if cnt[0] % 2 else 'dve'
            if eng == 'act':
                P.op('act', lambda e: e.activation(out=out, in_=in_, func=AF.Copy), reads=reads, writes=writes)
            else:
                P.op('dve', lambda e: e.tensor_copy(out=out, in_=in_), reads=reads, writes=writes)

        def rstd_chain(dst, src, inv_n, reads, writes):
            P.op('dve', lambda e: e.tensor_scalar(out=dst, in0=src, scalar1=inv_n, scalar2=EPS, op0=ALU.mult, op1=ALU.add),
                 reads=reads, writes=writes)
            P.op('act', lambda e: e.activation(out=dst, in_=dst, func=AF.Ln), reads=writes, writes=writes)
            P.op('act', lambda e: e.activation(out=dst, in_=dst, func=AF.Exp, scale=-0.5), reads=writes, writes=writes)

        def sigmoid_ops(dst, src, scale, R):
            P.op('act', lambda e: e.activation(out=dst, in_=src, func=AF.Exp, scale=-scale), reads=R, writes=R)
            P.op('dve', lambda e: e.tensor_scalar_add(out=dst, in0=dst, scalar1=1.0), reads=R, writes=R)
            P.op('dve', lambda e: e.reciprocal(out=dst, in_=dst), reads=R, writes=R)

        def silu_ops(dst, src, tmp, R):
            sigmoid_ops(tmp, src, 1.0, R)
            P.op('dve', lambda e: e.tensor_tensor(out=dst, in0=src, in1=tmp, op=ALU.mult), reads=R, writes=R)

        def gelu_ops(dst, src, tmp, tmp2, R):
            P.op('act', lambda e: e.activation(out=tmp, in_=src, func=AF.Square), reads=R, writes=R)
            P.op('dve', lambda e: e.tensor_scalar(out=tmp, in0=tmp, scalar1=0.044715, scalar2=1.0, op0=ALU.mult, op1=ALU.add),
                 reads=R, writes=R)
            P.op('dve', lambda e: e.tensor_tensor(out=tmp, in0=tmp, in1=src, op=ALU.mult), reads=R, writes=R)
            sigmoid_ops(tmp2, tmp, GK, R)
            P.op('dve', lambda e: e.tensor_tensor(out=dst, in0=src, in1=tmp2, op=ALU.mult), reads=R, writes=R)

        stages = os.environ.get("MK_STAGES", "all").split(',')
        ON = lambda nm: stages == ['all'] or nm in stages
        for l in (range(2) if ON('mod') else []):
            rps = new_phase()
            R = P.res('m')
            cT = alloc(KD * 2)
            tmpc = alloc(KD * 2)
            sT = alloc(KD * 2, BF16)
            bT = alloc(3 * KD)
            nwT = alloc(KD)
            modT = alloc(3 * KD * 2)
            wts = [alloc(KD * 512, BF16) for _ in range(2)]
            rW = [P.res('w0'), P.res('w1')]
            for c in range(2):
                P.dma('sp', V3(cT, 2, KD)[:, c, :], dr['cond2'][c].rearrange("(k p) -> p k", p=128), writes=[R],
                      allow_slow_non_contiguous=True)
            P.dma('sp', bT, dr['b_ada'][l].rearrange("(j p) -> p j", p=128), writes=[R], allow_slow_non_contiguous=True)
            P.dma('sp', nwT, dr['norm_w'][l].rearrange("(j p) -> p j", p=128), writes=[R], allow_slow_non_contiguous=True)
            silu_ops(cT, cT, tmpc, [R])
            P.op('dve', lambda e: e.tensor_copy(out=V3(sT, KD, 2), in_=V3(cT, 2, KD).rearrange("p c k -> p k c")), reads=[R],
                 writes=[R])
            ncb = 3 * D // 512
            for cb in range(ncb):
                s = cb % 2
                wt = V3(wts[s], KD, 512)
                P.dma('pool', wt, dr['w_ada'][l][:, cb * 512:(cb + 1) * 512].rearrange("(k p) c -> p k c", p=128),
                      writes=[rW[s]])
                b = cb % 8
                for j in range(4):
                    for k in range(KD):
                        P.op('pe', lambda e, b=b, j=j, k=k, wt=wt: e.matmul(
                            psum[b][:, 2 * j:2 * j + 2], lhsT=wt[:, k, j * 128:(j + 1) * 128], rhs=V3(sT, KD, 2)[:, k, :],
                            start=(k == 0), stop=(k == KD - 1)), reads=[rW[s], R], writes=[rps[b]],
                            sig=(j == 3 and k == KD - 1))
                P.op('dve', lambda e, b=b, cb=cb: e.tensor_tensor(
                    out=V3(modT, 3 * KD, 2)[:, cb * 4:(cb + 1) * 4, :], in0=V3(psum[b][:, 0:8], 4, 2),
                    in1=bT[:, cb * 4:(cb + 1) * 4].unsqueeze(2).broadcast_to([128, 4, 2]), op=ALU.add),
                    reads=[rps[b], R], writes=[R])
            m3 = V3(modT, 3 * KD, 2)
            P.op('dve', lambda e: e.tensor_copy(out=V3(SH[l], KD, 2), in_=m3[:, 0:KD, :]), reads=[R], writes=[rP])
            P.op('dve', lambda e: e.tensor_scalar_add(out=V3(AT[l], KD, 2), in0=m3[:, KD:2 * KD, :], scalar1=1.0),
                 reads=[R], writes=[rP])
            P.op('dve', lambda e: e.tensor_tensor(out=V3(AT[l], KD, 2), in0=V3(AT[l], KD, 2),
                                                  in1=nwT.unsqueeze(2).broadcast_to([128, KD, 2]), op=ALU.mult),
                 reads=[R, rP], writes=[rP])
            gtmp = alloc(2 * KD)
            P.op('dve', lambda e: e.tensor_copy(out=V3(gtmp, 2, KD), in_=m3[:, 2 * KD:3 * KD, :].rearrange("p k c -> p c k")),
                 reads=[R], writes=[R])
            for c in range(2):
                P.dma('sp', dr['s_gate'][l][c].rearrange("(k p) -> p k", p=128), V3(gtmp, 2, KD)[:, c, :], reads=[R],
                      allow_slow_non_contiguous=True)

        def phase_inproj(l, W, segs, xsrc):
            rps = new_phase()
            TB = min(1024, LS)
            hT = alloc(KD * TB, BF16)
            hT3 = V3(hT, KD, TB)
            wts = [V3(alloc(KD * 512, BF16), KD, 512) for _ in range(2)]
            rW = [P.res('w0'), P.res('w1')]
            xts = [alloc(D) for _ in range(2)]
            rX = [P.res('x0'), P.res('x1')]
            xn = alloc(D, BF16)
            rXn = P.res('xn')
            ss = alloc(1)
            rstd = alloc(1)
            rS = P.res('ss')
            stg = [alloc(512) for _ in range(4)]
            rStg = [P.res(f'stg{i}') for i in range(4)]
            rH = P.res('hT')
            sc = [0]
            wc = [0]
            bc = [0]
            SK = os.environ.get("MK_SKIP", "")
            for tb0 in range(0, T, TB):
                for ti in range(TB // 128 if 'L' not in SK else 0):
                    tok0 = tb0 + ti * 128
                    cond = 0 if tok0 < LS else 1
                    s = ti % 2
                    P.dma('sp', xts[s], xsrc[tok0:tok0 + 128, :], writes=[rX[s]])
                    if 'A' in SK:
                        continue
                    P.op('act', lambda e, s=s: e.activation(out=xn, in_=xts[s], func=AF.Square, accum_out=ss),
                         reads=[rX[s]], writes=[rXn, rS])
                    rstd_chain(rstd, ss, 1.0 / D, [rS], [rS])
                    P.op('act', lambda e, s=s: e.activation(out=xn, in_=xts[s], func=AF.Copy, scale=rstd),
                         reads=[rX[s], rS], writes=[rXn])
                    for kg in range(0, KD if 'T' not in SK else 0, 8):
                        nk = min(8, KD - kg)
                        b = bc[0] % 8
                        bc[0] += 1
                        pb = psum[b][:].bitcast(BF16)
                        for j in range(nk):
                            P.op('pe', lambda e, pb=pb, j=j, kg=kg: e.transpose(
                                pb[:, j * 128:(j + 1) * 128], xn[:, (kg + j) * 128:(kg + j + 1) * 128], identb),
                                reads=[rXn, rP], writes=[rps[b]], sig=(j == nk - 1))
                        for j in range(nk):
                            k = kg + j
                            P.op('dve', lambda e, pb=pb, j=j, k=k, ti=ti, cond=cond: e.tensor_scalar(
                                out=hT3[:, k, ti * 128:(ti + 1) * 128], in0=pb[:, j * 128:(j + 1) * 128],
                                scalar1=AT[l][:, 2 * k + cond:2 * k + cond + 1],
                                scalar2=SH[l][:, 2 * k + cond:2 * k + cond + 1], op0=ALU.mult, op1=ALU.add),
                                reads=[rps[b], rP], writes=[rH])
                off = 0
                for (nm, kind, w) in (segs if 'W' not in SK else []):
                    if (os.environ.get("MK_SEG") and kind != os.environ.get("MK_SEG")) or (os.environ.get("MK_ONLY") and nm != os.environ.get("MK_ONLY")):
                        off += w
                        continue
                    for c0 in range(0, w, 512):
                        cw = min(512, w - c0)
                        s = wc[0] % 2
                        wc[0] += 1
                        wt = wts[s]
                        P.dma('pool', wt[:, :, :cw],
                              W[:, off + c0:off + c0 + cw].rearrange("(k p) c -> p k c", p=128), writes=[rW[s]])
                        if kind == 'fm':
                            for j0 in range(0, cw, 128):
                                m = min(128, cw - j0)
                                for t0 in range(0, TB, 512):
                                    b = bc[0] % 8
                                    bc[0] += 1
                                    for k in range(KD):
                                        P.op('pe', lambda e, b=b, m=m, k=k, j0=j0, t0=t0, wt=wt: e.matmul(
                                            psum[b][:m, :], lhsT=wt[:, k, j0:j0 + m], rhs=hT3[:, k, t0:t0 + 512],
                                            start=(k == 0), stop=(k == KD - 1)), reads=[rW[s], rH], writes=[rps[b]],
                                            sig=(k == KD - 1))
                                    q = sc[0] % 4
                                    sc[0] += 1
                                    so = stg[q].bitcast(BF16)[:m, 0:512]
                                    evac(so, psum[b][:m, :], [rps[b]], [rStg[q]])
                                    P.dma('sp', dr['s_' + nm][c0 + j0:c0 + j0 + m, tb0 + t0:tb0 + t0 + 512], so,
                                          reads=[rStg[q]])
                        else:
                            for ti in range(TB // 128):
                                tok0 = tb0 + ti * 128
                                b = bc[0] % 8
                                bc[0] += 1
                                for k in range(KD):
                                    P.op('pe', lambda e, b=b, k=k, ti=ti, wt=wt, cw=cw: e.matmul(
                                        psum[b][:, :cw], lhsT=hT3[:, k, ti * 128:(ti + 1) * 128], rhs=wt[:, k, :cw],
                                        start=(k == 0), stop=(k == KD - 1)), reads=[rW[s], rH], writes=[rps[b]],
                                        sig=(k == KD - 1))
                                q = sc[0] % 4
                                sc[0] += 1
                                if nm == 'va' and tok0 >= LS:
                                    s32 = stg[q][:, 0:cw]
                                    evac(s32, psum[b][:, :cw], [rps[b]], [rStg[q]])
                                    P.dma('sp', dr['o_na_v'][tok0 - LS:tok0 - LS + 128, c0:c0 + cw], s32, reads=[rStg[q]])
                                    q2 = sc[0] % 4
                                    sc[0] += 1
                                    so = stg[q2].bitcast(BF16)[:, 0:cw]
                                    P.op('dve', lambda e: e.tensor_copy(out=so, in_=s32), reads=[rStg[q]], writes=[rStg[q2]])
                                    P.dma('sp', dr['s_' + nm][tok0:tok0 + 128, c0:c0 + cw], so, reads=[rStg[q2]])
                                    continue
                                if kind == 'tm':
                                    so = stg[q].bitcast(BF16)[:, 0:cw]
                                else:
                                    so = stg[q][:, 0:cw]
                                evac(so, psum[b][:, :cw], [rps[b]], [rStg[q]])
                                P.dma('sp', dr['s_' + nm][tok0:tok0 + 128, c0:c0 + cw], so, reads=[rStg[q]])
                    off += w

        def phase_outproj(l, xsrc, ydst):
            rps = new_phase()
            TB = min(1024, LS)
            mixT = V3(alloc(KD * TB, BF16), KD, TB)
            rM = P.res('mix')
            wts = [V3(alloc(KD * 512, BF16), KD, 512) for _ in range(2)]
            rW = [P.res('w0'), P.res('w1')]
            gbc = [alloc(D) for _ in range(2)]
            rG = P.res('g')
            xb = [alloc(512) for _ in range(4)]
            rXb = [P.res(f'xb{i}') for i in range(4)]
            tmps = [alloc(512) for _ in range(2)]
            rTmp = [P.res('t0'), P.res('t1')]
            for c in range(2):
                P.dma('sp', gbc[c], dr['s_gate'][l][c:c + 1, :].partition_broadcast(128), writes=[rG])
            wc = 0
            bc = 0
            xc = 0
            for tb0 in range(0, T, TB):
                P.dma('sp', mixT, dr['s_mix'][:, tb0:tb0 + TB].rearrange("(k p) t -> p k t", p=128), writes=[rM])
                for c0 in range(0, D, 512):
                    s = wc % 2
                    wc += 1
                    wt = wts[s]
                    P.dma('pool', wt, dr['w_out'][l][:, c0:c0 + 512].rearrange("(k p) c -> p k c", p=128), writes=[rW[s]])
                    for ti in range(TB // 128):
                        tok0 = tb0 + ti * 128
                        cond = 0 if tok0 < LS else 1
                        b = bc % 8
                        bc += 1
                        q = xc % 4
                        xc += 1
                        P.dma('sp', xb[q], xsrc[tok0:tok0 + 128, c0:c0 + 512], writes=[rXb[q]])
                        for k in range(KD):
                            P.op('pe', lambda e, b=b, k=k, ti=ti, wt=wt: e.matmul(
                                psum[b][:], lhsT=mixT[:, k, ti * 128:(ti + 1) * 128], rhs=wt[:, k, :],
                                start=(k == 0), stop=(k == KD - 1)), reads=[rW[s], rM], writes=[rps[b]], sig=(k == KD - 1))
                        P.op('dve', lambda e: e.tensor_tensor(out=tmps[q % 2], in0=psum[b][:], in1=gbc[cond][:, c0:c0 + 512],
                                                              op=ALU.mult), reads=[rps[b], rG], writes=[rTmp[q % 2]])
                        P.op('dve', lambda e: e.tensor_tensor(out=xb[q], in0=tmps[q % 2], in1=xb[q], op=ALU.add),
                             reads=[rTmp[q % 2]], writes=[rXb[q]])
                        P.dma('sp', ydst[tok0:tok0 + 128, c0:c0 + 512], xb[q], reads=[rXb[q]])

        def fm_norm(dst, src, gain, np_, n, R, rps, b, sq, rstd, extra_dst=None):
            P.op('act', lambda e: e.activation(out=sq[:np_, :n], in_=src, func=AF.Square), reads=R, writes=R)
            P.op('pe', lambda e: e.matmul(psum[b][:np_, :n], lhsT=onesb[:np_, :np_], rhs=sq[:np_, :n], start=True, stop=True),
                 reads=R + [rP], writes=[rps[b]])
            rstd_chain(rstd[:np_, :n], psum[b][:np_, :n], 1.0 / np_, [rps[b]] + R, R)
            P.op('dve', lambda e: e.scalar_tensor_tensor(out=dst, in0=src, scalar=gain, in1=rstd[:np_, :n],
                                                         op0=ALU.mult, op1=ALU.mult), reads=R + [rP], writes=R)
            if extra_dst is not None:
                P.op('dve', lambda e: e.scalar_tensor_tensor(out=extra_dst, in0=src, scalar=gain, in1=rstd[:np_, :n],
                                                             op0=ALU.mult, op1=ALU.mult), reads=R + [rP], writes=R)

        def attn_block(qT, q2T, keys, nq, o32, R, rps, pts, rPt, rec):
            n = len(keys)
            NBF = 3
            LA = 2

            def score(i):
                kT, k2T, v, mask = keys[i]
                sb = i % NBF
                P.op('pe', lambda e: e.matmul(psum[sb][:, :nq], lhsT=kT, rhs=qT, start=True, stop=(k2T is None)),
                     reads=R, writes=[rps[sb]], sig=(k2T is None))
                if k2T is not None:
                    P.op('pe', lambda e: e.matmul(psum[sb][:, :nq], lhsT=k2T, rhs=q2T, start=False, stop=True),
                         reads=R, writes=[rps[sb]])

            for i in range(min(LA, n)):
                score(i)
            for i in range(n):
                kT, k2T, v, mask = keys[i]
                sb = i % NBF
                pt = pts[sb]
                if i + LA < n:
                    score(i + LA)
                P.op('act', lambda e: e.activation(out=pt[:, :nq], in_=psum[sb][:, :nq], func=AF.Exp),
                     reads=[rps[sb]], writes=[rPt[sb]])
                if mask is not None:
                    P.op('dve', lambda e: e.tensor_tensor(out=pt[:, :nq], in0=pt[:, :nq], in1=mask, op=ALU.mult),
                         reads=R + [rPt[sb]], writes=[rPt[sb]])
                P.op('pe', lambda e: e.matmul(psum[3][:, :nq], lhsT=v, rhs=pt[:, :nq], start=(i == 0), stop=(i == n - 1)),
                     reads=R + [rPt[sb]], writes=[rps[3]], sig=False)
                P.op('pe', lambda e: e.matmul(psum[4][:, :nq], lhsT=onesb, rhs=pt[:, :nq], start=(i == 0), stop=(i == n - 1)),
                     reads=[rPt[sb], rP], writes=[rps[4]])
            P.op('dve', lambda e: e.reciprocal(out=rec[:, :nq], in_=psum[4][:, :nq]), reads=[rps[4]], writes=R)
            P.op('dve', lambda e: e.tensor_tensor(out=o32, in0=psum[3][:, :nq], in1=rec[:, :nq], op=ALU.mult),
                 reads=[rps[3], rps[4]] + R, writes=R)

        def phase_mixA():
            rps = new_phase()
            R = P.res('a')
            NP = cfg.NPAT
            mask01 = V3(alloc(NP * 512, BF16), NP, 512)
            P.dma('sp', mask01, dr['mask01'], writes=[R])
            qg = alloc(1)
            kg = alloc(1)
            P.dma('sp', qg, dr['na_q_norm'].rearrange("(p o) -> p o", o=1), writes=[R])
            P.dma('sp', kg, dr['na_k_norm'].rearrange("(p o) -> p o", o=1), writes=[R])
            P.op('dve', lambda e: e.tensor_scalar_mul(out=qg, in0=qg, scalar1=128.0 ** -0.5), reads=[R], writes=[R])
            rTF = P.res('tf')
            for ck in range(64):
                P.dma('sp', dr['s_tf'][:, :, ck, :], dr['rp'][:, :, 63 - ck:63 - ck + 64], writes=[rTF])
            qraw = alloc(T, BF16)
            kraw = alloc(T, BF16)
            qn = alloc(T, BF16)
            kn = alloc(T, BF16)
            sg = alloc(T, BF16)
            vt = V3(alloc(T, BF16), T // 128, 128)
            kc32 = V3(alloc(256, BF16), 2, 128)
            kcT = alloc(256, BF16)
            vc = V3(alloc(256, BF16), 2, 128)
            stage32 = V3(alloc(NP * 512), NP, 512)
            EB = V3(alloc(NP * 512, BF16), NP, 512)
            sq = alloc(512, BF16)
            rstd = alloc(512)
            knf = alloc(512)
            tmp = alloc(512)
            pts = [alloc(512, BF16) for _ in range(3)]
            rPt = [P.res('pt0'), P.res('pt1'), P.res('pt2')]
            rec = alloc(512)
            o32 = alloc(512)
            mst = alloc(512, BF16)
            rMst = P.res('mst')
            tst = alloc(128)
            rTst = P.res('tst')
            rL = P.res('loads')
            rE = P.res('eb')
            for h in range(NH):
                hs = slice(h * 128, (h + 1) * 128)
                P.dma('sp', qraw, dr['s_qa'][hs, :], writes=[rL])
                P.dma('sp', kraw, dr['s_ka'][hs, :], writes=[rL])
                P.dma('sp', sg, dr['s_ga'][hs, :], writes=[rL])
                P.dma('sp', vt, dr['s_va'][:, hs].rearrange("(n p) c -> p n c", p=128), writes=[rL])
                P.dma('pool', kc32, dr['c_na_k'][:, hs].rearrange("(n p) c -> p n c", p=128), writes=[rL])
                P.dma('pool', vc, dr['c_na_v'][:, hs].rearrange("(n p) c -> p n c", p=128), writes=[rL])
                for n_ in range(2):
                    pb = psum[7][:].bitcast(BF16)
                    P.op('pe', lambda e, n_=n_, pb=pb: e.transpose(pb[:, n_ * 128:(n_ + 1) * 128], kc32[:, n_, :], identb),
                         reads=[rL, rP], writes=[rps[7]])
                P.op('dve', lambda e: e.tensor_copy(out=kcT, in_=psum[7][:].bitcast(BF16)[:, 0:256]), reads=[rps[7]],
                     writes=[rL])
                for t0 in range(0, T, 512):
                    silu_ops(sg[:, t0:t0 + 512], sg[:, t0:t0 + 512], tmp, [rL, R])
                for t0 in range(0, T, 512):
                    fm_norm(qn[:, t0:t0 + 512], qraw[:, t0:t0 + 512], qg, 128, 512, [rL, R], rps, 5, sq, rstd)
                    fm_norm(kn[:, t0:t0 + 512], kraw[:, t0:t0 + 512], kg, 128, 512, [rL, R], rps, 6, sq, rstd,
                            extra_dst=(knf if t0 >= LS else None))
                    if t0 >= LS:
                        for j in range(4):
                            P.op('pe', lambda e, j=j: e.transpose(psum[7][:, j * 128:(j + 1) * 128],
                                                                  knf[:, j * 128:(j + 1) * 128], identf),
                                 reads=[rL, R, rP], writes=[rps[7]])
                            P.op('act', lambda e, j=j: e.activation(out=tst, in_=psum[7][:, j * 128:(j + 1) * 128],
                                                                    func=AF.Copy), reads=[rps[7]], writes=[rTst])
                            P.dma('sp', dr['o_na_k'][t0 - LS + j * 128:t0 - LS + (j + 1) * 128, hs], tst, reads=[rTst])
                for pi, (_, delta) in enumerate(cfg.pat_list):
                    for kr in range(2):
                        e0 = 11 - delta - kr
                        P.dma('sp', stage32[kr * 64:(kr + 1) * 64, pi, :].rearrange("p (a b) -> p a b", a=8, b=64),
                              dr['s_tf'][h, e0:e0 + 8, :, :].rearrange("e ck cq -> ck e cq"), reads=[rTF], writes=[rE])
                for pi in range(NP):
                    P.op('act', lambda e, pi=pi: e.activation(out=stage32[:, pi, :], in_=stage32[:, pi, :], func=AF.Exp),
                         reads=[rE], writes=[rE])
                    P.op('dve', lambda e, pi=pi: e.tensor_tensor(out=EB[:, pi, :], in0=stage32[:, pi, :],
                                                                 in1=mask01[:, pi, :], op=ALU.mult), reads=[rE, R],
                         writes=[rE])
                RR = [rL, R, rE]
                for qb in range(cfg.NQB):
                    keys = []
                    for (kt, pi) in cfg.qb_keys[qb]:
                        keys.append((kn[:, kt * 128:(kt + 1) * 128], None, vt[:, kt, :], EB[:, pi, :]))
                    for n_ in range(2):
                        keys.append((kcT[:, n_ * 128:(n_ + 1) * 128], None, vc[:, n_, :], None))
                    attn_block(qn[:, qb * 512:(qb + 1) * 512], None, keys, 512, o32, RR, rps, pts, rPt, rec)
                    P.op('dve', lambda e, qb=qb: e.tensor_tensor(out=mst, in0=o32, in1=sg[:, qb * 512:(qb + 1) * 512],
                                                                 op=ALU.mult), reads=RR, writes=[rMst])
                    P.dma('sp', dr['s_mix'][hs, qb * 512:(qb + 1) * 512], mst, reads=[rMst])
                for s_ in range(NPS):
                    t0 = LS + s_ * 256
                    keys = [(kn[:, t0 + n_ * 128:t0 + (n_ + 1) * 128], None, vt[:, t0 // 128 + n_, :], None)
                            for n_ in range(2)]
                    attn_block(qn[:, t0:t0 + 256], None, keys, 256, o32[:, :256], RR, rps, pts, rPt, rec)
                    P.op('dve', lambda e, t0=t0: e.tensor_tensor(out=mst[:, :256], in0=o32[:, :256], in1=sg[:, t0:t0 + 256],
                                                                 op=ALU.mult), reads=RR, writes=[rMst])
                    P.dma('sp', dr['s_mix'][hs, t0:t0 + 256], mst[:, :256], reads=[rMst])

        def phase_mixB():
            rps = new_phase()
            R = P.res('b')
            NB = NH * 128
            sgw32 = V3(alloc(NB), NH, 128)
            sgwb = V3(alloc(NB, BF16), NH, 128)
            sgwT = V3(alloc(NB, BF16), NH, 128)
            sgb = alloc(NB)
            sgn = alloc(BW)
            P.dma('sp', sgw32, dr['sg_w'].rearrange("g t s -> t g s"), writes=[R])
            P.dma('sp', sgb, dr['sg_b'].rearrange("(o n) -> o n", o=1).partition_broadcast(128), writes=[R])
            P.dma('sp', sgn, dr['sg_norm'].rearrange("(o n) -> o n", o=1).partition_broadcast(128), writes=[R])
            P.op('dve', lambda e: e.tensor_copy(out=sgwb, in_=sgw32), reads=[R], writes=[R])
            for g in range(NH):
                pb = psum[0][:].bitcast(BF16)
                P.op('pe', lambda e, g=g, pb=pb: e.transpose(pb[:, 0:128], sgwb[:, g, :], identb), reads=[R, rP],
                     writes=[rps[0]])
                P.op('dve', lambda e, g=g, pb=pb: e.tensor_copy(out=sgwT[:, g, :], in_=pb[:, 0:128]), reads=[rps[0]],
                     writes=[R])
            vb = alloc(BW, BF16)
            gv = alloc(BW)
            t1 = alloc(BW)
            t2 = alloc(BW)
            vn = alloc(BW, BF16)
            uT = alloc(NB, BF16)
            gT = alloc(NB, BF16)
            gu = alloc(NB)
            ss = alloc(1)
            rstd = alloc(1)
            mo = alloc(NB, BF16)
            rMo = P.res('mo')
            rL = P.res('ld')
            RR = [R, rL]
            for ti in range(T // 128):
                tk = slice(ti * 128, (ti + 1) * 128)
                P.dma('sp', vb, dr['s_vb'][tk, :], writes=[rL])
                P.dma('sp', V3(uT, NH, 128), dr['s_ub'][:, tk].rearrange("(g c) t -> c g t", c=128), writes=[rL])
                P.dma('sp', V3(gT, NH, 128), dr['s_gb'][:, tk].rearrange("(g c) t -> c g t", c=128), writes=[rL])
                gelu_ops(gv, vb, t1, t2, RR)
                P.op('act', lambda e: e.activation(out=t1, in_=gv, func=AF.Square, accum_out=ss), reads=RR, writes=RR)
                rstd_chain(rstd, ss, 1.0 / BW, RR, RR)
                P.op('dve', lambda e: e.scalar_tensor_tensor(out=vn, in0=gv, scalar=rstd, in1=sgn, op0=ALU.mult,
                                                             op1=ALU.mult), reads=RR, writes=RR)
                gelu_ops(gu, uT, t1[:, :NB], t2[:, :NB], RR)
                for g in range(NH):
                    b = (g * 128) // 512
                    P.op('pe', lambda e, g=g, b=b: e.matmul(psum[b][:, (g * 128) % 512:(g * 128) % 512 + 128],
                                                            lhsT=vn[:, g * 128:(g + 1) * 128], rhs=sgwT[:, g, :],
                                                            start=True, stop=True), reads=RR, writes=[rps[b]])
                for b in range((NB + 511) // 512):
                    w = min(512, NB - b * 512)
                    cs = slice(b * 512, b * 512 + w)
                    P.op('dve', lambda e, b=b, w=w, cs=cs: e.tensor_tensor(out=t1[:, cs], in0=psum[b][:, :w], in1=sgb[:, cs],
                                                                           op=ALU.add), reads=[rps[b]] + RR, writes=RR)
                P.op('dve', lambda e: e.tensor_tensor(out=gu, in0=gu, in1=t1[:, :NB], op=ALU.mult), reads=RR, writes=RR)
                silu_ops(t1[:, :NB], gT, t2[:, :NB], RR)
                P.op('dve', lambda e: e.tensor_tensor(out=mo, in0=gu, in1=t1[:, :NB], op=ALU.mult), reads=RR, writes=[rMo])
                P.dma('sp', dr['s_mix'][BW:2 * BW, tk].rearrange("(g c) t -> c g t", c=128), V3(mo, NH, 128), reads=[rMo])

        def phase_hgrn():
            rps = new_phase()
            R = P.res('c')
            hmat = V3(alloc(1024), 2, 512)
            P.dma('sp', hmat, dr['hmat'], writes=[R])
            maskb = V3(alloc(256, BF16), 2, 128)
            P.op('dve', lambda e: e.tensor_copy(out=maskb, in_=hmat[:, :, 0:128]), reads=[R], writes=[R])
            lbb = [alloc(BW) for _ in range(2)]
            oml = [alloc(BW) for _ in range(2)]
            tl = alloc(BW)
            og = alloc(1)
            P.dma('sp', og, dr['hgrn_out_norm'].rearrange("(p o) -> p o", o=1), writes=[R])
            for d in range(2):
                P.dma('sp', lbb[d], dr['hgrn_lb'][1, d:d + 1, :].partition_broadcast(128), writes=[R])
                P.dma('sp', tl, dr['hgrn_lb'][0, d:d + 1, :].partition_broadcast(128), writes=[R])
                P.op('dve', lambda e, d=d: e.tensor_tensor(out=lbb[d], in0=lbb[d], in1=tl, op=ALU.subtract), reads=[R],
                     writes=[R])
                sigmoid_ops(lbb[d], lbb[d], 1.0, [R])
                P.op('dve', lambda e, d=d: e.tensor_scalar(out=oml[d], in0=lbb[d], scalar1=-1.0, scalar2=1.0, op0=ALU.mult,
                                                           op1=ALU.add), reads=[R], writes=[R])
            z = alloc(BW)
            f = alloc(BW)
            gl = alloc(BW)
            kk = alloc(BW)
            khat = alloc(BW, BF16)
            ktil = alloc(BW, BF16)
            vt = alloc(BW, BF16)
            qraw = V3(alloc(BW, BF16), NH, 128)
            gcT = V3(alloc(BW, BF16), NH, 128)
            S = V3(alloc(NH * 128), NH, 128)
            EE = alloc(256)
            qinc = alloc(128)
            qtil = alloc(128, BF16)
            ktT = alloc(128, BF16)
            pTm = alloc(128, BF16)
            oall = V3(alloc(BW), NH, 128)
            ofl = V3(alloc(BW), NH, 128)
            sqb = alloc(BW, BF16)
            mo = alloc(BW, BF16)
            rMo = P.res('mo')
            rL = P.res('ld')
            rS = P.res('S')
            rT = P.res('tm')
            rF = P.res('fmh')
            rO = P.res('oall')
            nbk = (BW + 511) // 512
            seqs = [(0, LS // 128, None)] + [((LS + s_ * 256) // 128, 2, s_) for s_ in range(NPS)]
            for d in range(2):
                zname = 's_ffw' if d == 0 else 's_fbw'
                Mi = hmat[:, d, 0:128]
                MiMc = hmat[:, d, 0:256]
                Mc = hmat[:, d, 128:256]
                Mr = hmat[:, d, 256:384]
                for (tile0, ntl, ps_idx) in seqs:
                    if ps_idx is None:
                        P.dma('sp', S, dr['st_hgrn'][d].rearrange("h k v -> k h v"), writes=[rS])
                    else:
                        P.op('dve', lambda e: e.memset(S, 0.0), writes=[rS])
                    order = range(ntl) if d == 0 else range(ntl - 1, -1, -1)
                    for tl_ in order:
                        ti = tile0 + tl_
                        tk = slice(ti * 128, (ti + 1) * 128)
                        P.dma('sp', z, dr[zname][tk, :], writes=[rL])
                        P.dma('sp', vt, dr['s_ic'][tk, :], writes=[rL])
                        P.dma('sp', qraw, dr['s_qc'][:, tk].rearrange("(h k) t -> k h t", k=128), writes=[rL])
                        if d == 1:
                            P.dma('sp', gcT, dr['s_gc'][:, tk].rearrange("(h k) t -> k h t", k=128), writes=[rL])
                            P.dma('sp', ofl, dr['s_of'][:, tk].rearrange("(h k) t -> k h t", k=128), writes=[rL])
                        RT = [rL, R, rT]
                        sigmoid_ops(f, z, 1.0, RT)
                        P.op('dve', lambda e, d=d: e.tensor_tensor(out=f, in0=f, in1=oml[d], op=ALU.mult), reads=RT, writes=RT)
                        P.op('dve', lambda e, d=d: e.tensor_tensor(out=f, in0=f, in1=lbb[d], op=ALU.add), reads=RT, writes=RT)
                        P.op('act', lambda e: e.activation(out=gl, in_=f, func=AF.Ln), reads=RT, writes=RT)
                        P.op('dve', lambda e: e.tensor_scalar(out=kk, in0=f, scalar1=-1.0, scalar2=1.0, op0=ALU.mult,
                                                              op1=ALU.add), reads=RT, writes=RT)
                        for (Mx, dst, sc_) in ((Mr, khat, 1.0), (Mc, ktil, -1.0)):
                            for b in range(nbk):
                                w = min(512, BW - b * 512)
                                P.op('pe', lambda e, b=b, w=w, Mx=Mx: e.matmul(psum[b][:, :w], lhsT=Mx,
                                                                              rhs=gl[:, b * 512:b * 512 + w], start=True,
                                                                              stop=True), reads=RT, writes=[rps[b]])
                                P.op('act', lambda e, b=b, w=w, sc_=sc_: e.activation(out=z[:, b * 512:b * 512 + w],
                                                                                     in_=psum[b][:, :w], func=AF.Exp,
                                                                                     scale=sc_), reads=[rps[b]] + RT,
                                     writes=RT)
                            P.op('dve', lambda e, dst=dst: e.tensor_tensor(out=dst, in0=kk, in1=z, op=ALU.mult), reads=RT,
                                 writes=RT)
                        RF = [rL, R, rT, rF]
                        for h in range(NH):
                            hs = slice(h * 128, (h + 1) * 128)
                            P.op('pe', lambda e, hs=hs: e.matmul(psum[4][:, 0:256], lhsT=gl[:, hs], rhs=MiMc, start=True,
                                                                 stop=True), reads=RT, writes=[rps[4]])
                            P.op('act', lambda e: e.activation(out=EE, in_=psum[4][:, 0:256], func=AF.Exp), reads=[rps[4]] + RF,
                                 writes=RF)
                            P.op('dve', lambda e, h=h: e.tensor_tensor(out=qinc, in0=qraw[:, h, :], in1=EE[:, 0:128],
                                                                       op=ALU.mult), reads=RF, writes=RF)
                            P.op('dve', lambda e, h=h: e.tensor_tensor(out=qtil, in0=qraw[:, h, :], in1=EE[:, 128:256],
                                                                       op=ALU.mult), reads=RF, writes=RF)
                            pb = psum[5][:].bitcast(BF16)
                            P.op('pe', lambda e, hs=hs, pb=pb: e.transpose(pb[:, 0:128], ktil[:, hs], identb), reads=RT + [rP],
                                 writes=[rps[5]])
                            P.op('act', lambda e, pb=pb: e.activation(out=ktT, in_=pb[:, 0:128], func=AF.Copy),
                                 reads=[rps[5]] + RF, writes=RF)
                            P.op('pe', lambda e: e.matmul(psum[5][:, 128:256], lhsT=ktT, rhs=qtil, start=True, stop=True),
                                 reads=RF, writes=[rps[5]])
                            P.op('dve', lambda e, d=d: e.tensor_tensor(out=pTm, in0=psum[5][:, 128:256], in1=maskb[:, d, :],
                                                                       op=ALU.mult), reads=[rps[5]] + RF, writes=RF)
                            P.op('pe', lambda e, hs=hs: e.matmul(psum[6][:, 0:128], lhsT=vt[:, hs], rhs=pTm, start=True,
                                                                 stop=False), reads=RF, writes=[rps[6]], sig=False)
                            corder = range(4) if d == 0 else range(3, -1, -1)
                            for ci, c in enumerate(corder):
                                cs = slice(c * 32, (c + 1) * 32)
                                P.op('pe', lambda e, h=h, cs=cs, ci=ci: e.matmul(psum[6][:, cs], lhsT=S[:, h, :],
                                                                                 rhs=qinc[:, cs], start=False,
                                                                                 stop=(ci == 3)), reads=RF + [rS],
                                     writes=[rps[6]], sig=True)
                                P.op('pe', lambda e, hs=hs, cs=cs, c=c: e.matmul(psum[7][:, 0:128], lhsT=khat[cs, hs],
                                                                                 rhs=vt[cs, hs], start=True, stop=True,
                                                                                 tile_position=(c * 32, 0)), reads=RT,
                                     writes=[rps[7]])
                                col = c * 32 + 31 if d == 0 else c * 32
                                P.op('dve', lambda e, h=h, col=col: e.scalar_tensor_tensor(
                                    out=S[:, h, :], in0=S[:, h, :], scalar=EE[:, col:col + 1], in1=psum[7][:, 0:128],
                                    op0=ALU.mult, op1=ALU.add), reads=[rps[7]] + RF, writes=[rS])
                            P.op('act', lambda e, h=h: e.activation(out=oall[:, h, :], in_=psum[6][:, 0:128], func=AF.Copy),
                                 reads=[rps[6]], writes=[rO])
                        if d == 0:
                            P.dma('sp', dr['s_of'][:, tk].rearrange("(h k) t -> k h t", k=128), oall, reads=[rO])
                        else:
                            RO = [rO, rL, R]
                            oa2 = oall.rearrange("p a b -> p (a b)")
                            P.op('dve', lambda e: e.tensor_tensor(out=oall, in0=oall, in1=ofl, op=ALU.add), reads=RO, writes=RO)
                            P.op('act', lambda e: e.activation(out=sqb, in_=oa2, func=AF.Square), reads=RO, writes=RO)
                            for b in range(nbk):
                                w = min(512, BW - b * 512)
                                P.op('pe', lambda e, b=b, w=w: e.matmul(psum[b][:, :w], lhsT=onesb, rhs=sqb[:, b * 512:b * 512 + w],
                                                                        start=True, stop=True), reads=RO + [rP], writes=[rps[b]])
                                rstd_chain(f[:, b * 512:b * 512 + w], psum[b][:, :w], 1.0 / 128, [rps[b]] + RO + [rT], RO + [rT])
                            P.op('dve', lambda e: e.scalar_tensor_tensor(out=oa2, in0=oa2, scalar=og, in1=f, op0=ALU.mult,
                                                                         op1=ALU.mult), reads=RO + [rT], writes=RO)
                            g2 = gcT.rearrange("p a b -> p (a b)")
                            silu_ops(gl, g2, kk, RO + [rT])
                            P.op('dve', lambda e: e.tensor_tensor(out=mo, in0=oa2, in1=gl, op=ALU.mult), reads=RO + [rT],
                                 writes=[rMo])
                            P.dma('sp', dr['s_mix'][0:BW, tk].rearrange("(h k) t -> k h t", k=128), V3(mo, NH, 128),
                                  reads=[rMo])
                    if ps_idx is not None:
                        P.dma('sp', dr['o_hgrn'][ps_idx, d].rearrange("h k v -> k h v"), S, reads=[rS])

        def phase_mla():
            rps = new_phase()
            R = P.res('d')
            NK = 256 + T
            ckvT = V3(alloc(4 * NK, BF16), 4, NK)
            k2T = alloc(NK, BF16)
            qag = alloc(KQ)
            kvag = alloc(4)
            qng = alloc(1)
            qrg = alloc(1)
            kng = alloc(1)
            krg = alloc(1)
            prot = alloc(64)
            with_nc = dict(allow_slow_non_contiguous=True)
            P.dma('sp', qag, dr['mla_q_a_norm'].rearrange("(k p) -> p k", p=128), writes=[R], **with_nc)
            P.dma('sp', kvag, dr['mla_kv_a_norm'].rearrange("(k p) -> p k", p=128), writes=[R], **with_nc)
            P.dma('sp', qng, dr['mla_q_norm'][0:128].rearrange("(p o) -> p o", o=1), writes=[R])
            P.dma('sp', qrg[:64], dr['mla_q_norm'][128:192].rearrange("(p o) -> p o", o=1), writes=[R])
            P.dma('sp', kng, dr['mla_k_norm'][0:128].rearrange("(p o) -> p o", o=1), writes=[R])
            P.dma('sp', krg[:64], dr['mla_k_norm'][128:192].rearrange("(p o) -> p o", o=1), writes=[R])
            P.dma('sp', prot[:64], dr['prot'], writes=[R])
            P.op('dve', lambda e: e.tensor_scalar_mul(out=qng, in0=qng, scalar1=192.0 ** -0.5), reads=[R], writes=[R])
            P.op('dve', lambda e: e.tensor_scalar_mul(out=qrg[:64], in0=qrg[:64], scalar1=192.0 ** -0.5), reads=[R], writes=[R])
            raw = alloc(max(KQ, 4) * 512, BF16)
            sq = alloc(max(KQ, 4) * 512, BF16)
            rstd = alloc(512)
            cqn = V3(alloc(KQ * 512, BF16), KQ, 512)
            c32 = alloc(512)
            x32 = alloc(512)
            cosb = alloc(512)
            sinb = alloc(512)
            tst = alloc(512)
            rTst = P.res('tst')
            rL = P.res('ld')
            rCq = P.res('cqn')
            RR = [R, rL]
            cc = V3(alloc(1024, BF16), 2, 512)
            ck = V3(alloc(128, BF16), 2, 64)
            P.dma('pool', cc, dr['c_ckv'].rearrange("(n p) c -> p n c", p=128), writes=[rL])
            P.dma('pool', ck, dr['c_kr'].rearrange("(n p) c -> p n c", p=128), writes=[rL])
            for n_ in range(2):
                pb = psum[0][:].bitcast(BF16)
                for k in range(4):
                    P.op('pe', lambda e, n_=n_, k=k, pb=pb: e.transpose(pb[:, k * 128:(k + 1) * 128],
                                                                        cc[:, n_, k * 128:(k + 1) * 128], identb),
                         reads=[rL, rP], writes=[rps[0]])
                P.op('dve', lambda e, n_=n_, pb=pb: e.tensor_copy(out=ckvT[:, :, n_ * 128:(n_ + 1) * 128],
                                                                  in_=V3(pb[:, 0:512], 4, 128)), reads=[rps[0]], writes=[R])
                P.op('pe', lambda e, n_=n_, pb=pb: e.transpose(pb[:64, 512:640], ck[:, n_, :], identb), reads=[rL, rP],
                     writes=[rps[0]])
                P.op('dve', lambda e, n_=n_, pb=pb: e.tensor_copy(out=k2T[:64, n_ * 128:(n_ + 1) * 128], in_=pb[:64, 512:640]),
                     reads=[rps[0]], writes=[R])

            def rope(dst, src, t0, n):
                P.dma('sp', cosb[:64, :n], dr['ropecos'][:, t0:t0 + n], writes=[rL])
                P.dma('sp', sinb[:64, :n], dr['ropesin'][:, t0:t0 + n], writes=[rL])
                P.op('pe', lambda e: e.matmul(psum[7][:64, :n], lhsT=prot[:64, :64], rhs=src, start=True, stop=True),
                     reads=RR, writes=[rps[7]])
                P.op('dve', lambda e: e.tensor_tensor(out=sinb[:64, :n], in0=psum[7][:64, :n], in1=sinb[:64, :n], op=ALU.mult),
                     reads=[rps[7], rL], writes=[rL])
                P.op('dve', lambda e: e.tensor_tensor(out=cosb[:64, :n], in0=src, in1=cosb[:64, :n], op=ALU.mult),
                     reads=RR, writes=[rL])
                P.op('dve', lambda e: e.tensor_tensor(out=dst, in0=cosb[:64, :n], in1=sinb[:64, :n], op=ALU.add),
                     reads=[rL], writes=RR)

            for t0 in range(0, T, 512):
                is_p = t0 >= LS
                r3 = V3(raw[:, :KQ * 512], KQ, 512)
                P.dma('sp', r3, dr['s_cq'][:, t0:t0 + 512].rearrange("(k p) t -> p k t", p=128), writes=[rL])
                P.op('act', lambda e: e.activation(out=sq[:, :KQ * 512], in_=raw[:, :KQ * 512], func=AF.Square), reads=RR, writes=RR)
                for k in range(KQ):
                    P.op('pe', lambda e, k=k: e.matmul(psum[1][:], lhsT=onesb, rhs=sq[:, k * 512:(k + 1) * 512], start=(k == 0),
                                                       stop=(k == KQ - 1)), reads=RR + [rP], writes=[rps[1]], sig=(k == KQ - 1))
                rstd_chain(rstd, psum[1][:], 1.0 / QL, [rps[1]] + RR, RR)
                for k in range(KQ):
                    P.op('dve', lambda e, k=k: e.scalar_tensor_tensor(out=cqn[:, k, :], in0=r3[:, k, :], scalar=qag[:, k:k + 1],
                                                                      in1=rstd, op0=ALU.mult, op1=ALU.mult), reads=RR,
                         writes=[rCq])
                P.dma('sp', dr['s_cqn'][:, t0:t0 + 512].rearrange("(k p) t -> p k t", p=128), cqn, reads=[rCq])
                r3 = V3(raw[:, :4 * 512], 4, 512)
                P.dma('sp', r3, dr['s_ckv'][:, t0:t0 + 512].rearrange("(k p) t -> p k t", p=128), writes=[rL])
                P.op('act', lambda e: e.activation(out=sq[:, :4 * 512], in_=raw[:, :4 * 512], func=AF.Square), reads=RR, writes=RR)
                for k in range(4):
                    P.op('pe', lambda e, k=k: e.matmul(psum[2][:], lhsT=onesb, rhs=sq[:, k * 512:(k + 1) * 512], start=(k == 0),
                                                       stop=(k == 3)), reads=RR + [rP], writes=[rps[2]], sig=(k == 3))
                rstd_chain(rstd, psum[2][:], 1.0 / 512, [rps[2]] + RR, RR)
                for k in range(4):
                    P.op('dve', lambda e, k=k, r3=r3: e.scalar_tensor_tensor(
                        out=ckvT[:, k, 256 + t0:256 + t0 + 512], in0=r3[:, k, :], scalar=kvag[:, k:k + 1], in1=rstd,
                        op0=ALU.mult, op1=ALU.mult), reads=RR, writes=RR)
                    if is_p:
                        P.op('dve', lambda e, k=k, r3=r3: e.scalar_tensor_tensor(
                            out=c32, in0=r3[:, k, :], scalar=kvag[:, k:k + 1], in1=rstd, op0=ALU.mult, op1=ALU.mult),
                            reads=RR, writes=RR)
                        for j in range(4):
                            P.op('pe', lambda e, j=j: e.transpose(psum[3][:, j * 128:(j + 1) * 128],
                                                                  c32[:, j * 128:(j + 1) * 128], identf), reads=RR + [rP],
                                 writes=[rps[3]])
                        P.op('act', lambda e: e.activation(out=tst, in_=psum[3][:], func=AF.Copy), reads=[rps[3]], writes=[rTst])
                        P.dma('sp', dr['o_ckv'][t0 - LS:t0 - LS + 512, k * 128:(k + 1) * 128].rearrange("(j p) c -> p j c", p=128),
                              V3(tst, 4, 128), reads=[rTst])
                P.dma('sp', raw[:64, :512], dr['s_kr'][:, t0:t0 + 512], writes=[rL])
                fm_norm(x32[:64, :], raw[:64, :512], krg[:64], 64, 512, RR, rps, 4, sq, rstd)
                if is_p:
                    P.op('dve', lambda e, t0=t0: e.tensor_copy(out=k2T[:64, 256 + t0:256 + t0 + 512], in_=x32[:64, :]), reads=RR,
                         writes=RR)
                    for j in range(4):
                        P.op('pe', lambda e, j=j: e.transpose(psum[3][:, j * 64:(j + 1) * 64], x32[:64, j * 128:(j + 1) * 128],
                                                              identf[:64, :64]), reads=RR + [rP], writes=[rps[3]])
                    P.op('act', lambda e: e.activation(out=tst[:, :256], in_=psum[3][:, :256], func=AF.Copy), reads=[rps[3]],
                         writes=[rTst])
                    P.dma('sp', dr['o_kr'][t0 - LS:t0 - LS + 512, :].rearrange("(j p) c -> p j c", p=128), V3(tst[:, :256], 4, 64),
                          reads=[rTst])
                else:
                    rope(k2T[:64, 256 + t0:256 + t0 + 512], x32[:64, :], t0, 512)
            wq = V3(alloc(KQ * 192, BF16), KQ, 192)
            wkv = V3(alloc(4 * 256, BF16), 4, 256)
            kT = alloc(NK, BF16)
            vt = V3(alloc(NK, BF16), NK // 128, 128)
            gd = alloc(T, BF16)
            qn = alloc(512, BF16)
            qr = alloc(512, BF16)
            pts = [alloc(512, BF16) for _ in range(3)]
            rPt = [P.res('pt0'), P.res('pt1'), P.res('pt2')]
            rec = alloc(512)
            o32 = alloc(512)
            mst = alloc(512, BF16)
            rMst = P.res('mst')
            rK = P.res('kv')
            rQ = P.res('q')
            for h in range(NH):
                hs = slice(h * 128, (h + 1) * 128)
                P.dma('pool', wq, dr['mla_w_q_up'][:, h * 192:(h + 1) * 192].rearrange("(k p) c -> p k c", p=128), writes=[rK])
                P.dma('pool', wkv, dr['mla_w_kv_up'][:, h * 256:(h + 1) * 256].rearrange("(k p) c -> p k c", p=128), writes=[rK])
                P.dma('sp', gd, dr['s_gd'][hs, :], writes=[rK])
                RK = [R, rK]
                for t0 in range(0, T, 512):
                    silu_ops(gd[:, t0:t0 + 512], gd[:, t0:t0 + 512], c32, RK + [rL])
                for c0 in range(0, NK, 512):
                    n = min(512, NK - c0)
                    for k in range(4):
                        P.op('pe', lambda e, k=k, c0=c0, n=n: e.matmul(psum[5][:, :n], lhsT=wkv[:, k, 0:128],
                                                                      rhs=ckvT[:, k, c0:c0 + n], start=(k == 0), stop=(k == 3)),
                             reads=RK, writes=[rps[5]], sig=(k == 3))
                    P.op('act', lambda e, n=n: e.activation(out=x32[:, :n], in_=psum[5][:, :n], func=AF.Copy), reads=[rps[5]],
                         writes=[rL])
                    fm_norm(kT[:, c0:c0 + n], x32[:, :n], kng, 128, n, [rL, R, rK], rps, 6, sq, rstd)
                for n_ in range(NK // 128):
                    b = 7
                    for k in range(4):
                        P.op('pe', lambda e, k=k, n_=n_, b=b: e.matmul(psum[b][:, 0:128], lhsT=ckvT[:, k, n_ * 128:(n_ + 1) * 128],
                                                                      rhs=wkv[:, k, 128:256], start=(k == 0), stop=(k == 3)),
                             reads=RK, writes=[rps[b]], sig=(k == 3))
                    evac(vt[:, n_, :], psum[b][:, 0:128], [rps[b]], [rK])
                RA = [R, rK, rQ]
                blocks = [(qb * 512, 512, 0, (256 + LS) // 128, True) for qb in range(LS // 512)]
                blocks += [(LS + s_ * 256, 256, (256 + LS + s_ * 256) // 128, 2, False) for s_ in range(NPS)]
                for (t0, nq, kt0, nkt, rot) in blocks:
                    P.dma('sp', cqn[:, :, :nq], dr['s_cqn'][:, t0:t0 + nq].rearrange("(k p) t -> p k t", p=128), writes=[rCq])
                    for k in range(KQ):
                        P.op('pe', lambda e, k=k, nq=nq: e.matmul(psum[5][:, :nq], lhsT=wq[:, k, 0:128], rhs=cqn[:, k, :nq],
                                                                  start=(k == 0), stop=(k == KQ - 1)), reads=[rCq, rK],
                             writes=[rps[5]], sig=(k == KQ - 1))
                    P.op('act', lambda e, nq=nq: e.activation(out=x32[:, :nq], in_=psum[5][:, :nq], func=AF.Copy), reads=[rps[5]],
                         writes=[rL])
                    fm_norm(qn[:, :nq], x32[:, :nq], qng, 128, nq, [rL, R, rQ], rps, 6, sq, rstd)
                    for k in range(KQ):
                        P.op('pe', lambda e, k=k, nq=nq: e.matmul(psum[7][:64, :nq], lhsT=wq[:, k, 128:192], rhs=cqn[:, k, :nq],
                                                                  start=(k == 0), stop=(k == KQ - 1)), reads=[rCq, rK],
                             writes=[rps[7]], sig=(k == KQ - 1))
                    P.op('act', lambda e, nq=nq: e.activation(out=x32[:64, :nq], in_=psum[7][:64, :nq], func=AF.Copy),
                         reads=[rps[7]], writes=[rL])
                    if rot:
                        fm_norm(c32[:64, :nq], x32[:64, :nq], qrg[:64], 64, nq, [rL, R, rQ], rps, 6, sq, rstd)
                        rope(qr[:64, :nq], c32[:64, :nq], t0, nq)
                        P.op('dve', lambda e: e.tensor_copy(out=qr[:64, 0:1], in_=qr[:64, 0:1]), reads=[rL, R], writes=[rQ])
                    else:
                        fm_norm(qr[:64, :nq], x32[:64, :nq], qrg[:64], 64, nq, [rL, R, rQ], rps, 6, sq, rstd)
                    keys = [(kT[:, (kt0 + i) * 128:(kt0 + i + 1) * 128], k2T[:64, (kt0 + i) * 128:(kt0 + i + 1) * 128],
                             vt[:, kt0 + i, :], None) for i in range(nkt)]
                    attn_block(qn[:, :nq], qr[:64, :nq], keys, nq, o32[:, :nq], RA + [rL], rps, pts, rPt, rec)
                    P.op('dve', lambda e, t0=t0, nq=nq: e.tensor_tensor(out=mst[:, :nq], in0=o32[:, :nq], in1=gd[:, t0:t0 + nq],
                                                                        op=ALU.mult), reads=RA + [rL], writes=[rMst])
                    P.dma('sp', dr['s_mix'][BW + h * 128:BW + (h + 1) * 128, t0:t0 + nq], mst[:, :nq], reads=[rMst])

        if ON('in0'):
            phase_inproj(0, dr['w_in_ab'], cfg.AB, dr['x_all'])
        if ON('mixA'):
            phase_mixA()
        if ON('mixB'):
            phase_mixB()
        if ON('out0'):
            phase_outproj(0, dr['x_all'], dr['s_x1'])
        if ON('in1'):
            phase_inproj(1, dr['w_in_cd'], cfg.CD, dr['s_x1'])
        if ON('hgrn'):
            phase_hgrn()
        if ON('mla'):
            phase_mla()
        if ON('out1'):
            phase_outproj(1, dr['s_x1'], dr['y_all'])
        P.emit()
    return nc


def prep_rp(na_rpb):
    NH = na_rpb.shape[0]
    rp = np.zeros((NH, 23, 127), np.float32)
    rp[:, 4:19, 48:79] = na_rpb[:, ::-1, ::-1]
    return rp


_CACHE = {}


def make_in_maps(cfg, inputs, ncores):
    consts = host_consts(cfg)
    NPS, LS, D = cfg.NPS, cfg.LS, cfg.D
    f = lambda a: np.ascontiguousarray(np.asarray(a))
    shared = {
        'norm_w': f(inputs['norm_w']), 'w_ada': f(inputs['w_ada']), 'b_ada': f(inputs['b_ada']),
        'w_out': f(inputs['w_out']), 'w_in_ab': f(inputs['w_in_ab'][0]), 'na_q_norm': f(inputs['na_q_norm'][0]),
        'na_k_norm': f(inputs['na_k_norm'][0]), 'rp': prep_rp(np.asarray(inputs['na_rpb'][0])),
        'sg_norm': f(inputs['sg_norm'][0]), 'sg_w': f(inputs['sg_w'][0]), 'sg_b': f(inputs['sg_b'][0]).reshape(-1),
        'w_in_cd': f(inputs['w_in_cd'][0]), 'hgrn_lb': f(inputs['hgrn_lb']),
        'hgrn_out_norm': f(inputs['hgrn_out_norm'][0]), 'mla_q_a_norm': f(inputs['mla_q_a_norm'][0]),
        'mla_w_q_up': f(inputs['mla_w_q_up'][0]), 'mla_kv_a_norm': f(inputs['mla_kv_a_norm'][0]),
        'mla_w_kv_up': f(inputs['mla_w_kv_up'][0]), 'mla_q_norm': f(inputs['mla_q_norm'][0]),
        'mla_k_norm': f(inputs['mla_k_norm'][0]),
    }
    shared.update(consts)
    maps = []
    for c in range(ncores):
        m = dict(shared)
        xs = np.asarray(inputs['x_sample'][c])
        xp = np.asarray(inputs['x_prompt'][c * NPS:(c + 1) * NPS]).reshape(NPS * 256, D)
        m['x_all'] = np.ascontiguousarray(np.concatenate([xs, xp], axis=0))
        m['cond2'] = np.ascontiguousarray(np.stack([np.asarray(inputs['c'][c]), np.asarray(inputs['c_ctx'])], axis=0))
        m['c_na_k'] = f(inputs['cache_na_k'][c, 0]).reshape(256, -1)
        m['c_na_v'] = f(inputs['cache_na_v'][c, 0]).reshape(256, -1)
        m['st_hgrn'] = f(inputs['state_hgrn'][c, 0])
        m['c_ckv'] = f(inputs['cache_mla_ckv'][c, 0])
        m['c_kr'] = f(inputs['cache_mla_krope'][c, 0])
        maps.append(m)
    return maps


def assemble(cfg, results, ncores):
    NPS, LS, D, NH, BW = cfg.NPS, cfg.LS, cfg.D, cfg.NH, cfg.BW
    yp, ys, nk, nv, hg, ckv, kr = [], [], [], [], [], [], []
    for c in range(ncores):
        r = results[c]
        ya = np.asarray(r['y_all'])
        ys.append(ya[:LS][None])
        yp.append(ya[LS:].reshape(NPS, 256, D))
        nk.append(np.asarray(r['o_na_k']).reshape(NPS, 1, 256, NH, 128))
        nv.append(np.asarray(r['o_na_v']).reshape(NPS, 1, 256, NH, 128))
        hg.append(np.asarray(r['o_hgrn']).reshape(NPS, 1, 2, NH, 128, 128))
        ckv.append(np.asarray(r['o_ckv']).reshape(NPS, 1, 256, 512))
        kr.append(np.asarray(r['o_kr']).reshape(NPS, 1, 256, 64))
    cat = lambda l: np.ascontiguousarray(np.concatenate(l, axis=0).astype(np.float32))
    return (cat(yp), cat(ys), cat(nk), cat(nv), cat(hg), cat(ckv), cat(kr))


def kernel(**inputs):
    cfg = Cfg(D=4096, LS=4096, NPS=4)
    ncores = 8
    if 'nc' not in _CACHE:
        _CACHE['nc'] = build_program(cfg)
    nc = _CACHE['nc']
    maps = make_in_maps(cfg, inputs, ncores)
    res = run_bass_kernel_spmd(nc, maps, core_ids=list(range(ncores)))
    return assemble(cfg, res.results, ncores)
```

```python
import os
import numpy as np
import ml_dtypes
from contextlib import ExitStack
import concourse.bass as bass
import concourse.mybir as mybir
from concourse.bass_utils import run_bass_kernel_spmd

F32 = mybir.dt.float32
BF16 = mybir.dt.bfloat16
AF = mybir.ActivationFunctionType
ALU = mybir.AluOpType
AX = mybir.AxisListType

ENGS = ('pe', 'act', 'dve', 'pool', 'sp')
SEM_LIMIT = 8000
EPS = 1e-6
GK = 1.5957691216057308


class Rec:
    def __init__(self):
        self.call = None

    def __getattr__(self, name):
        def f(*a, **k):
            self.call = (name, a, k)
            return self
        return f


class Event:
    __slots__ = ('sv', 'eng')

    def __init__(self, eng=None):
        self.sv = None
        self.eng = eng


class Chan:
    def __init__(self, P):
        self.P = P
        self.sem = None
        self.count = 0
        self.last = None

    def bump(self, n, ev):
        if self.sem is None or self.count + n > SEM_LIMIT:
            self.sem = self.P.new_sem()
            self.count = 0
        self.count += n
        ev.sv = (self.sem, self.count)
        self.last = ev
        return ev


class Res:
    def __init__(self, P, name):
        self.P = P
        self.name = name
        self.last_write = None
        self.readers = {}
        self.chan = {}

    def dchan(self, eng):
        if eng not in self.chan:
            self.chan[eng] = self.P.get_dma_chan(eng)
        return self.chan[eng]


class Prog:
    def __init__(self, nc, stack):
        self.nc = nc
        self.stack = stack
        self.ops = {e: [] for e in ENGS}
        self.echan = {e: Chan(self) for e in ENGS}
        self.cur = {e: Event(e) for e in ENGS}
        self.dma_chans = []
        self.free_chans = {e: [] for e in ENGS}
        self.resources = []
        self.nsem = 0
        self.bar_seen = {}
        self.pending = {e: False for e in ENGS}

    def new_sem(self):
        self.nsem += 1
        return self.stack.enter_context(self.nc.semaphore(f"s{self.nsem}"))

    def get_dma_chan(self, eng):
        if self.free_chans[eng]:
            return self.free_chans[eng].pop()
        ch = Chan(self)
        self.dma_chans.append(ch)
        return ch

    def res(self, name):
        r = Res(self, name)
        self.resources.append(r)
        return r

    def _deps(self, eng, reads, writes):
        waits = []

        def add(ev):
            if ev is None:
                return
            if ev.sv is None:
                assert ev.eng == eng, (ev.eng, eng)
                return
            if ev.eng == eng and eng == 'pe':
                return
            waits.append(ev)

        for r in reads:
            add(r.last_write)
        for w in writes:
            add(w.last_write)
            for ev in w.readers.values():
                add(ev)
        return waits

    def _commit(self, ev, key, reads, writes):
        for w in writes:
            w.last_write = ev
            w.readers = {}
        for r in reads:
            r.readers[key] = ev

    def op(self, eng, fn, reads=(), writes=(), sig=True):
        waits = self._deps(eng, reads, writes)
        ev = self.cur[eng]
        inc = None
        if sig:
            self.echan[eng].bump(1, ev)
            inc = (ev.sv[0], 1)
            self.cur[eng] = Event(eng)
        self.pending[eng] = not sig
        rec = Rec()
        fn(rec)
        assert rec.call is not None
        self.ops[eng].append((rec.call, waits, inc))
        self._commit(ev, eng, reads, writes)
        return ev

    def dma(self, eng, out, in_, reads=(), writes=(), cres=None, **kw):
        waits = self._deps(eng, reads, writes)
        if cres is None:
            cres = writes[0] if writes else reads[0]
        ch = cres.dchan(eng)
        ev = Event('dma')
        ch.bump(16, ev)
        self.ops[eng].append((('dma_start', (), dict(out=out, in_=in_, **kw)), waits, (ev.sv[0], 16)))
        self._commit(ev, ch, reads, writes)
        return ev

    def barrier(self, release=True):
        evs = []
        for e in ENGS:
            if self.pending[e]:
                self.op(e, lambda eng: eng.nop())
            ch = self.echan[e]
            if ch.last is not None:
                evs.append(ch.last)
        for ch in self.dma_chans:
            if ch.last is not None:
                evs.append(ch.last)
        evs = [ev for ev in evs if self.bar_seen.get(id(ev.sv[0]), 0) < ev.sv[1]]
        for ev in evs:
            self.bar_seen[id(ev.sv[0])] = ev.sv[1]
        for e in ENGS:
            self.ops[e].append((None, list(evs), None))
        for r in self.resources:
            r.last_write = None
            r.readers = {}
            if release:
                for e_, ch_ in r.chan.items():
                    self.free_chans[e_].append(ch_)
                r.chan = {}
        if release:
            self.resources = []

    def emit(self):
        nc = self.nc
        self.barrier()
        P = self

        def run(eng_name, eng):
            known = {}
            for fn, waits, inc in P.ops[eng_name]:
                for ev in waits:
                    sem, val = ev.sv
                    if known.get(id(sem), 0) < val:
                        eng.wait_ge(sem, val)
                        known[id(sem)] = val
                if fn is not None:
                    inst = getattr(eng, fn[0])(*fn[1], **fn[2])
                    if inc is not None:
                        inst.then_inc(inc[0], inc[1])

        with nc.Block() as block:
            @block.tensor
            def _(e):
                run('pe', e)

            @block.scalar
            def _(e):
                run('act', e)

            @block.vector
            def _(e):
                run('dve', e)

            @block.gpsimd
            def _(e):
                run('pool', e)

            @block.sync
            def _(e):
                run('sp', e)


class Cfg:
    def __init__(self, D=4096, LS=4096, NPS=4):
        self.D = D
        self.LS = LS
        self.NPS = NPS
        self.SEQ = 256
        self.PAST = 256
        self.KD = D // 128
        self.BW = D // 2
        self.NH = self.BW // 128
        self.QL = D // 4
        self.KQ = self.QL // 128
        self.KVL = 512
        self.ROPE = 64
        self.T = LS + NPS * 256
        self.ROWS = LS // 64
        self.NQB = LS // 512
        self.AB = [('qa', 'fm', self.BW), ('ka', 'fm', self.BW), ('va', 'tm', self.BW), ('ga', 'fm', self.BW),
                   ('ub', 'fm', self.BW), ('vb', 'tm', self.BW), ('gb', 'fm', self.BW)]
        self.CD = [('qc', 'fm', self.BW), ('ffw', 'tm32', self.BW), ('fbw', 'tm32', self.BW), ('ic', 'tm', self.BW),
                   ('gc', 'fm', self.BW), ('cq', 'fm', self.QL), ('ckv', 'fm', 512), ('kr', 'fm', 64),
                   ('gd', 'fm', self.BW)]
        self.AB_IN = sum(s[2] for s in self.AB)
        self.CD_IN = sum(s[2] for s in self.CD)
        self.build_patterns()

    def build_patterns(self):
        ROWS = self.ROWS
        cols = np.arange(64)
        cstart = np.clip(cols - 8, 0, 48)
        colmask = (cols[None, :] >= cstart[:, None]) & (cols[None, :] < cstart[:, None] + 16)
        pats = {}
        self.pat_list = []
        self.qb_keys = []
        for qb in range(self.NQB):
            r = 8 * qb + np.arange(8)
            rs = np.clip(r - 4, 0, ROWS - 8)
            kt0 = rs.min() // 2
            kt1 = (rs.max() + 7) // 2
            lst = []
            for kt in range(kt0, kt1 + 1):
                m = np.zeros((2, 64, 8, 64), np.float32)
                for kr in range(2):
                    rp = 2 * kt + kr
                    for qr in range(8):
                        if rs[qr] <= rp < rs[qr] + 8:
                            m[kr, :, qr, :] = colmask.T
                m = m.reshape(128, 512)
                delta = 2 * kt - 8 * qb
                key = (m.tobytes(), delta)
                if key not in pats:
                    pats[key] = len(self.pat_list)
                    self.pat_list.append((m, delta))
                lst.append((kt, pats[key]))
            self.qb_keys.append(lst)
        self.NPAT = len(self.pat_list)


def host_consts(cfg):
    c = {}
    c['identb'] = np.eye(128, dtype=np.float32).astype(ml_dtypes.bfloat16)
    c['identf'] = np.eye(128, dtype=np.float32)
    c['mask01'] = np.stack([m for m, _ in cfg.pat_list], axis=1).astype(ml_dtypes.bfloat16)
    idx = np.arange(128)
    same = (idx[:, None] // 32) == (idx[None, :] // 32)
    hm = np.zeros((2, 128, 512), np.float32)
    for d in range(2):
        if d == 0:
            Mi = same & (idx[:, None] <= idx[None, :])
            Mr = same & (idx[:, None] > idx[None, :])
        else:
            Mi = same & (idx[:, None] >= idx[None, :])
            Mr = same & (idx[:, None] < idx[None, :])
        Mi = Mi.astype(np.float32)
        mid = (idx // 32) * 32 + 15
        Mc = Mi - Mi[:, mid]
        hm[d, :, 0:128] = Mi
        hm[d, :, 128:256] = Mc
        hm[d, :, 256:384] = Mr.astype(np.float32)
    c['hmat'] = np.ascontiguousarray(hm.transpose(1, 0, 2))
    t = np.arange(cfg.LS)
    inv = (10000.0 ** (-np.arange(16, dtype=np.float32) / 16)).astype(np.float32)
    cos = np.zeros((64, cfg.LS), np.float32)
    sin = np.zeros((64, cfg.LS), np.float32)
    prot = np.zeros((64, 64), np.float32)
    for j in range(64):
        b, jj = j // 32, j % 32
        i = jj % 16
        pos = (t // 64 if b == 0 else t % 64).astype(np.float32)
        ang = pos * inv[i]
        cos[j] = np.cos(ang)
        sin[j] = np.sin(ang)
        if jj < 16:
            prot[j + 16, j] = -1.0
        else:
            prot[j - 16, j] = 1.0
    c['ropecos'] = cos
    c['ropesin'] = sin
    c['prot'] = prot
    return c


def build_program(cfg, debug=()):
    D, KD, BW, NH, T, LS, NPS, QL, KQ = cfg.D, cfg.KD, cfg.BW, cfg.NH, cfg.T, cfg.LS, cfg.NPS, cfg.QL, cfg.KQ
    NPT = NPS * 256
    nc = bass.Bass("TRN2", target_bir_lowering=False)
    dr = {}

    def din(name, shape, dt=F32):
        dr[name] = nc.dram_tensor(name, list(shape), dt, kind="ExternalInput").ap()

    def dout(name, shape, dt=F32):
        dr[name] = nc.dram_tensor(name, list(shape), dt, kind="ExternalOutput").ap()

    def dscr(name, shape, dt):
        kind = "ExternalOutput" if name in debug else "Internal"
        dr[name] = nc.dram_tensor(name, list(shape), dt, kind=kind).ap()

    din('x_all', [T, D])
    din('cond2', [2, D])
    din('c_na_k', [256, BW])
    din('c_na_v', [256, BW])
    din('st_hgrn', [2, NH, 128, 128])
    din('c_ckv', [256, 512])
    din('c_kr', [256, 64])
    din('norm_w', [2, D])
    din('w_ada', [2, D, 3 * D])
    din('b_ada', [2, 3 * D])
    din('w_out', [2, D, D])
    din('w_in_ab', [D, cfg.AB_IN])
    din('na_q_norm', [128])
    din('na_k_norm', [128])
    din('rp', [NH, 23, 127])
    din('sg_norm', [BW])
    din('sg_w', [NH, 128, 128])
    din('sg_b', [NH * 128])
    din('w_in_cd', [D, cfg.CD_IN])
    din('hgrn_lb', [2, 2, BW])
    din('hgrn_out_norm', [128])
    din('mla_q_a_norm', [QL])
    din('mla_w_q_up', [QL, NH * 192])
    din('mla_kv_a_norm', [512])
    din('mla_w_kv_up', [512, NH * 256])
    din('mla_q_norm', [192])
    din('mla_k_norm', [192])
    din('identb', [128, 128], BF16)
    din('identf', [128, 128])
    din('mask01', [128, cfg.NPAT, 512], BF16)
    din('hmat', [128, 2, 512])
    din('ropecos', [64, LS])
    din('ropesin', [64, LS])
    din('prot', [64, 64])
    dout('y_all', [T, D])
    dout('o_na_k', [NPT, BW])
    dout('o_na_v', [NPT, BW])
    dout('o_hgrn', [NPS, 2, NH, 128, 128])
    dout('o_ckv', [NPT, 512])
    dout('o_kr', [NPT, 64])
    for (nm, kind, w) in cfg.AB + cfg.CD:
        if kind == 'fm':
            dscr('s_' + nm, [w, T], BF16)
        elif kind == 'tm':
            dscr('s_' + nm, [T, w], BF16)
        else:
            dscr('s_' + nm, [T, w], F32)
    dscr('s_mix', [D, T], BF16)
    dscr('s_x1', [T, D], F32)
    dscr('s_gate', [2, 2, D], F32)
    dscr('s_tf', [NH, 23, 64, 64], F32)
    dscr('s_of', [BW, T], F32)
    dscr('s_cqn', [QL, T], BF16)

    st = ExitStack()
    with st:
        P = Prog(nc, st)
        ARENA = 46 * 1024
        arena = st.enter_context(nc.sbuf_tensor("arena", [128, ARENA], F32))
        psum = [st.enter_context(nc.psum_tensor(f"ps{i}", [128, 512], F32)) for i in range(8)]
        PERS = 1024
        pos = [0]

        def alloc(n, dt=F32, np_=128):
            words = (n + 1) // 2 if dt == BF16 else n
            words = (words + 7) // 8 * 8
            a = arena[:, pos[0]:pos[0] + words]
            pos[0] += words
            assert pos[0] <= ARENA, pos[0]
            if dt == BF16:
                a = a.bitcast(BF16)[:, :n]
            else:
                a = a[:, :n]
            return a

        identb = alloc(128, BF16)
        identf = alloc(128)
        onesb = alloc(128, BF16)
        AT = [alloc(KD * 2) for _ in range(2)]
        SH = [alloc(KD * 2) for _ in range(2)]
        epsc = alloc(1)
        assert pos[0] <= PERS
        rP = P.res('pers')
        P.dma('sp', identb, dr['identb'], writes=[rP])
        P.dma('sp', identf, dr['identf'], writes=[rP])
        P.op('dve', lambda e: e.memset(onesb, 1.0), writes=[rP])
        P.op('dve', lambda e: e.memset(epsc, EPS), writes=[rP])
        P.barrier(release=False)
        P.resources = []

        bank_rr = [0]

        def new_phase():
            P.barrier()
            pos[0] = PERS
            rps = [P.res(f'ps{i}') for i in range(8)]
            return rps

        def V3(ap, a, b):
            return ap.rearrange("p (a b) -> p a b", a=a, b=b)

        cnt = [0]

        def evac(out, in_, reads, writes, eng=None):
            cnt[0] += 1
            if eng is None:
                eng = 'act' if cnt[0] % 2 else 'dve'
            if eng == 'act':
                P.op('act', lambda e: e.activation(out=out, in_=in_, func=AF.Copy), reads=reads, writes=writes)
            else:
                P.op('dve', lambda e: e.tensor_copy(out=out, in_=in_), reads=reads, writes=writes)

        def rstd_chain(dst, src, inv_n, reads, writes):
            P.op('dve', lambda e: e.tensor_scalar(out=dst, in0=src, scalar1=inv_n, scalar2=EPS, op0=ALU.mult, op1=ALU.add),
                 reads=reads, writes=writes)
            P.op('act', lambda e: e.activation(out=dst, in_=dst, func=AF.Ln), reads=writes, writes=writes)
            P.op('act', lambda e: e.activation(out=dst, in_=dst, func=AF.Exp, scale=-0.5), reads=writes, writes=writes)

        def sigmoid_ops(dst, src, scale, R):
            P.op('act', lambda e: e.activation(out=dst, in_=src, func=AF.Sigmoid, scale=scale), reads=R, writes=R)

        def silu_ops(dst, src, tmp, R):
            P.op('act', lambda e: e.activation(out=dst, in_=src, func=AF.Silu), reads=R, writes=R)

        def gelu_ops(dst, src, tmp, tmp2, R):
            P.op('act', lambda e: e.activation(out=tmp, in_=src, func=AF.Square), reads=R, writes=R)
            P.op('dve', lambda e: e.tensor_scalar(out=tmp, in0=tmp, scalar1=0.044715, scalar2=1.0, op0=ALU.mult, op1=ALU.add),
                 reads=R, writes=R)
            P.op('dve', lambda e: e.tensor_tensor(out=tmp, in0=tmp, in1=src, op=ALU.mult), reads=R, writes=R)
            sigmoid_ops(tmp2, tmp, GK, R)
            P.op('dve', lambda e: e.tensor_tensor(out=dst, in0=src, in1=tmp2, op=ALU.mult), reads=R, writes=R)

        stages = os.environ.get("MK_STAGES", "all").split(',')
        ON = lambda nm: stages == ['all'] or nm in stages
        for l in (range(2) if ON('mod') else []):
            rps = new_phase()
            R = P.res('m')
            cT = alloc(KD * 2)
            tmpc = alloc(KD * 2)
            sT = alloc(KD * 2, BF16)
            bT = alloc(3 * KD)
            nwT = alloc(KD)
            modT = alloc(3 * KD * 2)
            wts = [alloc(KD * 512, BF16) for _ in range(2)]
            rW = [P.res('w0'), P.res('w1')]
            for c in range(2):
                P.dma('sp', V3(cT, 2, KD)[:, c, :], dr['cond2'][c].rearrange("(k p) -> p k", p=128), writes=[R],
                      allow_slow_non_contiguous=True)
            P.dma('sp', bT, dr['b_ada'][l].rearrange("(j p) -> p j", p=128), writes=[R], allow_slow_non_contiguous=True)
            P.dma('sp', nwT, dr['norm_w'][l].rearrange("(j p) -> p j", p=128), writes=[R], allow_slow_non_contiguous=True)
            silu_ops(cT, cT, tmpc, [R])
            P.op('dve', lambda e: e.tensor_copy(out=V3(sT, KD, 2), in_=V3(cT, 2, KD).rearrange("p c k -> p k c")), reads=[R],
                 writes=[R])
            ncb = 3 * D // 512
            for cb in range(ncb):
                s = cb % 2
                wt = V3(wts[s], KD, 512)
                P.dma('pool', wt, dr['w_ada'][l][:, cb * 512:(cb + 1) * 512].rearrange("(k p) c -> p k c", p=128),
                      writes=[rW[s]])
                b = cb % 8
                for j in range(4):
                    for k in range(KD):
                        P.op('pe', lambda e, b=b, j=j, k=k, wt=wt: e.matmul(
                            psum[b][:, 2 * j:2 * j + 2], lhsT=wt[:, k, j * 128:(j + 1) * 128], rhs=V3(sT, KD, 2)[:, k, :],
                            start=(k == 0), stop=(k == KD - 1)), reads=[rW[s], R], writes=[rps[b]],
                            sig=(j == 3 and k == KD - 1))
                P.op('dve', lambda e, b=b, cb=cb: e.tensor_tensor(
                    out=V3(modT, 3 * KD, 2)[:, cb * 4:(cb + 1) * 4, :], in0=V3(psum[b][:, 0:8], 4, 2),
                    in1=bT[:, cb * 4:(cb + 1) * 4].unsqueeze(2).broadcast_to([128, 4, 2]), op=ALU.add),
                    reads=[rps[b], R], writes=[R])
            m3 = V3(modT, 3 * KD, 2)
            P.op('dve', lambda e: e.tensor_copy(out=V3(SH[l], KD, 2), in_=m3[:, 0:KD, :]), reads=[R], writes=[rP])
            P.op('dve', lambda e: e.tensor_scalar_add(out=V3(AT[l], KD, 2), in0=m3[:, KD:2 * KD, :], scalar1=1.0),
                 reads=[R], writes=[rP])
            P.op('dve', lambda e: e.tensor_tensor(out=V3(AT[l], KD, 2), in0=V3(AT[l], KD, 2),
                                                  in1=nwT.unsqueeze(2).broadcast_to([128, KD, 2]), op=ALU.mult),
                 reads=[R, rP], writes=[rP])
            gtmp = alloc(2 * KD)
            P.op('dve', lambda e: e.tensor_copy(out=V3(gtmp, 2, KD), in_=m3[:, 2 * KD:3 * KD, :].rearrange("p k c -> p c k")),
                 reads=[R], writes=[R])
            for c in range(2):
                P.dma('sp', dr['s_gate'][l][c].rearrange("(k p) -> p k", p=128), V3(gtmp, 2, KD)[:, c, :], reads=[R],
                      allow_slow_non_contiguous=True)

        def phase_inproj(l, W, segs, xsrc):
            rps = new_phase()
            TB = min(1024, LS)
            hT = alloc(KD * TB, BF16)
            hT3 = V3(hT, KD, TB)
            wts = [V3(alloc(KD * 512, BF16), KD, 512) for _ in range(2)]
            rW = [P.res('w0'), P.res('w1')]
            xts = [alloc(D) for _ in range(2)]
            rX = [P.res('x0'), P.res('x1')]
            xn = alloc(D, BF16)
            rXn = P.res('xn')
            ss = alloc(1)
            rstd = alloc(1)
            rS = P.res('ss')
            stg = [alloc(512) for _ in range(4)]
            rStg = [P.res(f'stg{i}') for i in range(4)]
            rH = P.res('hT')
            sc = [0]
            wc = [0]
            bc = [0]
            SK = os.environ.get("MK_SKIP", "")
            for tb0 in range(0, T, TB):
                for ti in range(TB // 128 if 'L' not in SK else 0):
                    tok0 = tb0 + ti * 128
                    cond = 0 if tok0 < LS else 1
                    s = ti % 2
                    P.dma('sp', xts[s], xsrc[tok0:tok0 + 128, :], writes=[rX[s]])
                    if 'A' in SK:
                        continue
                    P.op('act', lambda e, s=s: e.activation(out=xn, in_=xts[s], func=AF.Square, accum_out=ss),
                         reads=[rX[s]], writes=[rXn, rS])
                    rstd_chain(rstd, ss, 1.0 / D, [rS], [rS])
                    P.op('act', lambda e, s=s: e.activation(out=xn, in_=xts[s], func=AF.Copy, scale=rstd),
                         reads=[rX[s], rS], writes=[rXn])
                    for kg in range(0, KD if 'T' not in SK else 0, 8):
                        nk = min(8, KD - kg)
                        b = bc[0] % 8
                        bc[0] += 1
                        pb = psum[b][:].bitcast(BF16)
                        for j in range(nk):
                            P.op('pe', lambda e, pb=pb, j=j, kg=kg: e.transpose(
                                pb[:, j * 128:(j + 1) * 128], xn[:, (kg + j) * 128:(kg + j + 1) * 128], identb),
                                reads=[rXn, rP], writes=[rps[b]], sig=(j == nk - 1))
                        for j in range(nk):
                            k = kg + j
                            P.op('dve', lambda e, pb=pb, j=j, k=k, ti=ti, cond=cond: e.tensor_scalar(
                                out=hT3[:, k, ti * 128:(ti + 1) * 128], in0=pb[:, j * 128:(j + 1) * 128],
                                scalar1=AT[l][:, 2 * k + cond:2 * k + cond + 1],
                                scalar2=SH[l][:, 2 * k + cond:2 * k + cond + 1], op0=ALU.mult, op1=ALU.add),
                                reads=[rps[b], rP], writes=[rH])
                off = 0
                for (nm, kind, w) in (segs if 'W' not in SK else []):
                    if (os.environ.get("MK_SEG") and kind != os.environ.get("MK_SEG")) or (os.environ.get("MK_ONLY") and nm != os.environ.get("MK_ONLY")):
                        off += w
                        continue
                    for c0 in range(0, w, 512):
                        cw = min(512, w - c0)
                        s = wc[0] % 2
                        wc[0] += 1
                        wt = wts[s]
                        P.dma('pool', wt[:, :, :cw],
                              W[:, off + c0:off + c0 + cw].rearrange("(k p) c -> p k c", p=128), writes=[rW[s]])
                        if kind == 'fm':
                            for j0 in range(0, cw, 128):
                                m = min(128, cw - j0)
                                for t0 in range(0, TB, 512):
                                    b = bc[0] % 8
                                    bc[0] += 1
                                    for k in range(KD):
                                        P.op('pe', lambda e, b=b, m=m, k=k, j0=j0, t0=t0, wt=wt: e.matmul(
                                            psum[b][:m, :], lhsT=wt[:, k, j0:j0 + m], rhs=hT3[:, k, t0:t0 + 512],
                                            start=(k == 0), stop=(k == KD - 1)), reads=[rW[s], rH], writes=[rps[b]],
                                            sig=(k == KD - 1))
                                    q = sc[0] % 4
                                    sc[0] += 1
                                    so = stg[q].bitcast(BF16)[:m, 0:512]
                                    evac(so, psum[b][:m, :], [rps[b]], [rStg[q]])
                                    P.dma('sp', dr['s_' + nm][c0 + j0:c0 + j0 + m, tb0 + t0:tb0 + t0 + 512], so,
                                          reads=[rStg[q]])
                        else:
                            for ti in range(TB // 128):
                                tok0 = tb0 + ti * 128
                                b = bc[0] % 8
                                bc[0] += 1
                                for k in range(KD):
                                    P.op('pe', lambda e, b=b, k=k, ti=ti, wt=wt, cw=cw: e.matmul(
                                        psum[b][:, :cw], lhsT=hT3[:, k, ti * 128:(ti + 1) * 128], rhs=wt[:, k, :cw],
                                        start=(k == 0), stop=(k == KD - 1)), reads=[rW[s], rH], writes=[rps[b]],
                                        sig=(k == KD - 1))
                                q = sc[0] % 4
                                sc[0] += 1
                                if nm == 'va' and tok0 >= LS:
                                    s32 = stg[q][:, 0:cw]
                                    evac(s32, psum[b][:, :cw], [rps[b]], [rStg[q]])
                                    P.dma('sp', dr['o_na_v'][tok0 - LS:tok0 - LS + 128, c0:c0 + cw], s32, reads=[rStg[q]])
                                    q2 = sc[0] % 4
                                    sc[0] += 1
                                    so = stg[q2].bitcast(BF16)[:, 0:cw]
                                    P.op('dve', lambda e: e.tensor_copy(out=so, in_=s32), reads=[rStg[q]], writes=[rStg[q2]])
                                    P.dma('sp', dr['s_' + nm][tok0:tok0 + 128, c0:c0 + cw], so, reads=[rStg[q2]])
                                    continue
                                if kind == 'tm':
                                    so = stg[q].bitcast(BF16)[:, 0:cw]
                                else:
                                    so = stg[q][:, 0:cw]
                                evac(so, psum[b][:, :cw], [rps[b]], [rStg[q]])
                                P.dma('sp', dr['s_' + nm][tok0:tok0 + 128, c0:c0 + cw], so, reads=[rStg[q]])
                    off += w

        def phase_outproj(l, xsrc, ydst):
            rps = new_phase()
            TB = min(1024, LS)
            mixT = V3(alloc(KD * TB, BF16), KD, TB)
            rM = P.res('mix')
            wts = [V3(alloc(KD * 512, BF16), KD, 512) for _ in range(2)]
            rW = [P.res('w0'), P.res('w1')]
            gbc = [alloc(D) for _ in range(2)]
            rG = P.res('g')
            xb = [alloc(512) for _ in range(4)]
            rXb = [P.res(f'xb{i}') for i in range(4)]
            tmps = [alloc(512) for _ in range(2)]
            rTmp = [P.res('t0'), P.res('t1')]
            for c in range(2):
                P.dma('sp', gbc[c], dr['s_gate'][l][c:c + 1, :].partition_broadcast(128), writes=[rG])
            wc = 0
            bc = 0
            xc = 0
            for tb0 in range(0, T, TB):
                P.dma('sp', mixT, dr['s_mix'][:, tb0:tb0 + TB].rearrange("(k p) t -> p k t", p=128), writes=[rM])
                for c0 in range(0, D, 512):
                    s = wc % 2
                    wc += 1
                    wt = wts[s]
                    P.dma('pool', wt, dr['w_out'][l][:, c0:c0 + 512].rearrange("(k p) c -> p k c", p=128), writes=[rW[s]])
                    for ti in range(TB // 128):
                        tok0 = tb0 + ti * 128
                        cond = 0 if tok0 < LS else 1
                        b = bc % 8
                        bc += 1
                        q = xc % 4
                        xc += 1
                        P.dma('sp', xb[q], xsrc[tok0:tok0 + 128, c0:c0 + 512], writes=[rXb[q]])
                        for k in range(KD):
                            P.op('pe', lambda e, b=b, k=k, ti=ti, wt=wt: e.matmul(
                                psum[b][:], lhsT=mixT[:, k, ti * 128:(ti + 1) * 128], rhs=wt[:, k, :],
                                start=(k == 0), stop=(k == KD - 1)), reads=[rW[s], rM], writes=[rps[b]], sig=(k == KD - 1))
                        P.op('dve', lambda e: e.tensor_tensor(out=tmps[q % 2], in0=psum[b][:], in1=gbc[cond][:, c0:c0 + 512],
                                                              op=ALU.mult), reads=[rps[b], rG], writes=[rTmp[q % 2]])
                        P.op('dve', lambda e: e.tensor_tensor(out=xb[q], in0=tmps[q % 2], in1=xb[q], op=ALU.add),
                             reads=[rTmp[q % 2]], writes=[rXb[q]])
                        P.dma('sp', ydst[tok0:tok0 + 128, c0:c0 + 512], xb[q], reads=[rXb[q]])

        def fm_norm(dst, src, gain, np_, n, R, rps, b, sq, rstd, extra_dst=None):
            P.op('act', lambda e: e.activation(out=sq[:np_, :n], in_=src, func=AF.Square), reads=R, writes=R)
            P.op('pe', lambda e: e.matmul(psum[b][:np_, :n], lhsT=onesb[:np_, :np_], rhs=sq[:np_, :n], start=True, stop=True),
                 reads=R + [rP], writes=[rps[b]])
            rstd_chain(rstd[:np_, :n], psum[b][:np_, :n], 1.0 / np_, [rps[b]] + R, R)
            P.op('dve', lambda e: e.scalar_tensor_tensor(out=dst, in0=src, scalar=gain, in1=rstd[:np_, :n],
                                                         op0=ALU.mult, op1=ALU.mult), reads=R + [rP], writes=R)
            if extra_dst is not None:
                P.op('dve', lambda e: e.scalar_tensor_tensor(out=extra_dst, in0=src, scalar=gain, in1=rstd[:np_, :n],
                                                             op0=ALU.mult, op1=ALU.mult), reads=R + [rP], writes=R)

        def attn_block(qT, q2T, keys, nq, o32, R, rps, pts, rPt, rec):
            n = len(keys)
            NBF = 3
            LA = 2

            def score(i):
                kT, k2T, v, mask = keys[i]
                sb = i % NBF
                P.op('pe', lambda e: e.matmul(psum[sb][:, :nq], lhsT=kT, rhs=qT, start=True, stop=(k2T is None)),
                     reads=R, writes=[rps[sb]], sig=(k2T is None))
                if k2T is not None:
                    P.op('pe', lambda e: e.matmul(psum[sb][:, :nq], lhsT=k2T, rhs=q2T, start=False, stop=True),
                         reads=R, writes=[rps[sb]])

            for i in range(min(LA, n)):
                score(i)
            for i in range(n):
                kT, k2T, v, mask = keys[i]
                sb = i % NBF
                pt = pts[sb]
                if i + LA < n:
                    score(i + LA)
                P.op('act', lambda e: e.activation(out=pt[:, :nq], in_=psum[sb][:, :nq], func=AF.Exp),
                     reads=[rps[sb]], writes=[rPt[sb]])
                if mask is not None:
                    P.op('dve', lambda e: e.tensor_tensor(out=pt[:, :nq], in0=pt[:, :nq], in1=mask, op=ALU.mult),
                         reads=R + [rPt[sb]], writes=[rPt[sb]])
                P.op('pe', lambda e: e.matmul(psum[3][:, :nq], lhsT=v, rhs=pt[:, :nq], start=(i == 0), stop=(i == n - 1)),
                     reads=R + [rPt[sb]], writes=[rps[3]], sig=False)
                P.op('pe', lambda e: e.matmul(psum[4][:, :nq], lhsT=onesb, rhs=pt[:, :nq], start=(i == 0), stop=(i == n - 1)),
                     reads=[rPt[sb], rP], writes=[rps[4]])
            P.op('dve', lambda e: e.reciprocal(out=rec[:, :nq], in_=psum[4][:, :nq]), reads=[rps[4]], writes=R)
            P.op('dve', lambda e: e.tensor_tensor(out=o32, in0=psum[3][:, :nq], in1=rec[:, :nq], op=ALU.mult),
                 reads=[rps[3], rps[4]] + R, writes=R)

        def phase_mixA():
            rps = new_phase()
            R = P.res('a')
            NP = cfg.NPAT
            mask01 = V3(alloc(NP * 512, BF16), NP, 512)
            P.dma('sp', mask01, dr['mask01'], writes=[R])
            qg = alloc(1)
            kg = alloc(1)
            P.dma('sp', qg, dr['na_q_norm'].rearrange("(p o) -> p o", o=1), writes=[R])
            P.dma('sp', kg, dr['na_k_norm'].rearrange("(p o) -> p o", o=1), writes=[R])
            P.op('dve', lambda e: e.tensor_scalar_mul(out=qg, in0=qg, scalar1=128.0 ** -0.5), reads=[R], writes=[R])
            rTF = P.res('tf')
            for ck in range(64):
                P.dma('sp', dr['s_tf'][:, :, ck, :], dr['rp'][:, :, 63 - ck:63 - ck + 64], writes=[rTF])
            qraw = alloc(T, BF16)
            kraw = alloc(T, BF16)
            qn = alloc(T, BF16)
            kn = alloc(T, BF16)
            sg = alloc(T, BF16)
            vt = V3(alloc(T, BF16), T // 128, 128)
            kc32 = V3(alloc(256, BF16), 2, 128)
            kcT = alloc(256, BF16)
            vc = V3(alloc(256, BF16), 2, 128)
            stage32 = V3(alloc(NP * 512), NP, 512)
            EB = V3(alloc(NP * 512, BF16), NP, 512)
            sq = alloc(512, BF16)
            rstd = alloc(512)
            knf = alloc(512)
            tmp = alloc(512)
            pts = [alloc(512, BF16) for _ in range(3)]
            rPt = [P.res('pt0'), P.res('pt1'), P.res('pt2')]
            rec = alloc(512)
            o32 = alloc(512)
            mst = alloc(512, BF16)
            rMst = P.res('mst')
            tst = alloc(128)
            rTst = P.res('tst')
            rL = P.res('loads')
            rE = P.res('eb')
            for h in range(NH):
                hs = slice(h * 128, (h + 1) * 128)
                P.dma('sp', qraw, dr['s_qa'][hs, :], writes=[rL])
                P.dma('sp', kraw, dr['s_ka'][hs, :], writes=[rL])
                P.dma('sp', sg, dr['s_ga'][hs, :], writes=[rL])
                P.dma('sp', vt, dr['s_va'][:, hs].rearrange("(n p) c -> p n c", p=128), writes=[rL])
                P.dma('pool', kc32, dr['c_na_k'][:, hs].rearrange("(n p) c -> p n c", p=128), writes=[rL])
                P.dma('pool', vc, dr['c_na_v'][:, hs].rearrange("(n p) c -> p n c", p=128), writes=[rL])
                for n_ in range(2):
                    pb = psum[7][:].bitcast(BF16)
                    P.op('pe', lambda e, n_=n_, pb=pb: e.transpose(pb[:, n_ * 128:(n_ + 1) * 128], kc32[:, n_, :], identb),
                         reads=[rL, rP], writes=[rps[7]])
                P.op('dve', lambda e: e.tensor_copy(out=kcT, in_=psum[7][:].bitcast(BF16)[:, 0:256]), reads=[rps[7]],
                     writes=[rL])
                for t0 in range(0, T, 512):
                    silu_ops(sg[:, t0:t0 + 512], sg[:, t0:t0 + 512], tmp, [rL, R])
                for t0 in range(0, T, 512):
                    fm_norm(qn[:, t0:t0 + 512], qraw[:, t0:t0 + 512], qg, 128, 512, [rL, R], rps, 5, sq, rstd)
                    fm_norm(kn[:, t0:t0 + 512], kraw[:, t0:t0 + 512], kg, 128, 512, [rL, R], rps, 6, sq, rstd,
                            extra_dst=(knf if t0 >= LS else None))
                    if t0 >= LS:
                        for j in range(4):
                            P.op('pe', lambda e, j=j: e.transpose(psum[7][:, j * 128:(j + 1) * 128],
                                                                  knf[:, j * 128:(j + 1) * 128], identf),
                                 reads=[rL, R, rP], writes=[rps[7]])
                            P.op('act', lambda e, j=j: e.activation(out=tst, in_=psum[7][:, j * 128:(j + 1) * 128],
                                                                    func=AF.Copy), reads=[rps[7]], writes=[rTst])
                            P.dma('sp', dr['o_na_k'][t0 - LS + j * 128:t0 - LS + (j + 1) * 128, hs], tst, reads=[rTst])
                for pi, (_, delta) in enumerate(cfg.pat_list):
                    for kr in range(2):
                        e0 = 11 - delta - kr
                        P.dma('sp', stage32[kr * 64:(kr + 1) * 64, pi, :].rearrange("p (a b) -> p a b", a=8, b=64),
                              dr['s_tf'][h, e0:e0 + 8, :, :].rearrange("e ck cq -> ck e cq"), reads=[rTF], writes=[rE])
                for pi in range(NP):
                    P.op('act', lambda e, pi=pi: e.activation(out=stage32[:, pi, :], in_=stage32[:, pi, :], func=AF.Exp),
                         reads=[rE], writes=[rE])
                    P.op('dve', lambda e, pi=pi: e.tensor_tensor(out=EB[:, pi, :], in0=stage32[:, pi, :],
                                                                 in1=mask01[:, pi, :], op=ALU.mult), reads=[rE, R],
                         writes=[rE])
                RR = [rL, R, rE]
                for qb in range(cfg.NQB):
                    keys = []
                    for (kt, pi) in cfg.qb_keys[qb]:
                        keys.append((kn[:, kt * 128:(kt + 1) * 128], None, vt[:, kt, :], EB[:, pi, :]))
                    for n_ in range(2):
                        keys.append((kcT[:, n_ * 128:(n_ + 1) * 128], None, vc[:, n_, :], None))
                    attn_block(qn[:, qb * 512:(qb + 1) * 512], None, keys, 512, o32, RR, rps, pts, rPt, rec)
                    P.op('dve', lambda e, qb=qb: e.tensor_tensor(out=mst, in0=o32, in1=sg[:, qb * 512:(qb + 1) * 512],
                                                                 op=ALU.mult), reads=RR, writes=[rMst])
                    P.dma('sp', dr['s_mix'][hs, qb * 512:(qb + 1) * 512], mst, reads=[rMst])
                for s_ in range(NPS):
                    t0 = LS + s_ * 256
                    keys = [(kn[:, t0 + n_ * 128:t0 + (n_ + 1) * 128], None, vt[:, t0 // 128 + n_, :], None)
                            for n_ in range(2)]
                    attn_block(qn[:, t0:t0 + 256], None, keys, 256, o32[:, :256], RR, rps, pts, rPt, rec)
                    P.op('dve', lambda e, t0=t0: e.tensor_tensor(out=mst[:, :256], in0=o32[:, :256], in1=sg[:, t0:t0 + 256],
                                                                 op=ALU.mult), reads=RR, writes=[rMst])
                    P.dma('sp', dr['s_mix'][hs, t0:t0 + 256], mst[:, :256], reads=[rMst])

        def phase_mixB():
            rps = new_phase()
            R = P.res('b')
            NB = NH * 128
            sgw32 = V3(alloc(NB), NH, 128)
            sgwb = V3(alloc(NB, BF16), NH, 128)
            sgwT = V3(alloc(NB, BF16), NH, 128)
            sgb = alloc(NB)
            sgn = alloc(BW)
            P.dma('sp', sgw32, dr['sg_w'].rearrange("g t s -> t g s"), writes=[R])
            P.dma('sp', sgb, dr['sg_b'].rearrange("(o n) -> o n", o=1).partition_broadcast(128), writes=[R])
            P.dma('sp', sgn, dr['sg_norm'].rearrange("(o n) -> o n", o=1).partition_broadcast(128), writes=[R])
            P.op('dve', lambda e: e.tensor_copy(out=sgwb, in_=sgw32), reads=[R], writes=[R])
            for g in range(NH):
                pb = psum[0][:].bitcast(BF16)
                P.op('pe', lambda e, g=g, pb=pb: e.transpose(pb[:, 0:128], sgwb[:, g, :], identb), reads=[R, rP],
                     writes=[rps[0]])
                P.op('dve', lambda e, g=g, pb=pb: e.tensor_copy(out=sgwT[:, g, :], in_=pb[:, 0:128]), reads=[rps[0]],
                     writes=[R])
            vb = alloc(BW, BF16)
            gv = alloc(BW)
            t1 = alloc(BW)
            t2 = alloc(BW)
            vn = alloc(BW, BF16)
            uT = alloc(NB, BF16)
            gT = alloc(NB, BF16)
            gu = alloc(NB)
            ss = alloc(1)
            rstd = alloc(1)
            mo = alloc(NB, BF16)
            rMo = P.res('mo')
            rL = P.res('ld')
            RR = [R, rL]
            for ti in range(T // 128):
                tk = slice(ti * 128, (ti + 1) * 128)
                P.dma('sp', vb, dr['s_vb'][tk, :], writes=[rL])
                P.dma('sp', V3(uT, NH, 128), dr['s_ub'][:, tk].rearrange("(g c) t -> c g t", c=128), writes=[rL])
                P.dma('sp', V3(gT, NH, 128), dr['s_gb'][:, tk].rearrange("(g c) t -> c g t", c=128), writes=[rL])
                gelu_ops(gv, vb, t1, t2, RR)
                P.op('act', lambda e: e.activation(out=t1, in_=gv, func=AF.Square, accum_out=ss), reads=RR, writes=RR)
                rstd_chain(rstd, ss, 1.0 / BW, RR, RR)
                P.op('dve', lambda e: e.scalar_tensor_tensor(out=vn, in0=gv, scalar=rstd, in1=sgn, op0=ALU.mult,
                                                             op1=ALU.mult), reads=RR, writes=RR)
                gelu_ops(gu, uT, t1[:, :NB], t2[:, :NB], RR)
                for g in range(NH):
                    b = (g * 128) // 512
                    P.op('pe', lambda e, g=g, b=b: e.matmul(psum[b][:, (g * 128) % 512:(g * 128) % 512 + 128],
                                                            lhsT=vn[:, g * 128:(g + 1) * 128], rhs=sgwT[:, g, :],
                                                            start=True, stop=True), reads=RR, writes=[rps[b]])
                for b in range((NB + 511) // 512):
                    w = min(512, NB - b * 512)
                    cs = slice(b * 512, b * 512 + w)
                    P.op('dve', lambda e, b=b, w=w, cs=cs: e.tensor_tensor(out=t1[:, cs], in0=psum[b][:, :w], in1=sgb[:, cs],
                                                                           op=ALU.add), reads=[rps[b]] + RR, writes=RR)
                P.op('dve', lambda e: e.tensor_tensor(out=gu, in0=gu, in1=t1[:, :NB], op=ALU.mult), reads=RR, writes=RR)
                silu_ops(t1[:, :NB], gT, t2[:, :NB], RR)
                P.op('dve', lambda e: e.tensor_tensor(out=mo, in0=gu, in1=t1[:, :NB], op=ALU.mult), reads=RR, writes=[rMo])
                P.dma('sp', dr['s_mix'][BW:2 * BW, tk].rearrange("(g c) t -> c g t", c=128), V3(mo, NH, 128), reads=[rMo])

        def phase_hgrn():
            rps = new_phase()
            R = P.res('c')
            hmat = V3(alloc(1024), 2, 512)
            P.dma('sp', hmat, dr['hmat'], writes=[R])
            maskb = V3(alloc(256, BF16), 2, 128)
            P.op('dve', lambda e: e.tensor_copy(out=maskb, in_=hmat[:, :, 0:128]), reads=[R], writes=[R])
            lbb = [alloc(BW) for _ in range(2)]
            oml = [alloc(BW) for _ in range(2)]
            tl = alloc(BW)
            og = alloc(1)
            P.dma('sp', og, dr['hgrn_out_norm'].rearrange("(p o) -> p o", o=1), writes=[R])
            for d in range(2):
                P.dma('sp', lbb[d], dr['hgrn_lb'][1, d:d + 1, :].partition_broadcast(128), writes=[R])
                P.dma('sp', tl, dr['hgrn_lb'][0, d:d + 1, :].partition_broadcast(128), writes=[R])
                P.op('dve', lambda e, d=d: e.tensor_tensor(out=lbb[d], in0=lbb[d], in1=tl, op=ALU.subtract), reads=[R],
                     writes=[R])
                sigmoid_ops(lbb[d], lbb[d], 1.0, [R])
                P.op('dve', lambda e, d=d: e.tensor_scalar(out=oml[d], in0=lbb[d], scalar1=-1.0, scalar2=1.0, op0=ALU.mult,
                                                           op1=ALU.add), reads=[R], writes=[R])
            z = alloc(BW)
            f = alloc(BW)
            gl = alloc(BW)
            kk = alloc(BW)
            khat = alloc(BW, BF16)
            ktil = alloc(BW, BF16)
            vt = alloc(BW, BF16)
            qraw = V3(alloc(BW, BF16), NH, 128)
            gcT = V3(alloc(BW, BF16), NH, 128)
            S = V3(alloc(NH * 128), NH, 128)
            EEs = [alloc(256) for _ in range(4)]
            qincs = [alloc(128) for _ in range(4)]
            qtils = [alloc(128, BF16) for _ in range(4)]
            ktTs = [alloc(128, BF16) for _ in range(4)]
            pTms = [alloc(128, BF16) for _ in range(4)]
            rA = [rps[2], rps[2], rps[3], rps[3]]
            rEE = [P.res(f'EE{j}') for j in range(4)]
            rQi = [P.res(f'Qi{j}') for j in range(4)]
            rQt = [P.res(f'Qt{j}') for j in range(4)]
            rTr = [rps[4]] * 4
            rKt = [P.res(f'Kt{j}') for j in range(4)]
            rPT = [rps[5]] * 4
            rPm = [P.res(f'Pm{j}') for j in range(4)]
            rOj = [rps[6]] * 4
            rU = [rps[7]] * 4
            oall = V3(alloc(BW), NH, 128)
            ofl = V3(alloc(BW), NH, 128)
            sqb = alloc(BW, BF16)
            mo = alloc(BW, BF16)
            rMo = P.res('mo')
            rL = P.res('ld')
            rS = [P.res(f'S{h}') for h in range(NH)]
            rT = P.res('tm')
            rF = P.res('fmh')
            rO = P.res('oall')
            nbk = (BW + 511) // 512
            seqs = [(0, LS // 128, None)] + [((LS + s_ * 256) // 128, 2, s_) for s_ in range(NPS)]
            for d in range(2):
                zname = 's_ffw' if d == 0 else 's_fbw'
                Mi = hmat[:, d, 0:128]
                MiMc = hmat[:, d, 0:256]
                Mc = hmat[:, d, 128:256]
                Mr = hmat[:, d, 256:384]
                for (tile0, ntl, ps_idx) in seqs:
                    if ps_idx is None:
                        P.dma('sp', S, dr['st_hgrn'][d].rearrange("h k v -> k h v"), writes=rS)
                    else:
                        P.op('dve', lambda e: e.memset(S, 0.0), writes=rS)
                    order = range(ntl) if d == 0 else range(ntl - 1, -1, -1)
                    for tl_ in order:
                        ti = tile0 + tl_
                        tk = slice(ti * 128, (ti + 1) * 128)
                        P.dma('sp', z, dr[zname][tk, :], writes=[rL])
                        P.dma('sp', vt, dr['s_ic'][tk, :], writes=[rL])
                        P.dma('sp', qraw, dr['s_qc'][:, tk].rearrange("(h k) t -> k h t", k=128), writes=[rL])
                        if d == 1:
                            P.dma('sp', gcT, dr['s_gc'][:, tk].rearrange("(h k) t -> k h t", k=128), writes=[rL])
                            P.dma('sp', ofl, dr['s_of'][:, tk].rearrange("(h k) t -> k h t", k=128), writes=[rL])
                        RT = [rL, R, rT]
                        sigmoid_ops(f, z, 1.0, RT)
                        P.op('dve', lambda e, d=d: e.tensor_tensor(out=f, in0=f, in1=oml[d], op=ALU.mult), reads=RT, writes=RT)
                        P.op('dve', lambda e, d=d: e.tensor_tensor(out=f, in0=f, in1=lbb[d], op=ALU.add), reads=RT, writes=RT)
                        P.op('act', lambda e: e.activation(out=gl, in_=f, func=AF.Ln), reads=RT, writes=RT)
                        P.op('dve', lambda e: e.tensor_scalar(out=kk, in0=f, scalar1=-1.0, scalar2=1.0, op0=ALU.mult,
                                                              op1=ALU.add), reads=RT, writes=RT)
                        for (Mx, dst, sc_) in ((Mr, khat, 1.0), (Mc, ktil, -1.0)):
                            for b in range(nbk):
                                w = min(512, BW - b * 512)
                                pbk = b % 2
                                P.op('pe', lambda e: e.matmul(psum[pbk][:, :w], lhsT=Mx, rhs=gl[:, b * 512:b * 512 + w],
                                                              start=True, stop=True), reads=RT, writes=[rps[pbk]])
                                P.op('act', lambda e: e.activation(out=z[:, b * 512:b * 512 + w], in_=psum[pbk][:, :w],
                                                                   func=AF.Exp, scale=sc_), reads=[rps[pbk]] + RT, writes=RT)
                            P.op('dve', lambda e, dst=dst: e.tensor_tensor(out=dst, in0=kk, in1=z, op=ALU.mult), reads=RT,
                                 writes=RT)
                        G = min(4, NH)
                        corder = list(range(4)) if d == 0 else list(range(3, -1, -1))
                        for hg in range(0, NH, G):
                            hh = list(range(hg, hg + G))
                            RB = [rL, R, rT]
                            for j, h in enumerate(hh):
                                hs = slice(h * 128, (h + 1) * 128)
                                bA = 2 + j // 2
                                cA = (j % 2) * 256
                                P.op('pe', lambda e: e.matmul(psum[bA][:, cA:cA + 256], lhsT=gl[:, hs], rhs=MiMc, start=True,
                                                              stop=True), reads=RT, writes=[rA[j]])
                            for j, h in enumerate(hh):
                                bA = 2 + j // 2
                                cA = (j % 2) * 256
                                P.op('act', lambda e: e.activation(out=EEs[j], in_=psum[bA][:, cA:cA + 256], func=AF.Exp),
                                     reads=[rA[j], rS[h]], writes=[rEE[j]])
                            for j, h in enumerate(hh):
                                P.op('dve', lambda e: e.tensor_tensor(out=qincs[j], in0=qraw[:, h, :], in1=EEs[j][:, 0:128],
                                                                      op=ALU.mult), reads=[rEE[j], rL], writes=[rQi[j]])
                                P.op('dve', lambda e: e.tensor_tensor(out=qtils[j], in0=qraw[:, h, :], in1=EEs[j][:, 128:256],
                                                                      op=ALU.mult), reads=[rEE[j], rL], writes=[rQt[j]])
                            pb = psum[4][:].bitcast(BF16)
                            for j, h in enumerate(hh):
                                hs = slice(h * 128, (h + 1) * 128)
                                P.op('pe', lambda e: e.transpose(pb[:, j * 128:(j + 1) * 128], ktil[:, hs], identb),
                                     reads=RT + [rP], writes=[rTr[j]])
                            for j, h in enumerate(hh):
                                P.op('act', lambda e: e.activation(out=ktTs[j], in_=pb[:, j * 128:(j + 1) * 128], func=AF.Copy),
                                     reads=[rTr[j]], writes=[rKt[j]])
                            for j, h in enumerate(hh):
                                P.op('pe', lambda e: e.matmul(psum[5][:, j * 128:(j + 1) * 128], lhsT=ktTs[j], rhs=qtils[j],
                                                              start=True, stop=True), reads=[rKt[j], rQt[j]], writes=[rPT[j]])
                            for j, h in enumerate(hh):
                                P.op('dve', lambda e: e.tensor_tensor(out=pTms[j], in0=psum[5][:, j * 128:(j + 1) * 128],
                                                                      in1=maskb[:, d, :], op=ALU.mult), reads=[rPT[j], R],
                                     writes=[rPm[j]])
                            for j, h in enumerate(hh):
                                hs = slice(h * 128, (h + 1) * 128)
                                P.op('pe', lambda e: e.matmul(psum[6][:, j * 128:(j + 1) * 128], lhsT=vt[:, hs], rhs=pTms[j],
                                                              start=(j == 0), stop=False), reads=[rPm[j], rL], writes=[rOj[j]],
                                     sig=False)
                            for ci, c in enumerate(corder):
                                cs = slice(c * 32, (c + 1) * 32)
                                col = c * 32 + 31 if d == 0 else c * 32
                                for j, h in enumerate(hh):
                                    hs = slice(h * 128, (h + 1) * 128)
                                    P.op('pe', lambda e: e.matmul(psum[6][:, j * 128 + c * 32:j * 128 + (c + 1) * 32],
                                                                  lhsT=S[:, h, :], rhs=qincs[j][:, cs], start=False,
                                                                  stop=(ci == 3 and j == G - 1)), reads=[rQi[j], rS[h]], writes=[rOj[j]],
                                         sig=True)
                                    P.op('pe', lambda e: e.matmul(psum[7][:, j * 128:(j + 1) * 128], lhsT=khat[cs, hs],
                                                                  rhs=vt[cs, hs], start=True, stop=True,
                                                                  tile_position=(c * 32, 0)), reads=RT, writes=[rU[j]])
                                for j, h in enumerate(hh):
                                    P.op('dve', lambda e: e.scalar_tensor_tensor(
                                        out=S[:, h, :], in0=S[:, h, :], scalar=EEs[j][:, col:col + 1],
                                        in1=psum[7][:, j * 128:(j + 1) * 128], op0=ALU.mult, op1=ALU.add),
                                        reads=[rU[j], rEE[j]], writes=[rS[h]])
                            P.op('act', lambda e: e.activation(out=oall[:, hg:hg + G, :], in_=V3(psum[6][:, 0:G * 128], G, 128),
                                                               func=AF.Copy), reads=rOj[:G], writes=[rO])
                        if d == 0:
                            P.dma('sp', dr['s_of'][:, tk].rearrange("(h k) t -> k h t", k=128), oall, reads=[rO])
                        else:
                            RO = [rO, rL, R]
                            oa2 = oall.rearrange("p a b -> p (a b)")
                            P.op('dve', lambda e: e.tensor_tensor(out=oall, in0=oall, in1=ofl, op=ALU.add), reads=RO, writes=RO)
                            P.op('act', lambda e: e.activation(out=sqb, in_=oa2, func=AF.Square), reads=RO, writes=RO)
                            for b in range(nbk):
                                w = min(512, BW - b * 512)
                                P.op('pe', lambda e, b=b, w=w: e.matmul(psum[b][:, :w], lhsT=onesb, rhs=sqb[:, b * 512:b * 512 + w],
                                                                        start=True, stop=True), reads=RO + [rP], writes=[rps[b]])
                                rstd_chain(f[:, b * 512:b * 512 + w], psum[b][:, :w], 1.0 / 128, [rps[b]] + RO + [rT], RO + [rT])
                            P.op('dve', lambda e: e.scalar_tensor_tensor(out=oa2, in0=oa2, scalar=og, in1=f, op0=ALU.mult,
                                                                         op1=ALU.mult), reads=RO + [rT], writes=RO)
                            g2 = gcT.rearrange("p a b -> p (a b)")
                            silu_ops(gl, g2, kk, RO + [rT])
                            P.op('dve', lambda e: e.tensor_tensor(out=mo, in0=oa2, in1=gl, op=ALU.mult), reads=RO + [rT],
                                 writes=[rMo])
                            P.dma('sp', dr['s_mix'][0:BW, tk].rearrange("(h k) t -> k h t", k=128), V3(mo, NH, 128),
                                  reads=[rMo])
                    if ps_idx is not None:
                        P.dma('sp', dr['o_hgrn'][ps_idx, d].rearrange("h k v -> k h v"), S, reads=rS)

        def phase_mla():
            rps = new_phase()
            R = P.res('d')
            NK = 256 + T
            ckvT = V3(alloc(4 * NK, BF16), 4, NK)
            k2T = alloc(NK, BF16)
            qag = alloc(KQ)
            kvag = alloc(4)
            qng = alloc(1)
            qrg = alloc(1)
            kng = alloc(1)
            krg = alloc(1)
            prot = alloc(64)
            with_nc = dict(allow_slow_non_contiguous=True)
            P.dma('sp', qag, dr['mla_q_a_norm'].rearrange("(k p) -> p k", p=128), writes=[R], **with_nc)
            P.dma('sp', kvag, dr['mla_kv_a_norm'].rearrange("(k p) -> p k", p=128), writes=[R], **with_nc)
            P.dma('sp', qng, dr['mla_q_norm'][0:128].rearrange("(p o) -> p o", o=1), writes=[R])
            P.dma('sp', qrg[:64], dr['mla_q_norm'][128:192].rearrange("(p o) -> p o", o=1), writes=[R])
            P.dma('sp', kng, dr['mla_k_norm'][0:128].rearrange("(p o) -> p o", o=1), writes=[R])
            P.dma('sp', krg[:64], dr['mla_k_norm'][128:192].rearrange("(p o) -> p o", o=1), writes=[R])
            P.dma('sp', prot[:64], dr['prot'], writes=[R])
            P.op('dve', lambda e: e.tensor_scalar_mul(out=qng, in0=qng, scalar1=192.0 ** -0.5), reads=[R], writes=[R])
            P.op('dve', lambda e: e.tensor_scalar_mul(out=qrg[:64], in0=qrg[:64], scalar1=192.0 ** -0.5), reads=[R], writes=[R])
            raw = alloc(max(KQ, 4) * 512, BF16)
            sq = alloc(max(KQ, 4) * 512, BF16)
            rstd = alloc(512)
            cqn = V3(alloc(KQ * 512, BF16), KQ, 512)
            c32 = alloc(512)
            x32 = alloc(512)
            cosb = alloc(512)
            sinb = alloc(512)
            tst = alloc(512)
            rTst = P.res('tst')
            rL = P.res('ld')
            rCq = P.res('cqn')
            RR = [R, rL]
            cc = V3(alloc(1024, BF16), 2, 512)
            ck = V3(alloc(128, BF16), 2, 64)
            P.dma('pool', cc, dr['c_ckv'].rearrange("(n p) c -> p n c", p=128), writes=[rL])
            P.dma('pool', ck, dr['c_kr'].rearrange("(n p) c -> p n c", p=128), writes=[rL])
            for n_ in range(2):
                pb = psum[0][:].bitcast(BF16)
                for k in range(4):
                    P.op('pe', lambda e, n_=n_, k=k, pb=pb: e.transpose(pb[:, k * 128:(k + 1) * 128],
                                                                        cc[:, n_, k * 128:(k + 1) * 128], identb),
                         reads=[rL, rP], writes=[rps[0]])
                P.op('dve', lambda e, n_=n_, pb=pb: e.tensor_copy(out=ckvT[:, :, n_ * 128:(n_ + 1) * 128],
                                                                  in_=V3(pb[:, 0:512], 4, 128)), reads=[rps[0]], writes=[R])
                P.op('pe', lambda e, n_=n_, pb=pb: e.transpose(pb[:64, 512:640], ck[:, n_, :], identb), reads=[rL, rP],
                     writes=[rps[0]])
                P.op('dve', lambda e, n_=n_, pb=pb: e.tensor_copy(out=k2T[:64, n_ * 128:(n_ + 1) * 128], in_=pb[:64, 512:640]),
                     reads=[rps[0]], writes=[R])

            def rope(dst, src, t0, n):
                P.dma('sp', cosb[:64, :n], dr['ropecos'][:, t0:t0 + n], writes=[rL])
                P.dma('sp', sinb[:64, :n], dr['ropesin'][:, t0:t0 + n], writes=[rL])
                P.op('pe', lambda e: e.matmul(psum[7][:64, :n], lhsT=prot[:64, :64], rhs=src, start=True, stop=True),
                     reads=RR, writes=[rps[7]])
                P.op('dve', lambda e: e.tensor_tensor(out=sinb[:64, :n], in0=psum[7][:64, :n], in1=sinb[:64, :n], op=ALU.mult),
                     reads=[rps[7], rL], writes=[rL])
                P.op('dve', lambda e: e.tensor_tensor(out=cosb[:64, :n], in0=src, in1=cosb[:64, :n], op=ALU.mult),
                     reads=RR, writes=[rL])
                P.op('dve', lambda e: e.tensor_tensor(out=dst, in0=cosb[:64, :n], in1=sinb[:64, :n], op=ALU.add),
                     reads=[rL], writes=RR)

            for t0 in range(0, T, 512):
                is_p = t0 >= LS
                r3 = V3(raw[:, :KQ * 512], KQ, 512)
                P.dma('sp', r3, dr['s_cq'][:, t0:t0 + 512].rearrange("(k p) t -> p k t", p=128), writes=[rL])
                P.op('act', lambda e: e.activation(out=sq[:, :KQ * 512], in_=raw[:, :KQ * 512], func=AF.Square), reads=RR, writes=RR)
                for k in range(KQ):
                    P.op('pe', lambda e, k=k: e.matmul(psum[1][:], lhsT=onesb, rhs=sq[:, k * 512:(k + 1) * 512], start=(k == 0),
                                                       stop=(k == KQ - 1)), reads=RR + [rP], writes=[rps[1]], sig=(k == KQ - 1))
                rstd_chain(rstd, psum[1][:], 1.0 / QL, [rps[1]] + RR, RR)
                for k in range(KQ):
                    P.op('dve', lambda e, k=k: e.scalar_tensor_tensor(out=cqn[:, k, :], in0=r3[:, k, :], scalar=qag[:, k:k + 1],
                                                                      in1=rstd, op0=ALU.mult, op1=ALU.mult), reads=RR,
                         writes=[rCq])
                P.dma('sp', dr['s_cqn'][:, t0:t0 + 512].rearrange("(k p) t -> p k t", p=128), cqn, reads=[rCq])
                r3 = V3(raw[:, :4 * 512], 4, 512)
                P.dma('sp', r3, dr['s_ckv'][:, t0:t0 + 512].rearrange("(k p) t -> p k t", p=128), writes=[rL])
                P.op('act', lambda e: e.activation(out=sq[:, :4 * 512], in_=raw[:, :4 * 512], func=AF.Square), reads=RR, writes=RR)
                for k in range(4):
                    P.op('pe', lambda e, k=k: e.matmul(psum[2][:], lhsT=onesb, rhs=sq[:, k * 512:(k + 1) * 512], start=(k == 0),
                                                       stop=(k == 3)), reads=RR + [rP], writes=[rps[2]], sig=(k == 3))
                rstd_chain(rstd, psum[2][:], 1.0 / 512, [rps[2]] + RR, RR)
                for k in range(4):
                    P.op('dve', lambda e, k=k, r3=r3: e.scalar_tensor_tensor(
                        out=ckvT[:, k, 256 + t0:256 + t0 + 512], in0=r3[:, k, :], scalar=kvag[:, k:k + 1], in1=rstd,
                        op0=ALU.mult, op1=ALU.mult), reads=RR, writes=RR)
                    if is_p:
                        P.op('dve', lambda e, k=k, r3=r3: e.scalar_tensor_tensor(
                            out=c32, in0=r3[:, k, :], scalar=kvag[:, k:k + 1], in1=rstd, op0=ALU.mult, op1=ALU.mult),
                            reads=RR, writes=RR)
                        for j in range(4):
                            P.op('pe', lambda e, j=j: e.transpose(psum[3][:, j * 128:(j + 1) * 128],
                                                                  c32[:, j * 128:(j + 1) * 128], identf), reads=RR + [rP],
                                 writes=[rps[3]])
                        P.op('act', lambda e: e.activation(out=tst, in_=psum[3][:], func=AF.Copy), reads=[rps[3]], writes=[rTst])
                        P.dma('sp', dr['o_ckv'][t0 - LS:t0 - LS + 512, k * 128:(k + 1) * 128].rearrange("(j p) c -> p j c", p=128),
                              V3(tst, 4, 128), reads=[rTst])
                P.dma('sp', raw[:64, :512], dr['s_kr'][:, t0:t0 + 512], writes=[rL])
                fm_norm(x32[:64, :], raw[:64, :512], krg[:64], 64, 512, RR, rps, 4, sq, rstd)
                if is_p:
                    P.op('dve', lambda e, t0=t0: e.tensor_copy(out=k2T[:64, 256 + t0:256 + t0 + 512], in_=x32[:64, :]), reads=RR,
                         writes=RR)
                    for j in range(4):
                        P.op('pe', lambda e, j=j: e.transpose(psum[3][:, j * 64:(j + 1) * 64], x32[:64, j * 128:(j + 1) * 128],
                                                              identf[:64, :64]), reads=RR + [rP], writes=[rps[3]])
                    P.op('act', lambda e: e.activation(out=tst[:, :256], in_=psum[3][:, :256], func=AF.Copy), reads=[rps[3]],
                         writes=[rTst])
                    P.dma('sp', dr['o_kr'][t0 - LS:t0 - LS + 512, :].rearrange("(j p) c -> p j c", p=128), V3(tst[:, :256], 4, 64),
                          reads=[rTst])
                else:
                    rope(k2T[:64, 256 + t0:256 + t0 + 512], x32[:64, :], t0, 512)
            wq = V3(alloc(KQ * 192, BF16), KQ, 192)
            wkv = V3(alloc(4 * 256, BF16), 4, 256)
            kT = alloc(NK, BF16)
            vt = V3(alloc(NK, BF16), NK // 128, 128)
            gd = alloc(T, BF16)
            qn = alloc(512, BF16)
            qr = alloc(512, BF16)
            pts = [alloc(512, BF16) for _ in range(3)]
            rPt = [P.res('pt0'), P.res('pt1'), P.res('pt2')]
            rec = alloc(512)
            o32 = alloc(512)
            mst = alloc(512, BF16)
            rMst = P.res('mst')
            rK = P.res('kv')
            rQ = P.res('q')
            for h in range(NH):
                hs = slice(h * 128, (h + 1) * 128)
                P.dma('pool', wq, dr['mla_w_q_up'][:, h * 192:(h + 1) * 192].rearrange("(k p) c -> p k c", p=128), writes=[rK])
                P.dma('pool', wkv, dr['mla_w_kv_up'][:, h * 256:(h + 1) * 256].rearrange("(k p) c -> p k c", p=128), writes=[rK])
                P.dma('sp', gd, dr['s_gd'][hs, :], writes=[rK])
                RK = [R, rK]
                for t0 in range(0, T, 512):
                    silu_ops(gd[:, t0:t0 + 512], gd[:, t0:t0 + 512], c32, RK + [rL])
                for c0 in range(0, NK, 512):
                    n = min(512, NK - c0)
                    for k in range(4):
                        P.op('pe', lambda e, k=k, c0=c0, n=n: e.matmul(psum[5][:, :n], lhsT=wkv[:, k, 0:128],
                                                                      rhs=ckvT[:, k, c0:c0 + n], start=(k == 0), stop=(k == 3)),
                             reads=RK, writes=[rps[5]], sig=(k == 3))
                    P.op('act', lambda e, n=n: e.activation(out=x32[:, :n], in_=psum[5][:, :n], func=AF.Copy), reads=[rps[5]],
                         writes=[rL])
                    fm_norm(kT[:, c0:c0 + n], x32[:, :n], kng, 128, n, [rL, R, rK], rps, 6, sq, rstd)
                for n_ in range(NK // 128):
                    b = 7
                    for k in range(4):
                        P.op('pe', lambda e, k=k, n_=n_, b=b: e.matmul(psum[b][:, 0:128], lhsT=ckvT[:, k, n_ * 128:(n_ + 1) * 128],
                                                                      rhs=wkv[:, k, 128:256], start=(k == 0), stop=(k == 3)),
                             reads=RK, writes=[rps[b]], sig=(k == 3))
                    evac(vt[:, n_, :], psum[b][:, 0:128], [rps[b]], [rK])
                RA = [R, rK, rQ]
                blocks = [(qb * 512, 512, 0, (256 + LS) // 128, True) for qb in range(LS // 512)]
                blocks += [(LS + s_ * 256, 256, (256 + LS + s_ * 256) // 128, 2, False) for s_ in range(NPS)]
                for (t0, nq, kt0, nkt, rot) in blocks:
                    P.dma('sp', cqn[:, :, :nq], dr['s_cqn'][:, t0:t0 + nq].rearrange("(k p) t -> p k t", p=128), writes=[rCq])
                    for k in range(KQ):
                        P.op('pe', lambda e, k=k, nq=nq: e.matmul(psum[5][:, :nq], lhsT=wq[:, k, 0:128], rhs=cqn[:, k, :nq],
                                                                  start=(k == 0), stop=(k == KQ - 1)), reads=[rCq, rK],
                             writes=[rps[5]], sig=(k == KQ - 1))
                    P.op('act', lambda e, nq=nq: e.activation(out=x32[:, :nq], in_=psum[5][:, :nq], func=AF.Copy), reads=[rps[5]],
                         writes=[rL])
                    fm_norm(qn[:, :nq], x32[:, :nq], qng, 128, nq, [rL, R, rQ], rps, 6, sq, rstd)
                    for k in range(KQ):
                        P.op('pe', lambda e, k=k, nq=nq: e.matmul(psum[7][:64, :nq], lhsT=wq[:, k, 128:192], rhs=cqn[:, k, :nq],
                                                                  start=(k == 0), stop=(k == KQ - 1)), reads=[rCq, rK],
                             writes=[rps[7]], sig=(k == KQ - 1))
                    P.op('act', lambda e, nq=nq: e.activation(out=x32[:64, :nq], in_=psum[7][:64, :nq], func=AF.Copy),
                         reads=[rps[7]], writes=[rL])
                    if rot:
                        fm_norm(c32[:64, :nq], x32[:64, :nq], qrg[:64], 64, nq, [rL, R, rQ], rps, 6, sq, rstd)
                        rope(qr[:64, :nq], c32[:64, :nq], t0, nq)
                        P.op('dve', lambda e: e.tensor_copy(out=qr[:64, 0:1], in_=qr[:64, 0:1]), reads=[rL, R], writes=[rQ])
                    else:
                        fm_norm(qr[:64, :nq], x32[:64, :nq], qrg[:64], 64, nq, [rL, R, rQ], rps, 6, sq, rstd)
                    keys = [(kT[:, (kt0 + i) * 128:(kt0 + i + 1) * 128], k2T[:64, (kt0 + i) * 128:(kt0 + i + 1) * 128],
                             vt[:, kt0 + i, :], None) for i in range(nkt)]
                    attn_block(qn[:, :nq], qr[:64, :nq], keys, nq, o32[:, :nq], RA + [rL], rps, pts, rPt, rec)
                    P.op('dve', lambda e, t0=t0, nq=nq: e.tensor_tensor(out=mst[:, :nq], in0=o32[:, :nq], in1=gd[:, t0:t0 + nq],
                                                                        op=ALU.mult), reads=RA + [rL], writes=[rMst])
                    P.dma('sp', dr['s_mix'][BW + h * 128:BW + (h + 1) * 128, t0:t0 + nq], mst[:, :nq], reads=[rMst])

        if ON('in0'):
            phase_inproj(0, dr['w_in_ab'], cfg.AB, dr['x_all'])
        if ON('mixA'):
            phase_mixA()
        if ON('mixB'):
            phase_mixB()
        if ON('out0'):
            phase_outproj(0, dr['x_all'], dr['s_x1'])
        if ON('in1'):
            phase_inproj(1, dr['w_in_cd'], cfg.CD, dr['s_x1'])
        if ON('hgrn'):
            phase_hgrn()
        if ON('mla'):
            phase_mla()
        if ON('out1'):
            phase_outproj(1, dr['s_x1'], dr['y_all'])
        P.emit()
    return nc


def prep_rp(na_rpb):
    NH = na_rpb.shape[0]
    rp = np.zeros((NH, 23, 127), np.float32)
    rp[:, 4:19, 48:79] = na_rpb[:, ::-1, ::-1]
    return rp


_CACHE = {}


def make_in_maps(cfg, inputs, ncores):
    consts = host_consts(cfg)
    NPS, LS, D = cfg.NPS, cfg.LS, cfg.D
    f = lambda a: np.ascontiguousarray(np.asarray(a))
    shared = {
        'norm_w': f(inputs['norm_w']), 'w_ada': f(inputs['w_ada']), 'b_ada': f(inputs['b_ada']),
        'w_out': f(inputs['w_out']), 'w_in_ab': f(inputs['w_in_ab'][0]), 'na_q_norm': f(inputs['na_q_norm'][0]),
        'na_k_norm': f(inputs['na_k_norm'][0]), 'rp': prep_rp(np.asarray(inputs['na_rpb'][0])),
        'sg_norm': f(inputs['sg_norm'][0]), 'sg_w': f(inputs['sg_w'][0]), 'sg_b': f(inputs['sg_b'][0]).reshape(-1),
        'w_in_cd': f(inputs['w_in_cd'][0]), 'hgrn_lb': f(inputs['hgrn_lb']),
        'hgrn_out_norm': f(inputs['hgrn_out_norm'][0]), 'mla_q_a_norm': f(inputs['mla_q_a_norm'][0]),
        'mla_w_q_up': f(inputs['mla_w_q_up'][0]), 'mla_kv_a_norm': f(inputs['mla_kv_a_norm'][0]),
        'mla_w_kv_up': f(inputs['mla_w_kv_up'][0]), 'mla_q_norm': f(inputs['mla_q_norm'][0]),
        'mla_k_norm': f(inputs['mla_k_norm'][0]),
    }
    shared.update(consts)
    maps = []
    for c in range(ncores):
        m = dict(shared)
        xs = np.asarray(inputs['x_sample'][c])
        xp = np.asarray(inputs['x_prompt'][c * NPS:(c + 1) * NPS]).reshape(NPS * 256, D)
        m['x_all'] = np.ascontiguousarray(np.concatenate([xs, xp], axis=0))
        m['cond2'] = np.ascontiguousarray(np.stack([np.asarray(inputs['c'][c]), np.asarray(inputs['c_ctx'])], axis=0))
        m['c_na_k'] = f(inputs['cache_na_k'][c, 0]).reshape(256, -1)
        m['c_na_v'] = f(inputs['cache_na_v'][c, 0]).reshape(256, -1)
        m['st_hgrn'] = f(inputs['state_hgrn'][c, 0])
        m['c_ckv'] = f(inputs['cache_mla_ckv'][c, 0])
        m['c_kr'] = f(inputs['cache_mla_krope'][c, 0])
        maps.append(m)
    return maps


def assemble(cfg, results, ncores):
    NPS, LS, D, NH, BW = cfg.NPS, cfg.LS, cfg.D, cfg.NH, cfg.BW
    yp, ys, nk, nv, hg, ckv, kr = [], [], [], [], [], [], []
    for c in range(ncores):
        r = results[c]
        ya = np.asarray(r['y_all'])
        ys.append(ya[:LS][None])
        yp.append(ya[LS:].reshape(NPS, 256, D))
        nk.append(np.asarray(r['o_na_k']).reshape(NPS, 1, 256, NH, 128))
        nv.append(np.asarray(r['o_na_v']).reshape(NPS, 1, 256, NH, 128))
        hg.append(np.asarray(r['o_hgrn']).reshape(NPS, 1, 2, NH, 128, 128))
        ckv.append(np.asarray(r['o_ckv']).reshape(NPS, 1, 256, 512))
        kr.append(np.asarray(r['o_kr']).reshape(NPS, 1, 256, 64))
    cat = lambda l: np.ascontiguousarray(np.concatenate(l, axis=0).astype(np.float32))
    return (cat(yp), cat(ys), cat(nk), cat(nv), cat(hg), cat(ckv), cat(kr))


def kernel(**inputs):
    cfg = Cfg(D=4096, LS=4096, NPS=4)
    ncores = 8
    if 'nc' not in _CACHE:
        _CACHE['nc'] = build_program(cfg)
    nc = _CACHE['nc']
    maps = make_in_maps(cfg, inputs, ncores)
    res = run_bass_kernel_spmd(nc, maps, core_ids=list(range(ncores)))
    return assemble(cfg, res.results, ncores)
```

```python
import os
import numpy as np
import ml_dtypes
from contextlib import ExitStack
import concourse.bass as bass
import concourse.mybir as mybir
from concourse.bass_utils import run_bass_kernel_spmd

F32 = mybir.dt.float32
BF16 = mybir.dt.bfloat16
AF = mybir.ActivationFunctionType
ALU = mybir.AluOpType
AX = mybir.AxisListType

ENGS = ('pe', 'act', 'dve', 'pool', 'sp')
SEM_LIMIT = 8000
EPS = 1e-6
GK = 1.5957691216057308


class Rec:
    def __init__(self):
        self.call = None

    def __getattr__(self, name):
        def f(*a, **k):
            self.call = (name, a, k)
            return self
        return f


class Event:
    __slots__ = ('sv', 'eng')

    def __init__(self, eng=None):
        self.sv = None
        self.eng = eng


class Chan:
    def __init__(self, P):
        self.P = P
        self.sem = None
        self.count = 0
        self.last = None

    def bump(self, n, ev):
        if self.sem is None or self.count + n > SEM_LIMIT:
            self.sem = self.P.new_sem()
            self.count = 0
        self.count += n
        ev.sv = (self.sem, self.count)
        self.last = ev
        return ev


class Res:
    def __init__(self, P, name):
        self.P = P
        self.name = name
        self.last_write = None
        self.readers = {}
        self.chan = {}

    def dchan(self, eng):
        if eng not in self.chan:
            self.chan[eng] = self.P.get_dma_chan(eng)
        return self.chan[eng]


class Prog:
    def __init__(self, nc, stack):
        self.nc = nc
        self.stack = stack
        self.ops = {e: [] for e in ENGS}
        self.echan = {e: Chan(self) for e in ENGS}
        self.cur = {e: Event(e) for e in ENGS}
        self.dma_chans = []
        self.free_chans = {e: [] for e in ENGS}
        self.resources = []
        self.nsem = 0
        self.bar_seen = {}
        self.pending = {e: False for e in ENGS}

    def new_sem(self):
        self.nsem += 1
        return self.stack.enter_context(self.nc.semaphore(f"s{self.nsem}"))

    def get_dma_chan(self, eng):
        if self.free_chans[eng]:
            return self.free_chans[eng].pop()
        ch = Chan(self)
        self.dma_chans.append(ch)
        return ch

    def res(self, name):
        r = Res(self, name)
        self.resources.append(r)
        return r

    def _deps(self, eng, reads, writes):
        waits = []

        def add(ev):
            if ev is None:
                return
            if ev.sv is None:
                assert ev.eng == eng, (ev.eng, eng)
                return
            if ev.eng == eng and eng == 'pe':
                return
            waits.append(ev)

        for r in reads:
            add(r.last_write)
        for w in writes:
            add(w.last_write)
            for ev in w.readers.values():
                add(ev)
        return waits

    def _commit(self, ev, key, reads, writes):
        for w in writes:
            w.last_write = ev
            w.readers = {}
        for r in reads:
            r.readers[key] = ev

    def op(self, eng, fn, reads=(), writes=(), sig=True):
        waits = self._deps(eng, reads, writes)
        ev = self.cur[eng]
        inc = None
        if sig:
            self.echan[eng].bump(1, ev)
            inc = (ev.sv[0], 1)
            self.cur[eng] = Event(eng)
        self.pending[eng] = not sig
        rec = Rec()
        fn(rec)
        assert rec.call is not None
        self.ops[eng].append((rec.call, waits, inc))
        self._commit(ev, eng, reads, writes)
        return ev

    def dma(self, eng, out, in_, reads=(), writes=(), cres=None, **kw):
        waits = self._deps(eng, reads, writes)
        if cres is None:
            cres = writes[0] if writes else reads[0]
        ch = cres.dchan(eng)
        ev = Event('dma')
        ch.bump(16, ev)
        self.ops[eng].append((('dma_start', (), dict(out=out, in_=in_, **kw)), waits, (ev.sv[0], 16)))
        self._commit(ev, ch, reads, writes)
        return ev

    def barrier(self, release=True):
        evs = []
        for e in ENGS:
            if self.pending[e]:
                self.op(e, lambda eng: eng.nop())
            ch = self.echan[e]
            if ch.last is not None:
                evs.append(ch.last)
        for ch in self.dma_chans:
            if ch.last is not None:
                evs.append(ch.last)
        evs = [ev for ev in evs if self.bar_seen.get(id(ev.sv[0]), 0) < ev.sv[1]]
        for ev in evs:
            self.bar_seen[id(ev.sv[0])] = ev.sv[1]
        for e in ENGS:
            self.ops[e].append((None, list(evs), None))
        for r in self.resources:
            r.last_write = None
            r.readers = {}
            if release:
                for e_, ch_ in r.chan.items():
                    self.free_chans[e_].append(ch_)
                r.chan = {}
        if release:
            self.resources = []

    def emit(self):
        nc = self.nc
        self.barrier()
        P = self

        def run(eng_name, eng):
            known = {}
            for fn, waits, inc in P.ops[eng_name]:
                for ev in waits:
                    sem, val = ev.sv
                    if known.get(id(sem), 0) < val:
                        eng.wait_ge(sem, val)
                        known[id(sem)] = val
                if fn is not None:
                    inst = getattr(eng, fn[0])(*fn[1], **fn[2])
                    if inc is not None:
                        inst.then_inc(inc[0], inc[1])

        with nc.Block() as block:
            @block.tensor
            def _(e):
                run('pe', e)

            @block.scalar
            def _(e):
                run('act', e)

            @block.vector
            def _(e):
                run('dve', e)

            @block.gpsimd
            def _(e):
                run('pool', e)

            @block.sync
            def _(e):
                run('sp', e)


class Cfg:
    def __init__(self, D=4096, LS=4096, NPS=4):
        self.D = D
        self.LS = LS
        self.NPS = NPS
        self.SEQ = 256
        self.PAST = 256
        self.KD = D // 128
        self.BW = D // 2
        self.NH = self.BW // 128
        self.QL = D // 4
        self.KQ = self.QL // 128
        self.KVL = 512
        self.ROPE = 64
        self.T = LS + NPS * 256
        self.ROWS = LS // 64
        self.NQB = LS // 512
        self.AB = [('qa', 'fm', self.BW), ('ka', 'fm', self.BW), ('va', 'tm', self.BW), ('ga', 'fm', self.BW),
                   ('ub', 'fm', self.BW), ('vb', 'tm', self.BW), ('gb', 'fm', self.BW)]
        self.CD = [('qc', 'fm', self.BW), ('ffw', 'tm32', self.BW), ('fbw', 'tm32', self.BW), ('ic', 'tm', self.BW),
                   ('gc', 'fm', self.BW), ('cq', 'fm', self.QL), ('ckv', 'fm', 512), ('kr', 'fm', 64),
                   ('gd', 'fm', self.BW)]
        self.AB_IN = sum(s[2] for s in self.AB)
        self.CD_IN = sum(s[2] for s in self.CD)
        self.build_patterns()

    def build_patterns(self):
        ROWS = self.ROWS
        cols = np.arange(64)
        cstart = np.clip(cols - 8, 0, 48)
        colmask = (cols[None, :] >= cstart[:, None]) & (cols[None, :] < cstart[:, None] + 16)
        pats = {}
        self.pat_list = []
        self.qb_keys = []
        for qb in range(self.NQB):
            r = 8 * qb + np.arange(8)
            rs = np.clip(r - 4, 0, ROWS - 8)
            kt0 = rs.min() // 2
            kt1 = (rs.max() + 7) // 2
            lst = []
            for kt in range(kt0, kt1 + 1):
                m = np.zeros((2, 64, 8, 64), np.float32)
                for kr in range(2):
                    rp = 2 * kt + kr
                    for qr in range(8):
                        if rs[qr] <= rp < rs[qr] + 8:
                            m[kr, :, qr, :] = colmask.T
                m = m.reshape(128, 512)
                delta = 2 * kt - 8 * qb
                key = (m.tobytes(), delta)
                if key not in pats:
                    pats[key] = len(self.pat_list)
                    self.pat_list.append((m, delta))
                lst.append((kt, pats[key]))
            self.qb_keys.append(lst)
        self.NPAT = len(self.pat_list)


def host_consts(cfg):
    c = {}
    c['identb'] = np.eye(128, dtype=np.float32).astype(ml_dtypes.bfloat16)
    c['identf'] = np.eye(128, dtype=np.float32)
    c['mask01'] = np.stack([m for m, _ in cfg.pat_list], axis=1).astype(ml_dtypes.bfloat16)
    idx = np.arange(128)
    same = (idx[:, None] // 32) == (idx[None, :] // 32)
    hm = np.zeros((2, 128, 512), np.float32)
    for d in range(2):
        if d == 0:
            Mi = same & (idx[:, None] <= idx[None, :])
            Mr = same & (idx[:, None] > idx[None, :])
        else:
            Mi = same & (idx[:, None] >= idx[None, :])
            Mr = same & (idx[:, None] < idx[None, :])
        Mi = Mi.astype(np.float32)
        mid = (idx // 32) * 32 + 15
        Mc = Mi - Mi[:, mid]
        hm[d, :, 0:128] = Mi
        hm[d, :, 128:256] = Mc
        hm[d, :, 256:384] = Mr.astype(np.float32)
    c['hmat'] = np.ascontiguousarray(hm.transpose(1, 0, 2))
    t = np.arange(cfg.LS)
    inv = (10000.0 ** (-np.arange(16, dtype=np.float32) / 16)).astype(np.float32)
    cos = np.zeros((64, cfg.LS), np.float32)
    sin = np.zeros((64, cfg.LS), np.float32)
    prot = np.zeros((64, 64), np.float32)
    for j in range(64):
        b, jj = j // 32, j % 32
        i = jj % 16
        pos = (t // 64 if b == 0 else t % 64).astype(np.float32)
        ang = pos * inv[i]
        cos[j] = np.cos(ang)
        sin[j] = np.sin(ang)
        if jj < 16:
            prot[j + 16, j] = -1.0
        else:
            prot[j - 16, j] = 1.0
    c['ropecos'] = cos
    c['ropesin'] = sin
    c['prot'] = prot
    return c


def build_program(cfg, debug=()):
    D, KD, BW, NH, T, LS, NPS, QL, KQ = cfg.D, cfg.KD, cfg.BW, cfg.NH, cfg.T, cfg.LS, cfg.NPS, cfg.QL, cfg.KQ
    NPT = NPS * 256
    nc = bass.Bass("TRN2", target_bir_lowering=False)
    dr = {}

    def din(name, shape, dt=F32):
        dr[name] = nc.dram_tensor(name, list(shape), dt, kind="ExternalInput").ap()

    def dout(name, shape, dt=F32):
        dr[name] = nc.dram_tensor(name, list(shape), dt, kind="ExternalOutput").ap()

    def dscr(name, shape, dt):
        kind = "ExternalOutput" if name in debug else "Internal"
        dr[name] = nc.dram_tensor(name, list(shape), dt, kind=kind).ap()

    din('x_all', [T, D])
    din('cond2', [2, D])
    din('c_na_k', [256, BW])
    din('c_na_v', [256, BW])
    din('st_hgrn', [2, NH, 128, 128])
    din('c_ckv', [256, 512])
    din('c_kr', [256, 64])
    din('norm_w', [2, D])
    din('w_ada', [2, D, 3 * D])
    din('b_ada', [2, 3 * D])
    din('w_out', [2, D, D])
    din('w_in_ab', [D, cfg.AB_IN])
    din('na_q_norm', [128])
    din('na_k_norm', [128])
    din('rp', [NH, 23, 127])
    din('sg_norm', [BW])
    din('sg_w', [NH, 128, 128])
    din('sg_b', [NH * 128])
    din('w_in_cd', [D, cfg.CD_IN])
    din('hgrn_lb', [2, 2, BW])
    din('hgrn_out_norm', [128])
    din('mla_q_a_norm', [QL])
    din('mla_w_q_up', [QL, NH * 192])
    din('mla_kv_a_norm', [512])
    din('mla_w_kv_up', [512, NH * 256])
    din('mla_q_norm', [192])
    din('mla_k_norm', [192])
    din('identb', [128, 128], BF16)
    din('identf', [128, 128])
    din('mask01', [128, cfg.NPAT, 512], BF16)
    din('hmat', [128, 2, 512])
    din('ropecos', [64, LS])
    din('ropesin', [64, LS])
    din('prot', [64, 64])
    dout('y_all', [T, D])
    dout('o_na_k', [NPT, BW])
    dout('o_na_v', [NPT, BW])
    dout('o_hgrn', [NPS, 2, NH, 128, 128])
    dout('o_ckv', [NPT, 512])
    dout('o_kr', [NPT, 64])
    for (nm, kind, w) in cfg.AB + cfg.CD:
        if kind == 'fm':
            dscr('s_' + nm, [w, T], BF16)
        elif kind == 'tm':
            dscr('s_' + nm, [T, w], BF16)
        else:
            dscr('s_' + nm, [T, w], F32)
    dscr('s_mix', [D, T], BF16)
    dscr('s_x1', [T, D], F32)
    dscr('s_gate', [2, 2, D], F32)
    dscr('s_tf', [NH, 23, 64, 64], F32)
    dscr('s_of', [BW, T], F32)
    dscr('s_cqn', [QL, T], BF16)
    dscr('s_qn', [NH * 128, T], BF16)
    dscr('s_qr', [NH * 64, T], BF16)

    st = ExitStack()
    with st:
        P = Prog(nc, st)
        ARENA = 46 * 1024
        arena = st.enter_context(nc.sbuf_tensor("arena", [128, ARENA], F32))
        psum = [st.enter_context(nc.psum_tensor(f"ps{i}", [128, 512], F32)) for i in range(8)]
        PERS = 1024
        pos = [0]

        def alloc(n, dt=F32, np_=128):
            words = (n + 1) // 2 if dt == BF16 else n
            words = (words + 7) // 8 * 8
            a = arena[:, pos[0]:pos[0] + words]
            pos[0] += words
            assert pos[0] <= ARENA, pos[0]
            if dt == BF16:
                a = a.bitcast(BF16)[:, :n]
            else:
                a = a[:, :n]
            return a

        identb = alloc(128, BF16)
        identf = alloc(128)
        onesb = alloc(128, BF16)
        AT = [alloc(KD * 2) for _ in range(2)]
        SH = [alloc(KD * 2) for _ in range(2)]
        epsc = alloc(1)
        assert pos[0] <= PERS
        rP = P.res('pers')
        P.dma('sp', identb, dr['identb'], writes=[rP])
        P.dma('sp', identf, dr['identf'], writes=[rP])
        P.op('dve', lambda e: e.memset(onesb, 1.0), writes=[rP])
        P.op('dve', lambda e: e.memset(epsc, EPS), writes=[rP])
        P.barrier(release=False)
        P.resources = []

        bank_rr = [0]

        def new_phase():
            P.barrier()
            pos[0] = PERS
            rps = [P.res(f'ps{i}') for i in range(8)]
            return rps

        def V3(ap, a, b):
            return ap.rearrange("p (a b) -> p a b", a=a, b=b)

        cnt = [0]

        def evac(out, in_, reads, writes, eng=None):
            cnt[0] += 1
            if eng is None:
                eng = 'act' if cnt[0] % 2 else 'dve'
            if eng == 'act':
                P.op('act', lambda e: e.activation(out=out, in_=in_, func=AF.Copy), reads=reads, writes=writes)
            else:
                P.op('dve', lambda e: e.tensor_copy(out=out, in_=in_), reads=reads, writes=writes)

        def rstd_chain(dst, src, inv_n, reads, writes):
            P.op('dve', lambda e: e.tensor_scalar(out=dst, in0=src, scalar1=inv_n, scalar2=EPS, op0=ALU.mult, op1=ALU.add),
                 reads=reads, writes=writes)
            P.op('act', lambda e: e.activation(out=dst, in_=dst, func=AF.Ln), reads=writes, writes=writes)
            P.op('act', lambda e: e.activation(out=dst, in_=dst, func=AF.Exp, scale=-0.5), reads=writes, writes=writes)

        def sigmoid_ops(dst, src, scale, R):
            P.op('act', lambda e: e.activation(out=dst, in_=src, func=AF.Sigmoid, scale=scale), reads=R, writes=R)

        def silu_ops(dst, src, tmp, R):
            P.op('act', lambda e: e.activation(out=dst, in_=src, func=AF.Silu), reads=R, writes=R)

        def gelu_ops(dst, src, tmp, tmp2, R):
            P.op('act', lambda e: e.activation(out=tmp, in_=src, func=AF.Square), reads=R, writes=R)
            P.op('dve', lambda e: e.tensor_scalar(out=tmp, in0=tmp, scalar1=0.044715, scalar2=1.0, op0=ALU.mult, op1=ALU.add),
                 reads=R, writes=R)
            P.op('dve', lambda e: e.tensor_tensor(out=tmp, in0=tmp, in1=src, op=ALU.mult), reads=R, writes=R)
            sigmoid_ops(tmp2, tmp, GK, R)
            P.op('dve', lambda e: e.tensor_tensor(out=dst, in0=src, in1=tmp2, op=ALU.mult), reads=R, writes=R)

        stages = os.environ.get("MK_STAGES", "all").split(',')
        ON = lambda nm: stages == ['all'] or nm in stages
        for l in (range(2) if ON('mod') else []):
            rps = new_phase()
            R = P.res('m')
            cT = alloc(KD * 2)
            tmpc = alloc(KD * 2)
            sT = alloc(KD * 2, BF16)
            bT = alloc(3 * KD)
            nwT = alloc(KD)
            modT = alloc(3 * KD * 2)
            wts = [alloc(KD * 512, BF16) for _ in range(2)]
            rW = [P.res('w0'), P.res('w1')]
            for c in range(2):
                P.dma('sp', V3(cT, 2, KD)[:, c, :], dr['cond2'][c].rearrange("(k p) -> p k", p=128), writes=[R],
                      allow_slow_non_contiguous=True)
            P.dma('sp', bT, dr['b_ada'][l].rearrange("(j p) -> p j", p=128), writes=[R], allow_slow_non_contiguous=True)
            P.dma('sp', nwT, dr['norm_w'][l].rearrange("(j p) -> p j", p=128), writes=[R], allow_slow_non_contiguous=True)
            silu_ops(cT, cT, tmpc, [R])
            P.op('dve', lambda e: e.tensor_copy(out=V3(sT, KD, 2), in_=V3(cT, 2, KD).rearrange("p c k -> p k c")), reads=[R],
                 writes=[R])
            ncb = 3 * D // 512
            for cb in range(ncb):
                s = cb % 2
                wt = V3(wts[s], KD, 512)
                P.dma('pool', wt, dr['w_ada'][l][:, cb * 512:(cb + 1) * 512].rearrange("(k p) c -> p k c", p=128),
                      writes=[rW[s]])
                b = cb % 8
                for j in range(4):
                    for k in range(KD):
                        P.op('pe', lambda e, b=b, j=j, k=k, wt=wt: e.matmul(
                            psum[b][:, 2 * j:2 * j + 2], lhsT=wt[:, k, j * 128:(j + 1) * 128], rhs=V3(sT, KD, 2)[:, k, :],
                            start=(k == 0), stop=(k == KD - 1)), reads=[rW[s], R], writes=[rps[b]],
                            sig=(j == 3 and k == KD - 1))
                P.op('dve', lambda e, b=b, cb=cb: e.tensor_tensor(
                    out=V3(modT, 3 * KD, 2)[:, cb * 4:(cb + 1) * 4, :], in0=V3(psum[b][:, 0:8], 4, 2),
                    in1=bT[:, cb * 4:(cb + 1) * 4].unsqueeze(2).broadcast_to([128, 4, 2]), op=ALU.add),
                    reads=[rps[b], R], writes=[R])
            m3 = V3(modT, 3 * KD, 2)
            P.op('dve', lambda e: e.tensor_copy(out=V3(SH[l], KD, 2), in_=m3[:, 0:KD, :]), reads=[R], writes=[rP])
            P.op('dve', lambda e: e.tensor_scalar_add(out=V3(AT[l], KD, 2), in0=m3[:, KD:2 * KD, :], scalar1=1.0),
                 reads=[R], writes=[rP])
            P.op('dve', lambda e: e.tensor_tensor(out=V3(AT[l], KD, 2), in0=V3(AT[l], KD, 2),
                                                  in1=nwT.unsqueeze(2).broadcast_to([128, KD, 2]), op=ALU.mult),
                 reads=[R, rP], writes=[rP])
            gtmp = alloc(2 * KD)
            P.op('dve', lambda e: e.tensor_copy(out=V3(gtmp, 2, KD), in_=m3[:, 2 * KD:3 * KD, :].rearrange("p k c -> p c k")),
                 reads=[R], writes=[R])
            for c in range(2):
                P.dma('sp', dr['s_gate'][l][c].rearrange("(k p) -> p k", p=128), V3(gtmp, 2, KD)[:, c, :], reads=[R],
                      allow_slow_non_contiguous=True)

        def phase_inproj(l, W, segs, xsrc):
            rps = new_phase()
            TB = min(1024, LS)
            hT = alloc(KD * TB, BF16)
            hT3 = V3(hT, KD, TB)
            wts = [V3(alloc(KD * 512, BF16), KD, 512) for _ in range(2)]
            rW = [P.res('w0'), P.res('w1')]
            xts = [alloc(D) for _ in range(2)]
            rX = [P.res('x0'), P.res('x1')]
            xn = alloc(D, BF16)
            rXn = P.res('xn')
            ss = alloc(1)
            rstd = alloc(1)
            rS = P.res('ss')
            stg = [alloc(512) for _ in range(4)]
            rStg = [P.res(f'stg{i}') for i in range(4)]
            rH = P.res('hT')
            sc = [0]
            wc = [0]
            bc = [0]
            SK = os.environ.get("MK_SKIP", "")
            for tb0 in range(0, T, TB):
                for ti in range(TB // 128 if 'L' not in SK else 0):
                    tok0 = tb0 + ti * 128
                    cond = 0 if tok0 < LS else 1
                    s = ti % 2
                    P.dma('sp', xts[s], xsrc[tok0:tok0 + 128, :], writes=[rX[s]])
                    if 'A' in SK:
                        continue
                    P.op('act', lambda e, s=s: e.activation(out=xn, in_=xts[s], func=AF.Square, accum_out=ss),
                         reads=[rX[s]], writes=[rXn, rS])
                    rstd_chain(rstd, ss, 1.0 / D, [rS], [rS])
                    P.op('act', lambda e, s=s: e.activation(out=xn, in_=xts[s], func=AF.Copy, scale=rstd),
                         reads=[rX[s], rS], writes=[rXn])
                    for kg in range(0, KD if 'T' not in SK else 0, 8):
                        nk = min(8, KD - kg)
                        b = bc[0] % 8
                        bc[0] += 1
                        pb = psum[b][:].bitcast(BF16)
                        for j in range(nk):
                            P.op('pe', lambda e, pb=pb, j=j, kg=kg: e.transpose(
                                pb[:, j * 128:(j + 1) * 128], xn[:, (kg + j) * 128:(kg + j + 1) * 128], identb),
                                reads=[rXn, rP], writes=[rps[b]], sig=(j == nk - 1))
                        for j in range(nk):
                            k = kg + j
                            P.op('dve', lambda e, pb=pb, j=j, k=k, ti=ti, cond=cond: e.tensor_scalar(
                                out=hT3[:, k, ti * 128:(ti + 1) * 128], in0=pb[:, j * 128:(j + 1) * 128],
                                scalar1=AT[l][:, 2 * k + cond:2 * k + cond + 1],
                                scalar2=SH[l][:, 2 * k + cond:2 * k + cond + 1], op0=ALU.mult, op1=ALU.add),
                                reads=[rps[b], rP], writes=[rH])
                off = 0
                for (nm, kind, w) in (segs if 'W' not in SK else []):
                    if (os.environ.get("MK_SEG") and kind != os.environ.get("MK_SEG")) or (os.environ.get("MK_ONLY") and nm != os.environ.get("MK_ONLY")):
                        off += w
                        continue
                    for c0 in range(0, w, 512):
                        cw = min(512, w - c0)
                        s = wc[0] % 2
                        wc[0] += 1
                        wt = wts[s]
                        P.dma('pool', wt[:, :, :cw],
                              W[:, off + c0:off + c0 + cw].rearrange("(k p) c -> p k c", p=128), writes=[rW[s]])
                        if kind == 'fm':
                            for j0 in range(0, cw, 128):
                                m = min(128, cw - j0)
                                for t0 in range(0, TB, 512):
                                    b = bc[0] % 8
                                    bc[0] += 1
                                    for k in range(KD):
                                        P.op('pe', lambda e, b=b, m=m, k=k, j0=j0, t0=t0, wt=wt: e.matmul(
                                            psum[b][:m, :], lhsT=wt[:, k, j0:j0 + m], rhs=hT3[:, k, t0:t0 + 512],
                                            start=(k == 0), stop=(k == KD - 1)), reads=[rW[s], rH], writes=[rps[b]],
                                            sig=(k == KD - 1))
                                    q = sc[0] % 4
                                    sc[0] += 1
                                    so = stg[q].bitcast(BF16)[:m, 0:512]
                                    evac(so, psum[b][:m, :], [rps[b]], [rStg[q]])
                                    P.dma('sp', dr['s_' + nm][c0 + j0:c0 + j0 + m, tb0 + t0:tb0 + t0 + 512], so,
                                          reads=[rStg[q]])
                        else:
                            for ti in range(TB // 128):
                                tok0 = tb0 + ti * 128
                                b = bc[0] % 8
                                bc[0] += 1
                                for k in range(KD):
                                    P.op('pe', lambda e, b=b, k=k, ti=ti, wt=wt, cw=cw: e.matmul(
                                        psum[b][:, :cw], lhsT=hT3[:, k, ti * 128:(ti + 1) * 128], rhs=wt[:, k, :cw],
                                        start=(k == 0), stop=(k == KD - 1)), reads=[rW[s], rH], writes=[rps[b]],
                                        sig=(k == KD - 1))
                                q = sc[0] % 4
                                sc[0] += 1
                                if nm == 'va' and tok0 >= LS:
                                    s32 = stg[q][:, 0:cw]
                                    evac(s32, psum[b][:, :cw], [rps[b]], [rStg[q]])
                                    P.dma('sp', dr['o_na_v'][tok0 - LS:tok0 - LS + 128, c0:c0 + cw], s32, reads=[rStg[q]])
                                    q2 = sc[0] % 4
                                    sc[0] += 1
                                    so = stg[q2].bitcast(BF16)[:, 0:cw]
                                    P.op('dve', lambda e: e.tensor_copy(out=so, in_=s32), reads=[rStg[q]], writes=[rStg[q2]])
                                    P.dma('sp', dr['s_' + nm][tok0:tok0 + 128, c0:c0 + cw], so, reads=[rStg[q2]])
                                    continue
                                if kind == 'tm':
                                    so = stg[q].bitcast(BF16)[:, 0:cw]
                                else:
                                    so = stg[q][:, 0:cw]
                                evac(so, psum[b][:, :cw], [rps[b]], [rStg[q]])
                                P.dma('sp', dr['s_' + nm][tok0:tok0 + 128, c0:c0 + cw], so, reads=[rStg[q]])
                    off += w

        def phase_outproj(l, xsrc, ydst):
            rps = new_phase()
            TB = min(1024, LS)
            mixT = V3(alloc(KD * TB, BF16), KD, TB)
            rM = P.res('mix')
            wts = [V3(alloc(KD * 512, BF16), KD, 512) for _ in range(2)]
            rW = [P.res('w0'), P.res('w1')]
            gbc = [alloc(D) for _ in range(2)]
            rG = P.res('g')
            xb = [alloc(512) for _ in range(4)]
            rXb = [P.res(f'xb{i}') for i in range(4)]
            tmps = [alloc(512) for _ in range(2)]
            rTmp = [P.res('t0'), P.res('t1')]
            for c in range(2):
                P.dma('sp', gbc[c], dr['s_gate'][l][c:c + 1, :].partition_broadcast(128), writes=[rG])
            wc = 0
            bc = 0
            xc = 0
            for tb0 in range(0, T, TB):
                P.dma('sp', mixT, dr['s_mix'][:, tb0:tb0 + TB].rearrange("(k p) t -> p k t", p=128), writes=[rM])
                for c0 in range(0, D, 512):
                    s = wc % 2
                    wc += 1
                    wt = wts[s]
                    P.dma('pool', wt, dr['w_out'][l][:, c0:c0 + 512].rearrange("(k p) c -> p k c", p=128), writes=[rW[s]])
                    for ti in range(TB // 128):
                        tok0 = tb0 + ti * 128
                        cond = 0 if tok0 < LS else 1
                        b = bc % 8
                        bc += 1
                        q = xc % 4
                        xc += 1
                        P.dma('sp', xb[q], xsrc[tok0:tok0 + 128, c0:c0 + 512], writes=[rXb[q]])
                        for k in range(KD):
                            P.op('pe', lambda e, b=b, k=k, ti=ti, wt=wt: e.matmul(
                                psum[b][:], lhsT=mixT[:, k, ti * 128:(ti + 1) * 128], rhs=wt[:, k, :],
                                start=(k == 0), stop=(k == KD - 1)), reads=[rW[s], rM], writes=[rps[b]], sig=(k == KD - 1))
                        P.op('dve', lambda e: e.tensor_tensor(out=tmps[q % 2], in0=psum[b][:], in1=gbc[cond][:, c0:c0 + 512],
                                                              op=ALU.mult), reads=[rps[b], rG], writes=[rTmp[q % 2]])
                        P.op('dve', lambda e: e.tensor_tensor(out=xb[q], in0=tmps[q % 2], in1=xb[q], op=ALU.add),
                             reads=[rTmp[q % 2]], writes=[rXb[q]])
                        P.dma('sp', ydst[tok0:tok0 + 128, c0:c0 + 512], xb[q], reads=[rXb[q]])

        def fm_norm(dst, src, gain, np_, n, R, rps, b, sq, rstd, extra_dst=None, RO=None):
            RO = list(RO) if RO is not None else []
            P.op('act', lambda e: e.activation(out=sq[:np_, :n], in_=src, func=AF.Square), reads=R + RO, writes=R)
            P.op('pe', lambda e: e.matmul(psum[b][:np_, :n], lhsT=onesb[:np_, :np_], rhs=sq[:np_, :n], start=True, stop=True),
                 reads=R + [rP], writes=[rps[b]])
            rstd_chain(rstd[:np_, :n], psum[b][:np_, :n], 1.0 / np_, [rps[b]] + R, R)
            P.op('dve', lambda e: e.scalar_tensor_tensor(out=dst, in0=src, scalar=gain, in1=rstd[:np_, :n],
                                                         op0=ALU.mult, op1=ALU.mult), reads=R + RO + [rP], writes=R)
            if extra_dst is not None:
                P.op('dve', lambda e: e.scalar_tensor_tensor(out=extra_dst, in0=src, scalar=gain, in1=rstd[:np_, :n],
                                                             op0=ALU.mult, op1=ALU.mult), reads=R + RO + [rP], writes=R)

        def attn_block(qT, q2T, keys, nq, o32, R, rps, pts, rPt, rec, RO=None):
            n = len(keys)
            RD = R + (list(RO) if RO is not None else [])
            NBF = 3
            LA = 2

            def score(i):
                kT, k2T, v, mask = keys[i]
                sb = i % NBF
                P.op('pe', lambda e: e.matmul(psum[sb][:, :nq], lhsT=kT, rhs=qT, start=True, stop=(k2T is None)),
                     reads=RD, writes=[rps[sb]], sig=(k2T is None))
                if k2T is not None:
                    P.op('pe', lambda e: e.matmul(psum[sb][:, :nq], lhsT=k2T, rhs=q2T, start=False, stop=True),
                         reads=RD, writes=[rps[sb]])

            for i in range(min(LA, n)):
                score(i)
            for i in range(n):
                kT, k2T, v, mask = keys[i]
                sb = i % NBF
                pt = pts[sb]
                if i + LA < n:
                    score(i + LA)
                P.op('act', lambda e: e.activation(out=pt[:, :nq], in_=psum[sb][:, :nq], func=AF.Exp),
                     reads=[rps[sb]], writes=[rPt[sb]])
                if mask is not None:
                    P.op('dve', lambda e: e.tensor_tensor(out=pt[:, :nq], in0=pt[:, :nq], in1=mask, op=ALU.mult),
                         reads=RD + [rPt[sb]], writes=[rPt[sb]])
                P.op('pe', lambda e: e.matmul(psum[3][:, :nq], lhsT=v, rhs=pt[:, :nq], start=(i == 0), stop=(i == n - 1)),
                     reads=RD + [rPt[sb]], writes=[rps[3]], sig=False)
                P.op('pe', lambda e: e.matmul(psum[4][:, :nq], lhsT=onesb, rhs=pt[:, :nq], start=(i == 0), stop=(i == n - 1)),
                     reads=[rPt[sb], rP], writes=[rps[4]])
            P.op('dve', lambda e: e.reciprocal(out=rec[:, :nq], in_=psum[4][:, :nq]), reads=[rps[4]], writes=R)
            P.op('dve', lambda e: e.tensor_tensor(out=o32, in0=psum[3][:, :nq], in1=rec[:, :nq], op=ALU.mult),
                 reads=[rps[3], rps[4]] + R, writes=R)

        def phase_mixA():
            rps = new_phase()
            R = P.res('a')
            NP = cfg.NPAT
            mask01 = V3(alloc(NP * 512, BF16), NP, 512)
            P.dma('sp', mask01, dr['mask01'], writes=[R])
            qg = alloc(1)
            kg = alloc(1)
            P.dma('sp', qg, dr['na_q_norm'].rearrange("(p o) -> p o", o=1), writes=[R])
            P.dma('sp', kg, dr['na_k_norm'].rearrange("(p o) -> p o", o=1), writes=[R])
            P.op('dve', lambda e: e.tensor_scalar_mul(out=qg, in0=qg, scalar1=128.0 ** -0.5), reads=[R], writes=[R])
            rTF = P.res('tf')
            for ck in range(64):
                P.dma('sp', dr['s_tf'][:, :, ck, :], dr['rp'][:, :, 63 - ck:63 - ck + 64], writes=[rTF])
            qraw = alloc(T, BF16)
            kraw = alloc(T, BF16)
            qn = alloc(T, BF16)
            kn = alloc(T, BF16)
            sg = alloc(T, BF16)
            vt = V3(alloc(T, BF16), T // 128, 128)
            kc32 = V3(alloc(256, BF16), 2, 128)
            kcT = alloc(256, BF16)
            vc = V3(alloc(256, BF16), 2, 128)
            stage32 = V3(alloc(NP * 512), NP, 512)
            EB = V3(alloc(NP * 512, BF16), NP, 512)
            sq = alloc(512, BF16)
            rstd = alloc(512)
            knf = alloc(512)
            tmp = alloc(512)
            pts = [alloc(512, BF16) for _ in range(3)]
            rPt = [P.res('pt0'), P.res('pt1'), P.res('pt2')]
            rec = alloc(512)
            o32 = alloc(512)
            mst = alloc(512, BF16)
            rMst = P.res('mst')
            tst = alloc(128)
            rTst = P.res('tst')
            rL = P.res('loads')
            rE = P.res('eb')
            for h in range(NH):
                hs = slice(h * 128, (h + 1) * 128)
                P.dma('sp', qraw, dr['s_qa'][hs, :], writes=[rL])
                P.dma('sp', kraw, dr['s_ka'][hs, :], writes=[rL])
                P.dma('sp', sg, dr['s_ga'][hs, :], writes=[rL])
                P.dma('sp', vt, dr['s_va'][:, hs].rearrange("(n p) c -> p n c", p=128), writes=[rL])
                P.dma('pool', kc32, dr['c_na_k'][:, hs].rearrange("(n p) c -> p n c", p=128), writes=[rL])
                P.dma('pool', vc, dr['c_na_v'][:, hs].rearrange("(n p) c -> p n c", p=128), writes=[rL])
                for n_ in range(2):
                    pb = psum[7][:].bitcast(BF16)
                    P.op('pe', lambda e, n_=n_, pb=pb: e.transpose(pb[:, n_ * 128:(n_ + 1) * 128], kc32[:, n_, :], identb),
                         reads=[rL, rP], writes=[rps[7]])
                P.op('dve', lambda e: e.tensor_copy(out=kcT, in_=psum[7][:].bitcast(BF16)[:, 0:256]), reads=[rps[7]],
                     writes=[rL])
                for t0 in range(0, T, 512):
                    silu_ops(sg[:, t0:t0 + 512], sg[:, t0:t0 + 512], tmp, [rL, R])
                for t0 in range(0, T, 512):
                    fm_norm(qn[:, t0:t0 + 512], qraw[:, t0:t0 + 512], qg, 128, 512, [rL, R], rps, 5, sq, rstd)
                    fm_norm(kn[:, t0:t0 + 512], kraw[:, t0:t0 + 512], kg, 128, 512, [rL, R], rps, 6, sq, rstd,
                            extra_dst=(knf if t0 >= LS else None))
                    if t0 >= LS:
                        for j in range(4):
                            P.op('pe', lambda e, j=j: e.transpose(psum[7][:, j * 128:(j + 1) * 128],
                                                                  knf[:, j * 128:(j + 1) * 128], identf),
                                 reads=[rL, R, rP], writes=[rps[7]])
                            P.op('act', lambda e, j=j: e.activation(out=tst, in_=psum[7][:, j * 128:(j + 1) * 128],
                                                                    func=AF.Copy), reads=[rps[7]], writes=[rTst])
                            P.dma('sp', dr['o_na_k'][t0 - LS + j * 128:t0 - LS + (j + 1) * 128, hs], tst, reads=[rTst])
                for pi, (_, delta) in enumerate(cfg.pat_list):
                    for kr in range(2):
                        e0 = 11 - delta - kr
                        P.dma('sp', stage32[kr * 64:(kr + 1) * 64, pi, :].rearrange("p (a b) -> p a b", a=8, b=64),
                              dr['s_tf'][h, e0:e0 + 8, :, :].rearrange("e ck cq -> ck e cq"), reads=[rTF], writes=[rE])
                for pi in range(NP):
                    P.op('act', lambda e, pi=pi: e.activation(out=stage32[:, pi, :], in_=stage32[:, pi, :], func=AF.Exp),
                         reads=[rE], writes=[rE])
                    P.op('dve', lambda e, pi=pi: e.tensor_tensor(out=EB[:, pi, :], in0=stage32[:, pi, :],
                                                                 in1=mask01[:, pi, :], op=ALU.mult), reads=[rE, R],
                         writes=[rE])
                RR = [rL, R, rE]
                for qb in range(cfg.NQB):
                    keys = []
                    for (kt, pi) in cfg.qb_keys[qb]:
                        keys.append((kn[:, kt * 128:(kt + 1) * 128], None, vt[:, kt, :], EB[:, pi, :]))
                    for n_ in range(2):
                        keys.append((kcT[:, n_ * 128:(n_ + 1) * 128], None, vc[:, n_, :], None))
                    attn_block(qn[:, qb * 512:(qb + 1) * 512], None, keys, 512, o32, RR, rps, pts, rPt, rec)
                    P.op('dve', lambda e, qb=qb: e.tensor_tensor(out=mst, in0=o32, in1=sg[:, qb * 512:(qb + 1) * 512],
                                                                 op=ALU.mult), reads=RR, writes=[rMst])
                    P.dma('sp', dr['s_mix'][hs, qb * 512:(qb + 1) * 512], mst, reads=[rMst])
                for s_ in range(NPS):
                    t0 = LS + s_ * 256
                    keys = [(kn[:, t0 + n_ * 128:t0 + (n_ + 1) * 128], None, vt[:, t0 // 128 + n_, :], None)
                            for n_ in range(2)]
                    attn_block(qn[:, t0:t0 + 256], None, keys, 256, o32[:, :256], RR, rps, pts, rPt, rec)
                    P.op('dve', lambda e, t0=t0: e.tensor_tensor(out=mst[:, :256], in0=o32[:, :256], in1=sg[:, t0:t0 + 256],
                                                                 op=ALU.mult), reads=RR, writes=[rMst])
                    P.dma('sp', dr['s_mix'][hs, t0:t0 + 256], mst[:, :256], reads=[rMst])

        def phase_mixB():
            rps = new_phase()
            R = P.res('b')
            NB = NH * 128
            sgw32 = V3(alloc(NB), NH, 128)
            sgwb = V3(alloc(NB, BF16), NH, 128)
            sgwT = V3(alloc(NB, BF16), NH, 128)
            sgb = alloc(NB)
            sgn = alloc(BW)
            P.dma('sp', sgw32, dr['sg_w'].rearrange("g t s -> t g s"), writes=[R])
            P.dma('sp', sgb, dr['sg_b'].rearrange("(o n) -> o n", o=1).partition_broadcast(128), writes=[R])
            P.dma('sp', sgn, dr['sg_norm'].rearrange("(o n) -> o n", o=1).partition_broadcast(128), writes=[R])
            P.op('dve', lambda e: e.tensor_copy(out=sgwb, in_=sgw32), reads=[R], writes=[R])
            for g in range(NH):
                pb = psum[0][:].bitcast(BF16)
                P.op('pe', lambda e, g=g, pb=pb: e.transpose(pb[:, 0:128], sgwb[:, g, :], identb), reads=[R, rP],
                     writes=[rps[0]])
                P.op('dve', lambda e, g=g, pb=pb: e.tensor_copy(out=sgwT[:, g, :], in_=pb[:, 0:128]), reads=[rps[0]],
                     writes=[R])
            vb = alloc(BW, BF16)
            gv = alloc(BW)
            t1 = alloc(BW)
            t2 = alloc(BW)
            vn = alloc(BW, BF16)
            uT = alloc(NB, BF16)
            gT = alloc(NB, BF16)
            gu = alloc(NB)
            ss = alloc(1)
            rstd = alloc(1)
            mo = alloc(NB, BF16)
            rMo = P.res('mo')
            rL = P.res('ld')
            RR = [R, rL]
            for ti in range(T // 128):
                tk = slice(ti * 128, (ti + 1) * 128)
                P.dma('sp', vb, dr['s_vb'][tk, :], writes=[rL])
                P.dma('sp', V3(uT, NH, 128), dr['s_ub'][:, tk].rearrange("(g c) t -> c g t", c=128), writes=[rL])
                P.dma('sp', V3(gT, NH, 128), dr['s_gb'][:, tk].rearrange("(g c) t -> c g t", c=128), writes=[rL])
                gelu_ops(gv, vb, t1, t2, RR)
                P.op('act', lambda e: e.activation(out=t1, in_=gv, func=AF.Square, accum_out=ss), reads=RR, writes=RR)
                rstd_chain(rstd, ss, 1.0 / BW, RR, RR)
                P.op('dve', lambda e: e.scalar_tensor_tensor(out=vn, in0=gv, scalar=rstd, in1=sgn, op0=ALU.mult,
                                                             op1=ALU.mult), reads=RR, writes=RR)
                gelu_ops(gu, uT, t1[:, :NB], t2[:, :NB], RR)
                for g in range(NH):
                    b = (g * 128) // 512
                    P.op('pe', lambda e, g=g, b=b: e.matmul(psum[b][:, (g * 128) % 512:(g * 128) % 512 + 128],
                                                            lhsT=vn[:, g * 128:(g + 1) * 128], rhs=sgwT[:, g, :],
                                                            start=True, stop=True), reads=RR, writes=[rps[b]])
                for b in range((NB + 511) // 512):
                    w = min(512, NB - b * 512)
                    cs = slice(b * 512, b * 512 + w)
                    P.op('dve', lambda e, b=b, w=w, cs=cs: e.tensor_tensor(out=t1[:, cs], in0=psum[b][:, :w], in1=sgb[:, cs],
                                                                           op=ALU.add), reads=[rps[b]] + RR, writes=RR)
                P.op('dve', lambda e: e.tensor_tensor(out=gu, in0=gu, in1=t1[:, :NB], op=ALU.mult), reads=RR, writes=RR)
                silu_ops(t1[:, :NB], gT, t2[:, :NB], RR)
                P.op('dve', lambda e: e.tensor_tensor(out=mo, in0=gu, in1=t1[:, :NB], op=ALU.mult), reads=RR, writes=[rMo])
                P.dma('sp', dr['s_mix'][BW:2 * BW, tk].rearrange("(g c) t -> c g t", c=128), V3(mo, NH, 128), reads=[rMo])

        def phase_hgrn():
            rps = new_phase()
            R = P.res('c')
            hmat = V3(alloc(1024), 2, 512)
            P.dma('sp', hmat, dr['hmat'], writes=[R])
            maskb = V3(alloc(256, BF16), 2, 128)
            P.op('dve', lambda e: e.tensor_copy(out=maskb, in_=hmat[:, :, 0:128]), reads=[R], writes=[R])
            lbb = [alloc(BW) for _ in range(2)]
            oml = [alloc(BW) for _ in range(2)]
            tl = alloc(BW)
            og = alloc(1)
            P.dma('sp', og, dr['hgrn_out_norm'].rearrange("(p o) -> p o", o=1), writes=[R])
            for d in range(2):
                P.dma('sp', lbb[d], dr['hgrn_lb'][1, d:d + 1, :].partition_broadcast(128), writes=[R])
                P.dma('sp', tl, dr['hgrn_lb'][0, d:d + 1, :].partition_broadcast(128), writes=[R])
                P.op('dve', lambda e, d=d: e.tensor_tensor(out=lbb[d], in0=lbb[d], in1=tl, op=ALU.subtract), reads=[R],
                     writes=[R])
                sigmoid_ops(lbb[d], lbb[d], 1.0, [R])
                P.op('dve', lambda e, d=d: e.tensor_scalar(out=oml[d], in0=lbb[d], scalar1=-1.0, scalar2=1.0, op0=ALU.mult,
                                                           op1=ALU.add), reads=[R], writes=[R])
            z = alloc(BW)
            f = alloc(BW)
            gl = alloc(BW)
            kk = alloc(BW)
            khat = alloc(BW, BF16)
            ktil = alloc(BW, BF16)
            vt = alloc(BW, BF16)
            qraw = V3(alloc(BW, BF16), NH, 128)
            gcT = V3(alloc(BW, BF16), NH, 128)
            S = V3(alloc(NH * 128), NH, 128)
            EEs = [alloc(256) for _ in range(4)]
            qincs = [alloc(128, BF16) for _ in range(4)]
            Sb = V3(alloc(NH * 128, BF16), NH, 128)
            rSb = [P.res(f'Sb{h}') for h in range(NH)]
            qtils = [alloc(128, BF16) for _ in range(4)]
            ktTs = [alloc(128, BF16) for _ in range(4)]
            pTms = [alloc(128, BF16) for _ in range(4)]
            rA = [rps[2], rps[2], rps[3], rps[3]]
            rEE = [P.res(f'EE{j}') for j in range(4)]
            rQi = [P.res(f'Qi{j}') for j in range(4)]
            rQt = [P.res(f'Qt{j}') for j in range(4)]
            rTr = [rps[4]] * 4
            rKt = [P.res(f'Kt{j}') for j in range(4)]
            rPT = [rps[5]] * 4
            rPm = [P.res(f'Pm{j}') for j in range(4)]
            rOj = [rps[6]] * 4
            rU = [rps[7]] * 4
            oall = V3(alloc(BW), NH, 128)
            ofl = V3(alloc(BW), NH, 128)
            sqb = alloc(BW, BF16)
            mo = alloc(BW, BF16)
            rMo = P.res('mo')
            rL = P.res('ld')
            rS = [P.res(f'S{h}') for h in range(NH)]
            rT = P.res('tm')
            rF = P.res('fmh')
            rO = P.res('oall')
            nbk = (BW + 511) // 512
            seqs = [(0, LS // 128, None)] + [((LS + s_ * 256) // 128, 2, s_) for s_ in range(NPS)]
            for d in range(2):
                zname = 's_ffw' if d == 0 else 's_fbw'
                Mi = hmat[:, d, 0:128]
                MiMc = hmat[:, d, 0:256]
                Mc = hmat[:, d, 128:256]
                Mr = hmat[:, d, 256:384]
                for (tile0, ntl, ps_idx) in seqs:
                    if ps_idx is None:
                        P.dma('sp', S, dr['st_hgrn'][d].rearrange("h k v -> k h v"), writes=rS)
                    else:
                        P.op('dve', lambda e: e.memset(S, 0.0), writes=rS)
                    P.op('act', lambda e: e.activation(out=Sb, in_=S, func=AF.Copy), reads=rS, writes=rSb)
                    order = range(ntl) if d == 0 else range(ntl - 1, -1, -1)
                    for tl_ in order:
                        ti = tile0 + tl_
                        tk = slice(ti * 128, (ti + 1) * 128)
                        P.dma('sp', z, dr[zname][tk, :], writes=[rL])
                        P.dma('sp', vt, dr['s_ic'][tk, :], writes=[rL])
                        P.dma('sp', qraw, dr['s_qc'][:, tk].rearrange("(h k) t -> k h t", k=128), writes=[rL])
                        if d == 1:
                            P.dma('sp', gcT, dr['s_gc'][:, tk].rearrange("(h k) t -> k h t", k=128), writes=[rL])
                            P.dma('sp', ofl, dr['s_of'][:, tk].rearrange("(h k) t -> k h t", k=128), writes=[rL])
                        RT = [rL, R, rT]
                        sigmoid_ops(f, z, 1.0, RT)
                        P.op('dve', lambda e, d=d: e.tensor_tensor(out=f, in0=f, in1=oml[d], op=ALU.mult), reads=RT, writes=RT)
                        P.op('dve', lambda e, d=d: e.tensor_tensor(out=f, in0=f, in1=lbb[d], op=ALU.add), reads=RT, writes=RT)
                        P.op('act', lambda e: e.activation(out=gl, in_=f, func=AF.Ln), reads=RT, writes=RT)
                        P.op('dve', lambda e: e.tensor_scalar(out=kk, in0=f, scalar1=-1.0, scalar2=1.0, op0=ALU.mult,
                                                              op1=ALU.add), reads=RT, writes=RT)
                        for (Mx, dst, sc_) in ((Mr, khat, 1.0), (Mc, ktil, -1.0)):
                            for b in range(nbk):
                                w = min(512, BW - b * 512)
                                pbk = b % 2
                                P.op('pe', lambda e: e.matmul(psum[pbk][:, :w], lhsT=Mx, rhs=gl[:, b * 512:b * 512 + w],
                                                              start=True, stop=True), reads=RT, writes=[rps[pbk]])
                                P.op('act', lambda e: e.activation(out=z[:, b * 512:b * 512 + w], in_=psum[pbk][:, :w],
                                                                   func=AF.Exp, scale=sc_), reads=[rps[pbk]] + RT, writes=RT)
                            P.op('dve', lambda e, dst=dst: e.tensor_tensor(out=dst, in0=kk, in1=z, op=ALU.mult), reads=RT,
                                 writes=RT)
                        G = min(4, NH)
                        corder = list(range(4)) if d == 0 else list(range(3, -1, -1))
                        for hg in range(0, NH, G):
                            hh = list(range(hg, hg + G))
                            RB = [rL, R, rT]
                            for j, h in enumerate(hh):
                                hs = slice(h * 128, (h + 1) * 128)
                                bA = 2 + j // 2
                                cA = (j % 2) * 256
                                P.op('pe', lambda e: e.matmul(psum[bA][:, cA:cA + 256], lhsT=gl[:, hs], rhs=MiMc, start=True,
                                                              stop=True), reads=RT, writes=[rA[j]])
                            for j, h in enumerate(hh):
                                bA = 2 + j // 2
                                cA = (j % 2) * 256
                                P.op('act', lambda e: e.activation(out=EEs[j], in_=psum[bA][:, cA:cA + 256], func=AF.Exp),
                                     reads=[rA[j], rS[h]], writes=[rEE[j]])
                            for j, h in enumerate(hh):
                                P.op('dve', lambda e: e.tensor_tensor(out=qincs[j], in0=qraw[:, h, :], in1=EEs[j][:, 0:128],
                                                                      op=ALU.mult), reads=[rEE[j], rL], writes=[rQi[j]])
                                P.op('dve', lambda e: e.tensor_tensor(out=qtils[j], in0=qraw[:, h, :], in1=EEs[j][:, 128:256],
                                                                      op=ALU.mult), reads=[rEE[j], rL], writes=[rQt[j]])
                            pb = psum[4][:].bitcast(BF16)
                            for j, h in enumerate(hh):
                                hs = slice(h * 128, (h + 1) * 128)
                                P.op('pe', lambda e: e.transpose(pb[:, j * 128:(j + 1) * 128], ktil[:, hs], identb),
                                     reads=RT + [rP], writes=[rTr[j]])
                            for j, h in enumerate(hh):
                                P.op('act', lambda e: e.activation(out=ktTs[j], in_=pb[:, j * 128:(j + 1) * 128], func=AF.Copy),
                                     reads=[rTr[j]], writes=[rKt[j]])
                            for j, h in enumerate(hh):
                                P.op('pe', lambda e: e.matmul(psum[5][:, j * 128:(j + 1) * 128], lhsT=ktTs[j], rhs=qtils[j],
                                                              start=True, stop=True), reads=[rKt[j], rQt[j]], writes=[rPT[j]])
                            for j, h in enumerate(hh):
                                P.op('dve', lambda e: e.tensor_tensor(out=pTms[j], in0=psum[5][:, j * 128:(j + 1) * 128],
                                                                      in1=maskb[:, d, :], op=ALU.mult), reads=[rPT[j], R],
                                     writes=[rPm[j]])
                            for j, h in enumerate(hh):
                                hs = slice(h * 128, (h + 1) * 128)
                                P.op('pe', lambda e: e.matmul(psum[6][:, j * 128:(j + 1) * 128], lhsT=vt[:, hs], rhs=pTms[j],
                                                              start=(j == 0), stop=False), reads=[rPm[j], rL], writes=[rOj[j]],
                                     sig=False)
                            for ci, c in enumerate(corder):
                                cs = slice(c * 32, (c + 1) * 32)
                                col = c * 32 + 31 if d == 0 else c * 32
                                for j, h in enumerate(hh):
                                    hs = slice(h * 128, (h + 1) * 128)
                                    P.op('pe', lambda e: e.matmul(psum[6][:, j * 128 + c * 32:j * 128 + (c + 1) * 32],
                                                                  lhsT=Sb[:, h, :], rhs=qincs[j][:, cs], start=False,
                                                                  stop=(ci == 3 and j == G - 1)), reads=[rQi[j], rSb[h]], writes=[rOj[j]],
                                         sig=True)
                                    P.op('pe', lambda e: e.matmul(psum[7][:, j * 128:(j + 1) * 128], lhsT=khat[cs, hs],
                                                                  rhs=vt[cs, hs], start=True, stop=True,
                                                                  tile_position=(c * 32, 0)), reads=RT, writes=[rU[j]])
                                for j, h in enumerate(hh):
                                    P.op('dve', lambda e: e.scalar_tensor_tensor(
                                        out=S[:, h, :], in0=S[:, h, :], scalar=EEs[j][:, col:col + 1],
                                        in1=psum[7][:, j * 128:(j + 1) * 128], op0=ALU.mult, op1=ALU.add),
                                        reads=[rU[j], rEE[j]], writes=[rS[h]])
                                for j, h in enumerate(hh):
                                    P.op('act', lambda e: e.activation(out=Sb[:, h, :], in_=S[:, h, :], func=AF.Copy),
                                         reads=[rS[h]], writes=[rSb[h]])
                            P.op('act', lambda e: e.activation(out=oall[:, hg:hg + G, :], in_=V3(psum[6][:, 0:G * 128], G, 128),
                                                               func=AF.Copy), reads=rOj[:G], writes=[rO])
                        if d == 0:
                            P.dma('sp', dr['s_of'][:, tk].rearrange("(h k) t -> k h t", k=128), oall, reads=[rO])
                        else:
                            RO = [rO, rL, R]
                            oa2 = oall.rearrange("p a b -> p (a b)")
                            P.op('dve', lambda e: e.tensor_tensor(out=oall, in0=oall, in1=ofl, op=ALU.add), reads=RO, writes=RO)
                            P.op('act', lambda e: e.activation(out=sqb, in_=oa2, func=AF.Square), reads=RO, writes=RO)
                            for b in range(nbk):
                                w = min(512, BW - b * 512)
                                P.op('pe', lambda e, b=b, w=w: e.matmul(psum[b][:, :w], lhsT=onesb, rhs=sqb[:, b * 512:b * 512 + w],
                                                                        start=True, stop=True), reads=RO + [rP], writes=[rps[b]])
                                rstd_chain(f[:, b * 512:b * 512 + w], psum[b][:, :w], 1.0 / 128, [rps[b]] + RO + [rT], RO + [rT])
                            P.op('dve', lambda e: e.scalar_tensor_tensor(out=oa2, in0=oa2, scalar=og, in1=f, op0=ALU.mult,
                                                                         op1=ALU.mult), reads=RO + [rT], writes=RO)
                            g2 = gcT.rearrange("p a b -> p (a b)")
                            silu_ops(gl, g2, kk, RO + [rT])
                            P.op('dve', lambda e: e.tensor_tensor(out=mo, in0=oa2, in1=gl, op=ALU.mult), reads=RO + [rT],
                                 writes=[rMo])
                            P.dma('sp', dr['s_mix'][0:BW, tk].rearrange("(h k) t -> k h t", k=128), V3(mo, NH, 128),
                                  reads=[rMo])
                    if ps_idx is not None:
                        P.dma('sp', dr['o_hgrn'][ps_idx, d].rearrange("h k v -> k h v"), S, reads=rS)

        def phase_mla():
            rps = new_phase()
            R = P.res('d')
            NK = 256 + T
            ckvT = V3(alloc(4 * NK, BF16), 4, NK)
            k2T = alloc(NK, BF16)
            qag = alloc(KQ)
            kvag = alloc(4)
            qng = alloc(1)
            qrg = alloc(1)
            kng = alloc(1)
            krg = alloc(1)
            prot = alloc(64)
            with_nc = dict(allow_slow_non_contiguous=True)
            P.dma('sp', qag, dr['mla_q_a_norm'].rearrange("(k p) -> p k", p=128), writes=[R], **with_nc)
            P.dma('sp', kvag, dr['mla_kv_a_norm'].rearrange("(k p) -> p k", p=128), writes=[R], **with_nc)
            P.dma('sp', qng, dr['mla_q_norm'][0:128].rearrange("(p o) -> p o", o=1), writes=[R])
            P.dma('sp', qrg[:64], dr['mla_q_norm'][128:192].rearrange("(p o) -> p o", o=1), writes=[R])
            P.dma('sp', kng, dr['mla_k_norm'][0:128].rearrange("(p o) -> p o", o=1), writes=[R])
            P.dma('sp', krg[:64], dr['mla_k_norm'][128:192].rearrange("(p o) -> p o", o=1), writes=[R])
            P.dma('sp', prot[:64], dr['prot'], writes=[R])
            P.op('dve', lambda e: e.tensor_scalar_mul(out=qng, in0=qng, scalar1=192.0 ** -0.5), reads=[R], writes=[R])
            P.op('dve', lambda e: e.tensor_scalar_mul(out=qrg[:64], in0=qrg[:64], scalar1=192.0 ** -0.5), reads=[R], writes=[R])
            mla_mark = pos[0]
            raw = alloc(max(KQ, 4) * 512, BF16)
            sq = alloc(max(KQ, 4) * 512, BF16)
            rstd = alloc(512)
            cqn = V3(alloc(KQ * 512, BF16), KQ, 512)
            c32 = alloc(512)
            x32 = alloc(512)
            cosb = alloc(512)
            sinb = alloc(512)
            tst = alloc(512)
            rTst = P.res('tst')
            rL = P.res('ld')
            rCq = P.res('cqn')
            RR = [R, rL]
            cc = V3(alloc(1024, BF16), 2, 512)
            ck = V3(alloc(128, BF16), 2, 64)
            P.dma('pool', cc, dr['c_ckv'].rearrange("(n p) c -> p n c", p=128), writes=[rL])
            P.dma('pool', ck, dr['c_kr'].rearrange("(n p) c -> p n c", p=128), writes=[rL])
            for n_ in range(2):
                pb = psum[0][:].bitcast(BF16)
                for k in range(4):
                    P.op('pe', lambda e, n_=n_, k=k, pb=pb: e.transpose(pb[:, k * 128:(k + 1) * 128],
                                                                        cc[:, n_, k * 128:(k + 1) * 128], identb),
                         reads=[rL, rP], writes=[rps[0]])
                P.op('dve', lambda e, n_=n_, pb=pb: e.tensor_copy(out=ckvT[:, :, n_ * 128:(n_ + 1) * 128],
                                                                  in_=V3(pb[:, 0:512], 4, 128)), reads=[rps[0]], writes=[R])
                P.op('pe', lambda e, n_=n_, pb=pb: e.transpose(pb[:64, 512:640], ck[:, n_, :], identb), reads=[rL, rP],
                     writes=[rps[0]])
                P.op('dve', lambda e, n_=n_, pb=pb: e.tensor_copy(out=k2T[:64, n_ * 128:(n_ + 1) * 128], in_=pb[:64, 512:640]),
                     reads=[rps[0]], writes=[R])

            def rope(dst, src, t0, n):
                P.dma('sp', cosb[:64, :n], dr['ropecos'][:, t0:t0 + n], writes=[rL])
                P.dma('sp', sinb[:64, :n], dr['ropesin'][:, t0:t0 + n], writes=[rL])
                P.op('pe', lambda e: e.matmul(psum[7][:64, :n], lhsT=prot[:64, :64], rhs=src, start=True, stop=True),
                     reads=RR, writes=[rps[7]])
                P.op('dve', lambda e: e.tensor_tensor(out=sinb[:64, :n], in0=psum[7][:64, :n], in1=sinb[:64, :n], op=ALU.mult),
                     reads=[rps[7], rL], writes=[rL])
                P.op('dve', lambda e: e.tensor_tensor(out=cosb[:64, :n], in0=src, in1=cosb[:64, :n], op=ALU.mult),
                     reads=RR, writes=[rL])
                P.op('dve', lambda e: e.tensor_tensor(out=dst, in0=cosb[:64, :n], in1=sinb[:64, :n], op=ALU.add),
                     reads=[rL], writes=RR)

            for t0 in range(0, T, 512):
                is_p = t0 >= LS
                r3 = V3(raw[:, :KQ * 512], KQ, 512)
                P.dma('sp', r3, dr['s_cq'][:, t0:t0 + 512].rearrange("(k p) t -> p k t", p=128), writes=[rL])
                P.op('act', lambda e: e.activation(out=sq[:, :KQ * 512], in_=raw[:, :KQ * 512], func=AF.Square), reads=RR, writes=RR)
                for k in range(KQ):
                    P.op('pe', lambda e, k=k: e.matmul(psum[1][:], lhsT=onesb, rhs=sq[:, k * 512:(k + 1) * 512], start=(k == 0),
                                                       stop=(k == KQ - 1)), reads=RR + [rP], writes=[rps[1]], sig=(k == KQ - 1))
                rstd_chain(rstd, psum[1][:], 1.0 / QL, [rps[1]] + RR, RR)
                for k in range(KQ):
                    P.op('dve', lambda e, k=k: e.scalar_tensor_tensor(out=cqn[:, k, :], in0=r3[:, k, :], scalar=qag[:, k:k + 1],
                                                                      in1=rstd, op0=ALU.mult, op1=ALU.mult), reads=RR,
                         writes=[rCq])
                P.dma('sp', dr['s_cqn'][:, t0:t0 + 512].rearrange("(k p) t -> p k t", p=128), cqn, reads=[rCq])
                r3 = V3(raw[:, :4 * 512], 4, 512)
                P.dma('sp', r3, dr['s_ckv'][:, t0:t0 + 512].rearrange("(k p) t -> p k t", p=128), writes=[rL])
                P.op('act', lambda e: e.activation(out=sq[:, :4 * 512], in_=raw[:, :4 * 512], func=AF.Square), reads=RR, writes=RR)
                for k in range(4):
                    P.op('pe', lambda e, k=k: e.matmul(psum[2][:], lhsT=onesb, rhs=sq[:, k * 512:(k + 1) * 512], start=(k == 0),
                                                       stop=(k == 3)), reads=RR + [rP], writes=[rps[2]], sig=(k == 3))
                rstd_chain(rstd, psum[2][:], 1.0 / 512, [rps[2]] + RR, RR)
                for k in range(4):
                    P.op('dve', lambda e, k=k, r3=r3: e.scalar_tensor_tensor(
                        out=ckvT[:, k, 256 + t0:256 + t0 + 512], in0=r3[:, k, :], scalar=kvag[:, k:k + 1], in1=rstd,
                        op0=ALU.mult, op1=ALU.mult), reads=RR, writes=RR)
                    if is_p:
                        P.op('dve', lambda e, k=k, r3=r3: e.scalar_tensor_tensor(
                            out=c32, in0=r3[:, k, :], scalar=kvag[:, k:k + 1], in1=rstd, op0=ALU.mult, op1=ALU.mult),
                            reads=RR, writes=RR)
                        for j in range(4):
                            P.op('pe', lambda e, j=j: e.transpose(psum[3][:, j * 128:(j + 1) * 128],
                                                                  c32[:, j * 128:(j + 1) * 128], identf), reads=RR + [rP],
                                 writes=[rps[3]])
                        P.op('act', lambda e: e.activation(out=tst, in_=psum[3][:], func=AF.Copy), reads=[rps[3]], writes=[rTst])
                        P.dma('sp', dr['o_ckv'][t0 - LS:t0 - LS + 512, k * 128:(k + 1) * 128].rearrange("(j p) c -> p j c", p=128),
                              V3(tst, 4, 128), reads=[rTst])
                P.dma('sp', raw[:64, :512], dr['s_kr'][:, t0:t0 + 512], writes=[rL])
                fm_norm(x32[:64, :], raw[:64, :512], krg[:64], 64, 512, RR, rps, 4, sq, rstd)
                if is_p:
                    P.op('dve', lambda e, t0=t0: e.tensor_copy(out=k2T[:64, 256 + t0:256 + t0 + 512], in_=x32[:64, :]), reads=RR,
                         writes=RR)
                    for j in range(4):
                        P.op('pe', lambda e, j=j: e.transpose(psum[3][:, j * 64:(j + 1) * 64], x32[:64, j * 128:(j + 1) * 128],
                                                              identf[:64, :64]), reads=RR + [rP], writes=[rps[3]])
                    P.op('act', lambda e: e.activation(out=tst[:, :256], in_=psum[3][:, :256], func=AF.Copy), reads=[rps[3]],
                         writes=[rTst])
                    P.dma('sp', dr['o_kr'][t0 - LS:t0 - LS + 512, :].rearrange("(j p) c -> p j c", p=128), V3(tst[:, :256], 4, 64),
                          reads=[rTst])
                else:
                    rope(k2T[:64, 256 + t0:256 + t0 + 512], x32[:64, :], t0, 512)
            P.barrier(release=False)
            pos[0] = mla_mark
            sets = []
            for i in range(2):
                sets.append(dict(x32=alloc(512), c32=alloc(512), sq=alloc(512, BF16), rstd=alloc(512), cos=alloc(512),
                                 sin=alloc(512), cqn=V3(alloc(KQ * 512, BF16), KQ, 512), qn=alloc(512, BF16),
                                 qr=alloc(512, BF16), R=P.res(f'set{i}'), Rc=P.res(f'setc{i}'), Rq=P.res(f'setq{i}'),
                                 Rr=P.res(f'setr{i}'), banks=(3 * i, 3 * i + 1, 3 * i + 2)))
            wqs = [V3(alloc(KQ * 192, BF16), KQ, 192) for _ in range(2)]
            rWq = [P.res('wq0'), P.res('wq1')]
            blocks = [(qb * 512, 512, 0, (256 + LS) // 128, True) for qb in range(LS // 512)]
            blocks += [(LS + s_ * 256, 256, (256 + LS + s_ * 256) // 128, 2, False) for s_ in range(NPS)]
            bi = 0
            for h in range(NH):
                wq = wqs[h % 2]
                P.dma('pool', wq, dr['mla_w_q_up'][:, h * 192:(h + 1) * 192].rearrange("(k p) c -> p k c", p=128),
                      writes=[rWq[h % 2]])
                for (t0, nq, kt0, nkt, rot) in blocks:
                    S_ = sets[bi % 2]
                    bi += 1
                    bA, bB, bC = S_['banks']
                    x32s, c32s, sqs, rstds, cqs = S_['x32'], S_['c32'], S_['sq'], S_['rstd'], S_['cqn']
                    RS = [S_['R']]
                    P.dma('sp', cqs[:, :, :nq], dr['s_cqn'][:, t0:t0 + nq].rearrange("(k p) t -> p k t", p=128),
                          writes=[S_['Rc']])
                    for k in range(KQ):
                        P.op('pe', lambda e: e.matmul(psum[bA][:, :nq], lhsT=wq[:, k, 0:128], rhs=cqs[:, k, :nq],
                                                      start=(k == 0), stop=(k == KQ - 1)), reads=[S_['Rc'], rWq[h % 2]],
                             writes=[rps[bA]], sig=(k == KQ - 1))
                    P.op('act', lambda e: e.activation(out=x32s[:, :nq], in_=psum[bA][:, :nq], func=AF.Copy), reads=[rps[bA]],
                         writes=RS)
                    fm_norm(S_['qn'][:, :nq], x32s[:, :nq], qng, 128, nq, RS, rps, bB, sqs, rstds, RO=[R])
                    P.op('dve', lambda e: e.tensor_copy(out=S_['qn'][:, 0:1], in_=S_['qn'][:, 0:1]), reads=RS, writes=[S_['Rq']])
                    P.dma('sp', dr['s_qn'][h * 128:(h + 1) * 128, t0:t0 + nq], S_['qn'][:, :nq], reads=[S_['Rq']])
                    for k in range(KQ):
                        P.op('pe', lambda e: e.matmul(psum[bA][:64, :nq], lhsT=wq[:, k, 128:192], rhs=cqs[:, k, :nq],
                                                      start=(k == 0), stop=(k == KQ - 1)), reads=[S_['Rc'], rWq[h % 2]],
                             writes=[rps[bA]], sig=(k == KQ - 1))
                    P.op('act', lambda e: e.activation(out=x32s[:64, :nq], in_=psum[bA][:64, :nq], func=AF.Copy),
                         reads=[rps[bA]], writes=RS)
                    if rot:
                        fm_norm(c32s[:64, :nq], x32s[:64, :nq], qrg[:64], 64, nq, RS, rps, bB, sqs, rstds, RO=[R])
                        cosb_, sinb_ = S_['cos'], S_['sin']
                        P.dma('sp', cosb_[:64, :nq], dr['ropecos'][:, t0:t0 + nq], writes=[S_['Rr']])
                        P.dma('sp', sinb_[:64, :nq], dr['ropesin'][:, t0:t0 + nq], writes=[S_['Rr']])
                        P.op('pe', lambda e: e.matmul(psum[bC][:64, :nq], lhsT=prot[:64, :64], rhs=c32s[:64, :nq], start=True,
                                                      stop=True), reads=RS + [R], writes=[rps[bC]])
                        P.op('dve', lambda e: e.tensor_tensor(out=sinb_[:64, :nq], in0=psum[bC][:64, :nq], in1=sinb_[:64, :nq],
                                                              op=ALU.mult), reads=[rps[bC]], writes=[S_['Rr']])
                        P.op('dve', lambda e: e.tensor_tensor(out=cosb_[:64, :nq], in0=c32s[:64, :nq], in1=cosb_[:64, :nq],
                                                              op=ALU.mult), reads=RS, writes=[S_['Rr']])
                        P.op('dve', lambda e: e.tensor_tensor(out=S_['qr'][:64, :nq], in0=cosb_[:64, :nq], in1=sinb_[:64, :nq],
                                                              op=ALU.add), reads=[S_['Rr']], writes=[S_['Rq']])
                    else:
                        fm_norm(S_['qr'][:64, :nq], x32s[:64, :nq], qrg[:64], 64, nq, RS, rps, bB, sqs, rstds, RO=[R])
                        P.op('dve', lambda e: e.tensor_copy(out=S_['qr'][:64, 0:1], in_=S_['qr'][:64, 0:1]), reads=RS,
                             writes=[S_['Rq']])
                    P.dma('sp', dr['s_qr'][h * 64:(h + 1) * 64, t0:t0 + nq], S_['qr'][:64, :nq], reads=[S_['Rq']])
            P.barrier(release=False)
            wkv = V3(alloc(4 * 256, BF16), 4, 256)
            kT = alloc(NK, BF16)
            vt = V3(alloc(NK, BF16), NK // 128, 128)
            gd = alloc(T, BF16)
            qnb = [alloc(512, BF16) for _ in range(2)]
            qrb = [alloc(512, BF16) for _ in range(2)]
            rQb = [P.res('qb0'), P.res('qb1')]
            pts = [alloc(512, BF16) for _ in range(3)]
            rPt = [P.res('pt0'), P.res('pt1'), P.res('pt2')]
            rec = alloc(512)
            o32 = alloc(512)
            mst = alloc(512, BF16)
            rMst = P.res('mst')
            rK = P.res('kv')
            rG = P.res('gd')
            rO3 = P.res('o32')
            bi = 0
            for h in range(NH):
                hs = slice(h * 128, (h + 1) * 128)
                P.dma('pool', wkv, dr['mla_w_kv_up'][:, h * 256:(h + 1) * 256].rearrange("(k p) c -> p k c", p=128), writes=[rK])
                P.dma('sp', gd, dr['s_gd'][hs, :], writes=[rG])
                for t0 in range(0, T, 512):
                    silu_ops(gd[:, t0:t0 + 512], gd[:, t0:t0 + 512], None, [rG])
                for ci_, c0 in enumerate(range(0, NK, 512)):
                    n = min(512, NK - c0)
                    S_ = sets[ci_ % 2]
                    pb_ = 5 + (ci_ % 2)
                    for k in range(4):
                        P.op('pe', lambda e: e.matmul(psum[pb_][:, :n], lhsT=wkv[:, k, 0:128], rhs=ckvT[:, k, c0:c0 + n],
                                                      start=(k == 0), stop=(k == 3)), reads=[R, rK], writes=[rps[pb_]],
                             sig=(k == 3))
                    P.op('act', lambda e: e.activation(out=S_['x32'][:, :n], in_=psum[pb_][:, :n], func=AF.Copy),
                         reads=[rps[pb_]], writes=[S_['R']])
                    fm_norm(kT[:, c0:c0 + n], S_['x32'][:, :n], kng, 128, n, [S_['R']], rps, 7, S_['sq'], S_['rstd'], RO=[R])
                    P.op('dve', lambda e: e.tensor_copy(out=kT[:, c0:c0 + 1], in_=kT[:, c0:c0 + 1]), reads=[S_['R']], writes=[rK])
                for n_ in range(NK // 128):
                    b = 5 + (n_ % 2)
                    for k in range(4):
                        P.op('pe', lambda e: e.matmul(psum[b][:, 0:128], lhsT=ckvT[:, k, n_ * 128:(n_ + 1) * 128],
                                                      rhs=wkv[:, k, 128:256], start=(k == 0), stop=(k == 3)), reads=[R, rK],
                             writes=[rps[b]], sig=(k == 3))
                    evac(vt[:, n_, :], psum[b][:, 0:128], [rps[b]], [rK])
                for (t0, nq, kt0, nkt, rot) in blocks:
                    q_ = bi % 2
                    bi += 1
                    P.dma('sp', qnb[q_][:, :nq], dr['s_qn'][hs, t0:t0 + nq], writes=[rQb[q_]])
                    P.dma('sp', qrb[q_][:64, :nq], dr['s_qr'][h * 64:(h + 1) * 64, t0:t0 + nq], writes=[rQb[q_]])
                    keys = [(kT[:, (kt0 + i) * 128:(kt0 + i + 1) * 128], k2T[:64, (kt0 + i) * 128:(kt0 + i + 1) * 128],
                             vt[:, kt0 + i, :], None) for i in range(nkt)]
                    attn_block(qnb[q_][:, :nq], qrb[q_][:64, :nq], keys, nq, o32[:, :nq], [rO3], rps, pts, rPt, rec,
                               RO=[R, rK, rQb[q_]])
                    P.op('dve', lambda e: e.tensor_tensor(out=mst[:, :nq], in0=o32[:, :nq], in1=gd[:, t0:t0 + nq], op=ALU.mult),
                         reads=[rO3, rG], writes=[rMst])
                    P.dma('sp', dr['s_mix'][BW + h * 128:BW + (h + 1) * 128, t0:t0 + nq], mst[:, :nq], reads=[rMst])

        if ON('in0'):
            phase_inproj(0, dr['w_in_ab'], cfg.AB, dr['x_all'])
        if ON('mixA'):
            phase_mixA()
        if ON('mixB'):
            phase_mixB()
        if ON('out0'):
            phase_outproj(0, dr['x_all'], dr['s_x1'])
        if ON('in1'):
            phase_inproj(1, dr['w_in_cd'], cfg.CD, dr['s_x1'])
        if ON('hgrn'):
            phase_hgrn()
        if ON('mla'):
            phase_mla()
        if ON('out1'):
            phase_outproj(1, dr['s_x1'], dr['y_all'])
        P.emit()
    return nc


def prep_rp(na_rpb):
    NH = na_rpb.shape[0]
    rp = np.zeros((NH, 23, 127), np.float32)
    rp[:, 4:19, 48:79] = na_rpb[:, ::-1, ::-1]
    return rp


_CACHE = {}


def make_in_maps(cfg, inputs, ncores):
    consts = host_consts(cfg)
    NPS, LS, D = cfg.NPS, cfg.LS, cfg.D
    f = lambda a: np.ascontiguousarray(np.asarray(a))
    shared = {
        'norm_w': f(inputs['norm_w']), 'w_ada': f(inputs['w_ada']), 'b_ada': f(inputs['b_ada']),
        'w_out': f(inputs['w_out']), 'w_in_ab': f(inputs['w_in_ab'][0]), 'na_q_norm': f(inputs['na_q_norm'][0]),
        'na_k_norm': f(inputs['na_k_norm'][0]), 'rp': prep_rp(np.asarray(inputs['na_rpb'][0])),
        'sg_norm': f(inputs['sg_norm'][0]), 'sg_w': f(inputs['sg_w'][0]), 'sg_b': f(inputs['sg_b'][0]).reshape(-1),
        'w_in_cd': f(inputs['w_in_cd'][0]), 'hgrn_lb': f(inputs['hgrn_lb']),
        'hgrn_out_norm': f(inputs['hgrn_out_norm'][0]), 'mla_q_a_norm': f(inputs['mla_q_a_norm'][0]),
        'mla_w_q_up': f(inputs['mla_w_q_up'][0]), 'mla_kv_a_norm': f(inputs['mla_kv_a_norm'][0]),
        'mla_w_kv_up': f(inputs['mla_w_kv_up'][0]), 'mla_q_norm': f(inputs['mla_q_norm'][0]),
        'mla_k_norm': f(inputs['mla_k_norm'][0]),
    }
    shared.update(consts)
    maps = []
    for c in range(ncores):
        m = dict(shared)
        xs = np.asarray(inputs['x_sample'][c])
        xp = np.asarray(inputs['x_prompt'][c * NPS:(c + 1) * NPS]).reshape(NPS * 256, D)
        m['x_all'] = np.ascontiguousarray(np.concatenate([xs, xp], axis=0))
        m['cond2'] = np.ascontiguousarray(np.stack([np.asarray(inputs['c'][c]), np.asarray(inputs['c_ctx'])], axis=0))
        m['c_na_k'] = f(inputs['cache_na_k'][c, 0]).reshape(256, -1)
        m['c_na_v'] = f(inputs['cache_na_v'][c, 0]).reshape(256, -1)
        m['st_hgrn'] = f(inputs['state_hgrn'][c, 0])
        m['c_ckv'] = f(inputs['cache_mla_ckv'][c, 0])
        m['c_kr'] = f(inputs['cache_mla_krope'][c, 0])
        maps.append(m)
    return maps


def assemble(cfg, results, ncores):
    NPS, LS, D, NH, BW = cfg.NPS, cfg.LS, cfg.D, cfg.NH, cfg.BW
    yp, ys, nk, nv, hg, ckv, kr = [], [], [], [], [], [], []
    for c in range(ncores):
        r = results[c]
        ya = np.asarray(r['y_all'])
        ys.append(ya[:LS][None])
        yp.append(ya[LS:].reshape(NPS, 256, D))
        nk.append(np.asarray(r['o_na_k']).reshape(NPS, 1, 256, NH, 128))
        nv.append(np.asarray(r['o_na_v']).reshape(NPS, 1, 256, NH, 128))
        hg.append(np.asarray(r['o_hgrn']).reshape(NPS, 1, 2, NH, 128, 128))
        ckv.append(np.asarray(r['o_ckv']).reshape(NPS, 1, 256, 512))
        kr.append(np.asarray(r['o_kr']).reshape(NPS, 1, 256, 64))
    cat = lambda l: np.ascontiguousarray(np.concatenate(l, axis=0).astype(np.float32))
    return (cat(yp), cat(ys), cat(nk), cat(nv), cat(hg), cat(ckv), cat(kr))


def kernel(**inputs):
    cfg = Cfg(D=4096, LS=4096, NPS=4)
    ncores = 8
    if 'nc' not in _CACHE:
        _CACHE['nc'] = build_program(cfg)
    nc = _CACHE['nc']
    maps = make_in_maps(cfg, inputs, ncores)
    res = run_bass_kernel_spmd(nc, maps, core_ids=list(range(ncores)))
    return assemble(cfg, res.results, ncores)
```

```python
import os
import numpy as np
import ml_dtypes
from contextlib import ExitStack
import concourse.bass as bass
import concourse.mybir as mybir
from concourse.bass_utils import run_bass_kernel_spmd

F32 = mybir.dt.float32
BF16 = mybir.dt.bfloat16
AF = mybir.ActivationFunctionType
ALU = mybir.AluOpType
AX = mybir.AxisListType

ENGS = ('pe', 'act', 'dve', 'pool', 'sp')
SEM_LIMIT = 8000
EPS = 1e-6
GK = 1.5957691216057308


class Rec:
    def __init__(self):
        self.call = None

    def __getattr__(self, name):
        def f(*a, **k):
            self.call = (name, a, k)
            return self
        return f


class Event:
    __slots__ = ('sv', 'eng')

    def __init__(self, eng=None):
        self.sv = None
        self.eng = eng


class Chan:
    def __init__(self, P):
        self.P = P
        self.sem = None
        self.count = 0
        self.last = None

    def bump(self, n, ev):
        if self.sem is None or self.count + n > SEM_LIMIT:
            self.sem = self.P.new_sem()
            self.count = 0
        self.count += n
        ev.sv = (self.sem, self.count)
        self.last = ev
        return ev


class Res:
    def __init__(self, P, name):
        self.P = P
        self.name = name
        self.last_write = None
        self.readers = {}
        self.chan = {}

    def dchan(self, eng):
        if eng not in self.chan:
            self.chan[eng] = self.P.get_dma_chan(eng)
        return self.chan[eng]


class Prog:
    def __init__(self, nc, stack):
        self.nc = nc
        self.stack = stack
        self.ops = {e: [] for e in ENGS}
        self.echan = {e: Chan(self) for e in ENGS}
        self.cur = {e: Event(e) for e in ENGS}
        self.dma_chans = []
        self.free_chans = {e: [] for e in ENGS}
        self.resources = []
        self.nsem = 0
        self.bar_seen = {}
        self.pending = {e: False for e in ENGS}

    def new_sem(self):
        self.nsem += 1
        return self.stack.enter_context(self.nc.semaphore(f"s{self.nsem}"))

    def get_dma_chan(self, eng):
        if self.free_chans[eng]:
            return self.free_chans[eng].pop()
        ch = Chan(self)
        self.dma_chans.append(ch)
        return ch

    def res(self, name):
        r = Res(self, name)
        self.resources.append(r)
        return r

    def _deps(self, eng, reads, writes):
        waits = []

        def add(ev):
            if ev is None:
                return
            if ev.sv is None:
                assert ev.eng == eng, (ev.eng, eng)
                return
            if ev.eng == eng and eng == 'pe':
                return
            waits.append(ev)

        for r in reads:
            add(r.last_write)
        for w in writes:
            add(w.last_write)
            for ev in w.readers.values():
                add(ev)
        return waits

    def _commit(self, ev, key, reads, writes):
        for w in writes:
            w.last_write = ev
            w.readers = {}
        for r in reads:
            r.readers[key] = ev

    def op(self, eng, fn, reads=(), writes=(), sig=True):
        waits = self._deps(eng, reads, writes)
        ev = self.cur[eng]
        inc = None
        if sig:
            self.echan[eng].bump(1, ev)
            inc = (ev.sv[0], 1)
            self.cur[eng] = Event(eng)
        self.pending[eng] = not sig
        rec = Rec()
        fn(rec)
        assert rec.call is not None
        self.ops[eng].append((rec.call, waits, inc))
        self._commit(ev, eng, reads, writes)
        return ev

    def dma(self, eng, out, in_, reads=(), writes=(), cres=None, **kw):
        waits = self._deps(eng, reads, writes)
        if cres is None:
            cres = writes[0] if writes else reads[0]
        ch = cres.dchan(eng)
        ev = Event('dma')
        ch.bump(16, ev)
        self.ops[eng].append((('dma_start', (), dict(out=out, in_=in_, **kw)), waits, (ev.sv[0], 16)))
        self._commit(ev, ch, reads, writes)
        return ev

    def barrier(self, release=True):
        evs = []
        for e in ENGS:
            if self.pending[e]:
                self.op(e, lambda eng: eng.nop())
            ch = self.echan[e]
            if ch.last is not None:
                evs.append(ch.last)
        for ch in self.dma_chans:
            if ch.last is not None:
                evs.append(ch.last)
        evs = [ev for ev in evs if self.bar_seen.get(id(ev.sv[0]), 0) < ev.sv[1]]
        for ev in evs:
            self.bar_seen[id(ev.sv[0])] = ev.sv[1]
        for e in ENGS:
            self.ops[e].append((None, list(evs), None))
        for r in self.resources:
            r.last_write = None
            r.readers = {}
            if release:
                for e_, ch_ in r.chan.items():
                    self.free_chans[e_].append(ch_)
                r.chan = {}
        if release:
            self.resources = []

    def emit(self):
        nc = self.nc
        self.barrier()
        P = self

        def run(eng_name, eng):
            known = {}
            for fn, waits, inc in P.ops[eng_name]:
                for ev in waits:
                    sem, val = ev.sv
                    if known.get(id(sem), 0) < val:
                        eng.wait_ge(sem, val)
                        known[id(sem)] = val
                if fn is not None:
                    inst = getattr(eng, fn[0])(*fn[1], **fn[2])
                    if inc is not None:
                        inst.then_inc(inc[0], inc[1])

        with nc.Block() as block:
            @block.tensor
            def _(e):
                run('pe', e)

            @block.scalar
            def _(e):
                run('act', e)

            @block.vector
            def _(e):
                run('dve', e)

            @block.gpsimd
            def _(e):
                run('pool', e)

            @block.sync
            def _(e):
                run('sp', e)


class Cfg:
    def __init__(self, D=4096, LS=4096, NPS=4):
        self.D = D
        self.LS = LS
        self.NPS = NPS
        self.SEQ = 256
        self.PAST = 256
        self.KD = D // 128
        self.BW = D // 2
        self.NH = self.BW // 128
        self.QL = D // 4
        self.KQ = self.QL // 128
        self.KVL = 512
        self.ROPE = 64
        self.T = LS + NPS * 256
        self.ROWS = LS // 64
        self.NQB = LS // 512
        self.AB = [('qa', 'fm', self.BW), ('ka', 'fm', self.BW), ('va', 'tm', self.BW), ('ga', 'fm', self.BW),
                   ('ub', 'fm', self.BW), ('vb', 'tm', self.BW), ('gb', 'fm', self.BW)]
        self.CD = [('qc', 'fm', self.BW), ('ffw', 'tm32', self.BW), ('fbw', 'tm32', self.BW), ('ic', 'tm', self.BW),
                   ('gc', 'fm', self.BW), ('cq', 'fm', self.QL), ('ckv', 'fm', 512), ('kr', 'fm', 64),
                   ('gd', 'fm', self.BW)]
        self.AB_IN = sum(s[2] for s in self.AB)
        self.CD_IN = sum(s[2] for s in self.CD)
        self.build_patterns()

    def build_patterns(self):
        ROWS = self.ROWS
        cols = np.arange(64)
        cstart = np.clip(cols - 8, 0, 48)
        colmask = (cols[None, :] >= cstart[:, None]) & (cols[None, :] < cstart[:, None] + 16)
        pats = {}
        self.pat_list = []
        self.qb_keys = []
        for qb in range(self.NQB):
            r = 8 * qb + np.arange(8)
            rs = np.clip(r - 4, 0, ROWS - 8)
            kt0 = rs.min() // 2
            kt1 = (rs.max() + 7) // 2
            lst = []
            for kt in range(kt0, kt1 + 1):
                m = np.zeros((2, 64, 8, 64), np.float32)
                for kr in range(2):
                    rp = 2 * kt + kr
                    for qr in range(8):
                        if rs[qr] <= rp < rs[qr] + 8:
                            m[kr, :, qr, :] = colmask.T
                m = m.reshape(128, 512)
                delta = 2 * kt - 8 * qb
                key = (m.tobytes(), delta)
                if key not in pats:
                    pats[key] = len(self.pat_list)
                    self.pat_list.append((m, delta))
                lst.append((kt, pats[key]))
            self.qb_keys.append(lst)
        self.NPAT = len(self.pat_list)


def host_consts(cfg):
    c = {}
    c['identb'] = np.eye(128, dtype=np.float32).astype(ml_dtypes.bfloat16)
    c['identf'] = np.eye(128, dtype=np.float32)
    c['mask01'] = np.stack([m for m, _ in cfg.pat_list], axis=1).astype(ml_dtypes.bfloat16)
    idx = np.arange(128)
    same = (idx[:, None] // 32) == (idx[None, :] // 32)
    hm = np.zeros((2, 128, 512), np.float32)
    for d in range(2):
        if d == 0:
            Mi = same & (idx[:, None] <= idx[None, :])
            Mr = same & (idx[:, None] > idx[None, :])
        else:
            Mi = same & (idx[:, None] >= idx[None, :])
            Mr = same & (idx[:, None] < idx[None, :])
        Mi = Mi.astype(np.float32)
        mid = (idx // 32) * 32 + 15
        Mc = Mi - Mi[:, mid]
        hm[d, :, 0:128] = Mi
        hm[d, :, 128:256] = Mc
        hm[d, :, 256:384] = Mr.astype(np.float32)
    c['hmat'] = np.ascontiguousarray(hm.transpose(1, 0, 2))
    t = np.arange(cfg.LS)
    inv = (10000.0 ** (-np.arange(16, dtype=np.float32) / 16)).astype(np.float32)
    cos = np.zeros((64, cfg.LS), np.float32)
    sin = np.zeros((64, cfg.LS), np.float32)
    prot = np.zeros((64, 64), np.float32)
    for j in range(64):
        b, jj = j // 32, j % 32
        i = jj % 16
        pos = (t // 64 if b == 0 else t % 64).astype(np.float32)
        ang = pos * inv[i]
        cos[j] = np.cos(ang)
        sin[j] = np.sin(ang)
        if jj < 16:
            prot[j + 16, j] = -1.0
        else:
            prot[j - 16, j] = 1.0
    c['ropecos'] = cos
    c['ropesin'] = sin
    c['prot'] = prot
    return c


def build_program(cfg, debug=()):
    D, KD, BW, NH, T, LS, NPS, QL, KQ = cfg.D, cfg.KD, cfg.BW, cfg.NH, cfg.T, cfg.LS, cfg.NPS, cfg.QL, cfg.KQ
    NPT = NPS * 256
    nc = bass.Bass("TRN2", target_bir_lowering=False)
    dr = {}

    def din(name, shape, dt=F32):
        dr[name] = nc.dram_tensor(name, list(shape), dt, kind="ExternalInput").ap()

    def dout(name, shape, dt=F32):
        dr[name] = nc.dram_tensor(name, list(shape), dt, kind="ExternalOutput").ap()

    def dscr(name, shape, dt):
        kind = "ExternalOutput" if name in debug else "Internal"
        dr[name] = nc.dram_tensor(name, list(shape), dt, kind=kind).ap()

    din('x_all', [T, D])
    din('cond2', [2, D])
    din('c_na_k', [256, BW])
    din('c_na_v', [256, BW])
    din('st_hgrn', [2, NH, 128, 128])
    din('c_ckv', [256, 512])
    din('c_kr', [256, 64])
    din('norm_w', [2, D])
    din('w_ada', [2, D, 3 * D])
    din('b_ada', [2, 3 * D])
    din('w_out', [2, D, D])
    din('w_in_ab', [D, cfg.AB_IN])
    din('na_q_norm', [128])
    din('na_k_norm', [128])
    din('rp', [NH, 23, 127])
    din('sg_norm', [BW])
    din('sg_w', [NH, 128, 128])
    din('sg_b', [NH * 128])
    din('w_in_cd', [D, cfg.CD_IN])
    din('hgrn_lb', [2, 2, BW])
    din('hgrn_out_norm', [128])
    din('mla_q_a_norm', [QL])
    din('mla_w_q_up', [QL, NH * 192])
    din('mla_kv_a_norm', [512])
    din('mla_w_kv_up', [512, NH * 256])
    din('mla_q_norm', [192])
    din('mla_k_norm', [192])
    din('identb', [128, 128], BF16)
    din('identf', [128, 128])
    din('mask01', [128, cfg.NPAT, 512], BF16)
    din('hmat', [128, 2, 512])
    din('ropecos', [64, LS])
    din('ropesin', [64, LS])
    din('prot', [64, 64])
    dout('y_all', [T, D])
    dout('o_na_k', [NPT, BW])
    dout('o_na_v', [NPT, BW])
    dout('o_hgrn', [NPS, 2, NH, 128, 128])
    dout('o_ckv', [NPT, 512])
    dout('o_kr', [NPT, 64])
    for (nm, kind, w) in cfg.AB + cfg.CD:
        if kind == 'fm':
            dscr('s_' + nm, [w, T], BF16)
        elif kind == 'tm':
            dscr('s_' + nm, [T, w], BF16)
        else:
            dscr('s_' + nm, [T, w], F32)
    dscr('s_mix', [D, T], BF16)
    dscr('s_x1', [T, D], F32)
    dscr('s_gate', [2, 2, D], F32)
    dscr('s_tf', [NH, 23, 64, 64], F32)
    dscr('s_of', [BW, T], F32)
    dscr('s_cqn', [QL, T], BF16)
    dscr('s_qn', [NH * 128, T], BF16)
    dscr('s_qr', [NH * 64, T], BF16)

    st = ExitStack()
    with st:
        P = Prog(nc, st)
        ARENA = 46 * 1024
        arena = st.enter_context(nc.sbuf_tensor("arena", [128, ARENA], F32))
        psum = [st.enter_context(nc.psum_tensor(f"ps{i}", [128, 512], F32)) for i in range(8)]
        PERS = 1024
        pos = [0]

        def alloc(n, dt=F32, np_=128):
            words = (n + 1) // 2 if dt == BF16 else n
            words = (words + 7) // 8 * 8
            a = arena[:, pos[0]:pos[0] + words]
            pos[0] += words
            assert pos[0] <= ARENA, pos[0]
            if dt == BF16:
                a = a.bitcast(BF16)[:, :n]
            else:
                a = a[:, :n]
            return a

        identb = alloc(128, BF16)
        identf = alloc(128)
        onesb = alloc(128, BF16)
        AT = [alloc(KD * 2) for _ in range(2)]
        SH = [alloc(KD * 2) for _ in range(2)]
        epsc = alloc(1)
        assert pos[0] <= PERS
        rP = P.res('pers')
        P.dma('sp', identb, dr['identb'], writes=[rP])
        P.dma('sp', identf, dr['identf'], writes=[rP])
        P.op('dve', lambda e: e.memset(onesb, 1.0), writes=[rP])
        P.op('dve', lambda e: e.memset(epsc, EPS), writes=[rP])
        P.barrier(release=False)
        P.resources = []

        bank_rr = [0]

        def new_phase():
            P.barrier()
            pos[0] = PERS
            rps = [P.res(f'ps{i}') for i in range(8)]
            return rps

        def V3(ap, a, b):
            return ap.rearrange("p (a b) -> p a b", a=a, b=b)

        cnt = [0]

        def evac(out, in_, reads, writes, eng=None):
            cnt[0] += 1
            if eng is None:
                eng = 'act' if cnt[0] % 2 else 'dve'
            if eng == 'act':
                P.op('act', lambda e: e.activation(out=out, in_=in_, func=AF.Copy), reads=reads, writes=writes)
            else:
                P.op('dve', lambda e: e.tensor_copy(out=out, in_=in_), reads=reads, writes=writes)

        def rstd_chain(dst, src, inv_n, reads, writes):
            P.op('dve', lambda e: e.tensor_scalar(out=dst, in0=src, scalar1=inv_n, scalar2=EPS, op0=ALU.mult, op1=ALU.add),
                 reads=reads, writes=writes)
            P.op('act', lambda e: e.activation(out=dst, in_=dst, func=AF.Ln), reads=writes, writes=writes)
            P.op('act', lambda e: e.activation(out=dst, in_=dst, func=AF.Exp, scale=-0.5), reads=writes, writes=writes)

        def sigmoid_ops(dst, src, scale, R):
            P.op('act', lambda e: e.activation(out=dst, in_=src, func=AF.Sigmoid, scale=scale), reads=R, writes=R)

        def silu_ops(dst, src, tmp, R):
            P.op('act', lambda e: e.activation(out=dst, in_=src, func=AF.Silu), reads=R, writes=R)

        def gelu_ops(dst, src, tmp, tmp2, R):
            P.op('act', lambda e: e.activation(out=tmp, in_=src, func=AF.Square), reads=R, writes=R)
            P.op('dve', lambda e: e.tensor_scalar(out=tmp, in0=tmp, scalar1=0.044715, scalar2=1.0, op0=ALU.mult, op1=ALU.add),
                 reads=R, writes=R)
            P.op('dve', lambda e: e.tensor_tensor(out=tmp, in0=tmp, in1=src, op=ALU.mult), reads=R, writes=R)
            sigmoid_ops(tmp2, tmp, GK, R)
            P.op('dve', lambda e: e.tensor_tensor(out=dst, in0=src, in1=tmp2, op=ALU.mult), reads=R, writes=R)

        stages = os.environ.get("MK_STAGES", "all").split(',')
        ON = lambda nm: stages == ['all'] or nm in stages
        for l in (range(2) if ON('mod') else []):
            rps = new_phase()
            R = P.res('m')
            cT = alloc(KD * 2)
            tmpc = alloc(KD * 2)
            sT = alloc(KD * 2, BF16)
            bT = alloc(3 * KD)
            nwT = alloc(KD)
            modT = alloc(3 * KD * 2)
            wts = [alloc(KD * 512, BF16) for _ in range(2)]
            rW = [P.res('w0'), P.res('w1')]
            for c in range(2):
                P.dma('sp', V3(cT, 2, KD)[:, c, :], dr['cond2'][c].rearrange("(k p) -> p k", p=128), writes=[R],
                      allow_slow_non_contiguous=True)
            P.dma('sp', bT, dr['b_ada'][l].rearrange("(j p) -> p j", p=128), writes=[R], allow_slow_non_contiguous=True)
            P.dma('sp', nwT, dr['norm_w'][l].rearrange("(j p) -> p j", p=128), writes=[R], allow_slow_non_contiguous=True)
            silu_ops(cT, cT, tmpc, [R])
            P.op('dve', lambda e: e.tensor_copy(out=V3(sT, KD, 2), in_=V3(cT, 2, KD).rearrange("p c k -> p k c")), reads=[R],
                 writes=[R])
            ncb = 3 * D // 512
            for cb in range(ncb):
                s = cb % 2
                wt = V3(wts[s], KD, 512)
                P.dma('pool', wt, dr['w_ada'][l][:, cb * 512:(cb + 1) * 512].rearrange("(k p) c -> p k c", p=128),
                      writes=[rW[s]])
                b = cb % 8
                for j in range(4):
                    for k in range(KD):
                        P.op('pe', lambda e, b=b, j=j, k=k, wt=wt: e.matmul(
                            psum[b][:, 2 * j:2 * j + 2], lhsT=wt[:, k, j * 128:(j + 1) * 128], rhs=V3(sT, KD, 2)[:, k, :],
                            start=(k == 0), stop=(k == KD - 1)), reads=[rW[s], R], writes=[rps[b]],
                            sig=(j == 3 and k == KD - 1))
                P.op('dve', lambda e, b=b, cb=cb: e.tensor_tensor(
                    out=V3(modT, 3 * KD, 2)[:, cb * 4:(cb + 1) * 4, :], in0=V3(psum[b][:, 0:8], 4, 2),
                    in1=bT[:, cb * 4:(cb + 1) * 4].unsqueeze(2).broadcast_to([128, 4, 2]), op=ALU.add),
                    reads=[rps[b], R], writes=[R])
            m3 = V3(modT, 3 * KD, 2)
            P.op('dve', lambda e: e.tensor_copy(out=V3(SH[l], KD, 2), in_=m3[:, 0:KD, :]), reads=[R], writes=[rP])
            P.op('dve', lambda e: e.tensor_scalar_add(out=V3(AT[l], KD, 2), in0=m3[:, KD:2 * KD, :], scalar1=1.0),
                 reads=[R], writes=[rP])
            P.op('dve', lambda e: e.tensor_tensor(out=V3(AT[l], KD, 2), in0=V3(AT[l], KD, 2),
                                                  in1=nwT.unsqueeze(2).broadcast_to([128, KD, 2]), op=ALU.mult),
                 reads=[R, rP], writes=[rP])
            gtmp = alloc(2 * KD)
            P.op('dve', lambda e: e.tensor_copy(out=V3(gtmp, 2, KD), in_=m3[:, 2 * KD:3 * KD, :].rearrange("p k c -> p c k")),
                 reads=[R], writes=[R])
            for c in range(2):
                P.dma('sp', dr['s_gate'][l][c].rearrange("(k p) -> p k", p=128), V3(gtmp, 2, KD)[:, c, :], reads=[R],
                      allow_slow_non_contiguous=True)

        def phase_inproj(l, W, segs, xsrc):
            rps = new_phase()
            TB = min(1024, LS)
            hT = alloc(KD * TB, BF16)
            hT3 = V3(hT, KD, TB)
            wts = [V3(alloc(KD * 512, BF16), KD, 512) for _ in range(2)]
            rW = [P.res('w0'), P.res('w1')]
            xts = [alloc(D) for _ in range(2)]
            rX = [P.res('x0'), P.res('x1')]
            xn = alloc(D, BF16)
            rXn = P.res('xn')
            ss = alloc(1)
            rstd = alloc(1)
            rS = P.res('ss')
            stg = [alloc(512) for _ in range(4)]
            rStg = [P.res(f'stg{i}') for i in range(4)]
            rH = P.res('hT')
            sc = [0]
            wc = [0]
            bc = [0]
            SK = os.environ.get("MK_SKIP", "")
            for tb0 in range(0, T, TB):
                for ti in range(TB // 128 if 'L' not in SK else 0):
                    tok0 = tb0 + ti * 128
                    cond = 0 if tok0 < LS else 1
                    s = ti % 2
                    P.dma('sp', xts[s], xsrc[tok0:tok0 + 128, :], writes=[rX[s]])
                    if 'A' in SK:
                        continue
                    P.op('act', lambda e, s=s: e.activation(out=xn, in_=xts[s], func=AF.Square, accum_out=ss),
                         reads=[rX[s]], writes=[rXn, rS])
                    rstd_chain(rstd, ss, 1.0 / D, [rS], [rS])
                    P.op('act', lambda e, s=s: e.activation(out=xn, in_=xts[s], func=AF.Copy, scale=rstd),
                         reads=[rX[s], rS], writes=[rXn])
                    for kg in range(0, KD if 'T' not in SK else 0, 8):
                        nk = min(8, KD - kg)
                        b = bc[0] % 8
                        bc[0] += 1
                        pb = psum[b][:].bitcast(BF16)
                        for j in range(nk):
                            P.op('pe', lambda e, pb=pb, j=j, kg=kg: e.transpose(
                                pb[:, j * 128:(j + 1) * 128], xn[:, (kg + j) * 128:(kg + j + 1) * 128], identb),
                                reads=[rXn, rP], writes=[rps[b]], sig=(j == nk - 1))
                        for j in range(nk):
                            k = kg + j
                            P.op('dve', lambda e, pb=pb, j=j, k=k, ti=ti, cond=cond: e.tensor_scalar(
                                out=hT3[:, k, ti * 128:(ti + 1) * 128], in0=pb[:, j * 128:(j + 1) * 128],
                                scalar1=AT[l][:, 2 * k + cond:2 * k + cond + 1],
                                scalar2=SH[l][:, 2 * k + cond:2 * k + cond + 1], op0=ALU.mult, op1=ALU.add),
                                reads=[rps[b], rP], writes=[rH])
                off = 0
                for (nm, kind, w) in (segs if 'W' not in SK else []):
                    if (os.environ.get("MK_SEG") and kind != os.environ.get("MK_SEG")) or (os.environ.get("MK_ONLY") and nm != os.environ.get("MK_ONLY")):
                        off += w
                        continue
                    for c0 in range(0, w, 512):
                        cw = min(512, w - c0)
                        s = wc[0] % 2
                        wc[0] += 1
                        wt = wts[s]
                        P.dma('pool', wt[:, :, :cw],
                              W[:, off + c0:off + c0 + cw].rearrange("(k p) c -> p k c", p=128), writes=[rW[s]])
                        if kind == 'fm':
                            for j0 in range(0, cw, 128):
                                m = min(128, cw - j0)
                                for t0 in range(0, TB, 512):
                                    b = bc[0] % 8
                                    bc[0] += 1
                                    for k in range(KD):
                                        P.op('pe', lambda e, b=b, m=m, k=k, j0=j0, t0=t0, wt=wt: e.matmul(
                                            psum[b][:m, :], lhsT=wt[:, k, j0:j0 + m], rhs=hT3[:, k, t0:t0 + 512],
                                            start=(k == 0), stop=(k == KD - 1)), reads=[rW[s], rH], writes=[rps[b]],
                                            sig=(k == KD - 1))
                                    q = sc[0] % 4
                                    sc[0] += 1
                                    so = stg[q].bitcast(BF16)[:m, 0:512]
                                    evac(so, psum[b][:m, :], [rps[b]], [rStg[q]])
                                    P.dma('sp', dr['s_' + nm][c0 + j0:c0 + j0 + m, tb0 + t0:tb0 + t0 + 512], so,
                                          reads=[rStg[q]])
                        else:
                            for ti in range(TB // 128):
                                tok0 = tb0 + ti * 128
                                b = bc[0] % 8
                                bc[0] += 1
                                for k in range(KD):
                                    P.op('pe', lambda e, b=b, k=k, ti=ti, wt=wt, cw=cw: e.matmul(
                                        psum[b][:, :cw], lhsT=hT3[:, k, ti * 128:(ti + 1) * 128], rhs=wt[:, k, :cw],
                                        start=(k == 0), stop=(k == KD - 1)), reads=[rW[s], rH], writes=[rps[b]],
                                        sig=(k == KD - 1))
                                q = sc[0] % 4
                                sc[0] += 1
                                if nm == 'va' and tok0 >= LS:
                                    s32 = stg[q][:, 0:cw]
                                    evac(s32, psum[b][:, :cw], [rps[b]], [rStg[q]])
                                    P.dma('sp', dr['o_na_v'][tok0 - LS:tok0 - LS + 128, c0:c0 + cw], s32, reads=[rStg[q]])
                                    q2 = sc[0] % 4
                                    sc[0] += 1
                                    so = stg[q2].bitcast(BF16)[:, 0:cw]
                                    P.op('dve', lambda e: e.tensor_copy(out=so, in_=s32), reads=[rStg[q]], writes=[rStg[q2]])
                                    P.dma('sp', dr['s_' + nm][tok0:tok0 + 128, c0:c0 + cw], so, reads=[rStg[q2]])
                                    continue
                                if kind == 'tm':
                                    so = stg[q].bitcast(BF16)[:, 0:cw]
                                else:
                                    so = stg[q][:, 0:cw]
                                evac(so, psum[b][:, :cw], [rps[b]], [rStg[q]])
                                P.dma('sp', dr['s_' + nm][tok0:tok0 + 128, c0:c0 + cw], so, reads=[rStg[q]])
                    off += w

        def phase_outproj(l, xsrc, ydst):
            rps = new_phase()
            TB = min(1024, LS)
            mixT = V3(alloc(KD * TB, BF16), KD, TB)
            rM = P.res('mix')
            wts = [V3(alloc(KD * 512, BF16), KD, 512) for _ in range(2)]
            rW = [P.res('w0'), P.res('w1')]
            gbc = [alloc(D) for _ in range(2)]
            rG = P.res('g')
            xb = [alloc(512) for _ in range(4)]
            rXb = [P.res(f'xb{i}') for i in range(4)]
            tmps = [alloc(512) for _ in range(2)]
            rTmp = [P.res('t0'), P.res('t1')]
            for c in range(2):
                P.dma('sp', gbc[c], dr['s_gate'][l][c:c + 1, :].partition_broadcast(128), writes=[rG])
            wc = 0
            bc = 0
            xc = 0
            for tb0 in range(0, T, TB):
                P.dma('sp', mixT, dr['s_mix'][:, tb0:tb0 + TB].rearrange("(k p) t -> p k t", p=128), writes=[rM])
                for c0 in range(0, D, 512):
                    s = wc % 2
                    wc += 1
                    wt = wts[s]
                    P.dma('pool', wt, dr['w_out'][l][:, c0:c0 + 512].rearrange("(k p) c -> p k c", p=128), writes=[rW[s]])
                    for ti in range(TB // 128):
                        tok0 = tb0 + ti * 128
                        cond = 0 if tok0 < LS else 1
                        b = bc % 8
                        bc += 1
                        q = xc % 4
                        xc += 1
                        P.dma('sp', xb[q], xsrc[tok0:tok0 + 128, c0:c0 + 512], writes=[rXb[q]])
                        for k in range(KD):
                            P.op('pe', lambda e, b=b, k=k, ti=ti, wt=wt: e.matmul(
                                psum[b][:], lhsT=mixT[:, k, ti * 128:(ti + 1) * 128], rhs=wt[:, k, :],
                                start=(k == 0), stop=(k == KD - 1)), reads=[rW[s], rM], writes=[rps[b]], sig=(k == KD - 1))
                        P.op('dve', lambda e: e.tensor_tensor(out=tmps[q % 2], in0=psum[b][:], in1=gbc[cond][:, c0:c0 + 512],
                                                              op=ALU.mult), reads=[rps[b], rG], writes=[rTmp[q % 2]])
                        P.op('dve', lambda e: e.tensor_tensor(out=xb[q], in0=tmps[q % 2], in1=xb[q], op=ALU.add),
                             reads=[rTmp[q % 2]], writes=[rXb[q]])
                        P.dma('sp', ydst[tok0:tok0 + 128, c0:c0 + 512], xb[q], reads=[rXb[q]])

        def fm_norm(dst, src, gain, np_, n, R, rps, b, sq, rstd, extra_dst=None, RO=None, WX=None):
            RO = list(RO) if RO is not None else []
            P.op('act', lambda e: e.activation(out=sq[:np_, :n], in_=src, func=AF.Square), reads=R + RO, writes=R)
            P.op('pe', lambda e: e.matmul(psum[b][:np_, :n], lhsT=onesb[:np_, :np_], rhs=sq[:np_, :n], start=True, stop=True),
                 reads=R + [rP], writes=[rps[b]])
            rstd_chain(rstd[:np_, :n], psum[b][:np_, :n], 1.0 / np_, [rps[b]] + R, R)
            WX = list(WX) if WX is not None else []
            P.op('dve', lambda e: e.scalar_tensor_tensor(out=dst, in0=src, scalar=gain, in1=rstd[:np_, :n],
                                                         op0=ALU.mult, op1=ALU.mult), reads=R + RO + [rP], writes=R + WX)
            if extra_dst is not None:
                P.op('dve', lambda e: e.scalar_tensor_tensor(out=extra_dst, in0=src, scalar=gain, in1=rstd[:np_, :n],
                                                             op0=ALU.mult, op1=ALU.mult), reads=R + RO + [rP], writes=R)

        def attn_block(qT, q2T, keys, nq, o32, R, rps, pts, rPt, rec, RO=None, acc=(3, 4)):
            n = len(keys)
            RD = R + (list(RO) if RO is not None else [])
            NBF = 3
            LA = 2

            def score(i):
                kT, k2T, v, mask = keys[i]
                sb = i % NBF
                P.op('pe', lambda e: e.matmul(psum[sb][:, :nq], lhsT=kT, rhs=qT, start=True, stop=(k2T is None)),
                     reads=RD, writes=[rps[sb]], sig=(k2T is None))
                if k2T is not None:
                    P.op('pe', lambda e: e.matmul(psum[sb][:, :nq], lhsT=k2T, rhs=q2T, start=False, stop=True),
                         reads=RD, writes=[rps[sb]])

            for i in range(min(LA, n)):
                score(i)
            for i in range(n):
                kT, k2T, v, mask = keys[i]
                sb = i % NBF
                pt = pts[sb]
                if i + LA < n:
                    score(i + LA)
                P.op('act', lambda e: e.activation(out=pt[:, :nq], in_=psum[sb][:, :nq], func=AF.Exp),
                     reads=[rps[sb]], writes=[rPt[sb]])
                if mask is not None:
                    P.op('dve', lambda e: e.tensor_tensor(out=pt[:, :nq], in0=pt[:, :nq], in1=mask, op=ALU.mult),
                         reads=RD + [rPt[sb]], writes=[rPt[sb]])
                P.op('pe', lambda e: e.matmul(psum[acc[0]][:, :nq], lhsT=v, rhs=pt[:, :nq], start=(i == 0), stop=(i == n - 1)),
                     reads=RD + [rPt[sb]], writes=[rps[acc[0]]], sig=False)
                P.op('pe', lambda e: e.matmul(psum[acc[1]][:, :nq], lhsT=onesb, rhs=pt[:, :nq], start=(i == 0), stop=(i == n - 1)),
                     reads=[rPt[sb], rP], writes=[rps[acc[1]]])
            P.op('dve', lambda e: e.reciprocal(out=rec[:, :nq], in_=psum[acc[1]][:, :nq]), reads=[rps[acc[1]]], writes=R)
            P.op('dve', lambda e: e.tensor_tensor(out=o32, in0=psum[acc[0]][:, :nq], in1=rec[:, :nq], op=ALU.mult),
                 reads=[rps[acc[0]], rps[acc[1]]] + R, writes=R)

        def phase_mixA():
            rps = new_phase()
            R = P.res('a')
            NP = cfg.NPAT
            mask01 = V3(alloc(NP * 512, BF16), NP, 512)
            P.dma('sp', mask01, dr['mask01'], writes=[R])
            qg = alloc(1)
            kg = alloc(1)
            P.dma('sp', qg, dr['na_q_norm'].rearrange("(p o) -> p o", o=1), writes=[R])
            P.dma('sp', kg, dr['na_k_norm'].rearrange("(p o) -> p o", o=1), writes=[R])
            P.op('dve', lambda e: e.tensor_scalar_mul(out=qg, in0=qg, scalar1=128.0 ** -0.5), reads=[R], writes=[R])
            rTF = P.res('tf')
            for ck in range(64):
                P.dma('sp', dr['s_tf'][:, :, ck, :], dr['rp'][:, :, 63 - ck:63 - ck + 64], writes=[rTF])
            qraw = alloc(T, BF16)
            kraw = alloc(T, BF16)
            qn = alloc(T, BF16)
            kn = alloc(T, BF16)
            sg = alloc(T, BF16)
            vt = V3(alloc(T, BF16), T // 128, 128)
            kc32 = V3(alloc(256, BF16), 2, 128)
            kcT = alloc(256, BF16)
            vc = V3(alloc(256, BF16), 2, 128)
            stage32 = V3(alloc(NP * 512), NP, 512)
            EB = V3(alloc(NP * 512, BF16), NP, 512)
            nsets = [dict(sq=alloc(512, BF16), rstd=alloc(512), knf=alloc(512), R=P.res(f'ns{i}'), bank=5 + i) for i in range(2)]
            rQK = P.res('qk')
            tmp = alloc(512)
            pts = [alloc(512, BF16) for _ in range(3)]
            rPt = [P.res('pt0'), P.res('pt1'), P.res('pt2')]
            recs = [alloc(512) for _ in range(2)]
            o32s = [alloc(512) for _ in range(2)]
            msts = [alloc(512, BF16) for _ in range(2)]
            rMst = [P.res('mst0'), P.res('mst1')]
            rO3 = [P.res('o320'), P.res('o321')]
            tst = alloc(128)
            rTst = P.res('tst')
            rL = P.res('loads')
            rE = P.res('eb')
            for h in range(NH):
                hs = slice(h * 128, (h + 1) * 128)
                P.dma('sp', qraw, dr['s_qa'][hs, :], writes=[rL])
                P.dma('sp', kraw, dr['s_ka'][hs, :], writes=[rL])
                P.dma('sp', sg, dr['s_ga'][hs, :], writes=[rL])
                P.dma('sp', vt, dr['s_va'][:, hs].rearrange("(n p) c -> p n c", p=128), writes=[rL])
                P.dma('pool', kc32, dr['c_na_k'][:, hs].rearrange("(n p) c -> p n c", p=128), writes=[rL])
                P.dma('pool', vc, dr['c_na_v'][:, hs].rearrange("(n p) c -> p n c", p=128), writes=[rL])
                for n_ in range(2):
                    pb = psum[7][:].bitcast(BF16)
                    P.op('pe', lambda e, n_=n_, pb=pb: e.transpose(pb[:, n_ * 128:(n_ + 1) * 128], kc32[:, n_, :], identb),
                         reads=[rL, rP], writes=[rps[7]])
                P.op('dve', lambda e: e.tensor_copy(out=kcT, in_=psum[7][:].bitcast(BF16)[:, 0:256]), reads=[rps[7]],
                     writes=[rL])
                for t0 in range(0, T, 512):
                    silu_ops(sg[:, t0:t0 + 512], sg[:, t0:t0 + 512], tmp, [rL, R])
                for bi_, t0 in enumerate(range(0, T, 512)):
                    NS = nsets[bi_ % 2]
                    fm_norm(qn[:, t0:t0 + 512], qraw[:, t0:t0 + 512], qg, 128, 512, [NS['R']], rps, NS['bank'], NS['sq'],
                            NS['rstd'], RO=[rL, R], WX=[rQK])
                    fm_norm(kn[:, t0:t0 + 512], kraw[:, t0:t0 + 512], kg, 128, 512, [NS['R']], rps, NS['bank'], NS['sq'],
                            NS['rstd'], extra_dst=(NS['knf'] if t0 >= LS else None), RO=[rL, R], WX=[rQK])
                    if t0 >= LS:
                        for j in range(4):
                            P.op('pe', lambda e: e.transpose(psum[7][:, j * 128:(j + 1) * 128],
                                                             NS['knf'][:, j * 128:(j + 1) * 128], identf),
                                 reads=[NS['R'], rP], writes=[rps[7]])
                            P.op('act', lambda e: e.activation(out=tst, in_=psum[7][:, j * 128:(j + 1) * 128], func=AF.Copy),
                                 reads=[rps[7]], writes=[rTst])
                            P.dma('sp', dr['o_na_k'][t0 - LS + j * 128:t0 - LS + (j + 1) * 128, hs], tst, reads=[rTst])
                for pi, (_, delta) in enumerate(cfg.pat_list):
                    for kr in range(2):
                        e0 = 11 - delta - kr
                        P.dma('sp', stage32[kr * 64:(kr + 1) * 64, pi, :].rearrange("p (a b) -> p a b", a=8, b=64),
                              dr['s_tf'][h, e0:e0 + 8, :, :].rearrange("e ck cq -> ck e cq"), reads=[rTF], writes=[rE])
                for pi in range(NP):
                    P.op('act', lambda e, pi=pi: e.activation(out=stage32[:, pi, :], in_=stage32[:, pi, :], func=AF.Exp),
                         reads=[rE], writes=[rE])
                    P.op('dve', lambda e, pi=pi: e.tensor_tensor(out=EB[:, pi, :], in0=stage32[:, pi, :],
                                                                 in1=mask01[:, pi, :], op=ALU.mult), reads=[rE, R],
                         writes=[rE])
                RR = [rL, R, rE, rQK]
                ai = 0
                for qb in range(cfg.NQB):
                    keys = []
                    for (kt, pi) in cfg.qb_keys[qb]:
                        keys.append((kn[:, kt * 128:(kt + 1) * 128], None, vt[:, kt, :], EB[:, pi, :]))
                    for n_ in range(2):
                        keys.append((kcT[:, n_ * 128:(n_ + 1) * 128], None, vc[:, n_, :], None))
                    q_ = ai % 2
                    ai += 1
                    attn_block(qn[:, qb * 512:(qb + 1) * 512], None, keys, 512, o32s[q_], [rO3[q_]], rps, pts, rPt, recs[q_],
                               RO=RR, acc=((3, 4) if q_ == 0 else (5, 6)))
                    P.op('dve', lambda e: e.tensor_tensor(out=msts[q_], in0=o32s[q_], in1=sg[:, qb * 512:(qb + 1) * 512],
                                                          op=ALU.mult), reads=[rO3[q_], rL], writes=[rMst[q_]])
                    P.dma('pool', dr['s_mix'][hs, qb * 512:(qb + 1) * 512], msts[q_], reads=[rMst[q_]])
                for s_ in range(NPS):
                    t0 = LS + s_ * 256
                    keys = [(kn[:, t0 + n_ * 128:t0 + (n_ + 1) * 128], None, vt[:, t0 // 128 + n_, :], None)
                            for n_ in range(2)]
                    q_ = ai % 2
                    ai += 1
                    attn_block(qn[:, t0:t0 + 256], None, keys, 256, o32s[q_][:, :256], [rO3[q_]], rps, pts, rPt, recs[q_],
                               RO=RR, acc=((3, 4) if q_ == 0 else (5, 6)))
                    P.op('dve', lambda e: e.tensor_tensor(out=msts[q_][:, :256], in0=o32s[q_][:, :256], in1=sg[:, t0:t0 + 256],
                                                          op=ALU.mult), reads=[rO3[q_], rL], writes=[rMst[q_]])
                    P.dma('pool', dr['s_mix'][hs, t0:t0 + 256], msts[q_][:, :256], reads=[rMst[q_]])

        def phase_mixB():
            rps = new_phase()
            R = P.res('b')
            NB = NH * 128
            sgw32 = V3(alloc(NB), NH, 128)
            sgwb = V3(alloc(NB, BF16), NH, 128)
            sgwT = V3(alloc(NB, BF16), NH, 128)
            sgb = alloc(NB)
            sgn = alloc(BW)
            P.dma('sp', sgw32, dr['sg_w'].rearrange("g t s -> t g s"), writes=[R])
            P.dma('sp', sgb, dr['sg_b'].rearrange("(o n) -> o n", o=1).partition_broadcast(128), writes=[R])
            P.dma('sp', sgn, dr['sg_norm'].rearrange("(o n) -> o n", o=1).partition_broadcast(128), writes=[R])
            P.op('dve', lambda e: e.tensor_copy(out=sgwb, in_=sgw32), reads=[R], writes=[R])
            for g in range(NH):
                pb = psum[0][:].bitcast(BF16)
                P.op('pe', lambda e, g=g, pb=pb: e.transpose(pb[:, 0:128], sgwb[:, g, :], identb), reads=[R, rP],
                     writes=[rps[0]])
                P.op('dve', lambda e, g=g, pb=pb: e.tensor_copy(out=sgwT[:, g, :], in_=pb[:, 0:128]), reads=[rps[0]],
                     writes=[R])
            vb = alloc(BW, BF16)
            gv = alloc(BW)
            t1 = alloc(BW)
            t2 = alloc(BW)
            vn = alloc(BW, BF16)
            uT = alloc(NB, BF16)
            gT = alloc(NB, BF16)
            gu = alloc(NB)
            ss = alloc(1)
            rstd = alloc(1)
            mo = alloc(NB, BF16)
            rMo = P.res('mo')
            rL = P.res('ld')
            RR = [R, rL]
            for ti in range(T // 128):
                tk = slice(ti * 128, (ti + 1) * 128)
                P.dma('sp', vb, dr['s_vb'][tk, :], writes=[rL])
                P.dma('sp', V3(uT, NH, 128), dr['s_ub'][:, tk].rearrange("(g c) t -> c g t", c=128), writes=[rL])
                P.dma('sp', V3(gT, NH, 128), dr['s_gb'][:, tk].rearrange("(g c) t -> c g t", c=128), writes=[rL])
                gelu_ops(gv, vb, t1, t2, RR)
                P.op('act', lambda e: e.activation(out=t1, in_=gv, func=AF.Square, accum_out=ss), reads=RR, writes=RR)
                rstd_chain(rstd, ss, 1.0 / BW, RR, RR)
                P.op('dve', lambda e: e.scalar_tensor_tensor(out=vn, in0=gv, scalar=rstd, in1=sgn, op0=ALU.mult,
                                                             op1=ALU.mult), reads=RR, writes=RR)
                gelu_ops(gu, uT, t1[:, :NB], t2[:, :NB], RR)
                for g in range(NH):
                    b = (g * 128) // 512
                    P.op('pe', lambda e, g=g, b=b: e.matmul(psum[b][:, (g * 128) % 512:(g * 128) % 512 + 128],
                                                            lhsT=vn[:, g * 128:(g + 1) * 128], rhs=sgwT[:, g, :],
                                                            start=True, stop=True), reads=RR, writes=[rps[b]])
                for b in range((NB + 511) // 512):
                    w = min(512, NB - b * 512)
                    cs = slice(b * 512, b * 512 + w)
                    P.op('dve', lambda e, b=b, w=w, cs=cs: e.tensor_tensor(out=t1[:, cs], in0=psum[b][:, :w], in1=sgb[:, cs],
                                                                           op=ALU.add), reads=[rps[b]] + RR, writes=RR)
                P.op('dve', lambda e: e.tensor_tensor(out=gu, in0=gu, in1=t1[:, :NB], op=ALU.mult), reads=RR, writes=RR)
                silu_ops(t1[:, :NB], gT, t2[:, :NB], RR)
                P.op('dve', lambda e: e.tensor_tensor(out=mo, in0=gu, in1=t1[:, :NB], op=ALU.mult), reads=RR, writes=[rMo])
                P.dma('sp', dr['s_mix'][BW:2 * BW, tk].rearrange("(g c) t -> c g t", c=128), V3(mo, NH, 128), reads=[rMo])

        def phase_hgrn():
            rps = new_phase()
            R = P.res('c')
            hmat = V3(alloc(1024), 2, 512)
            P.dma('sp', hmat, dr['hmat'], writes=[R])
            maskb = V3(alloc(256, BF16), 2, 128)
            P.op('dve', lambda e: e.tensor_copy(out=maskb, in_=hmat[:, :, 0:128]), reads=[R], writes=[R])
            lbb = [alloc(BW) for _ in range(2)]
            oml = [alloc(BW) for _ in range(2)]
            tl = alloc(BW)
            og = alloc(1)
            P.dma('sp', og, dr['hgrn_out_norm'].rearrange("(p o) -> p o", o=1), writes=[R])
            for d in range(2):
                P.dma('sp', lbb[d], dr['hgrn_lb'][1, d:d + 1, :].partition_broadcast(128), writes=[R])
                P.dma('sp', tl, dr['hgrn_lb'][0, d:d + 1, :].partition_broadcast(128), writes=[R])
                P.op('dve', lambda e, d=d: e.tensor_tensor(out=lbb[d], in0=lbb[d], in1=tl, op=ALU.subtract), reads=[R],
                     writes=[R])
                sigmoid_ops(lbb[d], lbb[d], 1.0, [R])
                P.op('dve', lambda e, d=d: e.tensor_scalar(out=oml[d], in0=lbb[d], scalar1=-1.0, scalar2=1.0, op0=ALU.mult,
                                                           op1=ALU.add), reads=[R], writes=[R])
            z = alloc(BW)
            f = alloc(BW)
            gl = alloc(BW)
            kk = alloc(BW)
            khat = alloc(BW, BF16)
            ktil = alloc(BW, BF16)
            vt = alloc(BW, BF16)
            qraw = V3(alloc(BW, BF16), NH, 128)
            gcT = V3(alloc(BW, BF16), NH, 128)
            S = V3(alloc(NH * 128), NH, 128)
            EEs = [alloc(256) for _ in range(4)]
            qincs = [alloc(128, BF16) for _ in range(4)]
            Sb = V3(alloc(NH * 128, BF16), NH, 128)
            rSb = [P.res(f'Sb{h}') for h in range(NH)]
            qtils = [alloc(128, BF16) for _ in range(4)]
            ktTs = [alloc(128, BF16) for _ in range(4)]
            pTms = [alloc(128, BF16) for _ in range(4)]
            rA = [rps[2], rps[2], rps[3], rps[3]]
            rEE = [P.res(f'EE{j}') for j in range(4)]
            rQi = [P.res(f'Qi{j}') for j in range(4)]
            rQt = [P.res(f'Qt{j}') for j in range(4)]
            rTr = [rps[4]] * 4
            rKt = [P.res(f'Kt{j}') for j in range(4)]
            rPT = [rps[5]] * 4
            rPm = [P.res(f'Pm{j}') for j in range(4)]
            rOj = [rps[6]] * 4
            rU = [rps[7]] * 4
            oall = V3(alloc(BW), NH, 128)
            ofl = V3(alloc(BW), NH, 128)
            sqb = alloc(BW, BF16)
            mo = alloc(BW, BF16)
            rMo = P.res('mo')
            rL = P.res('ld')
            rS = [P.res(f'S{h}') for h in range(NH)]
            rT = P.res('tm')
            rF = P.res('fmh')
            rO = P.res('oall')
            nbk = (BW + 511) // 512
            seqs = [(0, LS // 128, None)] + [((LS + s_ * 256) // 128, 2, s_) for s_ in range(NPS)]
            for d in range(2):
                zname = 's_ffw' if d == 0 else 's_fbw'
                Mi = hmat[:, d, 0:128]
                MiMc = hmat[:, d, 0:256]
                Mc = hmat[:, d, 128:256]
                Mr = hmat[:, d, 256:384]
                for (tile0, ntl, ps_idx) in seqs:
                    if ps_idx is None:
                        P.dma('sp', S, dr['st_hgrn'][d].rearrange("h k v -> k h v"), writes=rS)
                    else:
                        P.op('dve', lambda e: e.memset(S, 0.0), writes=rS)
                    P.op('act', lambda e: e.activation(out=Sb, in_=S, func=AF.Copy), reads=rS, writes=rSb)
                    order = range(ntl) if d == 0 else range(ntl - 1, -1, -1)
                    for tl_ in order:
                        ti = tile0 + tl_
                        tk = slice(ti * 128, (ti + 1) * 128)
                        P.dma('sp', z, dr[zname][tk, :], writes=[rL])
                        P.dma('sp', vt, dr['s_ic'][tk, :], writes=[rL])
                        P.dma('sp', qraw, dr['s_qc'][:, tk].rearrange("(h k) t -> k h t", k=128), writes=[rL])
                        if d == 1:
                            P.dma('sp', gcT, dr['s_gc'][:, tk].rearrange("(h k) t -> k h t", k=128), writes=[rL])
                            P.dma('sp', ofl, dr['s_of'][:, tk].rearrange("(h k) t -> k h t", k=128), writes=[rL])
                        RT = [rL, R, rT]
                        sigmoid_ops(f, z, 1.0, RT)
                        P.op('dve', lambda e, d=d: e.tensor_tensor(out=f, in0=f, in1=oml[d], op=ALU.mult), reads=RT, writes=RT)
                        P.op('dve', lambda e, d=d: e.tensor_tensor(out=f, in0=f, in1=lbb[d], op=ALU.add), reads=RT, writes=RT)
                        P.op('act', lambda e: e.activation(out=gl, in_=f, func=AF.Ln), reads=RT, writes=RT)
                        P.op('dve', lambda e: e.tensor_scalar(out=kk, in0=f, scalar1=-1.0, scalar2=1.0, op0=ALU.mult,
                                                              op1=ALU.add), reads=RT, writes=RT)
                        for (Mx, dst, sc_) in ((Mr, khat, 1.0), (Mc, ktil, -1.0)):
                            for b in range(nbk):
                                w = min(512, BW - b * 512)
                                pbk = b % 2
                                P.op('pe', lambda e: e.matmul(psum[pbk][:, :w], lhsT=Mx, rhs=gl[:, b * 512:b * 512 + w],
                                                              start=True, stop=True), reads=RT, writes=[rps[pbk]])
                                P.op('act', lambda e: e.activation(out=z[:, b * 512:b * 512 + w], in_=psum[pbk][:, :w],
                                                                   func=AF.Exp, scale=sc_), reads=[rps[pbk]] + RT, writes=RT)
                            P.op('dve', lambda e, dst=dst: e.tensor_tensor(out=dst, in0=kk, in1=z, op=ALU.mult), reads=RT,
                                 writes=RT)
                        G = min(4, NH)
                        corder = list(range(4)) if d == 0 else list(range(3, -1, -1))
                        for hg in range(0, NH, G):
                            hh = list(range(hg, hg + G))
                            RB = [rL, R, rT]
                            for j, h in enumerate(hh):
                                hs = slice(h * 128, (h + 1) * 128)
                                bA = 2 + j // 2
                                cA = (j % 2) * 256
                                P.op('pe', lambda e: e.matmul(psum[bA][:, cA:cA + 256], lhsT=gl[:, hs], rhs=MiMc, start=True,
                                                              stop=True), reads=RT, writes=[rA[j]])
                            for j, h in enumerate(hh):
                                bA = 2 + j // 2
                                cA = (j % 2) * 256
                                P.op('act', lambda e: e.activation(out=EEs[j], in_=psum[bA][:, cA:cA + 256], func=AF.Exp),
                                     reads=[rA[j], rS[h]], writes=[rEE[j]])
                            for j, h in enumerate(hh):
                                P.op('dve', lambda e: e.tensor_tensor(out=qincs[j], in0=qraw[:, h, :], in1=EEs[j][:, 0:128],
                                                                      op=ALU.mult), reads=[rEE[j], rL], writes=[rQi[j]])
                                P.op('dve', lambda e: e.tensor_tensor(out=qtils[j], in0=qraw[:, h, :], in1=EEs[j][:, 128:256],
                                                                      op=ALU.mult), reads=[rEE[j], rL], writes=[rQt[j]])
                            pb = psum[4][:].bitcast(BF16)
                            for j, h in enumerate(hh):
                                hs = slice(h * 128, (h + 1) * 128)
                                P.op('pe', lambda e: e.transpose(pb[:, j * 128:(j + 1) * 128], ktil[:, hs], identb),
                                     reads=RT + [rP], writes=[rTr[j]])
                            for j, h in enumerate(hh):
                                P.op('act', lambda e: e.activation(out=ktTs[j], in_=pb[:, j * 128:(j + 1) * 128], func=AF.Copy),
                                     reads=[rTr[j]], writes=[rKt[j]])
                            for j, h in enumerate(hh):
                                P.op('pe', lambda e: e.matmul(psum[5][:, j * 128:(j + 1) * 128], lhsT=ktTs[j], rhs=qtils[j],
                                                              start=True, stop=True), reads=[rKt[j], rQt[j]], writes=[rPT[j]])
                            for j, h in enumerate(hh):
                                P.op('dve', lambda e: e.tensor_tensor(out=pTms[j], in0=psum[5][:, j * 128:(j + 1) * 128],
                                                                      in1=maskb[:, d, :], op=ALU.mult), reads=[rPT[j], R],
                                     writes=[rPm[j]])
                            for j, h in enumerate(hh):
                                hs = slice(h * 128, (h + 1) * 128)
                                P.op('pe', lambda e: e.matmul(psum[6][:, j * 128:(j + 1) * 128], lhsT=vt[:, hs], rhs=pTms[j],
                                                              start=(j == 0), stop=False), reads=[rPm[j], rL], writes=[rOj[j]],
                                     sig=False)
                            for ci, c in enumerate(corder):
                                cs = slice(c * 32, (c + 1) * 32)
                                col = c * 32 + 31 if d == 0 else c * 32
                                for j, h in enumerate(hh):
                                    hs = slice(h * 128, (h + 1) * 128)
                                    P.op('pe', lambda e: e.matmul(psum[6][:, j * 128 + c * 32:j * 128 + (c + 1) * 32],
                                                                  lhsT=Sb[:, h, :], rhs=qincs[j][:, cs], start=False,
                                                                  stop=(ci == 3 and j == G - 1)), reads=[rQi[j], rSb[h]], writes=[rOj[j]],
                                         sig=True)
                                    P.op('pe', lambda e: e.matmul(psum[7][:, j * 128:(j + 1) * 128], lhsT=khat[cs, hs],
                                                                  rhs=vt[cs, hs], start=True, stop=True,
                                                                  tile_position=(c * 32, 0)), reads=RT, writes=[rU[j]])
                                for j, h in enumerate(hh):
                                    P.op('dve', lambda e: e.scalar_tensor_tensor(
                                        out=S[:, h, :], in0=S[:, h, :], scalar=EEs[j][:, col:col + 1],
                                        in1=psum[7][:, j * 128:(j + 1) * 128], op0=ALU.mult, op1=ALU.add),
                                        reads=[rU[j], rEE[j]], writes=[rS[h]])
                                for j, h in enumerate(hh):
                                    P.op('act', lambda e: e.activation(out=Sb[:, h, :], in_=S[:, h, :], func=AF.Copy),
                                         reads=[rS[h]], writes=[rSb[h]])
                            P.op('act', lambda e: e.activation(out=oall[:, hg:hg + G, :], in_=V3(psum[6][:, 0:G * 128], G, 128),
                                                               func=AF.Copy), reads=rOj[:G], writes=[rO])
                        if d == 0:
                            P.dma('sp', dr['s_of'][:, tk].rearrange("(h k) t -> k h t", k=128), oall, reads=[rO])
                        else:
                            RO = [rO, rL, R]
                            oa2 = oall.rearrange("p a b -> p (a b)")
                            P.op('dve', lambda e: e.tensor_tensor(out=oall, in0=oall, in1=ofl, op=ALU.add), reads=RO, writes=RO)
                            P.op('act', lambda e: e.activation(out=sqb, in_=oa2, func=AF.Square), reads=RO, writes=RO)
                            for b in range(nbk):
                                w = min(512, BW - b * 512)
                                P.op('pe', lambda e, b=b, w=w: e.matmul(psum[b][:, :w], lhsT=onesb, rhs=sqb[:, b * 512:b * 512 + w],
                                                                        start=True, stop=True), reads=RO + [rP], writes=[rps[b]])
                                rstd_chain(f[:, b * 512:b * 512 + w], psum[b][:, :w], 1.0 / 128, [rps[b]] + RO + [rT], RO + [rT])
                            P.op('dve', lambda e: e.scalar_tensor_tensor(out=oa2, in0=oa2, scalar=og, in1=f, op0=ALU.mult,
                                                                         op1=ALU.mult), reads=RO + [rT], writes=RO)
                            g2 = gcT.rearrange("p a b -> p (a b)")
                            silu_ops(gl, g2, kk, RO + [rT])
                            P.op('dve', lambda e: e.tensor_tensor(out=mo, in0=oa2, in1=gl, op=ALU.mult), reads=RO + [rT],
                                 writes=[rMo])
                            P.dma('sp', dr['s_mix'][0:BW, tk].rearrange("(h k) t -> k h t", k=128), V3(mo, NH, 128),
                                  reads=[rMo])
                    if ps_idx is not None:
                        P.dma('sp', dr['o_hgrn'][ps_idx, d].rearrange("h k v -> k h v"), S, reads=rS)

        def phase_mla():
            rps = new_phase()
            R = P.res('d')
            NK = 256 + T
            ckvT = V3(alloc(4 * NK, BF16), 4, NK)
            k2T = alloc(NK, BF16)
            qag = alloc(KQ)
            kvag = alloc(4)
            qng = alloc(1)
            qrg = alloc(1)
            kng = alloc(1)
            krg = alloc(1)
            prot = alloc(64)
            with_nc = dict(allow_slow_non_contiguous=True)
            P.dma('sp', qag, dr['mla_q_a_norm'].rearrange("(k p) -> p k", p=128), writes=[R], **with_nc)
            P.dma('sp', kvag, dr['mla_kv_a_norm'].rearrange("(k p) -> p k", p=128), writes=[R], **with_nc)
            P.dma('sp', qng, dr['mla_q_norm'][0:128].rearrange("(p o) -> p o", o=1), writes=[R])
            P.dma('sp', qrg[:64], dr['mla_q_norm'][128:192].rearrange("(p o) -> p o", o=1), writes=[R])
            P.dma('sp', kng, dr['mla_k_norm'][0:128].rearrange("(p o) -> p o", o=1), writes=[R])
            P.dma('sp', krg[:64], dr['mla_k_norm'][128:192].rearrange("(p o) -> p o", o=1), writes=[R])
            P.dma('sp', prot[:64], dr['prot'], writes=[R])
            P.op('dve', lambda e: e.tensor_scalar_mul(out=qng, in0=qng, scalar1=192.0 ** -0.5), reads=[R], writes=[R])
            P.op('dve', lambda e: e.tensor_scalar_mul(out=qrg[:64], in0=qrg[:64], scalar1=192.0 ** -0.5), reads=[R], writes=[R])
            mla_mark = pos[0]
            raw = alloc(max(KQ, 4) * 512, BF16)
            sq = alloc(max(KQ, 4) * 512, BF16)
            rstd = alloc(512)
            cqn = V3(alloc(KQ * 512, BF16), KQ, 512)
            c32 = alloc(512)
            x32 = alloc(512)
            cosb = alloc(512)
            sinb = alloc(512)
            tst = alloc(512)
            rTst = P.res('tst')
            rL = P.res('ld')
            rCq = P.res('cqn')
            RR = [R, rL]
            cc = V3(alloc(1024, BF16), 2, 512)
            ck = V3(alloc(128, BF16), 2, 64)
            P.dma('pool', cc, dr['c_ckv'].rearrange("(n p) c -> p n c", p=128), writes=[rL])
            P.dma('pool', ck, dr['c_kr'].rearrange("(n p) c -> p n c", p=128), writes=[rL])
            for n_ in range(2):
                pb = psum[0][:].bitcast(BF16)
                for k in range(4):
                    P.op('pe', lambda e, n_=n_, k=k, pb=pb: e.transpose(pb[:, k * 128:(k + 1) * 128],
                                                                        cc[:, n_, k * 128:(k + 1) * 128], identb),
                         reads=[rL, rP], writes=[rps[0]])
                P.op('dve', lambda e, n_=n_, pb=pb: e.tensor_copy(out=ckvT[:, :, n_ * 128:(n_ + 1) * 128],
                                                                  in_=V3(pb[:, 0:512], 4, 128)), reads=[rps[0]], writes=[R])
                P.op('pe', lambda e, n_=n_, pb=pb: e.transpose(pb[:64, 512:640], ck[:, n_, :], identb), reads=[rL, rP],
                     writes=[rps[0]])
                P.op('dve', lambda e, n_=n_, pb=pb: e.tensor_copy(out=k2T[:64, n_ * 128:(n_ + 1) * 128], in_=pb[:64, 512:640]),
                     reads=[rps[0]], writes=[R])

            def rope(dst, src, t0, n):
                P.dma('sp', cosb[:64, :n], dr['ropecos'][:, t0:t0 + n], writes=[rL])
                P.dma('sp', sinb[:64, :n], dr['ropesin'][:, t0:t0 + n], writes=[rL])
                P.op('pe', lambda e: e.matmul(psum[7][:64, :n], lhsT=prot[:64, :64], rhs=src, start=True, stop=True),
                     reads=RR, writes=[rps[7]])
                P.op('dve', lambda e: e.tensor_tensor(out=sinb[:64, :n], in0=psum[7][:64, :n], in1=sinb[:64, :n], op=ALU.mult),
                     reads=[rps[7], rL], writes=[rL])
                P.op('dve', lambda e: e.tensor_tensor(out=cosb[:64, :n], in0=src, in1=cosb[:64, :n], op=ALU.mult),
                     reads=RR, writes=[rL])
                P.op('dve', lambda e: e.tensor_tensor(out=dst, in0=cosb[:64, :n], in1=sinb[:64, :n], op=ALU.add),
                     reads=[rL], writes=RR)

            for t0 in range(0, T, 512):
                is_p = t0 >= LS
                r3 = V3(raw[:, :KQ * 512], KQ, 512)
                P.dma('sp', r3, dr['s_cq'][:, t0:t0 + 512].rearrange("(k p) t -> p k t", p=128), writes=[rL])
                P.op('act', lambda e: e.activation(out=sq[:, :KQ * 512], in_=raw[:, :KQ * 512], func=AF.Square), reads=RR, writes=RR)
                for k in range(KQ):
                    P.op('pe', lambda e, k=k: e.matmul(psum[1][:], lhsT=onesb, rhs=sq[:, k * 512:(k + 1) * 512], start=(k == 0),
                                                       stop=(k == KQ - 1)), reads=RR + [rP], writes=[rps[1]], sig=(k == KQ - 1))
                rstd_chain(rstd, psum[1][:], 1.0 / QL, [rps[1]] + RR, RR)
                for k in range(KQ):
                    P.op('dve', lambda e, k=k: e.scalar_tensor_tensor(out=cqn[:, k, :], in0=r3[:, k, :], scalar=qag[:, k:k + 1],
                                                                      in1=rstd, op0=ALU.mult, op1=ALU.mult), reads=RR,
                         writes=[rCq])
                P.dma('sp', dr['s_cqn'][:, t0:t0 + 512].rearrange("(k p) t -> p k t", p=128), cqn, reads=[rCq])
                r3 = V3(raw[:, :4 * 512], 4, 512)
                P.dma('sp', r3, dr['s_ckv'][:, t0:t0 + 512].rearrange("(k p) t -> p k t", p=128), writes=[rL])
                P.op('act', lambda e: e.activation(out=sq[:, :4 * 512], in_=raw[:, :4 * 512], func=AF.Square), reads=RR, writes=RR)
                for k in range(4):
                    P.op('pe', lambda e, k=k: e.matmul(psum[2][:], lhsT=onesb, rhs=sq[:, k * 512:(k + 1) * 512], start=(k == 0),
                                                       stop=(k == 3)), reads=RR + [rP], writes=[rps[2]], sig=(k == 3))
                rstd_chain(rstd, psum[2][:], 1.0 / 512, [rps[2]] + RR, RR)
                for k in range(4):
                    P.op('dve', lambda e, k=k, r3=r3: e.scalar_tensor_tensor(
                        out=ckvT[:, k, 256 + t0:256 + t0 + 512], in0=r3[:, k, :], scalar=kvag[:, k:k + 1], in1=rstd,
                        op0=ALU.mult, op1=ALU.mult), reads=RR, writes=RR)
                    if is_p:
                        P.op('dve', lambda e, k=k, r3=r3: e.scalar_tensor_tensor(
                            out=c32, in0=r3[:, k, :], scalar=kvag[:, k:k + 1], in1=rstd, op0=ALU.mult, op1=ALU.mult),
                            reads=RR, writes=RR)
                        for j in range(4):
                            P.op('pe', lambda e, j=j: e.transpose(psum[3][:, j * 128:(j + 1) * 128],
                                                                  c32[:, j * 128:(j + 1) * 128], identf), reads=RR + [rP],
                                 writes=[rps[3]])
                        P.op('act', lambda e: e.activation(out=tst, in_=psum[3][:], func=AF.Copy), reads=[rps[3]], writes=[rTst])
                        P.dma('sp', dr['o_ckv'][t0 - LS:t0 - LS + 512, k * 128:(k + 1) * 128].rearrange("(j p) c -> p j c", p=128),
                              V3(tst, 4, 128), reads=[rTst])
                P.dma('sp', raw[:64, :512], dr['s_kr'][:, t0:t0 + 512], writes=[rL])
                fm_norm(x32[:64, :], raw[:64, :512], krg[:64], 64, 512, RR, rps, 4, sq, rstd)
                if is_p:
                    P.op('dve', lambda e, t0=t0: e.tensor_copy(out=k2T[:64, 256 + t0:256 + t0 + 512], in_=x32[:64, :]), reads=RR,
                         writes=RR)
                    for j in range(4):
                        P.op('pe', lambda e, j=j: e.transpose(psum[3][:, j * 64:(j + 1) * 64], x32[:64, j * 128:(j + 1) * 128],
                                                              identf[:64, :64]), reads=RR + [rP], writes=[rps[3]])
                    P.op('act', lambda e: e.activation(out=tst[:, :256], in_=psum[3][:, :256], func=AF.Copy), reads=[rps[3]],
                         writes=[rTst])
                    P.dma('sp', dr['o_kr'][t0 - LS:t0 - LS + 512, :].rearrange("(j p) c -> p j c", p=128), V3(tst[:, :256], 4, 64),
                          reads=[rTst])
                else:
                    rope(k2T[:64, 256 + t0:256 + t0 + 512], x32[:64, :], t0, 512)
            P.barrier(release=False)
            pos[0] = mla_mark
            sets = []
            for i in range(2):
                sets.append(dict(x32=alloc(512), c32=alloc(512), sq=alloc(512, BF16), rstd=alloc(512), cos=alloc(512),
                                 sin=alloc(512), cqn=V3(alloc(KQ * 512, BF16), KQ, 512), qn=alloc(512, BF16),
                                 qr=alloc(512, BF16), R=P.res(f'set{i}'), Rc=P.res(f'setc{i}'), Rq=P.res(f'setq{i}'),
                                 Rr=P.res(f'setr{i}'), banks=(3 * i, 3 * i + 1, 3 * i + 2)))
            wqs = [V3(alloc(KQ * 192, BF16), KQ, 192) for _ in range(2)]
            rWq = [P.res('wq0'), P.res('wq1')]
            blocks = [(qb * 512, 512, 0, (256 + LS) // 128, True) for qb in range(LS // 512)]
            blocks += [(LS + s_ * 256, 256, (256 + LS + s_ * 256) // 128, 2, False) for s_ in range(NPS)]
            bi = 0
            for h in range(NH):
                wq = wqs[h % 2]
                P.dma('pool', wq, dr['mla_w_q_up'][:, h * 192:(h + 1) * 192].rearrange("(k p) c -> p k c", p=128),
                      writes=[rWq[h % 2]])
                for (t0, nq, kt0, nkt, rot) in blocks:
                    S_ = sets[bi % 2]
                    bi += 1
                    bA, bB, bC = S_['banks']
                    x32s, c32s, sqs, rstds, cqs = S_['x32'], S_['c32'], S_['sq'], S_['rstd'], S_['cqn']
                    RS = [S_['R']]
                    P.dma('sp', cqs[:, :, :nq], dr['s_cqn'][:, t0:t0 + nq].rearrange("(k p) t -> p k t", p=128),
                          writes=[S_['Rc']])
                    for k in range(KQ):
                        P.op('pe', lambda e: e.matmul(psum[bA][:, :nq], lhsT=wq[:, k, 0:128], rhs=cqs[:, k, :nq],
                                                      start=(k == 0), stop=(k == KQ - 1)), reads=[S_['Rc'], rWq[h % 2]],
                             writes=[rps[bA]], sig=(k == KQ - 1))
                    P.op('act', lambda e: e.activation(out=x32s[:, :nq], in_=psum[bA][:, :nq], func=AF.Copy), reads=[rps[bA]],
                         writes=RS)
                    fm_norm(S_['qn'][:, :nq], x32s[:, :nq], qng, 128, nq, RS, rps, bB, sqs, rstds, RO=[R])
                    P.op('dve', lambda e: e.tensor_copy(out=S_['qn'][:, 0:1], in_=S_['qn'][:, 0:1]), reads=RS, writes=[S_['Rq']])
                    P.dma('pool', dr['s_qn'][h * 128:(h + 1) * 128, t0:t0 + nq], S_['qn'][:, :nq], reads=[S_['Rq']])
                    for k in range(KQ):
                        P.op('pe', lambda e: e.matmul(psum[bA][:64, :nq], lhsT=wq[:, k, 128:192], rhs=cqs[:, k, :nq],
                                                      start=(k == 0), stop=(k == KQ - 1)), reads=[S_['Rc'], rWq[h % 2]],
                             writes=[rps[bA]], sig=(k == KQ - 1))
                    P.op('act', lambda e: e.activation(out=x32s[:64, :nq], in_=psum[bA][:64, :nq], func=AF.Copy),
                         reads=[rps[bA]], writes=RS)
                    if rot:
                        fm_norm(c32s[:64, :nq], x32s[:64, :nq], qrg[:64], 64, nq, RS, rps, bB, sqs, rstds, RO=[R])
                        cosb_, sinb_ = S_['cos'], S_['sin']
                        P.dma('sp', cosb_[:64, :nq], dr['ropecos'][:, t0:t0 + nq], writes=[S_['Rr']])
                        P.dma('sp', sinb_[:64, :nq], dr['ropesin'][:, t0:t0 + nq], writes=[S_['Rr']])
                        P.op('pe', lambda e: e.matmul(psum[bC][:64, :nq], lhsT=prot[:64, :64], rhs=c32s[:64, :nq], start=True,
                                                      stop=True), reads=RS + [R], writes=[rps[bC]])
                        P.op('dve', lambda e: e.tensor_tensor(out=sinb_[:64, :nq], in0=psum[bC][:64, :nq], in1=sinb_[:64, :nq],
                                                              op=ALU.mult), reads=[rps[bC]], writes=[S_['Rr']])
                        P.op('dve', lambda e: e.tensor_tensor(out=cosb_[:64, :nq], in0=c32s[:64, :nq], in1=cosb_[:64, :nq],
                                                              op=ALU.mult), reads=RS, writes=[S_['Rr']])
                        P.op('dve', lambda e: e.tensor_tensor(out=S_['qr'][:64, :nq], in0=cosb_[:64, :nq], in1=sinb_[:64, :nq],
                                                              op=ALU.add), reads=[S_['Rr']], writes=[S_['Rq']])
                    else:
                        fm_norm(S_['qr'][:64, :nq], x32s[:64, :nq], qrg[:64], 64, nq, RS, rps, bB, sqs, rstds, RO=[R])
                        P.op('dve', lambda e: e.tensor_copy(out=S_['qr'][:64, 0:1], in_=S_['qr'][:64, 0:1]), reads=RS,
                             writes=[S_['Rq']])
                    P.dma('pool', dr['s_qr'][h * 64:(h + 1) * 64, t0:t0 + nq], S_['qr'][:64, :nq], reads=[S_['Rq']])
            P.barrier(release=False)
            wkv = V3(alloc(4 * 256, BF16), 4, 256)
            kT = alloc(NK, BF16)
            vt = V3(alloc(NK, BF16), NK // 128, 128)
            gd = alloc(T, BF16)
            qnb = [alloc(512, BF16) for _ in range(2)]
            qrb = [alloc(512, BF16) for _ in range(2)]
            rQb = [P.res('qb0'), P.res('qb1')]
            pts = [alloc(512, BF16) for _ in range(3)]
            rPt = [P.res('pt0'), P.res('pt1'), P.res('pt2')]
            recs = [alloc(512) for _ in range(2)]
            o32s = [alloc(512) for _ in range(2)]
            msts = [alloc(512, BF16) for _ in range(2)]
            rMst = [P.res('mst0'), P.res('mst1')]
            rK = P.res('wkv')
            rKT = P.res('kT')
            rVT = P.res('vT')
            rG = P.res('gd')
            rO3 = [P.res('o320'), P.res('o321')]
            bi = 0
            for h in range(NH):
                hs = slice(h * 128, (h + 1) * 128)
                P.dma('pool', wkv, dr['mla_w_kv_up'][:, h * 256:(h + 1) * 256].rearrange("(k p) c -> p k c", p=128), writes=[rK])
                P.dma('sp', gd, dr['s_gd'][hs, :], writes=[rG])
                for t0 in range(0, T, 512):
                    silu_ops(gd[:, t0:t0 + 512], gd[:, t0:t0 + 512], None, [rG])
                for ci_, c0 in enumerate(range(0, NK, 512)):
                    n = min(512, NK - c0)
                    S_ = sets[ci_ % 2]
                    pb_ = 5 + (ci_ % 2)
                    for k in range(4):
                        P.op('pe', lambda e: e.matmul(psum[pb_][:, :n], lhsT=wkv[:, k, 0:128], rhs=ckvT[:, k, c0:c0 + n],
                                                      start=(k == 0), stop=(k == 3)), reads=[R, rK], writes=[rps[pb_]],
                             sig=(k == 3))
                    P.op('act', lambda e: e.activation(out=S_['x32'][:, :n], in_=psum[pb_][:, :n], func=AF.Copy),
                         reads=[rps[pb_]], writes=[S_['R']])
                    fm_norm(kT[:, c0:c0 + n], S_['x32'][:, :n], kng, 128, n, [S_['R']], rps, 7, S_['sq'], S_['rstd'], RO=[R],
                            WX=[rKT])
                for n_ in range(NK // 128):
                    b = 5 + (n_ % 2)
                    for k in range(4):
                        P.op('pe', lambda e: e.matmul(psum[b][:, 0:128], lhsT=ckvT[:, k, n_ * 128:(n_ + 1) * 128],
                                                      rhs=wkv[:, k, 128:256], start=(k == 0), stop=(k == 3)), reads=[R, rK],
                             writes=[rps[b]], sig=(k == 3))
                    evac(vt[:, n_, :], psum[b][:, 0:128], [rps[b]], [rVT])
                for (t0, nq, kt0, nkt, rot) in blocks:
                    q_ = bi % 2
                    bi += 1
                    P.dma('sp', qnb[q_][:, :nq], dr['s_qn'][hs, t0:t0 + nq], writes=[rQb[q_]])
                    P.dma('sp', qrb[q_][:64, :nq], dr['s_qr'][h * 64:(h + 1) * 64, t0:t0 + nq], writes=[rQb[q_]])
                    keys = [(kT[:, (kt0 + i) * 128:(kt0 + i + 1) * 128], k2T[:64, (kt0 + i) * 128:(kt0 + i + 1) * 128],
                             vt[:, kt0 + i, :], None) for i in range(nkt)]
                    attn_block(qnb[q_][:, :nq], qrb[q_][:64, :nq], keys, nq, o32s[q_][:, :nq], [rO3[q_]], rps, pts, rPt,
                               recs[q_], RO=[R, rKT, rVT, rQb[q_]], acc=((3, 4) if q_ == 0 else (5, 6)))
                    P.op('dve', lambda e: e.tensor_tensor(out=msts[q_][:, :nq], in0=o32s[q_][:, :nq], in1=gd[:, t0:t0 + nq],
                                                          op=ALU.mult), reads=[rO3[q_], rG], writes=[rMst[q_]])
                    P.dma('pool', dr['s_mix'][BW + h * 128:BW + (h + 1) * 128, t0:t0 + nq], msts[q_][:, :nq], reads=[rMst[q_]])

        if ON('in0'):
            phase_inproj(0, dr['w_in_ab'], cfg.AB, dr['x_all'])
        if ON('mixA'):
            phase_mixA()
        if ON('mixB'):
            phase_mixB()
        if ON('out0'):
            phase_outproj(0, dr['x_all'], dr['s_x1'])
        if ON('in1'):
            phase_inproj(1, dr['w_in_cd'], cfg.CD, dr['s_x1'])
        if ON('hgrn'):
            phase_hgrn()
        if ON('mla'):
            phase_mla()
        if ON('out1'):
            phase_outproj(1, dr['s_x1'], dr['y_all'])
        P.emit()
    return nc


def prep_rp(na_rpb):
    NH = na_rpb.shape[0]
    rp = np.zeros((NH, 23, 127), np.float32)
    rp[:, 4:19, 48:79] = na_rpb[:, ::-1, ::-1]
    return rp


_CACHE = {}


def make_in_maps(cfg, inputs, ncores):
    consts = host_consts(cfg)
    NPS, LS, D = cfg.NPS, cfg.LS, cfg.D
    f = lambda a: np.ascontiguousarray(np.asarray(a))
    shared = {
        'norm_w': f(inputs['norm_w']), 'w_ada': f(inputs['w_ada']), 'b_ada': f(inputs['b_ada']),
        'w_out': f(inputs['w_out']), 'w_in_ab': f(inputs['w_in_ab'][0]), 'na_q_norm': f(inputs['na_q_norm'][0]),
        'na_k_norm': f(inputs['na_k_norm'][0]), 'rp': prep_rp(np.asarray(inputs['na_rpb'][0])),
        'sg_norm': f(inputs['sg_norm'][0]), 'sg_w': f(inputs['sg_w'][0]), 'sg_b': f(inputs['sg_b'][0]).reshape(-1),
        'w_in_cd': f(inputs['w_in_cd'][0]), 'hgrn_lb': f(inputs['hgrn_lb']),
        'hgrn_out_norm': f(inputs['hgrn_out_norm'][0]), 'mla_q_a_norm': f(inputs['mla_q_a_norm'][0]),
        'mla_w_q_up': f(inputs['mla_w_q_up'][0]), 'mla_kv_a_norm': f(inputs['mla_kv_a_norm'][0]),
        'mla_w_kv_up': f(inputs['mla_w_kv_up'][0]), 'mla_q_norm': f(inputs['mla_q_norm'][0]),
        'mla_k_norm': f(inputs['mla_k_norm'][0]),
    }
    shared.update(consts)
    maps = []
    for c in range(ncores):
        m = dict(shared)
        xs = np.asarray(inputs['x_sample'][c])
        xp = np.asarray(inputs['x_prompt'][c * NPS:(c + 1) * NPS]).reshape(NPS * 256, D)
        m['x_all'] = np.ascontiguousarray(np.concatenate([xs, xp], axis=0))
        m['cond2'] = np.ascontiguousarray(np.stack([np.asarray(inputs['c'][c]), np.asarray(inputs['c_ctx'])], axis=0))
        m['c_na_k'] = f(inputs['cache_na_k'][c, 0]).reshape(256, -1)
        m['c_na_v'] = f(inputs['cache_na_v'][c, 0]).reshape(256, -1)
        m['st_hgrn'] = f(inputs['state_hgrn'][c, 0])
        m['c_ckv'] = f(inputs['cache_mla_ckv'][c, 0])
        m['c_kr'] = f(inputs['cache_mla_krope'][c, 0])
        maps.append(m)
    return maps


def assemble(cfg, results, ncores):
    NPS, LS, D, NH, BW = cfg.NPS, cfg.LS, cfg.D, cfg.NH, cfg.BW
    yp, ys, nk, nv, hg, ckv, kr = [], [], [], [], [], [], []
    for c in range(ncores):
        r = results[c]
        ya = np.asarray(r['y_all'])
        ys.append(ya[:LS][None])
        yp.append(ya[LS:].reshape(NPS, 256, D))
        nk.append(np.asarray(r['o_na_k']).reshape(NPS, 1, 256, NH, 128))
        nv.append(np.asarray(r['o_na_v']).reshape(NPS, 1, 256, NH, 128))
        hg.append(np.asarray(r['o_hgrn']).reshape(NPS, 1, 2, NH, 128, 128))
        ckv.append(np.asarray(r['o_ckv']).reshape(NPS, 1, 256, 512))
        kr.append(np.asarray(r['o_kr']).reshape(NPS, 1, 256, 64))
    cat = lambda l: np.ascontiguousarray(np.concatenate(l, axis=0).astype(np.float32))
    return (cat(yp), cat(ys), cat(nk), cat(nv), cat(hg), cat(ckv), cat(kr))


def kernel(**inputs):
    cfg = Cfg(D=4096, LS=4096, NPS=4)
    ncores = 8
    if 'nc' not in _CACHE:
        _CACHE['nc'] = build_program(cfg)
    nc = _CACHE['nc']
    maps = make_in_maps(cfg, inputs, ncores)
    res = run_bass_kernel_spmd(nc, maps, core_ids=list(range(ncores)))
    return assemble(cfg, res.results, ncores)
```
